# Optimizing a Trainium2 kernel written in Bass

```python
import jax, jax.numpy as jnp
from jax import lax
import numpy as np

D_MODEL = 4096
BATCH = 8
SEQ = 2048
DEPTH = 1
DEC_BATCH = 16
DEC_SEQ = 32
PAST_LEN = 1024

CHUNK = 64
HEAD_DIM = 128
A_HEADS = D_MODEL // (2 * HEAD_DIM)
B_HEADS = D_MODEL // (2 * HEAD_DIM)
A_WIDTH = A_HEADS * HEAD_DIM
B_WIDTH = B_HEADS * HEAD_DIM
CONV_W = 4
FOX_QBLK = 128
FOX_F_BIAS_INIT = 4.0
PEER_HEADS = 8
PEER_NKEYS = 128
N_EXPERTS = PEER_NKEYS * PEER_NKEYS
PEER_TOPK = 16
PEER_DKEY = 256
PEER_BLK = 64
DN_ALPHA = (2 * DEPTH) ** 0.25
DN_BETA = (8 * DEPTH) ** -0.25
LN_EPS = 1e-5
RMS_EPS = 1e-6
OFF_A_Z = 3 * A_WIDTH
OFF_A_A = OFF_A_Z + A_WIDTH
OFF_A_B = OFF_A_A + A_HEADS
OFF_B_QKV = OFF_A_B + A_HEADS
OFF_B_F = OFF_B_QKV + 3 * B_WIDTH
PROJ_COLS = OFF_B_F + B_HEADS

kernel_name = 'hybrid_gdn_fox_peer_stream_step'


def layer_norm(x, g, b):
    xf = x.astype(jnp.float32)
    mu = jnp.mean(xf, axis=-1, keepdims=True)
    var = jnp.mean(jnp.square(xf - mu), axis=-1, keepdims=True)
    return ((xf - mu) * lax.rsqrt(var + LN_EPS) * g.astype(jnp.float32) + b.astype(jnp.float32)).astype(x.dtype)


def l2_normalize(x):
    return x * lax.rsqrt(jnp.sum(x * x, axis=-1, keepdims=True) + 1e-6)


def causal_dwconv(xfull, w):
    c = xfull.shape[-1]
    return lax.conv_general_dilated(xfull, w[:, None, :], window_strides=(1,), padding='VALID',
                                    dimension_numbers=('NWC', 'WIO', 'NWC'), feature_group_count=c)


def chunked_gated_delta(q, k, v, g, beta, s0, chunk):
    bn, t, h, _ = q.shape
    n = t // chunk

    def blocks(a):
        a = a.reshape((bn, n, chunk) + a.shape[2:])
        return jnp.moveaxis(jnp.moveaxis(a, 2, 3), 1, 0)

    qc, kc, vc, gc, bc = blocks(q), blocks(k), blocks(v), blocks(g), blocks(beta)
    cum = jnp.cumsum(gc, axis=-1)
    idx = jnp.arange(chunk)
    causal = idx[:, None] >= idx[None, :]
    strict = idx[:, None] > idx[None, :]
    decay = jnp.exp(jnp.where(causal, cum[..., :, None] - cum[..., None, :], -jnp.inf))
    kk = jnp.einsum('nbhik,nbhjk->nbhij', kc, kc)
    lower = jnp.where(strict, bc[..., :, None] * kk * decay, 0.0)
    eye = jnp.eye(chunk, dtype=jnp.float32)
    tinv = lax.linalg.triangular_solve(eye + lower, jnp.broadcast_to(eye, lower.shape),
                                       left_side=True, lower=True, unit_diagonal=True)
    w_v = jnp.einsum('nbhij,nbhjv->nbhiv', tinv, bc[..., None] * vc)
    w_k = jnp.einsum('nbhij,nbhjk->nbhik', tinv, (bc * jnp.exp(cum))[..., None] * kc)
    qk = jnp.einsum('nbhik,nbhjk->nbhij', qc, kc) * decay
    q_dec = qc * jnp.exp(cum)[..., None]
    k_dec = kc * jnp.exp(cum[..., -1:] - cum)[..., None]
    g_tot = jnp.exp(cum[..., -1])[..., None, None]

    def step(s, xs):
        w_v_n, w_k_n, qk_n, q_dec_n, k_dec_n, g_n = xs
        u = w_v_n - jnp.einsum('bhik,bhkv->bhiv', w_k_n, s)
        o = jnp.einsum('bhik,bhkv->bhiv', q_dec_n, s) + jnp.einsum('bhij,bhjv->bhiv', qk_n, u)
        s = s * g_n + jnp.einsum('bhik,bhiv->bhkv', k_dec_n, u)
        return s, o

    s_fin, o = lax.scan(step, s0, (w_v, w_k, qk, q_dec, k_dec, g_tot))
    o = jnp.swapaxes(jnp.moveaxis(o, 0, 1), 2, 3).reshape(bn, t, h, o.shape[-1])
    return o, s_fin


def gdn_mixer(qkv_raw, z, a_raw, b_raw, conv_hist, s0, conv_w, a_log, dt_bias, norm_w):
    f32 = jnp.float32
    bn, t, _ = qkv_raw.shape
    full = jnp.concatenate([conv_hist.astype(qkv_raw.dtype), qkv_raw], axis=1)
    new_conv = full[:, -(CONV_W - 1):]
    qkv = jax.nn.silu(causal_dwconv(full.astype(f32), conv_w.astype(f32)))
    q, k, v = jnp.split(qkv, 3, axis=-1)
    q = l2_normalize(q.reshape(bn, t, A_HEADS, HEAD_DIM)) * (HEAD_DIM ** -0.5)
    k = l2_normalize(k.reshape(bn, t, A_HEADS, HEAD_DIM))
    v = v.reshape(bn, t, A_HEADS, HEAD_DIM)
    g = -jnp.exp(a_log.astype(f32)) * jax.nn.softplus(a_raw.astype(f32) + dt_bias.astype(f32))
    beta = jax.nn.sigmoid(b_raw.astype(f32))
    o, s_new = chunked_gated_delta(q, k, v, g, beta, s0.astype(f32), min(CHUNK, t))
    o = o * lax.rsqrt(jnp.mean(o * o, axis=-1, keepdims=True) + RMS_EPS) * norm_w.astype(f32)
    o = o * jax.nn.silu(z.astype(f32).reshape(bn, t, A_HEADS, HEAD_DIM))
    return o.reshape(bn, t, A_WIDTH).astype(qkv_raw.dtype), s_new.astype(s0.dtype), new_conv


def fox_block(q, q_pos, c_q, k, v, c_k):
    s = jnp.einsum('bqhd,bkhd->bhqk', q, k, preferred_element_type=jnp.float32) * (HEAD_DIM ** -0.5)
    bias = jnp.swapaxes(c_q, 1, 2)[..., :, None] - jnp.swapaxes(c_k, 1, 2)[..., None, :]
    mask = jnp.arange(k.shape[1])[None, :] <= q_pos[:, None]
    p = jax.nn.softmax(jnp.where(mask, s + bias, -jnp.inf), axis=-1)
    return jnp.einsum('bhqk,bkhd->bqhd', p.astype(v.dtype), v)


def fox_prompt(q, k, v, logf):
    bn, s, h, d = q.shape
    c = jnp.cumsum(logf, axis=1)
    nb = s // FOX_QBLK
    qb = jnp.swapaxes(q.reshape(bn, nb, FOX_QBLK, h, d), 0, 1)
    cb = jnp.swapaxes(c.reshape(bn, nb, FOX_QBLK, h), 0, 1)
    pos = jnp.arange(nb)[:, None] * FOX_QBLK + jnp.arange(FOX_QBLK)[None, :]
    ob = lax.map(lambda a: fox_block(a[0], a[2], a[1], k, v, c), (qb, cb, pos))
    return jnp.swapaxes(ob, 0, 1).reshape(bn, s, h, d)


def fox_continue(q, k, v, logf, ck, cv, clf):
    p_len, t = ck.shape[1], q.shape[1]
    k_all = jnp.concatenate([ck.astype(k.dtype), k], axis=1)
    v_all = jnp.concatenate([cv.astype(v.dtype), v], axis=1)
    c = jnp.cumsum(jnp.concatenate([clf.astype(jnp.float32), logf], axis=1), axis=1)
    return fox_block(q, p_len + jnp.arange(t), c[:, p_len:], k_all, v_all, c)


def peer_ffn(h, w_q, sub_keys, u_tab, v_tab):
    bn, t, d = h.shape
    m = bn * t
    xf = h.reshape(m, d)
    q = jnp.einsum('md,dc->mc', xf, w_q).reshape(m, PEER_HEADS, 2, PEER_DKEY // 2)
    sc = jnp.einsum('mhpc,hpnc->mhpn', q, sub_keys).astype(jnp.float32)
    s1, i1 = lax.top_k(sc[:, :, 0], PEER_TOPK)
    s2, i2 = lax.top_k(sc[:, :, 1], PEER_TOPK)
    cand = (s1[..., :, None] + s2[..., None, :]).reshape(m, PEER_HEADS, PEER_TOPK * PEER_TOPK)
    cidx = (i1[..., :, None] * PEER_NKEYS + i2[..., None, :]).reshape(m, PEER_HEADS, PEER_TOPK * PEER_TOPK)
    top_s, pos = lax.top_k(cand, PEER_TOPK)
    eidx = jnp.take_along_axis(cidx, pos, axis=-1).reshape(m, PEER_HEADS * PEER_TOPK)
    gate = jax.nn.softmax(top_s, axis=-1).reshape(m, PEER_HEADS * PEER_TOPK)
    nblk = -(-m // PEER_BLK)
    pad = nblk * PEER_BLK - m
    xp = jnp.pad(xf, ((0, pad), (0, 0))).reshape(nblk, PEER_BLK, d)
    ip = jnp.pad(eidx, ((0, pad), (0, 0))).reshape(nblk, PEER_BLK, PEER_HEADS * PEER_TOPK)
    gp = jnp.pad(gate, ((0, pad), (0, 0))).reshape(nblk, PEER_BLK, PEER_HEADS * PEER_TOPK)

    def expert_block(args):
        xb, ib, gb = args
        pre = jnp.einsum('med,md->me', jnp.take(u_tab, ib, axis=0), xb).astype(jnp.float32)
        act = gb * jax.nn.gelu(pre, approximate=False)
        return jnp.einsum('me,med->md', act.astype(v_tab.dtype), jnp.take(v_tab, ib, axis=0))

    out = lax.map(expert_block, (xp, ip, gp))
    return out.reshape(nblk * PEER_BLK, d)[:m].reshape(bn, t, d).astype(h.dtype)


def trunk_layer(x, conv_hist, s0, fox_cache, w_in, conv_w, a_log, dt_bias, gdn_norm_w, fox_f_bias,
                w_out, ln1_g, ln1_b, peer_w_q, peer_sub_keys, peer_u, peer_v, ln2_g, ln2_b):
    bn, t, _ = x.shape
    proj = jnp.einsum('btd,dc->btc', x, w_in)
    a_qkv, a_z, a_a, a_b, b_qkv, b_f = jnp.split(proj, [OFF_A_Z, OFF_A_A, OFF_A_B, OFF_B_QKV, OFF_B_F], axis=-1)
    o_a, s_new, conv_new = gdn_mixer(a_qkv, a_z, a_a, a_b, conv_hist, s0, conv_w, a_log, dt_bias, gdn_norm_w)
    q, k, v = [u.reshape(bn, t, B_HEADS, HEAD_DIM) for u in jnp.split(b_qkv, 3, axis=-1)]
    logf = jax.nn.log_sigmoid(b_f.astype(jnp.float32) + fox_f_bias.astype(jnp.float32))
    if fox_cache is None:
        o_b = fox_prompt(q, k, v, logf)
    else:
        o_b = fox_continue(q, k, v, logf, fox_cache[0], fox_cache[1], fox_cache[2])
    mixed = jnp.einsum('btc,cd->btd', jnp.concatenate([o_a, o_b.reshape(bn, t, B_WIDTH).astype(o_a.dtype)], axis=-1), w_out)
    hid = layer_norm(DN_ALPHA * x + mixed, ln1_g, ln1_b)
    y = layer_norm(DN_ALPHA * hid + peer_ffn(hid, peer_w_q, peer_sub_keys, peer_u, peer_v), ln2_g, ln2_b)
    return y, (k, v, logf, s_new, conv_new)


def setup_inputs(seed: int = 0) -> dict:
    key = jax.random.key(seed)
    ks = jax.random.split(key, 32)
    f32 = jnp.float32

    def nrm(k, shape, scale):
        return jax.random.normal(k, shape, f32) * scale

    col_scale = np.ones((PROJ_COLS,), np.float32)
    col_scale[2 * A_WIDTH:3 * A_WIDTH] = DN_BETA
    col_scale[OFF_B_QKV + 2 * B_WIDTH:OFF_B_F] = DN_BETA
    col_scale = jnp.asarray(col_scale)
    dt = jnp.exp(jax.random.uniform(ks[7], (DEPTH, A_HEADS), f32, np.log(1e-3), np.log(1e-1)))
    return {
        'x_prompt': nrm(ks[0], (BATCH, SEQ, D_MODEL), 1.0),
        'x_sample': nrm(ks[1], (DEC_BATCH, DEC_SEQ, D_MODEL), 1.0),
        'cache_fox_k': nrm(ks[2], (DEPTH, DEC_BATCH, PAST_LEN, B_HEADS, HEAD_DIM), 1.0),
        'cache_fox_v': nrm(ks[3], (DEPTH, DEC_BATCH, PAST_LEN, B_HEADS, HEAD_DIM), DN_BETA),
        'cache_fox_logf': jax.nn.log_sigmoid(FOX_F_BIAS_INIT + nrm(ks[4], (DEPTH, DEC_BATCH, PAST_LEN, B_HEADS), 1.0)),
        'state_gdn': nrm(ks[5], (DEPTH, DEC_BATCH, A_HEADS, HEAD_DIM, HEAD_DIM), 0.1),
        'state_gdn_conv': nrm(ks[6], (DEPTH, DEC_BATCH, CONV_W - 1, 3 * A_WIDTH), 1.0) * col_scale[:3 * A_WIDTH],
        'w_in': nrm(ks[8], (DEPTH, D_MODEL, PROJ_COLS), D_MODEL ** -0.5) * col_scale,
        'gdn_conv_w': nrm(ks[9], (DEPTH, CONV_W, 3 * A_WIDTH), CONV_W ** -0.5),
        'gdn_a_log': jnp.log(jax.random.uniform(ks[10], (DEPTH, A_HEADS), f32, 1.0, 16.0)),
        'gdn_dt_bias': dt + jnp.log(-jnp.expm1(-dt)),
        'gdn_norm_w': 1.0 + nrm(ks[11], (DEPTH, HEAD_DIM), 0.02),
        'fox_f_bias': FOX_F_BIAS_INIT + nrm(ks[12], (DEPTH, B_HEADS), 0.1),
        'w_out': nrm(ks[13], (DEPTH, D_MODEL, D_MODEL), DN_BETA * D_MODEL ** -0.5),
        'ln1_g': 1.0 + nrm(ks[14], (DEPTH, D_MODEL), 0.02),
        'ln1_b': nrm(ks[15], (DEPTH, D_MODEL), 0.02),
        'peer_w_q': nrm(ks[16], (DEPTH, D_MODEL, PEER_HEADS * PEER_DKEY), D_MODEL ** -0.5),
        'peer_sub_keys': nrm(ks[17], (DEPTH, PEER_HEADS, 2, PEER_NKEYS, PEER_DKEY // 2), (PEER_DKEY // 2) ** -0.5),
        'peer_u': nrm(ks[18], (DEPTH, N_EXPERTS, D_MODEL), DN_BETA * D_MODEL ** -0.5),
        'peer_v': nrm(ks[19], (DEPTH, N_EXPERTS, D_MODEL), DN_BETA * (PEER_HEADS * PEER_TOPK) ** -0.5),
        'ln2_g': 1.0 + nrm(ks[20], (DEPTH, D_MODEL), 0.02),
        'ln2_b': nrm(ks[21], (DEPTH, D_MODEL), 0.02),
    }


def reference(x_prompt, x_sample, cache_fox_k, cache_fox_v, cache_fox_logf, state_gdn, state_gdn_conv,
              w_in, gdn_conv_w, gdn_a_log, gdn_dt_bias, gdn_norm_w, fox_f_bias, w_out, ln1_g, ln1_b,
              peer_w_q, peer_sub_keys, peer_u, peer_v, ln2_g, ln2_b):
    n_p = x_prompt.shape[0]
    yp, ys = x_prompt, x_sample
    outs_p = ([], [], [], [], [])
    outs_s = ([], [], [], [], [])
    for l in range(DEPTH):
        wl = (w_in[l], gdn_conv_w[l], gdn_a_log[l], gdn_dt_bias[l], gdn_norm_w[l], fox_f_bias[l], w_out[l],
              ln1_g[l], ln1_b[l], peer_w_q[l], peer_sub_keys[l], peer_u[l], peer_v[l], ln2_g[l], ln2_b[l])
        conv0 = jnp.zeros((n_p, CONV_W - 1, 3 * A_WIDTH), yp.dtype)
        s0 = jnp.zeros((n_p, A_HEADS, HEAD_DIM, HEAD_DIM), yp.dtype)
        yp, st_p = trunk_layer(yp, conv0, s0, None, *wl)
        ys, st_s = trunk_layer(ys, state_gdn_conv[l], state_gdn[l],
                               (cache_fox_k[l], cache_fox_v[l], cache_fox_logf[l]), *wl)
        for lst, arr in zip(outs_p, st_p):
            lst.append(arr)
        for lst, arr in zip(outs_s, st_s):
            lst.append(arr)
    fk_p, fv_p, fl_p, sg_p, sc_p = [jnp.stack(a, axis=0) for a in outs_p]
    fk_s, fv_s, fl_s, sg_s, sc_s = [jnp.stack(a, axis=0) for a in outs_s]
    return (yp, ys, fk_p, fv_p, fl_p, sg_p, sc_p, fk_s, fv_s, fl_s, sg_s, sc_s)
```

```python
import numpy as np
from contextlib import ExitStack
import concourse.bass as bass
import concourse.mybir as mybir
from concourse.bass_utils import run_bass_kernel_spmd

F32 = mybir.dt.float32
BF16 = mybir.dt.bfloat16
I32 = mybir.dt.int32
U32 = mybir.dt.uint32
AF = mybir.ActivationFunctionType
ALU = mybir.AluOpType
AX = mybir.AxisListType

ENGS = ['pe', 'act', 'dve', 'pool', 'sp']


class Sched:
    def __init__(self, nc, stack, n_dma=32):
        self.nc = nc
        self.q = {e: [] for e in ENGS}
        self.sem = {e: stack.enter_context(nc.semaphore("s_" + e)) for e in ENGS}
        self.cnt = {e: 0 for e in ENGS}
        self.nd = n_dma
        self.dsem = [stack.enter_context(nc.semaphore("d%d" % i)) for i in range(n_dma)]
        self.dcnt = [0] * n_dma
        self.dnext = 0
        self.seen = {e: {} for e in ENGS}
        self.lw = {}
        self.rd = {}
        self.ninst = 0

    def _deps(self, reads, writes):
        deps = {}
        for r in reads:
            kv = self.lw.get(r)
            if kv is not None and deps.get(kv[0], 0) < kv[1]:
                deps[kv[0]] = kv[1]
        for w in writes:
            kv = self.lw.get(w)
            if kv is not None and deps.get(kv[0], 0) < kv[1]:
                deps[kv[0]] = kv[1]
            for k, v in self.rd.get(w, {}).items():
                if deps.get(k, 0) < v:
                    deps[k] = v
        return deps

    def _waits(self, eng, deps):
        for k, v in deps.items():
            if k == 'pe' and eng == 'pe':
                continue
            if self.seen[eng].get(k, 0) >= v:
                continue
            self.seen[eng][k] = v
            sem = self.sem[k] if isinstance(k, str) else self.dsem[k[1]]
            self.q[eng].append(lambda e, sem=sem, v=v: e.wait_ge(sem, v))
            self.ninst += 1

    def _mark(self, key, v, reads, writes):
        for r in reads:
            d = self.rd.setdefault(r, {})
            if d.get(key, 0) < v:
                d[key] = v
        for w in writes:
            self.lw[w] = (key, v)
            self.rd[w] = {}

    def op(self, eng, fn, reads=(), writes=()):
        self._waits(eng, self._deps(reads, writes))
        self.cnt[eng] += 1
        v = self.cnt[eng]
        sem = self.sem[eng]
        self.q[eng].append(lambda e: fn(e).then_inc(sem, 1))
        self.ninst += 1
        self._mark(eng, v, reads, writes)

    def dma(self, q, out, in_, reads=(), writes=(), **kw):
        i = self.dnext
        self.dnext = (i + 1) % self.nd
        deps = self._deps(reads, writes)
        if self.dcnt[i] > 0:
            deps[('d', i)] = max(deps.get(('d', i), 0), self.dcnt[i])
        self._waits(q, deps)
        self.dcnt[i] += 16
        v = self.dcnt[i]
        sem = self.dsem[i]
        self.q[q].append(lambda e: e.dma_start(out=out, in_=in_, **kw).then_inc(sem, 16))
        self.ninst += 1
        self._mark(('d', i), v, reads, writes)

    def gather(self, out, table, idx_ap, reads=(), writes=(), bounds=None):
        q = 'pool'
        i = self.dnext
        self.dnext = (i + 1) % self.nd
        deps = self._deps(reads, writes)
        if self.dcnt[i] > 0:
            deps[('d', i)] = max(deps.get(('d', i), 0), self.dcnt[i])
        self._waits(q, deps)
        self.dcnt[i] += 16
        v = self.dcnt[i]
        sem = self.dsem[i]

        def f(e):
            self._gdbg = getattr(self, '_gdbg', 0) + 1
            if self._gdbg > 4351:
                print("GATHER", self._gdbg, out.shape, idx_ap, flush=True)
            return e.indirect_dma_start(
                out=out, out_offset=None, in_=table,
                in_offset=bass.IndirectOffsetOnAxis(ap=idx_ap, axis=0),
                bounds_check=bounds, oob_is_err=False).then_inc(sem, 16)
        self.q[q].append(f)
        self.ninst += 1
        self._mark(('d', i), v, reads, writes)

    def barrier(self):
        for e in ENGS:
            deps = {k: self.cnt[k] for k in ENGS if self.cnt[k] > 0 and k != e}
            if e != 'pe' and self.cnt[e] > 0:
                deps[e] = self.cnt[e]
            for i in range(self.nd):
                if self.dcnt[i] > 0:
                    deps[('d', i)] = self.dcnt[i]
            self._waits(e, deps)
        self.lw = {}
        self.rd = {}

    def run(self):
        self.barrier()
        q = self.q
        with self.nc.Block() as block:
            @block.tensor
            def _(e):
                for f in q['pe']:
                    f(e)

            @block.scalar
            def _(e):
                for f in q['act']:
                    f(e)

            @block.vector
            def _(e):
                for f in q['dve']:
                    f(e)

            @block.gpsimd
            def _(e):
                for f in q['pool']:
                    f(e)

            @block.sync
            def _(e):
                for f in q['sp']:
                    f(e)


D = 4096
SEQ = 2048
DS = 32
NS = 2
PAST = 1024
NT = SEQ + NS * DS
HD = 128
NH = 16
AW = 2048
PROJ = 14384
OFF_Z = 6144
OFF_A = 8192
OFF_B = 8208
OFF_BQ = 8224
OFF_F = 14368
GT = 1056
KC = 32
LN_EPS = 1e-5
ALPHA = 2.0 ** 0.25


class Ctx:
    pass


def tokmap(g, c0, n):
    if c0 < 1024:
        return g * 1024 + c0
    return SEQ + DS * g + (c0 - 1024)


def build_xT(S, C, g, src_rows):
    XT = C.XT
    for ti, (src, n, c0) in enumerate(src_rows):
        sl = ti % 2
        xr = C.xrow[sl]
        S.dma('sp', xr[:n, :], src, writes=[('xr', sl)])
        for kq in range(8):
            b = C.ps_next()
            for j in range(4):
                kc = kq * 4 + j
                S.op('pe', lambda e, b=b, j=j, kc=kc, xr=xr, n=n: e.transpose(
                    C.PS[b][:, j * 128:j * 128 + n], xr[:n, kc * 128:(kc + 1) * 128], C.ident[:n, :n]),
                    reads=[('xr', sl), 'ident'], writes=[('ps', b)])
            src_ps = C.PS[b][:, :].rearrange("p (j t) -> p j t", j=4)[:, :, :n]
            dst = XT[:, kq * 4:(kq + 1) * 4, c0:c0 + n]
            if kq % 2 == 0:
                S.op('act', lambda e, dst=dst, src_ps=src_ps: e.activation(dst, src_ps, AF.Copy),
                     reads=[('ps', b)], writes=[('XT', c0 // 128)])
            else:
                S.op('dve', lambda e, dst=dst, src_ps=src_ps: e.tensor_copy(dst, src_ps),
                     reads=[('ps', b)], writes=[('XT', c0 // 128)])


def load_w(S, C, wdram, col0, ncols, slot):
    for p0 in range(0, ncols, 128):
        pn = min(128, ncols - p0)
        ss = C.wst_i % 2
        C.wst_i += 1
        wst = C.wst[ss]
        src = wdram[:, col0 + p0:col0 + p0 + pn].rearrange("(kc p) c -> p kc c", p=128)
        S.dma('sp', wst[:, :, :pn], src, writes=[('wst', ss)])
        dst = C.wbf[slot][:, :, p0:p0 + pn]
        S.op('pool', lambda e, dst=dst, wst=wst, pn=pn: e.tensor_copy(dst, wst[:, :, :pn]),
             reads=[('wst', ss)], writes=[('wbf', slot)])


def xt_reads(c0, n):
    return [('XT', i) for i in range(c0 // 128, (c0 + n - 1) // 128 + 1)]


def gemm_F(S, C, slot, ncols, emit):
    XT, wbf = C.XT, C.wbf[slot]
    for (t0, n) in ((0, 512), (512, 512), (1024, 32)):
        b = C.ps_next()
        for kc in range(KC):
            S.op('pe', lambda e, b=b, kc=kc, t0=t0, n=n: e.matmul(
                C.PS[b][:ncols, :n], lhsT=wbf[:, kc, :ncols], rhs=XT[:, kc, t0:t0 + n],
                start=(kc == 0), stop=(kc == KC - 1)),
                reads=[('wbf', slot)] + xt_reads(t0, n), writes=[('ps', b)])
        emit(b, t0, n)


def gemm_T(S, C, slot, ncols, emit):
    XT, wbf = C.XT, C.wbf[slot]
    for ti in range(9):
        c0 = ti * 128
        n = 128 if ti < 8 else 32
        b = C.ps_next()
        for kc in range(KC):
            S.op('pe', lambda e, b=b, kc=kc, c0=c0, n=n: e.matmul(
                C.PS[b][:n, :ncols], lhsT=XT[:, kc, c0:c0 + n], rhs=wbf[:, kc, :ncols],
                start=(kc == 0), stop=(kc == KC - 1)),
                reads=[('wbf', slot)] + xt_reads(c0, n), writes=[('ps', b)])
        emit(b, c0, n)


def phase_A(S, nc, C, T):
    with ExitStack() as st:
        C.XT = st.enter_context(nc.sbuf_tensor("XT", [128, KC, GT], BF16))
        C.xrow = [st.enter_context(nc.sbuf_tensor("xrow%d" % i, [128, D], F32)) for i in range(2)]
        C.wst = [st.enter_context(nc.sbuf_tensor("wst%d" % i, [128, KC, 128], F32)) for i in range(2)]
        C.wbf = [st.enter_context(nc.sbuf_tensor("wbf%d" % i, [128, KC, 128], BF16)) for i in range(2)]
        stg = [st.enter_context(nc.sbuf_tensor("stg%d" % i, [128, GT], F32)) for i in range(2)]
        tst = [st.enter_context(nc.sbuf_tensor("tst%d" % i, [128, 128], F32)) for i in range(4)]
        C.wst_i = 0
        cnt = {'stg': 0, 'tst': 0, 'ev': 0}

        chunks = []
        for j in range(48):
            chunks.append((j * 128, 128, 'F', (T['AqkvT'], j * 128, 1.0)))
        for j in range(16):
            chunks.append((OFF_Z + j * 128, 128, 'F', (T['AzT'], j * 128, 1.0)))
        chunks.append((OFF_A, 32, 'T', ('ab', 0)))
        for j in range(16):
            chunks.append((OFF_BQ + j * 128, 128, 'F', (T['BqT'], j * 128, HD ** -0.5)))
        for j in range(16):
            chunks.append((OFF_BQ + AW + j * 128, 128, 'FT', (T['BkT'], j * 128, 1.0, 'k', j * 128)))
        for j in range(16):
            chunks.append((OFF_BQ + 2 * AW + j * 128, 128, 'T', ('v', j * 128)))
        chunks.append((OFF_F, 16, 'F', (T['BfT'], 0, 1.0)))

        for g in range(2):
            rows = [(T['xp'][g * 1024 + i * 128:g * 1024 + (i + 1) * 128, :], 128, i * 128) for i in range(8)]
            rows.append((T['xs'][g * DS:(g + 1) * DS, :], DS, 1024))
            build_xT(S, C, g, rows)

            def emit_F(info):
                dst, row0, scale = info[0], info[1], info[2]

                def emit(b, t0, n, ncols):
                    pass
                return emit

            load_w(S, C, T['w_in'], chunks[0][0], chunks[0][1], 0)
            for ci, (col0, ncols, mode, info) in enumerate(chunks):
                slot = ci % 2
                if ci + 1 < len(chunks):
                    load_w(S, C, T['w_in'], chunks[ci + 1][0], chunks[ci + 1][1], (ci + 1) % 2)
                if 'F' in mode:
                    dst, row0, scale = info[0], info[1], info[2]
                    ss = cnt['stg'] % 2
                    cnt['stg'] += 1
                    sg_t = stg[ss]

                    def emitF(b, t0, n, ncols=ncols, sg_t=sg_t, ss=ss, scale=scale):
                        cnt['ev'] += 1
                        if cnt['ev'] % 2 == 0:
                            S.op('act', lambda e: e.activation(sg_t[:ncols, t0:t0 + n], C.PS[b][:ncols, :n],
                                                               AF.Identity, scale=float(scale)),
                                 reads=[('ps', b)], writes=[('stg', ss)])
                        else:
                            S.op('dve', lambda e: e.tensor_scalar(sg_t[:ncols, t0:t0 + n], C.PS[b][:ncols, :n],
                                                                  float(scale), None, ALU.mult),
                                 reads=[('ps', b)], writes=[('stg', ss)])
                    gemm_F(S, C, slot, ncols, emitF)
                    S.dma('act', dst[row0:row0 + ncols, g * 1024:(g + 1) * 1024], sg_t[:ncols, 0:1024],
                          reads=[('stg', ss)])
                    S.dma('act', dst[row0:row0 + ncols, SEQ + DS * g:SEQ + DS * (g + 1)], sg_t[:ncols, 1024:GT],
                          reads=[('stg', ss)])
                if 'T' in mode:
                    kind, coff = (info[3], info[4]) if mode == 'FT' else (info[0], info[1])

                    def emitT(b, c0, n, ncols=ncols, kind=kind, coff=coff):
                        ts_ = cnt['tst'] % 4
                        cnt['tst'] += 1
                        tt = tst[ts_]
                        cnt['ev'] += 1
                        if cnt['ev'] % 2 == 0:
                            S.op('act', lambda e: e.activation(tt[:n, :ncols], C.PS[b][:n, :ncols], AF.Copy),
                                 reads=[('ps', b)], writes=[('tst', ts_)])
                        else:
                            S.op('dve', lambda e: e.tensor_copy(tt[:n, :ncols], C.PS[b][:n, :ncols]),
                                 reads=[('ps', b)], writes=[('tst', ts_)])
                        if kind == 'ab':
                            r0 = tokmap(g, c0, n)
                            dd = T['Aab'][r0:r0 + n, 0:ncols]
                        else:
                            if c0 < 1024:
                                base = T['fkp'] if kind == 'k' else T['fvp']
                                r0 = g * 1024 + c0
                            else:
                                base = T['fks'] if kind == 'k' else T['fvs']
                                r0 = g * DS
                            dd = base[r0:r0 + n, coff:coff + ncols]
                        S.dma('act', dd, tt[:n, :ncols], reads=[('tst', ts_)])
                    gemm_T(S, C, slot, ncols, emitT)
    S.barrier()


IN_SPECS = [
    ("xp", [SEQ, D], F32), ("xs", [NS * DS, D], F32),
    ("ck", [NS, PAST, AW], F32), ("cv", [NS, PAST, AW], F32), ("clf", [NS, PAST, NH], F32),
    ("sg", [NS, NH, HD, HD], F32), ("sconv_t", [NS, 128, 48 * 3], F32),
    ("w_in", [D, PROJ], F32), ("w_out", [D, D], F32), ("w_q", [D, 2048], F32),
    ("subk", [16, 128, 128], F32), ("pu", [16384, D], F32), ("pv", [16384, D], F32),
    ("convw_t", [128, 48 * 4], F32), ("alog_b", [128, NH], F32), ("dtb_b", [128, NH], F32),
    ("normw_c", [128, 1], F32), ("ffb_c", [NH, 1], F32),
    ("ln1g_b", [128, D], F32), ("ln1b_b", [128, D], F32), ("ln2g_b", [128, D], F32), ("ln2b_b", [128, D], F32),
    ("ident_in", [128, 128], F32), ("cmat", [128, 8 * 128], F32), ("selc", [NH, NH * 128], F32),
]
OUT_SPECS = [
    ("yp", [SEQ, D]), ("ys", [NS * DS, D]),
    ("fkp", [SEQ, AW]), ("fvp", [SEQ, AW]), ("flp", [SEQ, NH]),
    ("sgp", [NH, HD, HD]), ("scp", [3, 3 * AW]),
    ("fks", [NS * DS, AW]), ("fvs", [NS * DS, AW]), ("fls", [NS * DS, NH]),
    ("sgs", [NS, NH, HD, HD]), ("scs", [NS, 3, 3 * AW]),
]
SCRATCH = [
    ("BIG1", [3 * AW * NT], F32), ("BIG2", [3 * AW * NT], F32), ("Aab", [NT, 32], F32),
    ("BfT", [NH, NT], F32), ("OT", [D, NT], BF16),
]


def build_program(phases="ABCDE"):
    nc = bass.Bass("TRN2", target_bir_lowering=False)
    T = {}
    for name, shape, dt in IN_SPECS:
        if name in ("pu", "pv") and 'E' not in phases:
            shape = [128, D]
        T[name] = nc.dram_tensor(name, shape, dt, kind="ExternalInput")
    for name, shape in OUT_SPECS:
        T[name] = nc.dram_tensor(name, shape, F32, kind="ExternalOutput")
    for name, shape, dt in SCRATCH:
        T[name] = nc.dram_tensor(name, shape, dt, kind="Internal")
    b1, b2 = T['BIG1'], T['BIG2']
    T['AqkvT'] = b1[0:3 * AW * NT].rearrange("(a b) -> a b", b=NT)
    T['MIX'] = b1[0:NT * D].rearrange("(a b) -> a b", b=D)
    T['SC'] = b1[NT * D:NT * D + NT * 2048].rearrange("(a b) -> a b", b=2048)
    T['AzT'] = b2[0:AW * NT].rearrange("(a b) -> a b", b=NT)
    T['BqT'] = b2[AW * NT:2 * AW * NT].rearrange("(a b) -> a b", b=NT)
    T['BkT'] = b2[2 * AW * NT:3 * AW * NT].rearrange("(a b) -> a b", b=NT)
    T['H'] = b2[0:NT * D].rearrange("(a b) -> a b", b=D)
    with ExitStack() as st:
        S = Sched(nc, st)
        C = Ctx()
        C.PS = [st.enter_context(nc.psum_tensor("ps%d" % i, [128, 512], F32)) for i in range(8)]
        C.ps_i = 0

        def ps_next():
            b = C.ps_i
            C.ps_i = (b + 1) % 8
            return b
        C.ps_next = ps_next
        C.ident = st.enter_context(nc.sbuf_tensor("ident_sb", [128, 128], F32))
        C.cmat = st.enter_context(nc.sbuf_tensor("cmat_sb", [128, 8 * 128], F32))
        S.dma('sp', C.ident[:], T['ident_in'][:, :], writes=['ident'])
        S.dma('sp', C.cmat[:], T['cmat'][:, :], writes=['cmat'])
        if 'A' in phases:
            phase_A(S, nc, C, T)
        if 'B' in phases:
            phase_B(S, nc, C, T)
        if 'C' in phases:
            phase_C(S, nc, C, T)
        if 'D' in phases:
            phase_D(S, nc, C, T)
        if 'E' in phases:
            phase_E(S, nc, C, T)
        S.run()
        print("instructions:", S.ninst, {e: S.cnt[e] for e in ENGS})
    return nc


def make_consts():
    c = np.zeros((128, 8, 128), np.float32)
    i = np.arange(128)
    c[:, 0, :] = (i[:, None] <= i[None, :])
    c[:, 1, :] = (i[:, None] > i[None, :])
    c[:, 2, :] = 1.0
    c[:, 3, :] = (i[:, None] < i[None, :])
    c[:, 4, :] = 128.0
    c[:, 5, :] = (i[:, None] >= i[None, :])
    return c.reshape(128, 8 * 128)


def kernel(x_prompt, x_sample, cache_fox_k, cache_fox_v, cache_fox_logf, state_gdn, state_gdn_conv,
           w_in, gdn_conv_w, gdn_a_log, gdn_dt_bias, gdn_norm_w, fox_f_bias, w_out, ln1_g, ln1_b,
           peer_w_q, peer_sub_keys, peer_u, peer_v, ln2_g, ln2_b, _phases="ABCDE", _cores=None):
    import time as _time
    _t0 = _time.time()
    f = lambda a: np.ascontiguousarray(np.asarray(a), dtype=np.float32)
    nc = build_program(_phases)
    cw = f(gdn_conv_w)[0]
    convw_t = np.ascontiguousarray(cw.reshape(4, 48, 128).transpose(2, 1, 0)).reshape(128, 48 * 4)
    bc = lambda v, n=128: np.ascontiguousarray(np.broadcast_to(f(v).reshape(1, -1), (n, f(v).size)))
    shared = {
        "w_in": f(w_in)[0], "w_out": f(w_out)[0], "w_q": f(peer_w_q)[0],
        "subk": f(peer_sub_keys)[0].reshape(16, 128, 128), "pu": f(peer_u)[0], "pv": f(peer_v)[0],
        "convw_t": convw_t, "alog_b": bc(gdn_a_log), "dtb_b": bc(gdn_dt_bias),
        "normw_c": f(gdn_norm_w).reshape(128, 1), "ffb_c": f(fox_f_bias).reshape(NH, 1),
        "ln1g_b": bc(ln1_g), "ln1b_b": bc(ln1_b), "ln2g_b": bc(ln2_g), "ln2b_b": bc(ln2_b),
        "ident_in": np.eye(128, dtype=np.float32), "cmat": make_consts(),
        "selc": np.ascontiguousarray(np.repeat(np.eye(NH, dtype=np.float32), 128, axis=1)),
    }
    xp = f(x_prompt)
    xs = f(x_sample)
    ck = f(cache_fox_k)[0]
    cv = f(cache_fox_v)[0]
    clf = f(cache_fox_logf)[0]
    sg = f(state_gdn)[0]
    sc = f(state_gdn_conv)[0]
    in_maps = []
    for c in range(8):
        m = dict(shared)
        m["xp"] = xp[c]
        m["xs"] = xs[2 * c:2 * c + 2].reshape(NS * DS, D)
        m["ck"] = ck[2 * c:2 * c + 2].reshape(NS, PAST, AW)
        m["cv"] = cv[2 * c:2 * c + 2].reshape(NS, PAST, AW)
        m["clf"] = clf[2 * c:2 * c + 2]
        m["sg"] = sg[2 * c:2 * c + 2]
        m["sconv_t"] = np.ascontiguousarray(sc[2 * c:2 * c + 2].reshape(NS, 3, 48, 128).transpose(0, 3, 2, 1)).reshape(NS, 128, 48 * 3)
        in_maps.append(m)
    if 'E' not in _phases:
        for m in in_maps:
            m["pu"] = m["pu"][:128]
            m["pv"] = m["pv"][:128]
    print("kernel: built+staged %.1fs" % (_time.time() - _t0), flush=True)
    if _cores is not None:
        res = run_bass_kernel_spmd(nc, [in_maps[c] for c in _cores], core_ids=list(range(len(_cores))))
        R = {c: res.results[i] for i, c in enumerate(_cores)}
        R = [R.get(c, R[_cores[0]]) for c in range(8)]
    else:
        res = run_bass_kernel_spmd(nc, in_maps, core_ids=list(range(8)))
        R = res.results
    print("kernel: ran %.1fs" % (_time.time() - _t0), flush=True)
    cat = lambda k: np.stack([np.asarray(R[c][k]) for c in range(8)], axis=0)
    yp = cat("yp")
    ys = cat("ys").reshape(16, DS, D)
    fkp = cat("fkp").reshape(1, 8, SEQ, NH, HD)
    fvp = cat("fvp").reshape(1, 8, SEQ, NH, HD)
    flp = cat("flp").reshape(1, 8, SEQ, NH)
    sgp = cat("sgp").reshape(1, 8, NH, HD, HD)
    scp = cat("scp").reshape(1, 8, 3, 3 * AW)
    fks = cat("fks").reshape(1, 16, DS, NH, HD)
    fvs = cat("fvs").reshape(1, 16, DS, NH, HD)
    fls = cat("fls").reshape(1, 16, DS, NH)
    sgs = cat("sgs").reshape(1, 16, NH, HD, HD)
    scs = cat("scs").reshape(1, 16, 3, 3 * AW)
    return (yp, ys, fkp, fvp, flp, sgp, scp, fks, fvs, fls, sgs, scs)


class KB:
    def __init__(self, c0, n, vb, qb_first, diag):
        self.c0, self.n, self.vb, self.qb_first, self.diag = c0, n, vb, qb_first, diag


def fox_core(S, C, P, kT, Vb, nck, qT, cqT, selh, nqb, QB, kblocks, out_cb, tag):
    import os as _os
    _kg = int(_os.environ.get("KG", "99"))
    for g0 in range(0, min(nqb, 4 * _kg), 4):
        qbs = list(range(g0, min(g0 + 4, nqb)))
        vis = {qb: [kb for kb in kblocks if kb.qb_first <= qb] for qb in qbs}
        for kb in [k for k in kblocks if k.qb_first <= qbs[-1]]:
            qlo = max(kb.qb_first, g0)
            q0, q1 = qlo * QB, (qbs[-1] + 1) * QB
            N = q1 - q0
            bS = 4 + (P.fx_i % 2)
            psl = P.fx_i % 2
            P.fx_i += 1
            S.op('pe', lambda e, kb=kb, bS=bS, q0=q0, q1=q1, N=N: e.matmul(
                C.PS[bS][:kb.n, :N], lhsT=kT[:, kb.c0:kb.c0 + kb.n], rhs=qT[:, q0:q1], start=True, stop=False),
                reads=tag['k'] + tag['q'], writes=[('ps', bS)])
            S.op('pe', lambda e, kb=kb, bS=bS, q0=q0, q1=q1, N=N: e.matmul(
                C.PS[bS][:kb.n, :N], lhsT=selh[:, :kb.n], rhs=cqT[:, q0:q1], start=False, stop=True),
                reads=tag['c'] + ['selc'], writes=[('ps', bS)])
            pt = P.PT[psl]
            S.op('act', lambda e, kb=kb, bS=bS, N=N, pt=pt: e.activation(
                pt[:kb.n, :N], C.PS[bS][:kb.n, :N], AF.Exp, bias=nck(kb), scale=1.0),
                reads=[('ps', bS)] + tag['nck'], writes=[('pt', psl)])
            if kb.diag is not None and qlo <= kb.diag <= qbs[-1]:
                off = (kb.diag - qlo) * QB
                S.op('dve', lambda e, kb=kb, off=off, pt=pt: e.tensor_tensor(
                    pt[:kb.n, off:off + QB], pt[:kb.n, off:off + QB], P.maskb[:kb.n, :QB], ALU.mult),
                    reads=[('pt', psl), 'maskb'], writes=[('pt', psl)])
            for qb in range(qlo, qbs[-1] + 1):
                bank = qb % 4
                first = kb is vis[qb][0]
                last = kb is vis[qb][-1]
                S.op('pe', lambda e, kb=kb, qb=qb, bank=bank, first=first, last=last, pt=pt, qlo=qlo: e.matmul(
                    C.PS[bank][:QB, :129], lhsT=pt[:kb.n, (qb - qlo) * QB:(qb - qlo + 1) * QB],
                    rhs=Vb[:kb.n, kb.vb, :], start=first, stop=last),
                    reads=[('pt', psl)] + tag['v'], writes=[('ps', bank)])
        for qb in qbs:
            out_cb(qb, qb % 4)


def fox_out(S, C, P, QB, ots, ots_tag):
    def cb(qb, bank):
        i = P.oc_i % 2
        P.oc_i += 1
        rec = P.rec[i]
        osb = P.osb[i]
        S.op('dve', lambda e: e.reciprocal(rec[:QB, :], C.PS[bank][:QB, 128:129]),
             reads=[('ps', bank)], writes=[('rec', i)])
        S.op('act', lambda e: e.activation(osb[:QB, :], C.PS[bank][:QB, 0:128], AF.Identity, scale=rec[:QB, 0:1]),
             reads=[('ps', bank), ('rec', i)], writes=[('osb', i)])
        bt = 6 + i
        S.op('pe', lambda e: e.transpose(C.PS[bt][:, :QB], osb[:QB, :], C.ident[:QB, :QB]),
             reads=[('osb', i), 'ident'], writes=[('ps', bt)])
        S.op('dve', lambda e: e.tensor_copy(ots[:, qb * QB:(qb + 1) * QB], C.PS[bt][:, :QB]),
             reads=[('ps', bt)], writes=[ots_tag])
    return cb


def phase_C(S, nc, C, T):
    with ExitStack() as st:
        P = Ctx()
        P.fx_i = 0
        P.oc_i = 0
        sb = lambda name, shape, dt=F32: st.enter_context(nc.sbuf_tensor(name, shape, dt))
        selc = sb("selc_sb", [NH, NH * 128])
        S.dma('sp', selc[:], T['selc'][:, :], writes=['selc'])
        P.maskb = sb("maskb", [128, 128], BF16)
        S.op('dve', lambda e: e.tensor_copy(P.maskb[:], C.cmat[:, 0:128]), reads=['cmat'], writes=['maskb'])
        P.PT = [sb("PT%d" % i, [128, 512], BF16) for i in range(2)]
        P.rec = [sb("rec%d" % i, [128, 1]) for i in range(2)]
        P.osb = [sb("osb%d" % i, [128, 128]) for i in range(2)]
        bfT = sb("bfT", [NH, NT])
        logfT = sb("logfT", [NH, NT])
        ffb = sb("ffb", [NH, 1])
        nffb = sb("nffb", [NH, 1])
        onesr = sb("onesr", [NH, SEQ])
        cTp = sb("cTp", [NH, SEQ])
        S.dma('sp', bfT[:], T['BfT'][:, :], writes=['bfT'])
        S.dma('sp', ffb[:], T['ffb_c'][:, :], writes=['ffb'])
        S.op('dve', lambda e: e.tensor_scalar(nffb[:], ffb[:], -1.0, None, ALU.mult), reads=['ffb'], writes=['nffb'])
        S.op('dve', lambda e: e.memset(onesr[:], 1.0), writes=['onesr'])
        S.op('act', lambda e: e.activation(logfT[:], bfT[:], AF.Exp, bias=nffb[:, 0:1], scale=-1.0),
             reads=['bfT', 'nffb'], writes=['logfT'])
        S.op('act', lambda e: e.activation(logfT[:], logfT[:], AF.Ln, bias=1.0, scale=1.0),
             reads=['logfT'], writes=['logfT'])
        S.op('dve', lambda e: e.tensor_scalar(logfT[:], logfT[:], -1.0, None, ALU.mult),
             reads=['logfT'], writes=['logfT'])
        S.op('dve', lambda e: e.tensor_tensor_scan(cTp[:], onesr[:], logfT[:, 0:SEQ], 0.0, ALU.mult, ALU.add),
             reads=['onesr', 'logfT'], writes=['cTp'])
        lf_tm = sb("lf_tm", [128, 16, NH])
        nck_p = sb("nck_p", [128, 16, NH])
        for blk in range(16):
            S.op('pe', lambda e, blk=blk: e.transpose(C.PS[6][:, blk * 16:(blk + 1) * 16],
                                                      logfT[:, blk * 128:(blk + 1) * 128], C.ident[:NH, :NH]),
                 reads=['logfT', 'ident'], writes=[('ps', 6)])
            S.op('pe', lambda e, blk=blk: e.transpose(C.PS[7][:, blk * 16:(blk + 1) * 16],
                                                      cTp[:, blk * 128:(blk + 1) * 128], C.ident[:NH, :NH]),
                 reads=['cTp', 'ident'], writes=[('ps', 7)])
        S.op('act', lambda e: e.activation(lf_tm[:].rearrange("p b h -> p (b h)"), C.PS[6][:, 0:256], AF.Copy),
             reads=[('ps', 6)], writes=['lf_tm'])
        S.op('dve', lambda e: e.tensor_scalar(nck_p[:].rearrange("p b h -> p (b h)"), C.PS[7][:, 0:256], -1.0, None,
                                              ALU.mult), reads=[('ps', 7)], writes=['nck_p'])
        S.dma('act', T['flp'][:, :].rearrange("(b p) h -> p b h", p=128), lf_tm[:], reads=['lf_tm'])
        lfs = sb("lfs", [64, NH])
        S.op('pe', lambda e: e.transpose(C.PS[6][:64, 0:NH], logfT[:, SEQ:NT], C.ident[:NH, :NH]),
             reads=['logfT', 'ident'], writes=[('ps', 6)])
        S.op('act', lambda e: e.activation(lfs[:], C.PS[6][:64, 0:NH], AF.Copy), reads=[('ps', 6)], writes=['lfs'])
        S.dma('act', T['fls'][:, :], lfs[:], reads=['lfs'])
        clf_tm = sb("clf_tm", [128, 8, NH])
        clfT = [sb("clfT%d" % s, [NH, PAST]) for s in range(NS)]
        ccT = [sb("ccT%d" % s, [NH, PAST]) for s in range(NS)]
        cnT = [sb("cnT%d" % s, [NH, DS]) for s in range(NS)]
        nck_c = [sb("nck_c%d" % s, [128, 8, NH]) for s in range(NS)]
        nck_n = [sb("nck_n%d" % s, [DS, NH]) for s in range(NS)]
        for s in range(NS):
            S.dma('sp', clf_tm[:], T['clf'][s].rearrange("(b p) h -> p b h", p=128), writes=['clf_tm'])
            for blk in range(8):
                bk = 6 + blk // 4
                S.op('pe', lambda e, blk=blk, bk=bk: e.transpose(
                    C.PS[bk][:NH, (blk % 4) * 128:(blk % 4 + 1) * 128], clf_tm[:, blk, :], C.ident[:, :]),
                    reads=['clf_tm', 'ident'], writes=[('ps', bk)])
            for hf in range(2):
                S.op('act', lambda e, hf=hf, s=s: e.activation(clfT[s][:, hf * 512:(hf + 1) * 512],
                                                              C.PS[6 + hf][:NH, :], AF.Copy),
                     reads=[('ps', 6 + hf)], writes=[('clfT', s)])
            S.op('dve', lambda e, s=s: e.tensor_tensor_scan(ccT[s][:], onesr[:, 0:PAST], clfT[s][:], 0.0,
                                                            ALU.mult, ALU.add),
                 reads=['onesr', ('clfT', s)], writes=[('ccT', s)])
            S.op('dve', lambda e, s=s: e.tensor_tensor_scan(cnT[s][:], onesr[:, 0:DS],
                                                            logfT[:, SEQ + s * DS:SEQ + (s + 1) * DS],
                                                            ccT[s][:, PAST - 1:PAST], ALU.mult, ALU.add),
                 reads=['onesr', 'logfT', ('ccT', s)], writes=[('cnT', s)])
            for blk in range(8):
                S.op('pe', lambda e, blk=blk, s=s: e.transpose(C.PS[7][:, blk * 16:(blk + 1) * 16],
                                                               ccT[s][:, blk * 128:(blk + 1) * 128],
                                                               C.ident[:NH, :NH]),
                     reads=[('ccT', s), 'ident'], writes=[('ps', 7)])
            S.op('dve', lambda e, s=s: e.tensor_scalar(nck_c[s][:].rearrange("p b h -> p (b h)"),
                                                       C.PS[7][:, 0:128], -1.0, None, ALU.mult),
                 reads=[('ps', 7)], writes=[('nck_c', s)])
            S.op('pe', lambda e, s=s: e.transpose(C.PS[6][:DS, 0:NH], cnT[s][:, :], C.ident[:NH, :NH]),
                 reads=[('cnT', s), 'ident'], writes=[('ps', 6)])
            S.op('dve', lambda e, s=s: e.tensor_scalar(nck_n[s][:], C.PS[6][:DS, 0:NH], -1.0, None, ALU.mult),
                 reads=[('ps', 6)], writes=[('nck_n', s)])

        import os as _os
        _stop = int(_os.environ.get("KSTOP", "99"))
        if _stop <= 1:
            S.barrier()
            return
        qf = [sb("qf%d" % i, [128, SEQ]) for i in range(2)]
        kf = [sb("kf%d" % i, [128, SEQ]) for i in range(2)]
        vf = [sb("vf%d" % i, [128, 16, 128]) for i in range(2)]
        qb_ = [sb("qb%d" % i, [128, SEQ], BF16) for i in range(2)]
        kb_ = [sb("kb%d" % i, [128, SEQ], BF16) for i in range(2)]
        Vb = [sb("Vb%d" % i, [128, 16, 129], BF16) for i in range(2)]
        ots = [sb("ots%d" % i, [128, SEQ], BF16) for i in range(2)]
        for i in range(2):
            S.op('pool', lambda e, i=i: e.memset(Vb[i][:, :, 128:129], 1.0), writes=[('Vb', i)])

        def load_head(h):
            i = h % 2
            S.dma('sp', qf[i][:], T['BqT'][h * 128:(h + 1) * 128, 0:SEQ], writes=[('qf', i)])
            S.dma('sp', kf[i][:], T['BkT'][h * 128:(h + 1) * 128, 0:SEQ], writes=[('kf', i)])
            S.dma('sp', vf[i][:], T['fvp'][:, h * 128:(h + 1) * 128].rearrange("(b p) d -> p b d", p=128),
                  writes=[('vf', i)])
            S.op('pool', lambda e: e.tensor_copy(qb_[i][:], qf[i][:]), reads=[('qf', i)], writes=[('qb', i)])
            S.op('pool', lambda e: e.tensor_copy(kb_[i][:], kf[i][:]), reads=[('kf', i)], writes=[('kb', i)])
            S.op('pool', lambda e: e.tensor_copy(Vb[i][:, :, 0:128], vf[i][:]), reads=[('vf', i)], writes=[('Vb', i)])

        kblocks_p = [KB(b * 128, 128, b, b, b) for b in range(16)]
        load_head(0)
        _kh = int(_os.environ.get("KH", "16"))
        for h in range(_kh):
            i = h % 2
            if h + 1 < _kh:
                load_head(h + 1)
            tag = {'k': [('kb', i)], 'q': [('qb', i)], 'c': ['cTp'], 'nck': ['nck_p'], 'v': [('Vb', i)]}
            fox_core(S, C, P, kb_[i], Vb[i], lambda kb, h=h: nck_p[:kb.n, kb.vb, h:h + 1], qb_[i], cTp,
                     selc[:, h * 128:(h + 1) * 128], 16, 128, kblocks_p,
                     fox_out(S, C, P, 128, ots[i], ('ots', i)), tag)
            S.dma('act', T['OT'][AW + h * 128:AW + (h + 1) * 128, 0:SEQ], ots[i][:], reads=[('ots', i)])

        if _stop <= 2:
            S.barrier()
            return
        ckf = [sb("ckf%d" % i, [128, 8, 128]) for i in range(2)]
        cvf = [sb("cvf%d" % i, [128, 8, 128]) for i in range(2)]
        knf = [sb("knf%d" % i, [128, DS]) for i in range(2)]
        qnf = [sb("qnf%d" % i, [128, DS]) for i in range(2)]
        vnf = [sb("vnf%d" % i, [DS, 128]) for i in range(2)]
        kTs = [sb("kTs%d" % i, [128, PAST + DS], BF16) for i in range(2)]
        qTs = [sb("qTs%d" % i, [128, DS], BF16) for i in range(2)]
        Vs = [sb("Vs%d" % i, [128, 9, 129], BF16) for i in range(2)]
        otss = [sb("otss%d" % i, [128, DS], BF16) for i in range(2)]
        for i in range(2):
            S.op('pool', lambda e, i=i: e.memset(Vs[i][:, :, 128:129], 1.0), writes=[('Vs', i)])
        kblocks_s = [KB(b * 128, 128, b, 0, None) for b in range(8)] + [KB(PAST, DS, 8, 0, 0)]
        it = 0
        for s in range(NS):
            for h in range(NH):
                i = it % 2
                it += 1
                tcol = SEQ + s * DS
                S.dma('sp', ckf[i][:], T['ck'][s][:, h * 128:(h + 1) * 128].rearrange("(b p) d -> p b d", p=128),
                      writes=[('ckf', i)])
                S.dma('sp', cvf[i][:], T['cv'][s][:, h * 128:(h + 1) * 128].rearrange("(b p) d -> p b d", p=128),
                      writes=[('cvf', i)])
                S.dma('sp', knf[i][:], T['BkT'][h * 128:(h + 1) * 128, tcol:tcol + DS], writes=[('knf', i)])
                S.dma('sp', qnf[i][:], T['BqT'][h * 128:(h + 1) * 128, tcol:tcol + DS], writes=[('qnf', i)])
                S.dma('sp', vnf[i][:], T['fvs'][s * DS:(s + 1) * DS, h * 128:(h + 1) * 128], writes=[('vnf', i)])
                for blk in range(8):
                    bt = 6 + blk % 2
                    S.op('pe', lambda e, blk=blk, bt=bt, i=i: e.transpose(C.PS[bt][:, 0:128], ckf[i][:, blk, :],
                                                                          C.ident[:, :]),
                         reads=[('ckf', i), 'ident'], writes=[('ps', bt)])
                    S.op('dve' if blk % 2 else 'act',
                         (lambda e, blk=blk, bt=bt, i=i: e.tensor_copy(kTs[i][:, blk * 128:(blk + 1) * 128],
                                                                       C.PS[bt][:, 0:128])) if blk % 2 else
                         (lambda e, blk=blk, bt=bt, i=i: e.activation(kTs[i][:, blk * 128:(blk + 1) * 128],
                                                                      C.PS[bt][:, 0:128], AF.Copy)),
                         reads=[('ps', bt)], writes=[('kTs', i)])
                S.op('pool', lambda e, i=i: e.tensor_copy(kTs[i][:, PAST:PAST + DS], knf[i][:]),
                     reads=[('knf', i)], writes=[('kTs', i)])
                S.op('pool', lambda e, i=i: e.tensor_copy(qTs[i][:], qnf[i][:]), reads=[('qnf', i)],
                     writes=[('qTs', i)])
                S.op('pool', lambda e, i=i: e.tensor_copy(Vs[i][:, 0:8, 0:128], cvf[i][:]), reads=[('cvf', i)],
                     writes=[('Vs', i)])
                S.op('pool', lambda e, i=i: e.tensor_copy(Vs[i][:DS, 8, 0:128], vnf[i][:]), reads=[('vnf', i)],
                     writes=[('Vs', i)])
                tag = {'k': [('kTs', i)], 'q': [('qTs', i)], 'c': [('cnT', s)],
                       'nck': [('nck_c', s), ('nck_n', s)], 'v': [('Vs', i)]}

                def ncks(kb, s=s, h=h):
                    if kb.vb < 8:
                        return nck_c[s][:kb.n, kb.vb, h:h + 1]
                    return nck_n[s][:kb.n, h:h + 1]
                fox_core(S, C, P, kTs[i], Vs[i], ncks, qTs[i], cnT[s], selc[:, h * 128:(h + 1) * 128], 1, DS,
                         kblocks_s, fox_out(S, C, P, DS, otss[i], ('otss', i)), tag)
                S.dma('act', T['OT'][AW + h * 128:AW + (h + 1) * 128, tcol:tcol + DS], otss[i][:],
                      reads=[('otss', i)])
    S.barrier()


def phase_B(S, nc, C, T):
    with ExitStack() as st:
        sb = lambda name, shape, dt=F32: st.enter_context(nc.sbuf_tensor("B_" + name, shape, dt))
        CM = 64
        U_ = C.cmat[:, 0:128]
        SL_ = C.cmat[:, 128:256]
        ONES = C.cmat[:, 256:384]
        SU_ = C.cmat[:, 384:512]
        C128 = C.cmat[:, 512:640]
        I_ = C.ident
        cw = sb("cw", [128, 48, 4])
        nA = sb("nA", [CM, NH])
        dtb = sb("dtb", [CM, NH])
        normw = sb("normw", [128, 1])
        S.dma('sp', cw[:].rearrange("p j r -> p (j r)"), T['convw_t'][:, :], writes=['cw'])
        S.dma('sp', nA[:], T['alog_b'][0:CM, :], writes=['nA'])
        S.dma('sp', dtb[:], T['dtb_b'][0:CM, :], writes=['dtb'])
        S.dma('sp', normw[:], T['normw_c'][:, :], writes=['normw'])
        S.op('act', lambda e: e.activation(nA[:], nA[:], AF.Exp), reads=['nA'], writes=['nA'])
        S.op('dve', lambda e: e.tensor_scalar(nA[:], nA[:], -1.0, None, ALU.mult), reads=['nA'], writes=['nA'])
        NCHM = 32
        ab = sb("ab", [CM, NCHM, 32])
        g_all = sb("g_all", [CM, NCHM, NH])
        lnb = sb("lnb_unused", [CM, 1])
        beta = sb("beta", [CM, NCHM, NH])
        ecum = sb("ecum", [CM, NCHM, NH])
        ekd = sb("ekd", [CM, NCHM, NH])
        bec = sb("bec", [CM, NCHM, NH])
        gtot = sb("gtot", [128, NCHM, NH])
        xin = [sb("xin%d" % i, [128, 48, 3 + CM]) for i in range(2)]
        zin = sb("zin", [128, NH, CM])
        zs = sb("zs", [128, NH, CM])
        ta = sb("ta", [128, 48, CM])
        tb = sb("tb", [128, 48, CM])
        ys = sb("ys", [128, 48, CM])
        qkn = sb("qkn", [128, 32, CM])
        Sst = sb("Sst", [128, NH, 128])
        Stmp = sb("Stmp", [128, 4, 128])
        OTb = sb("OTb", [128, NH, 256], BF16)
        q4 = lambda name, w: [sb(name + str(i), [CM, 4, w]) for i in range(2)]
        rhsG = q4("rhsG", CM)
        rhsGT = q4("rhsGT", CM)
        rhsB = q4("rhsB", CM)
        rhsE = q4("rhsE", CM)
        ex1 = q4("ex1", CM)
        ex2 = q4("ex2", CM)
        decU = q4("decU", CM)
        NN = [q4("NNa", CM), q4("NNb", CM)]
        NT_ = [q4("NTa", CM), q4("NTb", CM)]
        TT = [q4("TTa", CM), q4("TTb", CM)]
        bv = q4("bv", 128)
        bek = q4("bek", 128)
        wv = sb("wv", [CM, NH, 128])
        kdec = sb("kdec", [CM, NH, 128])
        qkm = sb("qkm", [CM, NH, CM])
        wkT = sb("wkT", [128, NH, CM])
        qdT = sb("qdT", [128, NH, CM])
        u_ = q4("u_", 128)
        osq = q4("osq", 128)
        on_ = q4("on_", 128)
        ms = [sb("ms%d" % i, [CM, 4]) for i in range(2)]
        rstd = [sb("rstd%d" % i, [CM, 4]) for i in range(2)]
        PS = C.PS
        nb = C.ps_next
        V = lambda fn, r, w: S.op('dve', fn, reads=r, writes=w)
        A = lambda fn, r, w: S.op('act', fn, reads=r, writes=w)
        G = lambda fn, r, w: S.op('pool', fn, reads=r, writes=w)
        PE = lambda fn, r, w: S.op('pe', fn, reads=r, writes=w)

        def seq(tok0, Tlen, Cn, conv_hist, s0, s_out, conv_out, name):
            NCH = Tlen // Cn
            S.dma('sp', ab[:Cn, :NCH, :], T['Aab'][tok0:tok0 + Tlen, :].rearrange("(n c) x -> c n x", c=Cn),
                  writes=['ab'])
            gv = g_all[:Cn, :NCH, :]
            V(lambda e: e.tensor_tensor(gv, ab[:Cn, :NCH, 0:NH], dtb[:Cn, None, :].to_broadcast([Cn, NCH, NH]),
                                        ALU.add), ['ab', 'dtb'], ['g_all'])
            A(lambda e: e.activation(gv, gv, AF.Exp), ['g_all'], ['g_all'])
            A(lambda e: e.activation(gv, gv, AF.Ln, bias=1.0, scale=1.0), ['g_all'], ['g_all'])
            V(lambda e: e.tensor_tensor(gv, gv, nA[:Cn, None, :].to_broadcast([Cn, NCH, NH]), ALU.mult),
              ['g_all', 'nA'], ['g_all'])
            bvw = beta[:Cn, :NCH, :]
            A(lambda e: e.activation(bvw, ab[:Cn, :NCH, NH:2 * NH], AF.Sigmoid), ['ab'], ['beta'])
            g2 = g_all[:Cn, :NCH, :].rearrange("c n h -> c (n h)") if NCH * NH <= 512 else None
            ncol = NCH * NH
            b1 = nb()
            PE(lambda e: e.matmul(PS[b1][:Cn, :ncol], lhsT=U_[:Cn, :Cn], rhs=g_all[:Cn, :NCH, :], start=True, stop=True),
               ['g_all', 'cmat'], [('ps', b1)])
            A(lambda e: e.activation(ecum[:Cn, :NCH, :].rearrange("c n h -> c (n h)"), PS[b1][:Cn, :ncol], AF.Exp),
              [('ps', b1)], ['ecum'])
            b2 = nb()
            PE(lambda e: e.matmul(PS[b2][:Cn, :ncol], lhsT=SL_[:Cn, :Cn], rhs=g_all[:Cn, :NCH, :], start=True, stop=True),
               ['g_all', 'cmat'], [('ps', b2)])
            A(lambda e: e.activation(ekd[:Cn, :NCH, :].rearrange("c n h -> c (n h)"), PS[b2][:Cn, :ncol], AF.Exp),
              [('ps', b2)], ['ekd'])
            b3 = nb()
            PE(lambda e: e.matmul(PS[b3][:, :ncol], lhsT=ONES[:Cn, :], rhs=g_all[:Cn, :NCH, :], start=True, stop=True),
               ['g_all', 'cmat'], [('ps', b3)])
            A(lambda e: e.activation(gtot[:, :NCH, :].rearrange("c n h -> c (n h)"), PS[b3][:, :ncol], AF.Exp),
              [('ps', b3)], ['gtot'])
            V(lambda e: e.tensor_tensor(bec[:Cn, :NCH, :], beta[:Cn, :NCH, :], ecum[:Cn, :NCH, :], ALU.mult),
              ['beta', 'ecum'], ['bec'])
            if s0 is None:
                V(lambda e: e.memset(Sst[:], 0.0), [], [('Sst', q) for q in range(4)])
            else:
                S.dma('sp', Sst[:], s0.rearrange("h k v -> k h v"), writes=[('Sst', q) for q in range(4)])

            def load_x(n):
                xi = xin[n % 2]
                t0 = tok0 + n * Cn
                if n == 0:
                    if conv_hist is None:
                        V(lambda e: e.memset(xi[:, :, 0:3], 0.0), [], [('xin', n % 2)])
                    else:
                        S.dma('sp', xi[:, :, 0:3], conv_hist.rearrange("p (j r) -> p j r", r=3), writes=[('xin', n % 2)])
                    S.dma('sp', xi[:, :, 3:3 + Cn], T['AqkvT'][:, t0:t0 + Cn].rearrange("(j p) t -> p j t", p=128),
                          writes=[('xin', n % 2)])
                else:
                    S.dma('sp', xi[:, :, 0:3 + Cn],
                          T['AqkvT'][:, t0 - 3:t0 + Cn].rearrange("(j p) t -> p j t", p=128), writes=[('xin', n % 2)])

            load_x(0)

            def chunk(n):
                if n + 1 < NCH:
                    load_x(n + 1)
                xi = xin[n % 2]
                xt = ('xin', n % 2)
                t0 = tok0 + n * Cn
                S.dma('sp', zin[:, :, :Cn], T['AzT'][:, t0:t0 + Cn].rearrange("(j p) t -> p j t", p=128), writes=['zin'])
                if n == NCH - 1:
                    for r in range(3):
                        S.dma('act', conv_out[r:r + 1, :].rearrange("o (j p) -> p j o", p=128), xi[:, :, Cn + r:Cn + r + 1],
                              reads=[xt], allow_slow_non_contiguous=True)
                cwb = lambda r: cw[:, :, r:r + 1].to_broadcast([128, 48, Cn])
                V(lambda e: e.tensor_tensor(ta[:, :, :Cn], xi[:, :, 0:Cn], cwb(0), ALU.mult), [xt, 'cw'], ['ta'])
                G(lambda e: e.tensor_tensor(tb[:, :, :Cn], xi[:, :, 1:1 + Cn], cwb(1), ALU.mult), [xt, 'cw'], ['tb'])
                V(lambda e: e.tensor_tensor(ys[:, :, :Cn], xi[:, :, 2:2 + Cn], cwb(2), ALU.mult), [xt, 'cw'], ['ys'])
                G(lambda e: e.tensor_tensor(tb[:, :, :Cn], tb[:, :, :Cn], ta[:, :, :Cn], ALU.add), ['ta', 'tb'], ['tb'])
                V(lambda e: e.tensor_tensor(ta[:, :, :Cn], xi[:, :, 3:3 + Cn], cwb(3), ALU.mult), [xt, 'cw'], ['ta'])
                V(lambda e: e.tensor_tensor(ys[:, :, :Cn], ys[:, :, :Cn], ta[:, :, :Cn], ALU.add), ['ta', 'ys'], ['ys'])
                G(lambda e: e.tensor_tensor(tb[:, :, :Cn], tb[:, :, :Cn], ys[:, :, :Cn], ALU.add), ['tb', 'ys'], ['tb'])
                A(lambda e: e.activation(ys[:, :, :Cn], tb[:, :, :Cn], AF.Silu), ['tb'], ['ys'])
                A(lambda e: e.activation(zs[:, :, :Cn], zin[:, :, :Cn], AF.Silu), ['zin'], ['zs'])
                V(lambda e: e.tensor_tensor(ta[:, 0:32, :Cn], ys[:, 0:32, :Cn], ys[:, 0:32, :Cn], ALU.mult), ['ys'], ['ta'])
                hp = 512 // Cn
                for part in range(32 // hp):
                    j0 = part * hp
                    isq = j0 < 16
                    bb = nb()
                    PE(lambda e, bb=bb, j0=j0, isq=isq: e.matmul(
                        PS[bb][:, :hp * Cn], lhsT=(C128 if isq else ONES), rhs=ta[:, j0:j0 + hp, :Cn],
                        start=True, stop=True), ['ta', 'cmat'], [('ps', bb)])
                    A(lambda e, bb=bb, j0=j0, isq=isq: e.activation(
                        tb[:, j0:j0 + hp, :Cn].rearrange("p j c -> p (j c)") if Cn == CM else tb[:, j0:j0 + hp, :Cn],
                        PS[bb][:, :hp * Cn] if Cn == CM else PS[bb][:, :hp * Cn].rearrange("p (j c) -> p j c", c=Cn),
                        AF.Sqrt, bias=(128e-6 if isq else 1e-6), scale=1.0),
                      [('ps', bb)], ['tb'])
                V(lambda e: e.reciprocal(tb[:, 0:32, :Cn], tb[:, 0:32, :Cn]), ['tb'], ['tb'])
                V(lambda e: e.tensor_tensor(qkn[:, :, :Cn], ys[:, 0:32, :Cn], tb[:, 0:32, :Cn], ALU.mult), ['ys', 'tb'],
                  ['qkn'])
                def quad(qd):
                    h0 = qd * 4
                    p = qd % 2
                    gq = g_all[:Cn, n, h0:h0 + 4]
                    W = 4 * Cn
                    flat = lambda t: t[:Cn, :, :Cn]
                    V(lambda e, p=p, gq=gq: e.tensor_tensor(flat(rhsG[p]), U_[:Cn, None, :Cn].to_broadcast([Cn, 4, Cn]),
                                                            gq[:, :, None].to_broadcast([Cn, 4, Cn]), ALU.mult),
                      ['g_all', 'cmat'], [('rhsG', p)])
                    V(lambda e, p=p, gq=gq: e.tensor_tensor(flat(rhsGT[p]), SL_[:Cn, None, :Cn].to_broadcast([Cn, 4, Cn]),
                                                            gq[:, :, None].to_broadcast([Cn, 4, Cn]), ALU.mult),
                      ['g_all', 'cmat'], [('rhsGT', p)])
                    V(lambda e, p=p, h0=h0: e.tensor_tensor(flat(rhsB[p]), I_[:Cn, None, :Cn].to_broadcast([Cn, 4, Cn]),
                                                            beta[:Cn, n, h0:h0 + 4][:, :, None].to_broadcast([Cn, 4, Cn]),
                                                            ALU.mult), ['beta', 'ident'], [('rhsB', p)])
                    V(lambda e, p=p, h0=h0: e.tensor_tensor(flat(rhsE[p]), I_[:Cn, None, :Cn].to_broadcast([Cn, 4, Cn]),
                                                            ecum[:Cn, n, h0:h0 + 4][:, :, None].to_broadcast([Cn, 4, Cn]),
                                                            ALU.mult), ['ecum', 'ident'], [('rhsE', p)])
                    bDT, bD, bB, bE, bKQ = nb(), nb(), nb(), nb(), nb()
                    v3 = lambda b, P_=Cn: PS[b][:P_, :W].rearrange("p (h c) -> p h c", c=Cn)
                    PE(lambda e, p=p, b=bDT: e.matmul(PS[b][:Cn, :W], lhsT=SL_[:Cn, :Cn], rhs=flat(rhsG[p]), start=True, stop=True),
                       [('rhsG', p), 'cmat'], [('ps', bDT)])
                    PE(lambda e, p=p, b=bD: e.matmul(PS[b][:Cn, :W], lhsT=U_[:Cn, :Cn], rhs=flat(rhsGT[p]), start=True, stop=True),
                       [('rhsGT', p), 'cmat'], [('ps', bD)])
                    PE(lambda e, p=p, b=bB: e.matmul(PS[b][:Cn, :W], lhsT=ONES[:Cn, :Cn], rhs=flat(rhsB[p]), start=True, stop=True),
                       [('rhsB', p), 'cmat'], [('ps', bB)])
                    PE(lambda e, p=p, b=bE: e.matmul(PS[b][:, :W], lhsT=ONES[:Cn, :], rhs=flat(rhsE[p]), start=True, stop=True),
                       [('rhsE', p), 'cmat'], [('ps', bE)])
                    for hh in range(4):
                        h = h0 + hh
                        PE(lambda e, h=h, hh=hh, b=bKQ: e.matmul(PS[b][:Cn, hh * Cn:(hh + 1) * Cn], lhsT=qkn[:, 16 + h, :Cn],
                                                                 rhs=qkn[:, 16 + h, :Cn], start=True, stop=True),
                           ['qkn'], [('ps', bKQ)])
                    bQK = nb()
                    for hh in range(4):
                        h = h0 + hh
                        PE(lambda e, h=h, hh=hh, b=bQK: e.matmul(PS[b][:Cn, hh * Cn:(hh + 1) * Cn], lhsT=qkn[:, 16 + h, :Cn],
                                                                 rhs=qkn[:, h, :Cn], start=True, stop=True),
                           ['qkn'], [('ps', bQK)])
                    A(lambda e, p=p, b=bDT: e.activation(flat(ex1[p]), v3(b), AF.Exp), [('ps', bDT)], [('ex1', p)])
                    A(lambda e, p=p, b=bD: e.activation(flat(ex2[p]), v3(b), AF.Exp), [('ps', bD)], [('ex2', p)])
                    V(lambda e, p=p: e.tensor_tensor(flat(decU[p]), flat(ex1[p]), U_[:Cn, None, :Cn].to_broadcast([Cn, 4, Cn]),
                                                     ALU.mult), [('ex1', p), 'cmat'], [('decU', p)])
                    V(lambda e, p=p, b=bQK, h0=h0: e.tensor_tensor(qkm[:Cn, h0:h0 + 4, :Cn], v3(b), flat(decU[p]), ALU.mult),
                      [('ps', bQK), ('decU', p)], ['qkm'])
                    V(lambda e, p=p: e.tensor_tensor(flat(ex1[p]), flat(ex1[p]), SU_[:Cn, None, :Cn].to_broadcast([Cn, 4, Cn]),
                                                     ALU.mult), [('ex1', p), 'cmat'], [('ex1', p)])
                    V(lambda e, p=p, b=bKQ: e.tensor_tensor(flat(ex1[p]), v3(b), flat(ex1[p]), ALU.mult),
                      [('ps', bKQ), ('ex1', p)], [('ex1', p)])
                    V(lambda e, p=p, b=bB: e.scalar_tensor_tensor(
                        out=NT_[0][p][:Cn, :, :Cn].rearrange("p h c -> p (h c)") if Cn == CM else NT_[0][p][:Cn, :, :Cn],
                        in0=ex1[p][:Cn, :, :Cn].rearrange("p h c -> p (h c)") if Cn == CM else ex1[p][:Cn, :, :Cn],
                        scalar=-1.0,
                        in1=PS[b][:Cn, :W] if Cn == CM else v3(b), op0=ALU.mult, op1=ALU.mult),
                      [('ps', bB), ('ex1', p)], [('NT0', p)])
                    V(lambda e, p=p: e.tensor_tensor(flat(ex2[p]), flat(ex2[p]), SL_[:Cn, None, :Cn].to_broadcast([Cn, 4, Cn]),
                                                     ALU.mult), [('ex2', p), 'cmat'], [('ex2', p)])
                    V(lambda e, p=p, b=bKQ: e.tensor_tensor(flat(ex2[p]), v3(b), flat(ex2[p]), ALU.mult),
                      [('ps', bKQ), ('ex2', p)], [('ex2', p)])
                    V(lambda e, p=p, h0=h0: e.tensor_tensor(flat(ex2[p]), flat(ex2[p]),
                                                            beta[:Cn, n, h0:h0 + 4][:, :, None].to_broadcast([Cn, 4, Cn]),
                                                            ALU.mult), [('ex2', p), 'beta'], [('ex2', p)])
                    V(lambda e, p=p: e.tensor_scalar(flat(NN[0][p]), flat(ex2[p]), -1.0, None, ALU.mult),
                      [('ex2', p)], [('NN0', p)])
                    V(lambda e, b=bE, h0=h0: e.tensor_tensor(qdT[:, h0:h0 + 4, :Cn], qkn[:, h0:h0 + 4, :Cn],
                                                             PS[b][:, :W].rearrange("p (h c) -> p h c", c=Cn), ALU.mult),
                      [('ps', bE), 'qkn'], ['qdT'])
                    V(lambda e, p=p: e.tensor_tensor(flat(TT[0][p]), flat(NT_[0][p]), I_[:Cn, None, :Cn].to_broadcast([Cn, 4, Cn]),
                                                     ALU.add), [('NT0', p), 'ident'], [('TT0', p)])
                    nlev = 5 if Cn == 64 else 4
                    for lv in range(nlev):
                        a, bq = lv % 2, (lv + 1) % 2
                        last = lv == nlev - 1
                        bN, bNT, bT = nb(), (None if last else nb()), nb()
                        for hh in range(4):
                            PE(lambda e, hh=hh, a=a, p=p, b=bN: e.matmul(PS[b][:Cn, hh * Cn:(hh + 1) * Cn], lhsT=NT_[a][p][:Cn, hh, :Cn],
                                                                         rhs=NN[a][p][:Cn, hh, :Cn], start=True, stop=True),
                               [('NT%d' % a, p), ('NN%d' % a, p)], [('ps', bN)])
                        if not last:
                            for hh in range(4):
                                PE(lambda e, hh=hh, a=a, p=p, b=bNT: e.matmul(PS[b][:Cn, hh * Cn:(hh + 1) * Cn], lhsT=NN[a][p][:Cn, hh, :Cn],
                                                                              rhs=NT_[a][p][:Cn, hh, :Cn], start=True, stop=True),
                                   [('NT%d' % a, p), ('NN%d' % a, p)], [('ps', bNT)])
                        A(lambda e, bq=bq, p=p, b=bN: e.activation(flat(NN[bq][p]), v3(b), AF.Copy), [('ps', bN)], [('NN%d' % bq, p)])
                        if not last:
                            V(lambda e, bq=bq, p=p, b=bNT: e.tensor_copy(flat(NT_[bq][p]), v3(b)), [('ps', bNT)], [('NT%d' % bq, p)])
                        for hh in range(4):
                            PE(lambda e, hh=hh, a=a, bq=bq, p=p, b=bT: e.matmul(PS[b][:Cn, hh * Cn:(hh + 1) * Cn], lhsT=NN[bq][p][:Cn, hh, :Cn],
                                                                                rhs=TT[a][p][:Cn, hh, :Cn], start=True, stop=True),
                               [('NN%d' % bq, p), ('TT%d' % a, p)], [('ps', bT)])
                        V(lambda e, a=a, bq=bq, p=p, b=bT: e.tensor_tensor(flat(TT[bq][p]), flat(TT[a][p]), v3(b), ALU.add),
                          [('ps', bT), ('TT%d' % a, p)], [('TT%d' % bq, p)])
                    tf = nlev % 2
                    TTf = TT[tf][p]
                    ttag = ('TT%d' % tf, p)
                    bK, bV = nb(), nb()
                    for hh in range(4):
                        h = h0 + hh
                        PE(lambda e, h=h, hh=hh, b=bK: e.transpose(PS[b][:Cn, hh * 128:(hh + 1) * 128], qkn[:, 16 + h, :Cn], I_[:, :]),
                           ['qkn', 'ident'], [('ps', bK)])
                        PE(lambda e, h=h, hh=hh, b=bV: e.transpose(PS[b][:Cn, hh * 128:(hh + 1) * 128], ys[:, 32 + h, :Cn], I_[:, :]),
                           ['ys', 'ident'], [('ps', bV)])
                    k3 = lambda b: PS[b][:Cn, :512].rearrange("p (h d) -> p h d", d=128)
                    bc3 = lambda t, h0=h0: t[:Cn, n, h0:h0 + 4][:, :, None].to_broadcast([Cn, 4, 128])
                    V(lambda e, p=p, b=bK: e.tensor_tensor(bek[p][:Cn, :, :], k3(b), bc3(bec), ALU.mult), [('ps', bK), 'bec'],
                      [('bek', p)])
                    V(lambda e, b=bK, h0=h0: e.tensor_tensor(kdec[:Cn, h0:h0 + 4, :], k3(b), bc3(ekd), ALU.mult),
                      [('ps', bK), 'ekd'], ['kdec'])
                    V(lambda e, p=p, b=bV: e.tensor_tensor(bv[p][:Cn, :, :], k3(b), bc3(beta), ALU.mult), [('ps', bV), 'beta'],
                      [('bv', p)])
                    bWV, bWK = nb(), nb()
                    for hh in range(4):
                        PE(lambda e, hh=hh, p=p, b=bWV, TTf=TTf: e.matmul(PS[b][:Cn, hh * 128:(hh + 1) * 128], lhsT=TTf[:Cn, hh, :Cn],
                                                                          rhs=bv[p][:Cn, hh, :], start=True, stop=True),
                           [ttag, ('bv', p)], [('ps', bWV)])
                        PE(lambda e, hh=hh, p=p, b=bWK, TTf=TTf: e.matmul(PS[b][:, hh * Cn:(hh + 1) * Cn], lhsT=bek[p][:Cn, hh, :],
                                                                          rhs=TTf[:Cn, hh, :Cn], start=True, stop=True),
                           [ttag, ('bek', p)], [('ps', bWK)])
                    A(lambda e, b=bWV, h0=h0: e.activation(wv[:Cn, h0:h0 + 4, :], k3(b), AF.Copy), [('ps', bWV)], ['wv'])
                    A(lambda e, b=bWK, h0=h0: e.activation(wkT[:, h0:h0 + 4, :Cn], PS[b][:, :W].rearrange("p (h c) -> p h c", c=Cn),
                                                           AF.Copy), [('ps', bWK)], ['wkT'])
                    bU = nb()
                    for hh in range(4):
                        h = h0 + hh
                        PE(lambda e, h=h, hh=hh, b=bU: e.matmul(PS[b][:Cn, hh * 128:(hh + 1) * 128], lhsT=wkT[:, h, :Cn], rhs=Sst[:, h, :],
                                                                start=True, stop=True), ['wkT', ('Sst', qd)], [('ps', bU)])
                    V(lambda e, p=p, b=bU, h0=h0: e.tensor_tensor(u_[p][:Cn, :, :], wv[:Cn, h0:h0 + 4, :], k3(b), ALU.subtract),
                      [('ps', bU), 'wv'], [('u', p)])
                    bO = nb()
                    for hh in range(4):
                        h = h0 + hh
                        PE(lambda e, h=h, hh=hh, b=bO: e.matmul(PS[b][:Cn, hh * 128:(hh + 1) * 128], lhsT=qdT[:, h, :Cn], rhs=Sst[:, h, :],
                                                                start=True, stop=False), ['qdT', ('Sst', qd)], [('ps', bO)])
                        PE(lambda e, h=h, hh=hh, p=p, b=bO: e.matmul(PS[b][:Cn, hh * 128:(hh + 1) * 128], lhsT=qkm[:Cn, h, :Cn],
                                                                     rhs=u_[p][:Cn, hh, :], start=False, stop=True),
                           ['qkm', ('u', p)], [('ps', bO)])
                    bS_ = nb()
                    for hh in range(4):
                        h = h0 + hh
                        PE(lambda e, h=h, hh=hh, p=p, b=bS_: e.matmul(PS[b][:, hh * 128:(hh + 1) * 128], lhsT=kdec[:Cn, h, :],
                                                                      rhs=u_[p][:Cn, hh, :], start=True, stop=True),
                           ['kdec', ('u', p)], [('ps', bS_)])
                    A(lambda e, p=p, b=bO: e.activation(osq[p][:Cn, :, :], k3(b), AF.Square), [('ps', bO)], [('osq', p)])
                    V(lambda e, p=p: e.tensor_reduce(ms[p][:Cn, :], osq[p][:Cn, :, :], AX.X, ALU.add), [('osq', p)], [('ms', p)])
                    A(lambda e, p=p: e.activation(rstd[p][:Cn, :], ms[p][:Cn, :], AF.Sqrt, bias=1e-6, scale=1.0 / 128), [('ms', p)],
                      [('rstd', p)])
                    V(lambda e, p=p: e.reciprocal(rstd[p][:Cn, :], rstd[p][:Cn, :]), [('rstd', p)], [('rstd', p)])
                    V(lambda e, p=p, b=bO: e.tensor_tensor(on_[p][:Cn, :, :], k3(b),
                                                           rstd[p][:Cn, :][:, :, None].to_broadcast([Cn, 4, 128]), ALU.mult),
                      [('ps', bO), ('rstd', p)], [('on', p)])
                    bOT = nb()
                    for hh in range(4):
                        PE(lambda e, hh=hh, p=p, b=bOT: e.transpose(PS[b][:, hh * Cn:(hh + 1) * Cn], on_[p][:Cn, hh, :], I_[:Cn, :Cn]),
                           [('on', p), 'ident'], [('ps', bOT)])
                    oc = (n % (256 // Cn)) * Cn
                    V(lambda e, b=bOT, h0=h0, oc=oc: e.scalar_tensor_tensor(
                        out=OTb[:, h0:h0 + 4, oc:oc + Cn], in0=PS[b][:, :W].rearrange("p (h c) -> p h c", c=Cn),
                        scalar=normw[:, 0:1], in1=zs[:, h0:h0 + 4, :Cn], op0=ALU.mult, op1=ALU.mult),
                      [('ps', bOT), 'zs', 'normw'], ['OTb'])
                    V(lambda e, h0=h0: e.tensor_tensor(Stmp[:, :, :], Sst[:, h0:h0 + 4, :],
                                                       gtot[:, n, h0:h0 + 4][:, :, None].to_broadcast([128, 4, 128]), ALU.mult),
                      [('Sst', qd), 'gtot'], ['Stmp'])
                    V(lambda e, h0=h0, b=bS_: e.tensor_tensor(Sst[:, h0:h0 + 4, :], Stmp[:, :, :],
                                                              PS[b][:, :512].rearrange("p (h d) -> p h d", d=128), ALU.add),
                      [('ps', bS_), 'Stmp'], [('Sst', qd)])
                for qd in range(4):
                    quad(qd)
                per = 256 // Cn
                if (n + 1) % per == 0 or n == NCH - 1:
                    nfl = ((n % per) + 1) * Cn
                    tf0 = tok0 + (n // per) * 256
                    S.dma('act', T['OT'][0:AW, tf0:tf0 + nfl].rearrange("(h p) t -> p h t", p=128), OTb[:, :, 0:nfl],
                          reads=['OTb'])
            for n in range(NCH):
                chunk(n)
            S.dma('act', s_out.rearrange("h k v -> k h v"), Sst[:], reads=[('Sst', q) for q in range(4)])

        seq(0, SEQ, 64, None, None, T['sgp'][:, :, :], T['scp'][:, :], "p")
        for s in range(NS):
            seq(SEQ + s * DS, DS, DS, T['sconv_t'][s], T['sg'][s], T['sgs'][s], T['scs'][s], "s%d" % s)
    S.barrier()


def ln_rows(S, nc, L, n, vt, vtag, gB, bB, out_ap_sb, out_tag):
    i = L.i % 2
    L.i += 1
    sm, ssq, rs = L.sm[i], L.ssq[i], L.rs[i]
    S.op('act', lambda e: e.activation(L.junk[:n, :], vt[:n, :], AF.Identity, accum_out=sm[:n, :]),
         reads=[vtag], writes=[('lnsm', i), 'lnjunk'])
    S.op('dve', lambda e: e.tensor_scalar(sm[:n, :], sm[:n, :], 1.0 / D, None, ALU.mult), reads=[('lnsm', i)],
         writes=[('lnsm', i)])
    S.op('dve', lambda e: e.tensor_scalar(vt[:n, :], vt[:n, :], sm[:n, 0:1], None, ALU.subtract),
         reads=[vtag, ('lnsm', i)], writes=[vtag])
    S.op('act', lambda e: e.activation(L.junk[:n, :], vt[:n, :], AF.Square, accum_out=ssq[:n, :]),
         reads=[vtag], writes=[('lnssq', i), 'lnjunk'])
    S.op('act', lambda e: e.activation(rs[:n, :], ssq[:n, :], AF.Sqrt, bias=L.eps[:n, 0:1], scale=1.0 / D),
         reads=[('lnssq', i), 'lneps'], writes=[('lnrs', i)])
    S.op('dve', lambda e: e.reciprocal(rs[:n, :], rs[:n, :]), reads=[('lnrs', i)], writes=[('lnrs', i)])
    S.op('dve', lambda e: e.scalar_tensor_tensor(out=vt[:n, :], in0=vt[:n, :], scalar=rs[:n, 0:1], in1=gB[:n, :],
                                                 op0=ALU.mult, op1=ALU.mult),
         reads=[vtag, ('lnrs', i), 'lng'], writes=[vtag])
    S.op('dve', lambda e: e.tensor_tensor(out_ap_sb[:n, :], vt[:n, :], bB[:n, :], ALU.add),
         reads=[vtag, 'lnb'], writes=[out_tag])


def ln_setup(S, nc, st, T, gname, bname, pref):
    L = Ctx()
    sb = lambda name, shape, dt=F32: st.enter_context(nc.sbuf_tensor(pref + name, shape, dt))
    L.i = 0
    L.sm = [sb("sm%d" % i, [128, 1]) for i in range(2)]
    L.ssq = [sb("ssq%d" % i, [128, 1]) for i in range(2)]
    L.rs = [sb("rs%d" % i, [128, 1]) for i in range(2)]
    L.eps = sb("eps", [128, 1])
    L.junk = sb("junk", [128, D], BF16)
    L.g = sb("g", [128, D])
    L.b = sb("b", [128, D])
    S.op('dve', lambda e: e.memset(L.eps[:], LN_EPS), writes=['lneps'])
    S.dma('sp', L.g[:], T[gname][:, :], writes=['lng'])
    S.dma('sp', L.b[:], T[bname][:, :], writes=['lnb'])
    return L


def row_tiles():
    return [(i * 128, 128) for i in range(16)] + [(SEQ, NS * DS)]


def x_rows(T, r0, n):
    return T['xp'][r0:r0 + n, :] if r0 < SEQ else T['xs'][r0 - SEQ:r0 - SEQ + n, :]


def y_rows(T, r0, n):
    return T['yp'][r0:r0 + n, :] if r0 < SEQ else T['ys'][r0 - SEQ:r0 - SEQ + n, :]


def phase_D(S, nc, C, T):
    import os as _os
    _kd = int(_os.environ.get("KD", "3"))
    with ExitStack() as st:
        if not (_kd & 1):
            st.close()
            return phase_D2(S, nc, C, T, _kd)
        C.XT = st.enter_context(nc.sbuf_tensor("D_XT", [128, KC, GT], BF16))
        C.wst = [st.enter_context(nc.sbuf_tensor("D_wst%d" % i, [128, KC, 128], F32)) for i in range(2)]
        C.wbf = [st.enter_context(nc.sbuf_tensor("D_wbf%d" % i, [128, KC, 512], BF16)) for i in range(2)]
        mst = [st.enter_context(nc.sbuf_tensor("D_mst%d" % i, [128, 512], F32)) for i in range(3)]
        C.wst_i = 0
        cnt = {'m': 0}
        for g in range(2):
            S.dma('sp', C.XT[:, :, 0:1024], T['OT'][:, g * 1024:(g + 1) * 1024].rearrange("(kc p) t -> p kc t", p=128),
                  writes=[('XT', i) for i in range(8)])
            S.dma('sp', C.XT[:, :, 1024:GT], T['OT'][:, SEQ + g * DS:SEQ + (g + 1) * DS].rearrange("(kc p) t -> p kc t", p=128),
                  writes=[('XT', 8)])
            load_w(S, C, T['w_out'], 0, 512, 0)
            for cg in range(8):
                if cg + 1 < 8:
                    load_w(S, C, T['w_out'], (cg + 1) * 512, 512, (cg + 1) % 2)

                def emit(b, c0, n, cg=cg, g=g):
                    mi = cnt['m'] % 3
                    cnt['m'] += 1
                    mt = mst[mi]
                    if cnt['m'] % 2:
                        S.op('act', lambda e: e.activation(mt[:n, :], C.PS[b][:n, :], AF.Copy), reads=[('ps', b)],
                             writes=[('mst', mi)])
                    else:
                        S.op('dve', lambda e: e.tensor_copy(mt[:n, :], C.PS[b][:n, :]), reads=[('ps', b)],
                             writes=[('mst', mi)])
                    r0 = tokmap(g, c0, n)
                    S.dma('act', T['MIX'][r0:r0 + n, cg * 512:(cg + 1) * 512], mt[:n, :], reads=[('mst', mi)])
                gemm_T(S, C, cg % 2, 512, emit)
    S.barrier()
    phase_D2(S, nc, C, T, _kd)


def phase_D2(S, nc, C, T, _kd):
    if not (_kd & 2):
        return
    with ExitStack() as st:
        L = ln_setup(S, nc, st, T, 'ln1g_b', 'ln1b_b', "D_ln")
        xt_ = [st.enter_context(nc.sbuf_tensor("D_x%d" % i, [128, D], F32)) for i in range(2)]
        mx_ = [st.enter_context(nc.sbuf_tensor("D_m%d" % i, [128, D], F32)) for i in range(2)]

        def tile(ti, r0, n):
            i = ti % 2
            S.dma('sp', xt_[i][:n, :], x_rows(T, r0, n), writes=[('dx', i)])
            S.dma('sp', mx_[i][:n, :], T['MIX'][r0:r0 + n, :], writes=[('dm', i)])
            S.op('dve', lambda e: e.scalar_tensor_tensor(out=xt_[i][:n, :], in0=xt_[i][:n, :], scalar=float(ALPHA),
                                                         in1=mx_[i][:n, :], op0=ALU.mult, op1=ALU.add),
                 reads=[('dx', i), ('dm', i)], writes=[('dx', i)])
            ln_rows(S, nc, L, n, xt_[i], ('dx', i), L.g, L.b, mx_[i], ('dm', i))
            S.dma('act', T['H'][r0:r0 + n, :], mx_[i][:n, :], reads=[('dm', i)])
        for ti, (r0, n) in enumerate(row_tiles()):
            tile(ti, r0, n)
    S.barrier()


def phase_E(S, nc, C, T):
    NEG = -1.0e30
    with ExitStack() as st:
        sb = lambda name, shape, dt=F32: st.enter_context(nc.sbuf_tensor("E1_" + name, shape, dt))
        C.XT = sb("XT", [128, KC, GT], BF16)
        C.xrow = [sb("xrow%d" % i, [128, D]) for i in range(2)]
        C.wst = [sb("wst%d" % i, [128, KC, 128]) for i in range(2)]
        C.wbf = [sb("wbf%d" % i, [128, KC, 128], BF16) for i in range(2)]
        C.wst_i = 0
        skn = sb("skn", [128, 16, 128])
        KT = sb("KT", [128, 16, 128])
        qst = [sb("qst%d" % i, [128, GT]) for i in range(2)]
        scs_ = [sb("scs%d" % i, [128, 128]) for i in range(3)]
        S.dma('sp', skn[:], T['subk'][:, :, :].rearrange("j n c -> n j c"), writes=['skn'])
        for j in range(16):
            b = C.ps_next()
            S.op('pe', lambda e, j=j, b=b: e.transpose(C.PS[b][:, 0:128], skn[:, j, :], C.ident[:, :]),
                 reads=['skn', 'ident'], writes=[('ps', b)])
            S.op('dve', lambda e, j=j, b=b: e.tensor_copy(KT[:, j, :], C.PS[b][:, 0:128]), reads=[('ps', b)],
                 writes=['KT'])
        cnt = {'s': 0, 'e': 0}
        for g in range(2):
            rows = [(T['H'][g * 1024 + i * 128:g * 1024 + (i + 1) * 128, :], 128, i * 128) for i in range(8)]
            rows.append((T['H'][SEQ + g * DS:SEQ + (g + 1) * DS, :], DS, 1024))
            build_xT(S, C, g, rows)
            load_w(S, C, T['w_q'], 0, 128, 0)
            for j in range(16):
                if j + 1 < 16:
                    load_w(S, C, T['w_q'], (j + 1) * 128, 128, (j + 1) % 2)
                qs = qst[j % 2]

                def emitF(b, t0, n, qs=qs, j=j):
                    cnt['e'] += 1
                    if cnt['e'] % 2:
                        S.op('act', lambda e: e.activation(qs[:, t0:t0 + n], C.PS[b][:, :n], AF.Copy),
                             reads=[('ps', b)], writes=[('qst', j % 2)])
                    else:
                        S.op('dve', lambda e: e.tensor_copy(qs[:, t0:t0 + n], C.PS[b][:, :n]),
                             reads=[('ps', b)], writes=[('qst', j % 2)])
                gemm_F(S, C, j % 2, 128, emitF)

                def score(ti, j=j, qs=qs, g=g):
                    c0 = ti * 128
                    n = 128 if ti < 8 else DS
                    b = C.ps_next()
                    S.op('pe', lambda e: e.matmul(C.PS[b][:n, 0:128], lhsT=qs[:, c0:c0 + n], rhs=KT[:, j, :],
                                                  start=True, stop=True),
                         reads=[('qst', j % 2), 'KT'], writes=[('ps', b)])
                    si = cnt['s'] % 3
                    cnt['s'] += 1
                    sc_t = scs_[si]
                    S.op('dve' if si % 2 else 'act',
                         (lambda e: e.tensor_copy(sc_t[:n, :], C.PS[b][:n, 0:128])) if si % 2 else
                         (lambda e: e.activation(sc_t[:n, :], C.PS[b][:n, 0:128], AF.Copy)),
                         reads=[('ps', b)], writes=[('scs', si)])
                    r0 = tokmap(g, c0, n)
                    S.dma('act', T['SC'][r0:r0 + n, j * 128:(j + 1) * 128], sc_t[:n, :], reads=[('scs', si)])
                for ti in range(9):
                    score(ti)
    S.barrier()
    with ExitStack() as st:
        sb = lambda name, shape, dt=F32: st.enter_context(nc.sbuf_tensor("E2_" + name, shape, dt))
        L = ln_setup(S, nc, st, T, 'ln2g_b', 'ln2b_b', "E_ln")
        sc = sb("sc", [128, 16, 128])
        sc2 = sb("sc2", [128, 16, 128])
        stop_ = sb("stop", [128, 16, 16])
        itop = sb("itop", [128, 16, 16], U32)
        itf = sb("itf", [128, 16, 16])
        cand = sb("cand", [128, 8, 256])
        cand2 = sb("cand2", [128, 8, 256])
        cidx = sb("cidx", [128, 8, 256])
        tops = sb("tops", [128, 8, 16])
        eidf = sb("eidf", [128, 128])
        eidi = sb("eidi", [128, 128], I32)
        gate = sb("gate", [128, 8, 16])
        zs_ = sb("zsum", [128, 8])
        pre = sb("pre", [128, 128])
        actv = sb("actv", [128, 128])
        ht = sb("ht", [128, D])
        acc = sb("acc", [128, D])
        ub = [sb("ub%d" % i, [128, D]) for i in range(2)]
        vb = ub
        tmp1 = sb("tmp", [128, D])
        tmp = [tmp1, tmp1]
        sj = [sb("sj%d" % i, [128, 256]) for i in range(2)]
        V = lambda fn, r, w: S.op('dve', fn, reads=r, writes=w)
        A = lambda fn, r, w: S.op('act', fn, reads=r, writes=w)
        G = lambda fn, r, w: S.op('pool', fn, reads=r, writes=w)

        def tile(r0, n):
            S.dma('sp', sc[:n, :, :], T['SC'][r0:r0 + n, :].rearrange("t (j k) -> t j k", k=128), writes=['sc'])
            S.dma('sp', ht[:n, :], T['H'][r0:r0 + n, :], writes=['ht'])
            for j in range(16):
                def pair(j):
                    V(lambda e: e.max(out=stop_[:n, j, 0:8], in_=sc[:n, j, :]), ['sc'], ['stop'])
                    V(lambda e: e.max_index(out=itop[:n, j, 0:8], in_max=stop_[:n, j, 0:8], in_values=sc[:n, j, :]),
                      ['sc', 'stop'], ['itop'])
                    V(lambda e: e.match_replace(out=sc2[:n, j, :], in_to_replace=stop_[:n, j, 0:8], in_values=sc[:n, j, :],
                                                imm_value=NEG), ['sc', 'stop'], ['sc2'])
                    V(lambda e: e.max(out=stop_[:n, j, 8:16], in_=sc2[:n, j, :]), ['sc2'], ['stop'])
                    V(lambda e: e.max_index(out=itop[:n, j, 8:16], in_max=stop_[:n, j, 8:16], in_values=sc2[:n, j, :]),
                      ['sc2', 'stop'], ['itop'])
                pair(j)
            V(lambda e: e.tensor_copy(itf[:n, :, :], itop[:n, :, :]), ['itop'], ['itf'])
            s4 = stop_[:n, :, :].rearrange("t (h p) k -> t h p k", p=2)
            i4 = itf[:n, :, :].rearrange("t (h p) k -> t h p k", p=2)
            c4 = lambda t: t[:n, :, :].rearrange("t h (a b) -> t h a b", b=16)
            V(lambda e: e.tensor_tensor(c4(cand), s4[:, :, 0, :][:, :, :, None].to_broadcast([n, 8, 16, 16]),
                                        s4[:, :, 1, :][:, :, None, :].to_broadcast([n, 8, 16, 16]), ALU.add),
              ['stop'], ['cand'])
            V(lambda e: e.tensor_scalar(i4[:, :, 0, :], i4[:, :, 0, :], 128.0, None, ALU.mult), ['itf'], ['itf'])
            V(lambda e: e.tensor_tensor(c4(cidx), i4[:, :, 0, :][:, :, :, None].to_broadcast([n, 8, 16, 16]),
                                        i4[:, :, 1, :][:, :, None, :].to_broadcast([n, 8, 16, 16]), ALU.add),
              ['itf'], ['cidx'])
            for hd in range(8):
                def head(hd):
                    V(lambda e: e.max(out=tops[:n, hd, 0:8], in_=cand[:n, hd, :]), ['cand'], ['tops'])
                    V(lambda e: e.match_replace(out=cand2[:n, hd, :], in_to_replace=tops[:n, hd, 0:8],
                                                in_values=cand[:n, hd, :], imm_value=NEG), ['cand', 'tops'], ['cand2'])
                    V(lambda e: e.max(out=tops[:n, hd, 8:16], in_=cand2[:n, hd, :]), ['cand2'], ['tops'])
                    for k in range(16):
                        def pick(k):
                            m = hd * 16 + k
                            V(lambda e: e.scalar_tensor_tensor(out=sj[m % 2][:n, :], in0=cand[:n, hd, :],
                                                               scalar=tops[:n, hd, k:k + 1], in1=cidx[:n, hd, :],
                                                               op0=ALU.is_equal, op1=ALU.mult,
                                                               accum_out=eidf[:n, m:m + 1]),
                              ['cand', 'tops', 'cidx'], [('sj', m % 2), 'eidf'])
                        pick(k)
                head(hd)
            V(lambda e: e.tensor_scalar(eidf[:n, :], eidf[:n, :], 16383.0, 0.0, ALU.min, ALU.max), ['eidf'], ['eidf'])
            V(lambda e: e.tensor_copy(eidi[:n, :], eidf[:n, :]), ['eidf'], ['eidi'])
            V(lambda e: e.tensor_tensor(gate[:n, :, :], tops[:n, :, :], tops[:n, :, 0:1].to_broadcast([n, 8, 16]),
                                        ALU.subtract), ['tops'], ['gate'])
            A(lambda e: e.activation(gate[:n, :, :], gate[:n, :, :], AF.Exp), ['gate'], ['gate'])
            V(lambda e: e.tensor_reduce(zs_[:n, :], gate[:n, :, :], AX.X, ALU.add), ['gate'], ['zsum'])
            V(lambda e: e.reciprocal(zs_[:n, :], zs_[:n, :]), ['zsum'], ['zsum'])
            V(lambda e: e.tensor_tensor(gate[:n, :, :], gate[:n, :, :], zs_[:n, :][:, :, None].to_broadcast([n, 8, 16]),
                                        ALU.mult), ['gate', 'zsum'], ['gate'])
            for m in range(128):
                def upick(m):
                    i = m % 2
                    S.gather(ub[i][:n, :], T['pu'][:, :], eidi[:n, m:m + 1], reads=['eidi'], writes=[('ub', i)],
                             bounds=None)
                    V(lambda e: e.scalar_tensor_tensor(out=L.junk[:n, :], in0=ub[i][:n, :], scalar=1.0, in1=ht[:n, :],
                                                       op0=ALU.mult, op1=ALU.mult, accum_out=pre[:n, m:m + 1]),
                      [('ub', i), 'ht'], ['lnjunk', 'pre'])
                upick(m)
            A(lambda e: e.activation(actv[:n, :], pre[:n, :], AF.Gelu), ['pre'], ['actv'])
            V(lambda e: e.tensor_tensor(actv[:n, :], actv[:n, :], gate[:n, :, :].rearrange("t h k -> t (h k)"), ALU.mult),
              ['actv', 'gate'], ['actv'])
            G(lambda e: e.memset(acc[:n, :], 0.0), [], ['acc'])
            for m in range(128):
                def vpick(m):
                    i = m % 2
                    S.gather(vb[i][:n, :], T['pv'][:, :], eidi[:n, m:m + 1], reads=['eidi'], writes=[('ub', i)],
                             bounds=None)
                    A(lambda e: e.activation(tmp[i][:n, :], vb[i][:n, :], AF.Identity, scale=actv[:n, m:m + 1]),
                      [('ub', i), 'actv'], ['tmp'])
                    G(lambda e: e.tensor_tensor(acc[:n, :], acc[:n, :], tmp[i][:n, :], ALU.add), ['tmp', 'acc'],
                      ['acc'])
                vpick(m)
            V(lambda e: e.scalar_tensor_tensor(out=ht[:n, :], in0=ht[:n, :], scalar=float(ALPHA), in1=acc[:n, :],
                                               op0=ALU.mult, op1=ALU.add), ['ht', 'acc'], ['ht'])
            ln_rows(S, nc, L, n, ht, 'ht', L.g, L.b, acc, 'acc')
            S.dma('act', y_rows(T, r0, n), acc[:n, :], reads=['acc'])
        import os as _os
        for (r0, n) in row_tiles()[:int(_os.environ.get("KT", "99"))]:
            tile(r0, n)
    S.barrier()
```

```python
import numpy as np
from contextlib import ExitStack
import concourse.bass as bass
import concourse.mybir as mybir
from concourse.bass_utils import run_bass_kernel_spmd

F32 = mybir.dt.float32
BF16 = mybir.dt.bfloat16
I32 = mybir.dt.int32
U32 = mybir.dt.uint32
AF = mybir.ActivationFunctionType
ALU = mybir.AluOpType
AX = mybir.AxisListType

ENGS = ['pe', 'act', 'dve', 'pool', 'sp']


class Sched:
    def __init__(self, nc, stack, n_dma=32):
        self.nc = nc
        self.q = {e: [] for e in ENGS}
        self.sem = {e: stack.enter_context(nc.semaphore("s_" + e)) for e in ENGS}
        self.cnt = {e: 0 for e in ENGS}
        self.nd = n_dma
        self.dsem = [stack.enter_context(nc.semaphore("d%d" % i)) for i in range(n_dma)]
        self.dcnt = [0] * n_dma
        self.dnext = 0
        self.seen = {e: {} for e in ENGS}
        self.lw = {}
        self.rd = {}
        self.ninst = 0

    def _deps(self, reads, writes):
        deps = {}
        for r in reads:
            kv = self.lw.get(r)
            if kv is not None and deps.get(kv[0], 0) < kv[1]:
                deps[kv[0]] = kv[1]
        for w in writes:
            kv = self.lw.get(w)
            if kv is not None and deps.get(kv[0], 0) < kv[1]:
                deps[kv[0]] = kv[1]
            for k, v in self.rd.get(w, {}).items():
                if deps.get(k, 0) < v:
                    deps[k] = v
        return deps

    def _waits(self, eng, deps):
        for k, v in deps.items():
            if k == 'pe' and eng == 'pe':
                continue
            if self.seen[eng].get(k, 0) >= v:
                continue
            self.seen[eng][k] = v
            sem = self.sem[k] if isinstance(k, str) else self.dsem[k[1]]
            self.q[eng].append(lambda e, sem=sem, v=v: e.wait_ge(sem, v))
            self.ninst += 1

    def _mark(self, key, v, reads, writes):
        for r in reads:
            d = self.rd.setdefault(r, {})
            if d.get(key, 0) < v:
                d[key] = v
        for w in writes:
            self.lw[w] = (key, v)
            self.rd[w] = {}

    def op(self, eng, fn, reads=(), writes=()):
        self._waits(eng, self._deps(reads, writes))
        self.cnt[eng] += 1
        v = self.cnt[eng]
        sem = self.sem[eng]
        self.q[eng].append(lambda e: fn(e).then_inc(sem, 1))
        self.ninst += 1
        self._mark(eng, v, reads, writes)

    def dma(self, q, out, in_, reads=(), writes=(), **kw):
        i = self.dnext
        self.dnext = (i + 1) % self.nd
        deps = self._deps(reads, writes)
        if self.dcnt[i] > 0:
            deps[('d', i)] = max(deps.get(('d', i), 0), self.dcnt[i])
        self._waits(q, deps)
        self.dcnt[i] += 16
        v = self.dcnt[i]
        sem = self.dsem[i]
        self.q[q].append(lambda e: e.dma_start(out=out, in_=in_, **kw).then_inc(sem, 16))
        self.ninst += 1
        self._mark(('d', i), v, reads, writes)

    def gather(self, out, table, idx_ap, reads=(), writes=(), bounds=None):
        q = 'pool'
        i = self.dnext
        self.dnext = (i + 1) % self.nd
        deps = self._deps(reads, writes)
        if self.dcnt[i] > 0:
            deps[('d', i)] = max(deps.get(('d', i), 0), self.dcnt[i])
        self._waits(q, deps)
        self.dcnt[i] += 16
        v = self.dcnt[i]
        sem = self.dsem[i]

        def f(e):
            self._gdbg = getattr(self, '_gdbg', 0) + 1
            if self._gdbg > 4351:
                print("GATHER", self._gdbg, out.shape, idx_ap, flush=True)
            return e.indirect_dma_start(
                out=out, out_offset=None, in_=table,
                in_offset=bass.IndirectOffsetOnAxis(ap=idx_ap, axis=0),
                bounds_check=bounds, oob_is_err=False).then_inc(sem, 16)
        self.q[q].append(f)
        self.ninst += 1
        self._mark(('d', i), v, reads, writes)

    def barrier(self):
        for e in ENGS:
            deps = {k: self.cnt[k] for k in ENGS if self.cnt[k] > 0 and k != e}
            if e != 'pe' and self.cnt[e] > 0:
                deps[e] = self.cnt[e]
            for i in range(self.nd):
                if self.dcnt[i] > 0:
                    deps[('d', i)] = self.dcnt[i]
            self._waits(e, deps)
        self.lw = {}
        self.rd = {}

    def run(self):
        self.barrier()
        q = self.q
        with self.nc.Block() as block:
            @block.tensor
            def _(e):
                for f in q['pe']:
                    f(e)

            @block.scalar
            def _(e):
                for f in q['act']:
                    f(e)

            @block.vector
            def _(e):
                for f in q['dve']:
                    f(e)

            @block.gpsimd
            def _(e):
                for f in q['pool']:
                    f(e)

            @block.sync
            def _(e):
                for f in q['sp']:
                    f(e)


D = 4096
SEQ = 2048
DS = 32
NS = 2
PAST = 1024
NT = SEQ + NS * DS
HD = 128
NH = 16
AW = 2048
PROJ = 14384
OFF_Z = 6144
OFF_A = 8192
OFF_B = 8208
OFF_BQ = 8224
OFF_F = 14368
GT = 1056
KC = 32
LN_EPS = 1e-5
ALPHA = 2.0 ** 0.25


class Ctx:
    pass


def tokmap(g, c0, n):
    if c0 < 1024:
        return g * 1024 + c0
    return SEQ + DS * g + (c0 - 1024)


def build_xT(S, C, g, src_rows):
    XT = C.XT
    for ti, (src, n, c0) in enumerate(src_rows):
        sl = ti % 2
        xr = C.xrow[sl]
        S.dma('sp', xr[:n, :], src, writes=[('xr', sl)])
        for kq in range(8):
            b = C.ps_next()
            for j in range(4):
                kc = kq * 4 + j
                S.op('pe', lambda e, b=b, j=j, kc=kc, xr=xr, n=n: e.transpose(
                    C.PS[b][:, j * 128:j * 128 + n], xr[:n, kc * 128:(kc + 1) * 128], C.ident[:n, :n]),
                    reads=[('xr', sl), 'ident'], writes=[('ps', b)])
            src_ps = C.PS[b][:, :].rearrange("p (j t) -> p j t", j=4)[:, :, :n]
            dst = XT[:, kq * 4:(kq + 1) * 4, c0:c0 + n]
            if kq % 2 == 0:
                S.op('act', lambda e, dst=dst, src_ps=src_ps: e.activation(dst, src_ps, AF.Copy),
                     reads=[('ps', b)], writes=[('XT', c0 // 128)])
            else:
                S.op('dve', lambda e, dst=dst, src_ps=src_ps: e.tensor_copy(dst, src_ps),
                     reads=[('ps', b)], writes=[('XT', c0 // 128)])


def load_w(S, C, wdram, col0, ncols, slot):
    for p0 in range(0, ncols, 128):
        pn = min(128, ncols - p0)
        ss = C.wst_i % 2
        C.wst_i += 1
        wst = C.wst[ss]
        src = wdram[:, col0 + p0:col0 + p0 + pn].rearrange("(kc p) c -> p kc c", p=128)
        S.dma('sp', wst[:, :, :pn], src, writes=[('wst', ss)])
        dst = C.wbf[slot][:, :, p0:p0 + pn]
        S.op('pool', lambda e, dst=dst, wst=wst, pn=pn: e.tensor_copy(dst, wst[:, :, :pn]),
             reads=[('wst', ss)], writes=[('wbf', slot)])


def xt_reads(c0, n):
    return [('XT', i) for i in range(c0 // 128, (c0 + n - 1) // 128 + 1)]


def gemm_F(S, C, slot, ncols, emit):
    XT, wbf = C.XT, C.wbf[slot]
    for (t0, n) in ((0, 512), (512, 512), (1024, 32)):
        b = C.ps_next()
        for kc in range(KC):
            S.op('pe', lambda e, b=b, kc=kc, t0=t0, n=n: e.matmul(
                C.PS[b][:ncols, :n], lhsT=wbf[:, kc, :ncols], rhs=XT[:, kc, t0:t0 + n],
                start=(kc == 0), stop=(kc == KC - 1)),
                reads=[('wbf', slot)] + xt_reads(t0, n), writes=[('ps', b)])
        emit(b, t0, n)


def gemm_T(S, C, slot, ncols, emit):
    XT, wbf = C.XT, C.wbf[slot]
    for ti in range(9):
        c0 = ti * 128
        n = 128 if ti < 8 else 32
        b = C.ps_next()
        for kc in range(KC):
            S.op('pe', lambda e, b=b, kc=kc, c0=c0, n=n: e.matmul(
                C.PS[b][:n, :ncols], lhsT=XT[:, kc, c0:c0 + n], rhs=wbf[:, kc, :ncols],
                start=(kc == 0), stop=(kc == KC - 1)),
                reads=[('wbf', slot)] + xt_reads(c0, n), writes=[('ps', b)])
        emit(b, c0, n)


def phase_A(S, nc, C, T):
    with ExitStack() as st:
        C.XT = st.enter_context(nc.sbuf_tensor("XT", [128, KC, GT], BF16))
        C.xrow = [st.enter_context(nc.sbuf_tensor("xrow%d" % i, [128, D], F32)) for i in range(2)]
        C.wst = [st.enter_context(nc.sbuf_tensor("wst%d" % i, [128, KC, 128], F32)) for i in range(2)]
        C.wbf = [st.enter_context(nc.sbuf_tensor("wbf%d" % i, [128, KC, 128], BF16)) for i in range(2)]
        stg = [st.enter_context(nc.sbuf_tensor("stg%d" % i, [128, GT], F32)) for i in range(2)]
        tst = [st.enter_context(nc.sbuf_tensor("tst%d" % i, [128, 128], F32)) for i in range(4)]
        C.wst_i = 0
        cnt = {'stg': 0, 'tst': 0, 'ev': 0}

        chunks = []
        for j in range(48):
            chunks.append((j * 128, 128, 'F', (T['AqkvT'], j * 128, 1.0)))
        for j in range(16):
            chunks.append((OFF_Z + j * 128, 128, 'F', (T['AzT'], j * 128, 1.0)))
        chunks.append((OFF_A, 32, 'T', ('ab', 0)))
        for j in range(16):
            chunks.append((OFF_BQ + j * 128, 128, 'F', (T['BqT'], j * 128, HD ** -0.5)))
        for j in range(16):
            chunks.append((OFF_BQ + AW + j * 128, 128, 'FT', (T['BkT'], j * 128, 1.0, 'k', j * 128)))
        for j in range(16):
            chunks.append((OFF_BQ + 2 * AW + j * 128, 128, 'T', ('v', j * 128)))
        chunks.append((OFF_F, 16, 'F', (T['BfT'], 0, 1.0)))

        for g in range(2):
            rows = [(T['xp'][g * 1024 + i * 128:g * 1024 + (i + 1) * 128, :], 128, i * 128) for i in range(8)]
            rows.append((T['xs'][g * DS:(g + 1) * DS, :], DS, 1024))
            build_xT(S, C, g, rows)

            def emit_F(info):
                dst, row0, scale = info[0], info[1], info[2]

                def emit(b, t0, n, ncols):
                    pass
                return emit

            load_w(S, C, T['w_in'], chunks[0][0], chunks[0][1], 0)
            for ci, (col0, ncols, mode, info) in enumerate(chunks):
                slot = ci % 2
                if ci + 1 < len(chunks):
                    load_w(S, C, T['w_in'], chunks[ci + 1][0], chunks[ci + 1][1], (ci + 1) % 2)
                if 'F' in mode:
                    dst, row0, scale = info[0], info[1], info[2]
                    ss = cnt['stg'] % 2
                    cnt['stg'] += 1
                    sg_t = stg[ss]

                    def emitF(b, t0, n, ncols=ncols, sg_t=sg_t, ss=ss, scale=scale):
                        cnt['ev'] += 1
                        if cnt['ev'] % 2 == 0:
                            S.op('act', lambda e: e.activation(sg_t[:ncols, t0:t0 + n], C.PS[b][:ncols, :n],
                                                               AF.Identity, scale=float(scale)),
                                 reads=[('ps', b)], writes=[('stg', ss)])
                        else:
                            S.op('dve', lambda e: e.tensor_scalar(sg_t[:ncols, t0:t0 + n], C.PS[b][:ncols, :n],
                                                                  float(scale), None, ALU.mult),
                                 reads=[('ps', b)], writes=[('stg', ss)])
                    gemm_F(S, C, slot, ncols, emitF)
                    S.dma('act', dst[row0:row0 + ncols, g * 1024:(g + 1) * 1024], sg_t[:ncols, 0:1024],
                          reads=[('stg', ss)])
                    S.dma('act', dst[row0:row0 + ncols, SEQ + DS * g:SEQ + DS * (g + 1)], sg_t[:ncols, 1024:GT],
                          reads=[('stg', ss)])
                if 'T' in mode:
                    kind, coff = (info[3], info[4]) if mode == 'FT' else (info[0], info[1])

                    def emitT(b, c0, n, ncols=ncols, kind=kind, coff=coff):
                        ts_ = cnt['tst'] % 4
                        cnt['tst'] += 1
                        tt = tst[ts_]
                        cnt['ev'] += 1
                        if cnt['ev'] % 2 == 0:
                            S.op('act', lambda e: e.activation(tt[:n, :ncols], C.PS[b][:n, :ncols], AF.Copy),
                                 reads=[('ps', b)], writes=[('tst', ts_)])
                        else:
                            S.op('dve', lambda e: e.tensor_copy(tt[:n, :ncols], C.PS[b][:n, :ncols]),
                                 reads=[('ps', b)], writes=[('tst', ts_)])
                        if kind == 'ab':
                            r0 = tokmap(g, c0, n)
                            dd = T['Aab'][r0:r0 + n, 0:ncols]
                        else:
                            if c0 < 1024:
                                base = T['fkp'] if kind == 'k' else T['fvp']
                                r0 = g * 1024 + c0
                            else:
                                base = T['fks'] if kind == 'k' else T['fvs']
                                r0 = g * DS
                            dd = base[r0:r0 + n, coff:coff + ncols]
                        S.dma('act', dd, tt[:n, :ncols], reads=[('tst', ts_)])
                    gemm_T(S, C, slot, ncols, emitT)
    S.barrier()


IN_SPECS = [
    ("xp", [SEQ, D], F32), ("xs", [NS * DS, D], F32),
    ("ck", [NS, PAST, AW], F32), ("cv", [NS, PAST, AW], F32), ("clf", [NS, PAST, NH], F32),
    ("sg", [NS, NH, HD, HD], F32), ("sconv_t", [NS, 128, 48 * 3], F32),
    ("w_in", [D, PROJ], F32), ("w_out", [D, D], F32), ("w_q", [D, 2048], F32),
    ("subk", [16, 128, 128], F32), ("pu", [16384, D], F32), ("pv", [16384, D], F32),
    ("convw_t", [128, 48 * 4], F32), ("alog_b", [128, NH], F32), ("dtb_b", [128, NH], F32),
    ("normw_c", [128, 1], F32), ("ffb_c", [NH, 1], F32),
    ("ln1g_b", [128, D], F32), ("ln1b_b", [128, D], F32), ("ln2g_b", [128, D], F32), ("ln2b_b", [128, D], F32),
    ("ident_in", [128, 128], F32), ("cmat", [128, 8 * 128], F32), ("selc", [NH, NH * 128], F32),
]
OUT_SPECS = [
    ("yp", [SEQ, D]), ("ys", [NS * DS, D]),
    ("fkp", [SEQ, AW]), ("fvp", [SEQ, AW]), ("flp", [SEQ, NH]),
    ("sgp", [NH, HD, HD]), ("scp", [3, 3 * AW]),
    ("fks", [NS * DS, AW]), ("fvs", [NS * DS, AW]), ("fls", [NS * DS, NH]),
    ("sgs", [NS, NH, HD, HD]), ("scs", [NS, 3, 3 * AW]),
]
SCRATCH = [
    ("BIG1", [3 * AW * NT], F32), ("BIG2", [3 * AW * NT], F32), ("Aab", [NT, 32], F32),
    ("BfT", [NH, NT], F32), ("OT", [D, NT], BF16),
]


def build_program(phases="ABCDE"):
    nc = bass.Bass("TRN2", target_bir_lowering=False)
    T = {}
    for name, shape, dt in IN_SPECS:
        if name in ("pu", "pv") and 'E' not in phases:
            shape = [128, D]
        T[name] = nc.dram_tensor(name, shape, dt, kind="ExternalInput")
    for name, shape in OUT_SPECS:
        T[name] = nc.dram_tensor(name, shape, F32, kind="ExternalOutput")
    for name, shape, dt in SCRATCH:
        T[name] = nc.dram_tensor(name, shape, dt, kind="Internal")
    b1, b2 = T['BIG1'], T['BIG2']
    T['AqkvT'] = b1[0:3 * AW * NT].rearrange("(a b) -> a b", b=NT)
    T['MIX'] = b1[0:NT * D].rearrange("(a b) -> a b", b=D)
    T['SC'] = b1[NT * D:NT * D + NT * 2048].rearrange("(a b) -> a b", b=2048)
    T['AzT'] = b2[0:AW * NT].rearrange("(a b) -> a b", b=NT)
    T['BqT'] = b2[AW * NT:2 * AW * NT].rearrange("(a b) -> a b", b=NT)
    T['BkT'] = b2[2 * AW * NT:3 * AW * NT].rearrange("(a b) -> a b", b=NT)
    T['H'] = b2[0:NT * D].rearrange("(a b) -> a b", b=D)
    with ExitStack() as st:
        S = Sched(nc, st)
        C = Ctx()
        C.PS = [st.enter_context(nc.psum_tensor("ps%d" % i, [128, 512], F32)) for i in range(8)]
        C.ps_i = 0

        def ps_next():
            b = C.ps_i
            C.ps_i = (b + 1) % 8
            return b
        C.ps_next = ps_next
        C.ident = st.enter_context(nc.sbuf_tensor("ident_sb", [128, 128], F32))
        C.cmat = st.enter_context(nc.sbuf_tensor("cmat_sb", [128, 8 * 128], F32))
        S.dma('sp', C.ident[:], T['ident_in'][:, :], writes=['ident'])
        S.dma('sp', C.cmat[:], T['cmat'][:, :], writes=['cmat'])
        if 'A' in phases:
            phase_A(S, nc, C, T)
        if 'B' in phases:
            phase_B(S, nc, C, T)
        if 'C' in phases:
            phase_C(S, nc, C, T)
        if 'D' in phases:
            phase_D(S, nc, C, T)
        if 'E' in phases:
            phase_E(S, nc, C, T)
        S.run()
        print("instructions:", S.ninst, {e: S.cnt[e] for e in ENGS})
    return nc


def make_consts():
    c = np.zeros((128, 8, 128), np.float32)
    i = np.arange(128)
    c[:, 0, :] = (i[:, None] <= i[None, :])
    c[:, 1, :] = (i[:, None] > i[None, :])
    c[:, 2, :] = 1.0
    c[:, 3, :] = (i[:, None] < i[None, :])
    c[:, 4, :] = 128.0
    c[:, 5, :] = (i[:, None] >= i[None, :])
    return c.reshape(128, 8 * 128)


def kernel(x_prompt, x_sample, cache_fox_k, cache_fox_v, cache_fox_logf, state_gdn, state_gdn_conv,
           w_in, gdn_conv_w, gdn_a_log, gdn_dt_bias, gdn_norm_w, fox_f_bias, w_out, ln1_g, ln1_b,
           peer_w_q, peer_sub_keys, peer_u, peer_v, ln2_g, ln2_b, _phases="ABCDE", _cores=None):
    import time as _time
    _t0 = _time.time()
    f = lambda a: np.ascontiguousarray(np.asarray(a), dtype=np.float32)
    nc = build_program(_phases)
    cw = f(gdn_conv_w)[0]
    convw_t = np.ascontiguousarray(cw.reshape(4, 48, 128).transpose(2, 1, 0)).reshape(128, 48 * 4)
    bc = lambda v, n=128: np.ascontiguousarray(np.broadcast_to(f(v).reshape(1, -1), (n, f(v).size)))
    shared = {
        "w_in": f(w_in)[0], "w_out": f(w_out)[0], "w_q": f(peer_w_q)[0],
        "subk": f(peer_sub_keys)[0].reshape(16, 128, 128), "pu": f(peer_u)[0], "pv": f(peer_v)[0],
        "convw_t": convw_t, "alog_b": bc(gdn_a_log), "dtb_b": bc(gdn_dt_bias),
        "normw_c": f(gdn_norm_w).reshape(128, 1), "ffb_c": f(fox_f_bias).reshape(NH, 1),
        "ln1g_b": bc(ln1_g), "ln1b_b": bc(ln1_b), "ln2g_b": bc(ln2_g), "ln2b_b": bc(ln2_b),
        "ident_in": np.eye(128, dtype=np.float32), "cmat": make_consts(),
        "selc": np.ascontiguousarray(np.repeat(np.eye(NH, dtype=np.float32), 128, axis=1)),
    }
    xp = f(x_prompt)
    xs = f(x_sample)
    ck = f(cache_fox_k)[0]
    cv = f(cache_fox_v)[0]
    clf = f(cache_fox_logf)[0]
    sg = f(state_gdn)[0]
    sc = f(state_gdn_conv)[0]
    in_maps = []
    for c in range(8):
        m = dict(shared)
        m["xp"] = xp[c]
        m["xs"] = xs[2 * c:2 * c + 2].reshape(NS * DS, D)
        m["ck"] = ck[2 * c:2 * c + 2].reshape(NS, PAST, AW)
        m["cv"] = cv[2 * c:2 * c + 2].reshape(NS, PAST, AW)
        m["clf"] = clf[2 * c:2 * c + 2]
        m["sg"] = sg[2 * c:2 * c + 2]
        m["sconv_t"] = np.ascontiguousarray(sc[2 * c:2 * c + 2].reshape(NS, 3, 48, 128).transpose(0, 3, 2, 1)).reshape(NS, 128, 48 * 3)
        in_maps.append(m)
    if 'E' not in _phases:
        for m in in_maps:
            m["pu"] = m["pu"][:128]
            m["pv"] = m["pv"][:128]
    print("kernel: built+staged %.1fs" % (_time.time() - _t0), flush=True)
    if _cores is not None:
        import os as _os
        res = run_bass_kernel_spmd(nc, [in_maps[c] for c in _cores], core_ids=list(range(len(_cores))),
                                   trace=bool(_os.environ.get("KTRACE")))
        print("EXEC_NS", getattr(res, "exec_time_ns", None), flush=True)
        R = {c: res.results[i] for i, c in enumerate(_cores)}
        R = [R.get(c, R[_cores[0]]) for c in range(8)]
    else:
        res = run_bass_kernel_spmd(nc, in_maps, core_ids=list(range(8)))
        R = res.results
    print("kernel: ran %.1fs" % (_time.time() - _t0), flush=True)
    cat = lambda k: np.stack([np.asarray(R[c][k]) for c in range(8)], axis=0)
    yp = cat("yp")
    ys = cat("ys").reshape(16, DS, D)
    fkp = cat("fkp").reshape(1, 8, SEQ, NH, HD)
    fvp = cat("fvp").reshape(1, 8, SEQ, NH, HD)
    flp = cat("flp").reshape(1, 8, SEQ, NH)
    sgp = cat("sgp").reshape(1, 8, NH, HD, HD)
    scp = cat("scp").reshape(1, 8, 3, 3 * AW)
    fks = cat("fks").reshape(1, 16, DS, NH, HD)
    fvs = cat("fvs").reshape(1, 16, DS, NH, HD)
    fls = cat("fls").reshape(1, 16, DS, NH)
    sgs = cat("sgs").reshape(1, 16, NH, HD, HD)
    scs = cat("scs").reshape(1, 16, 3, 3 * AW)
    return (yp, ys, fkp, fvp, flp, sgp, scp, fks, fvs, fls, sgs, scs)


class KB:
    def __init__(self, c0, n, vb, qb_first, diag):
        self.c0, self.n, self.vb, self.qb_first, self.diag = c0, n, vb, qb_first, diag


def fox_core(S, C, P, kT, Vb, nck, qT, cqT, selh, nqb, QB, kblocks, out_cb, tag):
    import os as _os
    _kg = int(_os.environ.get("KG", "99"))
    for g0 in range(0, min(nqb, 4 * _kg), 4):
        qbs = list(range(g0, min(g0 + 4, nqb)))
        vis = {qb: [kb for kb in kblocks if kb.qb_first <= qb] for qb in qbs}
        for kb in [k for k in kblocks if k.qb_first <= qbs[-1]]:
            qlo = max(kb.qb_first, g0)
            q0, q1 = qlo * QB, (qbs[-1] + 1) * QB
            N = q1 - q0
            bS = 4 + (P.fx_i % 2)
            psl = P.fx_i % 2
            P.fx_i += 1
            S.op('pe', lambda e, kb=kb, bS=bS, q0=q0, q1=q1, N=N: e.matmul(
                C.PS[bS][:kb.n, :N], lhsT=kT[:, kb.c0:kb.c0 + kb.n], rhs=qT[:, q0:q1], start=True, stop=False),
                reads=tag['k'] + tag['q'], writes=[('ps', bS)])
            S.op('pe', lambda e, kb=kb, bS=bS, q0=q0, q1=q1, N=N: e.matmul(
                C.PS[bS][:kb.n, :N], lhsT=selh[:, :kb.n], rhs=cqT[:, q0:q1], start=False, stop=True),
                reads=tag['c'] + ['selc'], writes=[('ps', bS)])
            pt = P.PT[psl]
            S.op('act', lambda e, kb=kb, bS=bS, N=N, pt=pt: e.activation(
                pt[:kb.n, :N], C.PS[bS][:kb.n, :N], AF.Exp, bias=nck(kb), scale=1.0),
                reads=[('ps', bS)] + tag['nck'], writes=[('pt', psl)])
            if kb.diag is not None and qlo <= kb.diag <= qbs[-1]:
                off = (kb.diag - qlo) * QB
                S.op('dve', lambda e, kb=kb, off=off, pt=pt: e.tensor_tensor(
                    pt[:kb.n, off:off + QB], pt[:kb.n, off:off + QB], P.maskb[:kb.n, :QB], ALU.mult),
                    reads=[('pt', psl), 'maskb'], writes=[('pt', psl)])
            for qb in range(qlo, qbs[-1] + 1):
                bank = qb % 4
                first = kb is vis[qb][0]
                last = kb is vis[qb][-1]
                S.op('pe', lambda e, kb=kb, qb=qb, bank=bank, first=first, last=last, pt=pt, qlo=qlo: e.matmul(
                    C.PS[bank][:QB, :129], lhsT=pt[:kb.n, (qb - qlo) * QB:(qb - qlo + 1) * QB],
                    rhs=Vb[:kb.n, kb.vb, :], start=first, stop=last),
                    reads=[('pt', psl)] + tag['v'], writes=[('ps', bank)])
        for qb in qbs:
            out_cb(qb, qb % 4)


def fox_out(S, C, P, QB, ots, ots_tag):
    def cb(qb, bank):
        i = P.oc_i % 2
        P.oc_i += 1
        rec = P.rec[i]
        osb = P.osb[i]
        S.op('dve', lambda e: e.reciprocal(rec[:QB, :], C.PS[bank][:QB, 128:129]),
             reads=[('ps', bank)], writes=[('rec', i)])
        S.op('act', lambda e: e.activation(osb[:QB, :], C.PS[bank][:QB, 0:128], AF.Identity, scale=rec[:QB, 0:1]),
             reads=[('ps', bank), ('rec', i)], writes=[('osb', i)])
        bt = 6 + i
        S.op('pe', lambda e: e.transpose(C.PS[bt][:, :QB], osb[:QB, :], C.ident[:QB, :QB]),
             reads=[('osb', i), 'ident'], writes=[('ps', bt)])
        S.op('dve', lambda e: e.tensor_copy(ots[:, qb * QB:(qb + 1) * QB], C.PS[bt][:, :QB]),
             reads=[('ps', bt)], writes=[ots_tag])
    return cb


def phase_C(S, nc, C, T):
    with ExitStack() as st:
        P = Ctx()
        P.fx_i = 0
        P.oc_i = 0
        sb = lambda name, shape, dt=F32: st.enter_context(nc.sbuf_tensor(name, shape, dt))
        selc = sb("selc_sb", [NH, NH * 128])
        S.dma('sp', selc[:], T['selc'][:, :], writes=['selc'])
        P.maskb = sb("maskb", [128, 128], BF16)
        S.op('dve', lambda e: e.tensor_copy(P.maskb[:], C.cmat[:, 0:128]), reads=['cmat'], writes=['maskb'])
        P.PT = [sb("PT%d" % i, [128, 512], BF16) for i in range(2)]
        P.rec = [sb("rec%d" % i, [128, 1]) for i in range(2)]
        P.osb = [sb("osb%d" % i, [128, 128]) for i in range(2)]
        bfT = sb("bfT", [NH, NT])
        logfT = sb("logfT", [NH, NT])
        ffb = sb("ffb", [NH, 1])
        nffb = sb("nffb", [NH, 1])
        onesr = sb("onesr", [NH, SEQ])
        cTp = sb("cTp", [NH, SEQ])
        S.dma('sp', bfT[:], T['BfT'][:, :], writes=['bfT'])
        S.dma('sp', ffb[:], T['ffb_c'][:, :], writes=['ffb'])
        S.op('dve', lambda e: e.tensor_scalar(nffb[:], ffb[:], -1.0, None, ALU.mult), reads=['ffb'], writes=['nffb'])
        S.op('dve', lambda e: e.memset(onesr[:], 1.0), writes=['onesr'])
        S.op('act', lambda e: e.activation(logfT[:], bfT[:], AF.Exp, bias=nffb[:, 0:1], scale=-1.0),
             reads=['bfT', 'nffb'], writes=['logfT'])
        S.op('act', lambda e: e.activation(logfT[:], logfT[:], AF.Ln, bias=1.0, scale=1.0),
             reads=['logfT'], writes=['logfT'])
        S.op('dve', lambda e: e.tensor_scalar(logfT[:], logfT[:], -1.0, None, ALU.mult),
             reads=['logfT'], writes=['logfT'])
        S.op('dve', lambda e: e.tensor_tensor_scan(cTp[:], onesr[:], logfT[:, 0:SEQ], 0.0, ALU.mult, ALU.add),
             reads=['onesr', 'logfT'], writes=['cTp'])
        lf_tm = sb("lf_tm", [128, 16, NH])
        nck_p = sb("nck_p", [128, 16, NH])
        for blk in range(16):
            S.op('pe', lambda e, blk=blk: e.transpose(C.PS[6][:, blk * 16:(blk + 1) * 16],
                                                      logfT[:, blk * 128:(blk + 1) * 128], C.ident[:NH, :NH]),
                 reads=['logfT', 'ident'], writes=[('ps', 6)])
            S.op('pe', lambda e, blk=blk: e.transpose(C.PS[7][:, blk * 16:(blk + 1) * 16],
                                                      cTp[:, blk * 128:(blk + 1) * 128], C.ident[:NH, :NH]),
                 reads=['cTp', 'ident'], writes=[('ps', 7)])
        S.op('act', lambda e: e.activation(lf_tm[:].rearrange("p b h -> p (b h)"), C.PS[6][:, 0:256], AF.Copy),
             reads=[('ps', 6)], writes=['lf_tm'])
        S.op('dve', lambda e: e.tensor_scalar(nck_p[:].rearrange("p b h -> p (b h)"), C.PS[7][:, 0:256], -1.0, None,
                                              ALU.mult), reads=[('ps', 7)], writes=['nck_p'])
        S.dma('act', T['flp'][:, :].rearrange("(b p) h -> p b h", p=128), lf_tm[:], reads=['lf_tm'])
        lfs = sb("lfs", [64, NH])
        S.op('pe', lambda e: e.transpose(C.PS[6][:64, 0:NH], logfT[:, SEQ:NT], C.ident[:NH, :NH]),
             reads=['logfT', 'ident'], writes=[('ps', 6)])
        S.op('act', lambda e: e.activation(lfs[:], C.PS[6][:64, 0:NH], AF.Copy), reads=[('ps', 6)], writes=['lfs'])
        S.dma('act', T['fls'][:, :], lfs[:], reads=['lfs'])
        clf_tm = sb("clf_tm", [128, 8, NH])
        clfT = [sb("clfT%d" % s, [NH, PAST]) for s in range(NS)]
        ccT = [sb("ccT%d" % s, [NH, PAST]) for s in range(NS)]
        cnT = [sb("cnT%d" % s, [NH, DS]) for s in range(NS)]
        nck_c = [sb("nck_c%d" % s, [128, 8, NH]) for s in range(NS)]
        nck_n = [sb("nck_n%d" % s, [DS, NH]) for s in range(NS)]
        for s in range(NS):
            S.dma('sp', clf_tm[:], T['clf'][s].rearrange("(b p) h -> p b h", p=128), writes=['clf_tm'])
            for blk in range(8):
                bk = 6 + blk // 4
                S.op('pe', lambda e, blk=blk, bk=bk: e.transpose(
                    C.PS[bk][:NH, (blk % 4) * 128:(blk % 4 + 1) * 128], clf_tm[:, blk, :], C.ident[:, :]),
                    reads=['clf_tm', 'ident'], writes=[('ps', bk)])
            for hf in range(2):
                S.op('act', lambda e, hf=hf, s=s: e.activation(clfT[s][:, hf * 512:(hf + 1) * 512],
                                                              C.PS[6 + hf][:NH, :], AF.Copy),
                     reads=[('ps', 6 + hf)], writes=[('clfT', s)])
            S.op('dve', lambda e, s=s: e.tensor_tensor_scan(ccT[s][:], onesr[:, 0:PAST], clfT[s][:], 0.0,
                                                            ALU.mult, ALU.add),
                 reads=['onesr', ('clfT', s)], writes=[('ccT', s)])
            S.op('dve', lambda e, s=s: e.tensor_tensor_scan(cnT[s][:], onesr[:, 0:DS],
                                                            logfT[:, SEQ + s * DS:SEQ + (s + 1) * DS],
                                                            ccT[s][:, PAST - 1:PAST], ALU.mult, ALU.add),
                 reads=['onesr', 'logfT', ('ccT', s)], writes=[('cnT', s)])
            for blk in range(8):
                S.op('pe', lambda e, blk=blk, s=s: e.transpose(C.PS[7][:, blk * 16:(blk + 1) * 16],
                                                               ccT[s][:, blk * 128:(blk + 1) * 128],
                                                               C.ident[:NH, :NH]),
                     reads=[('ccT', s), 'ident'], writes=[('ps', 7)])
            S.op('dve', lambda e, s=s: e.tensor_scalar(nck_c[s][:].rearrange("p b h -> p (b h)"),
                                                       C.PS[7][:, 0:128], -1.0, None, ALU.mult),
                 reads=[('ps', 7)], writes=[('nck_c', s)])
            S.op('pe', lambda e, s=s: e.transpose(C.PS[6][:DS, 0:NH], cnT[s][:, :], C.ident[:NH, :NH]),
                 reads=[('cnT', s), 'ident'], writes=[('ps', 6)])
            S.op('dve', lambda e, s=s: e.tensor_scalar(nck_n[s][:], C.PS[6][:DS, 0:NH], -1.0, None, ALU.mult),
                 reads=[('ps', 6)], writes=[('nck_n', s)])

        import os as _os
        _stop = int(_os.environ.get("KSTOP", "99"))
        if _stop <= 1:
            S.barrier()
            return
        qf = [sb("qf%d" % i, [128, SEQ]) for i in range(2)]
        kf = [sb("kf%d" % i, [128, SEQ]) for i in range(2)]
        vf = [sb("vf%d" % i, [128, 16, 128]) for i in range(2)]
        qb_ = [sb("qb%d" % i, [128, SEQ], BF16) for i in range(2)]
        kb_ = [sb("kb%d" % i, [128, SEQ], BF16) for i in range(2)]
        Vb = [sb("Vb%d" % i, [128, 16, 129], BF16) for i in range(2)]
        ots = [sb("ots%d" % i, [128, SEQ], BF16) for i in range(2)]
        for i in range(2):
            S.op('pool', lambda e, i=i: e.memset(Vb[i][:, :, 128:129], 1.0), writes=[('Vb', i)])

        def load_head(h):
            i = h % 2
            S.dma('sp', qf[i][:], T['BqT'][h * 128:(h + 1) * 128, 0:SEQ], writes=[('qf', i)])
            S.dma('sp', kf[i][:], T['BkT'][h * 128:(h + 1) * 128, 0:SEQ], writes=[('kf', i)])
            S.dma('sp', vf[i][:], T['fvp'][:, h * 128:(h + 1) * 128].rearrange("(b p) d -> p b d", p=128),
                  writes=[('vf', i)])
            S.op('pool', lambda e: e.tensor_copy(qb_[i][:], qf[i][:]), reads=[('qf', i)], writes=[('qb', i)])
            S.op('pool', lambda e: e.tensor_copy(kb_[i][:], kf[i][:]), reads=[('kf', i)], writes=[('kb', i)])
            S.op('pool', lambda e: e.tensor_copy(Vb[i][:, :, 0:128], vf[i][:]), reads=[('vf', i)], writes=[('Vb', i)])

        kblocks_p = [KB(b * 128, 128, b, b, b) for b in range(16)]
        load_head(0)
        _kh = int(_os.environ.get("KH", "16"))
        for h in range(_kh):
            i = h % 2
            if h + 1 < _kh:
                load_head(h + 1)
            tag = {'k': [('kb', i)], 'q': [('qb', i)], 'c': ['cTp'], 'nck': ['nck_p'], 'v': [('Vb', i)]}
            fox_core(S, C, P, kb_[i], Vb[i], lambda kb, h=h: nck_p[:kb.n, kb.vb, h:h + 1], qb_[i], cTp,
                     selc[:, h * 128:(h + 1) * 128], 16, 128, kblocks_p,
                     fox_out(S, C, P, 128, ots[i], ('ots', i)), tag)
            S.dma('act', T['OT'][AW + h * 128:AW + (h + 1) * 128, 0:SEQ], ots[i][:], reads=[('ots', i)])

        if _stop <= 2:
            S.barrier()
            return
        ckf = [sb("ckf%d" % i, [128, 8, 128]) for i in range(2)]
        cvf = [sb("cvf%d" % i, [128, 8, 128]) for i in range(2)]
        knf = [sb("knf%d" % i, [128, DS]) for i in range(2)]
        qnf = [sb("qnf%d" % i, [128, DS]) for i in range(2)]
        vnf = [sb("vnf%d" % i, [DS, 128]) for i in range(2)]
        kTs = [sb("kTs%d" % i, [128, PAST + DS], BF16) for i in range(2)]
        qTs = [sb("qTs%d" % i, [128, DS], BF16) for i in range(2)]
        Vs = [sb("Vs%d" % i, [128, 9, 129], BF16) for i in range(2)]
        otss = [sb("otss%d" % i, [128, DS], BF16) for i in range(2)]
        for i in range(2):
            S.op('pool', lambda e, i=i: e.memset(Vs[i][:, :, 128:129], 1.0), writes=[('Vs', i)])
        kblocks_s = [KB(b * 128, 128, b, 0, None) for b in range(8)] + [KB(PAST, DS, 8, 0, 0)]
        it = 0
        for s in range(NS):
            for h in range(NH):
                i = it % 2
                it += 1
                tcol = SEQ + s * DS
                S.dma('sp', ckf[i][:], T['ck'][s][:, h * 128:(h + 1) * 128].rearrange("(b p) d -> p b d", p=128),
                      writes=[('ckf', i)])
                S.dma('sp', cvf[i][:], T['cv'][s][:, h * 128:(h + 1) * 128].rearrange("(b p) d -> p b d", p=128),
                      writes=[('cvf', i)])
                S.dma('sp', knf[i][:], T['BkT'][h * 128:(h + 1) * 128, tcol:tcol + DS], writes=[('knf', i)])
                S.dma('sp', qnf[i][:], T['BqT'][h * 128:(h + 1) * 128, tcol:tcol + DS], writes=[('qnf', i)])
                S.dma('sp', vnf[i][:], T['fvs'][s * DS:(s + 1) * DS, h * 128:(h + 1) * 128], writes=[('vnf', i)])
                for blk in range(8):
                    bt = 6 + blk % 2
                    S.op('pe', lambda e, blk=blk, bt=bt, i=i: e.transpose(C.PS[bt][:, 0:128], ckf[i][:, blk, :],
                                                                          C.ident[:, :]),
                         reads=[('ckf', i), 'ident'], writes=[('ps', bt)])
                    S.op('dve' if blk % 2 else 'act',
                         (lambda e, blk=blk, bt=bt, i=i: e.tensor_copy(kTs[i][:, blk * 128:(blk + 1) * 128],
                                                                       C.PS[bt][:, 0:128])) if blk % 2 else
                         (lambda e, blk=blk, bt=bt, i=i: e.activation(kTs[i][:, blk * 128:(blk + 1) * 128],
                                                                      C.PS[bt][:, 0:128], AF.Copy)),
                         reads=[('ps', bt)], writes=[('kTs', i)])
                S.op('pool', lambda e, i=i: e.tensor_copy(kTs[i][:, PAST:PAST + DS], knf[i][:]),
                     reads=[('knf', i)], writes=[('kTs', i)])
                S.op('pool', lambda e, i=i: e.tensor_copy(qTs[i][:], qnf[i][:]), reads=[('qnf', i)],
                     writes=[('qTs', i)])
                S.op('pool', lambda e, i=i: e.tensor_copy(Vs[i][:, 0:8, 0:128], cvf[i][:]), reads=[('cvf', i)],
                     writes=[('Vs', i)])
                S.op('pool', lambda e, i=i: e.tensor_copy(Vs[i][:DS, 8, 0:128], vnf[i][:]), reads=[('vnf', i)],
                     writes=[('Vs', i)])
                tag = {'k': [('kTs', i)], 'q': [('qTs', i)], 'c': [('cnT', s)],
                       'nck': [('nck_c', s), ('nck_n', s)], 'v': [('Vs', i)]}

                def ncks(kb, s=s, h=h):
                    if kb.vb < 8:
                        return nck_c[s][:kb.n, kb.vb, h:h + 1]
                    return nck_n[s][:kb.n, h:h + 1]
                fox_core(S, C, P, kTs[i], Vs[i], ncks, qTs[i], cnT[s], selc[:, h * 128:(h + 1) * 128], 1, DS,
                         kblocks_s, fox_out(S, C, P, DS, otss[i], ('otss', i)), tag)
                S.dma('act', T['OT'][AW + h * 128:AW + (h + 1) * 128, tcol:tcol + DS], otss[i][:],
                      reads=[('otss', i)])
    S.barrier()


def phase_B(S, nc, C, T):
    with ExitStack() as st:
        sb = lambda name, shape, dt=F32: st.enter_context(nc.sbuf_tensor("B_" + name, shape, dt))
        CM = 64
        U_ = C.cmat[:, 0:128]
        SL_ = C.cmat[:, 128:256]
        ONES = C.cmat[:, 256:384]
        SU_ = C.cmat[:, 384:512]
        C128 = C.cmat[:, 512:640]
        I_ = C.ident
        cw = sb("cw", [128, 48, 4])
        nA = sb("nA", [CM, NH])
        dtb = sb("dtb", [CM, NH])
        normw = sb("normw", [128, 1])
        S.dma('sp', cw[:].rearrange("p j r -> p (j r)"), T['convw_t'][:, :], writes=['cw'])
        S.dma('sp', nA[:], T['alog_b'][0:CM, :], writes=['nA'])
        S.dma('sp', dtb[:], T['dtb_b'][0:CM, :], writes=['dtb'])
        S.dma('sp', normw[:], T['normw_c'][:, :], writes=['normw'])
        S.op('act', lambda e: e.activation(nA[:], nA[:], AF.Exp), reads=['nA'], writes=['nA'])
        S.op('dve', lambda e: e.tensor_scalar(nA[:], nA[:], -1.0, None, ALU.mult), reads=['nA'], writes=['nA'])
        NCHM = 32
        ab = sb("ab", [CM, NCHM, 32])
        g_all = sb("g_all", [CM, NCHM, NH])
        lnb = sb("lnb_unused", [CM, 1])
        beta = sb("beta", [CM, NCHM, NH])
        ecum = sb("ecum", [CM, NCHM, NH])
        ekd = sb("ekd", [CM, NCHM, NH])
        bec = sb("bec", [CM, NCHM, NH])
        gtot = sb("gtot", [128, NCHM, NH])
        xin = [sb("xin%d" % i, [128, 48, 3 + CM]) for i in range(2)]
        zin = sb("zin", [128, NH, CM])
        zs = sb("zs", [128, NH, CM])
        ta = sb("ta", [128, 48, CM])
        tb = sb("tb", [128, 48, CM])
        ys = sb("ys", [128, 48, CM])
        qkn = sb("qkn", [128, 32, CM])
        Sst = sb("Sst", [128, NH, 128])
        Stmp = sb("Stmp", [128, 4, 128])
        OTb = sb("OTb", [128, NH, 256], BF16)
        q4 = lambda name, w: [sb(name + str(i), [CM, 4, w]) for i in range(2)]
        rhsG = q4("rhsG", CM)
        rhsGT = q4("rhsGT", CM)
        rhsB = q4("rhsB", CM)
        rhsE = q4("rhsE", CM)
        ex1 = q4("ex1", CM)
        ex2 = q4("ex2", CM)
        decU = q4("decU", CM)
        NN = [q4("NNa", CM), q4("NNb", CM)]
        NT_ = [q4("NTa", CM), q4("NTb", CM)]
        TT = [q4("TTa", CM), q4("TTb", CM)]
        bv = q4("bv", 128)
        bek = q4("bek", 128)
        wv = sb("wv", [CM, NH, 128])
        kdec = sb("kdec", [CM, NH, 128])
        qkm = sb("qkm", [CM, NH, CM])
        wkT = sb("wkT", [128, NH, CM])
        qdT = sb("qdT", [128, NH, CM])
        u_ = q4("u_", 128)
        osq = q4("osq", 128)
        on_ = q4("on_", 128)
        ms = [sb("ms%d" % i, [CM, 4]) for i in range(2)]
        rstd = [sb("rstd%d" % i, [CM, 4]) for i in range(2)]
        PS = C.PS
        nb = C.ps_next
        V = lambda fn, r, w: S.op('dve', fn, reads=r, writes=w)
        A = lambda fn, r, w: S.op('act', fn, reads=r, writes=w)
        G = lambda fn, r, w: S.op('pool', fn, reads=r, writes=w)
        PE = lambda fn, r, w: S.op('pe', fn, reads=r, writes=w)

        def seq(tok0, Tlen, Cn, conv_hist, s0, s_out, conv_out, name):
            NCH = Tlen // Cn
            S.dma('sp', ab[:Cn, :NCH, :], T['Aab'][tok0:tok0 + Tlen, :].rearrange("(n c) x -> c n x", c=Cn),
                  writes=['ab'])
            gv = g_all[:Cn, :NCH, :]
            V(lambda e: e.tensor_tensor(gv, ab[:Cn, :NCH, 0:NH], dtb[:Cn, None, :].to_broadcast([Cn, NCH, NH]),
                                        ALU.add), ['ab', 'dtb'], ['g_all'])
            A(lambda e: e.activation(gv, gv, AF.Exp), ['g_all'], ['g_all'])
            A(lambda e: e.activation(gv, gv, AF.Ln, bias=1.0, scale=1.0), ['g_all'], ['g_all'])
            V(lambda e: e.tensor_tensor(gv, gv, nA[:Cn, None, :].to_broadcast([Cn, NCH, NH]), ALU.mult),
              ['g_all', 'nA'], ['g_all'])
            bvw = beta[:Cn, :NCH, :]
            A(lambda e: e.activation(bvw, ab[:Cn, :NCH, NH:2 * NH], AF.Sigmoid), ['ab'], ['beta'])
            g2 = g_all[:Cn, :NCH, :].rearrange("c n h -> c (n h)") if NCH * NH <= 512 else None
            ncol = NCH * NH
            b1 = nb()
            PE(lambda e: e.matmul(PS[b1][:Cn, :ncol], lhsT=U_[:Cn, :Cn], rhs=g_all[:Cn, :NCH, :], start=True, stop=True),
               ['g_all', 'cmat'], [('ps', b1)])
            A(lambda e: e.activation(ecum[:Cn, :NCH, :].rearrange("c n h -> c (n h)"), PS[b1][:Cn, :ncol], AF.Exp),
              [('ps', b1)], ['ecum'])
            b2 = nb()
            PE(lambda e: e.matmul(PS[b2][:Cn, :ncol], lhsT=SL_[:Cn, :Cn], rhs=g_all[:Cn, :NCH, :], start=True, stop=True),
               ['g_all', 'cmat'], [('ps', b2)])
            A(lambda e: e.activation(ekd[:Cn, :NCH, :].rearrange("c n h -> c (n h)"), PS[b2][:Cn, :ncol], AF.Exp),
              [('ps', b2)], ['ekd'])
            b3 = nb()
            PE(lambda e: e.matmul(PS[b3][:, :ncol], lhsT=ONES[:Cn, :], rhs=g_all[:Cn, :NCH, :], start=True, stop=True),
               ['g_all', 'cmat'], [('ps', b3)])
            A(lambda e: e.activation(gtot[:, :NCH, :].rearrange("c n h -> c (n h)"), PS[b3][:, :ncol], AF.Exp),
              [('ps', b3)], ['gtot'])
            V(lambda e: e.tensor_tensor(bec[:Cn, :NCH, :], beta[:Cn, :NCH, :], ecum[:Cn, :NCH, :], ALU.mult),
              ['beta', 'ecum'], ['bec'])
            if s0 is None:
                V(lambda e: e.memset(Sst[:], 0.0), [], [('Sst', q) for q in range(4)])
            else:
                S.dma('sp', Sst[:], s0.rearrange("h k v -> k h v"), writes=[('Sst', q) for q in range(4)])

            def load_x(n):
                xi = xin[n % 2]
                t0 = tok0 + n * Cn
                if n == 0:
                    if conv_hist is None:
                        V(lambda e: e.memset(xi[:, :, 0:3], 0.0), [], [('xin', n % 2)])
                    else:
                        S.dma('sp', xi[:, :, 0:3], conv_hist.rearrange("p (j r) -> p j r", r=3), writes=[('xin', n % 2)])
                    S.dma('sp', xi[:, :, 3:3 + Cn], T['AqkvT'][:, t0:t0 + Cn].rearrange("(j p) t -> p j t", p=128),
                          writes=[('xin', n % 2)])
                else:
                    S.dma('sp', xi[:, :, 0:3 + Cn],
                          T['AqkvT'][:, t0 - 3:t0 + Cn].rearrange("(j p) t -> p j t", p=128), writes=[('xin', n % 2)])

            load_x(0)

            def chunk(n):
                if n + 1 < NCH:
                    load_x(n + 1)
                xi = xin[n % 2]
                xt = ('xin', n % 2)
                t0 = tok0 + n * Cn
                S.dma('sp', zin[:, :, :Cn], T['AzT'][:, t0:t0 + Cn].rearrange("(j p) t -> p j t", p=128), writes=['zin'])
                if n == NCH - 1:
                    for r in range(3):
                        S.dma('act', conv_out[r:r + 1, :].rearrange("o (j p) -> p j o", p=128), xi[:, :, Cn + r:Cn + r + 1],
                              reads=[xt], allow_slow_non_contiguous=True)
                cwb = lambda r: cw[:, :, r:r + 1].to_broadcast([128, 48, Cn])
                V(lambda e: e.tensor_tensor(ta[:, :, :Cn], xi[:, :, 0:Cn], cwb(0), ALU.mult), [xt, 'cw'], ['ta'])
                G(lambda e: e.tensor_tensor(tb[:, :, :Cn], xi[:, :, 1:1 + Cn], cwb(1), ALU.mult), [xt, 'cw'], ['tb'])
                V(lambda e: e.tensor_tensor(ys[:, :, :Cn], xi[:, :, 2:2 + Cn], cwb(2), ALU.mult), [xt, 'cw'], ['ys'])
                G(lambda e: e.tensor_tensor(tb[:, :, :Cn], tb[:, :, :Cn], ta[:, :, :Cn], ALU.add), ['ta', 'tb'], ['tb'])
                V(lambda e: e.tensor_tensor(ta[:, :, :Cn], xi[:, :, 3:3 + Cn], cwb(3), ALU.mult), [xt, 'cw'], ['ta'])
                V(lambda e: e.tensor_tensor(ys[:, :, :Cn], ys[:, :, :Cn], ta[:, :, :Cn], ALU.add), ['ta', 'ys'], ['ys'])
                G(lambda e: e.tensor_tensor(tb[:, :, :Cn], tb[:, :, :Cn], ys[:, :, :Cn], ALU.add), ['tb', 'ys'], ['tb'])
                A(lambda e: e.activation(ys[:, :, :Cn], tb[:, :, :Cn], AF.Silu), ['tb'], ['ys'])
                A(lambda e: e.activation(zs[:, :, :Cn], zin[:, :, :Cn], AF.Silu), ['zin'], ['zs'])
                V(lambda e: e.tensor_tensor(ta[:, 0:32, :Cn], ys[:, 0:32, :Cn], ys[:, 0:32, :Cn], ALU.mult), ['ys'], ['ta'])
                hp = 512 // Cn
                for part in range(32 // hp):
                    j0 = part * hp
                    isq = j0 < 16
                    bb = nb()
                    PE(lambda e, bb=bb, j0=j0, isq=isq: e.matmul(
                        PS[bb][:, :hp * Cn], lhsT=(C128 if isq else ONES), rhs=ta[:, j0:j0 + hp, :Cn],
                        start=True, stop=True), ['ta', 'cmat'], [('ps', bb)])
                    A(lambda e, bb=bb, j0=j0, isq=isq: e.activation(
                        tb[:, j0:j0 + hp, :Cn].rearrange("p j c -> p (j c)") if Cn == CM else tb[:, j0:j0 + hp, :Cn],
                        PS[bb][:, :hp * Cn] if Cn == CM else PS[bb][:, :hp * Cn].rearrange("p (j c) -> p j c", c=Cn),
                        AF.Sqrt, bias=(128e-6 if isq else 1e-6), scale=1.0),
                      [('ps', bb)], ['tb'])
                V(lambda e: e.reciprocal(tb[:, 0:32, :Cn], tb[:, 0:32, :Cn]), ['tb'], ['tb'])
                V(lambda e: e.tensor_tensor(qkn[:, :, :Cn], ys[:, 0:32, :Cn], tb[:, 0:32, :Cn], ALU.mult), ['ys', 'tb'],
                  ['qkn'])
                def quad(qd):
                    h0 = qd * 4
                    p = qd % 2
                    gq = g_all[:Cn, n, h0:h0 + 4]
                    W = 4 * Cn
                    flat = lambda t: t[:Cn, :, :Cn]
                    V(lambda e, p=p, gq=gq: e.tensor_tensor(flat(rhsG[p]), U_[:Cn, None, :Cn].to_broadcast([Cn, 4, Cn]),
                                                            gq[:, :, None].to_broadcast([Cn, 4, Cn]), ALU.mult),
                      ['g_all', 'cmat'], [('rhsG', p)])
                    V(lambda e, p=p, gq=gq: e.tensor_tensor(flat(rhsGT[p]), SL_[:Cn, None, :Cn].to_broadcast([Cn, 4, Cn]),
                                                            gq[:, :, None].to_broadcast([Cn, 4, Cn]), ALU.mult),
                      ['g_all', 'cmat'], [('rhsGT', p)])
                    V(lambda e, p=p, h0=h0: e.tensor_tensor(flat(rhsB[p]), I_[:Cn, None, :Cn].to_broadcast([Cn, 4, Cn]),
                                                            beta[:Cn, n, h0:h0 + 4][:, :, None].to_broadcast([Cn, 4, Cn]),
                                                            ALU.mult), ['beta', 'ident'], [('rhsB', p)])
                    V(lambda e, p=p, h0=h0: e.tensor_tensor(flat(rhsE[p]), I_[:Cn, None, :Cn].to_broadcast([Cn, 4, Cn]),
                                                            ecum[:Cn, n, h0:h0 + 4][:, :, None].to_broadcast([Cn, 4, Cn]),
                                                            ALU.mult), ['ecum', 'ident'], [('rhsE', p)])
                    bDT, bD, bB, bE, bKQ = nb(), nb(), nb(), nb(), nb()
                    v3 = lambda b, P_=Cn: PS[b][:P_, :W].rearrange("p (h c) -> p h c", c=Cn)
                    PE(lambda e, p=p, b=bDT: e.matmul(PS[b][:Cn, :W], lhsT=SL_[:Cn, :Cn], rhs=flat(rhsG[p]), start=True, stop=True),
                       [('rhsG', p), 'cmat'], [('ps', bDT)])
                    PE(lambda e, p=p, b=bD: e.matmul(PS[b][:Cn, :W], lhsT=U_[:Cn, :Cn], rhs=flat(rhsGT[p]), start=True, stop=True),
                       [('rhsGT', p), 'cmat'], [('ps', bD)])
                    PE(lambda e, p=p, b=bB: e.matmul(PS[b][:Cn, :W], lhsT=ONES[:Cn, :Cn], rhs=flat(rhsB[p]), start=True, stop=True),
                       [('rhsB', p), 'cmat'], [('ps', bB)])
                    PE(lambda e, p=p, b=bE: e.matmul(PS[b][:, :W], lhsT=ONES[:Cn, :], rhs=flat(rhsE[p]), start=True, stop=True),
                       [('rhsE', p), 'cmat'], [('ps', bE)])
                    for hh in range(4):
                        h = h0 + hh
                        PE(lambda e, h=h, hh=hh, b=bKQ: e.matmul(PS[b][:Cn, hh * Cn:(hh + 1) * Cn], lhsT=qkn[:, 16 + h, :Cn],
                                                                 rhs=qkn[:, 16 + h, :Cn], start=True, stop=True),
                           ['qkn'], [('ps', bKQ)])
                    bQK = nb()
                    for hh in range(4):
                        h = h0 + hh
                        PE(lambda e, h=h, hh=hh, b=bQK: e.matmul(PS[b][:Cn, hh * Cn:(hh + 1) * Cn], lhsT=qkn[:, 16 + h, :Cn],
                                                                 rhs=qkn[:, h, :Cn], start=True, stop=True),
                           ['qkn'], [('ps', bQK)])
                    A(lambda e, p=p, b=bDT: e.activation(flat(ex1[p]), v3(b), AF.Exp), [('ps', bDT)], [('ex1', p)])
                    A(lambda e, p=p, b=bD: e.activation(flat(ex2[p]), v3(b), AF.Exp), [('ps', bD)], [('ex2', p)])
                    V(lambda e, p=p: e.tensor_tensor(flat(decU[p]), flat(ex1[p]), U_[:Cn, None, :Cn].to_broadcast([Cn, 4, Cn]),
                                                     ALU.mult), [('ex1', p), 'cmat'], [('decU', p)])
                    V(lambda e, p=p, b=bQK, h0=h0: e.tensor_tensor(qkm[:Cn, h0:h0 + 4, :Cn], v3(b), flat(decU[p]), ALU.mult),
                      [('ps', bQK), ('decU', p)], ['qkm'])
                    V(lambda e, p=p: e.tensor_tensor(flat(ex1[p]), flat(ex1[p]), SU_[:Cn, None, :Cn].to_broadcast([Cn, 4, Cn]),
                                                     ALU.mult), [('ex1', p), 'cmat'], [('ex1', p)])
                    V(lambda e, p=p, b=bKQ: e.tensor_tensor(flat(ex1[p]), v3(b), flat(ex1[p]), ALU.mult),
                      [('ps', bKQ), ('ex1', p)], [('ex1', p)])
                    V(lambda e, p=p, b=bB: e.scalar_tensor_tensor(
                        out=NT_[0][p][:Cn, :, :Cn].rearrange("p h c -> p (h c)") if Cn == CM else NT_[0][p][:Cn, :, :Cn],
                        in0=ex1[p][:Cn, :, :Cn].rearrange("p h c -> p (h c)") if Cn == CM else ex1[p][:Cn, :, :Cn],
                        scalar=-1.0,
                        in1=PS[b][:Cn, :W] if Cn == CM else v3(b), op0=ALU.mult, op1=ALU.mult),
                      [('ps', bB), ('ex1', p)], [('NT0', p)])
                    V(lambda e, p=p: e.tensor_tensor(flat(ex2[p]), flat(ex2[p]), SL_[:Cn, None, :Cn].to_broadcast([Cn, 4, Cn]),
                                                     ALU.mult), [('ex2', p), 'cmat'], [('ex2', p)])
                    V(lambda e, p=p, b=bKQ: e.tensor_tensor(flat(ex2[p]), v3(b), flat(ex2[p]), ALU.mult),
                      [('ps', bKQ), ('ex2', p)], [('ex2', p)])
                    V(lambda e, p=p, h0=h0: e.tensor_tensor(flat(ex2[p]), flat(ex2[p]),
                                                            beta[:Cn, n, h0:h0 + 4][:, :, None].to_broadcast([Cn, 4, Cn]),
                                                            ALU.mult), [('ex2', p), 'beta'], [('ex2', p)])
                    V(lambda e, p=p: e.tensor_scalar(flat(NN[0][p]), flat(ex2[p]), -1.0, None, ALU.mult),
                      [('ex2', p)], [('NN0', p)])
                    V(lambda e, b=bE, h0=h0: e.tensor_tensor(qdT[:, h0:h0 + 4, :Cn], qkn[:, h0:h0 + 4, :Cn],
                                                             PS[b][:, :W].rearrange("p (h c) -> p h c", c=Cn), ALU.mult),
                      [('ps', bE), 'qkn'], ['qdT'])
                    V(lambda e, p=p: e.tensor_tensor(flat(TT[0][p]), flat(NT_[0][p]), I_[:Cn, None, :Cn].to_broadcast([Cn, 4, Cn]),
                                                     ALU.add), [('NT0', p), 'ident'], [('TT0', p)])
                    nlev = 5 if Cn == 64 else 4
                    for lv in range(nlev):
                        a, bq = lv % 2, (lv + 1) % 2
                        last = lv == nlev - 1
                        bN, bNT, bT = nb(), (None if last else nb()), nb()
                        for hh in range(4):
                            PE(lambda e, hh=hh, a=a, p=p, b=bN: e.matmul(PS[b][:Cn, hh * Cn:(hh + 1) * Cn], lhsT=NT_[a][p][:Cn, hh, :Cn],
                                                                         rhs=NN[a][p][:Cn, hh, :Cn], start=True, stop=True),
                               [('NT%d' % a, p), ('NN%d' % a, p)], [('ps', bN)])
                        if not last:
                            for hh in range(4):
                                PE(lambda e, hh=hh, a=a, p=p, b=bNT: e.matmul(PS[b][:Cn, hh * Cn:(hh + 1) * Cn], lhsT=NN[a][p][:Cn, hh, :Cn],
                                                                              rhs=NT_[a][p][:Cn, hh, :Cn], start=True, stop=True),
                                   [('NT%d' % a, p), ('NN%d' % a, p)], [('ps', bNT)])
                        A(lambda e, bq=bq, p=p, b=bN: e.activation(flat(NN[bq][p]), v3(b), AF.Copy), [('ps', bN)], [('NN%d' % bq, p)])
                        if not last:
                            V(lambda e, bq=bq, p=p, b=bNT: e.tensor_copy(flat(NT_[bq][p]), v3(b)), [('ps', bNT)], [('NT%d' % bq, p)])
                        for hh in range(4):
                            PE(lambda e, hh=hh, a=a, bq=bq, p=p, b=bT: e.matmul(PS[b][:Cn, hh * Cn:(hh + 1) * Cn], lhsT=NN[bq][p][:Cn, hh, :Cn],
                                                                                rhs=TT[a][p][:Cn, hh, :Cn], start=True, stop=True),
                               [('NN%d' % bq, p), ('TT%d' % a, p)], [('ps', bT)])
                        V(lambda e, a=a, bq=bq, p=p, b=bT: e.tensor_tensor(flat(TT[bq][p]), flat(TT[a][p]), v3(b), ALU.add),
                          [('ps', bT), ('TT%d' % a, p)], [('TT%d' % bq, p)])
                    tf = nlev % 2
                    TTf = TT[tf][p]
                    ttag = ('TT%d' % tf, p)
                    bK, bV = nb(), nb()
                    for hh in range(4):
                        h = h0 + hh
                        PE(lambda e, h=h, hh=hh, b=bK: e.transpose(PS[b][:Cn, hh * 128:(hh + 1) * 128], qkn[:, 16 + h, :Cn], I_[:, :]),
                           ['qkn', 'ident'], [('ps', bK)])
                        PE(lambda e, h=h, hh=hh, b=bV: e.transpose(PS[b][:Cn, hh * 128:(hh + 1) * 128], ys[:, 32 + h, :Cn], I_[:, :]),
                           ['ys', 'ident'], [('ps', bV)])
                    k3 = lambda b: PS[b][:Cn, :512].rearrange("p (h d) -> p h d", d=128)
                    bc3 = lambda t, h0=h0: t[:Cn, n, h0:h0 + 4][:, :, None].to_broadcast([Cn, 4, 128])
                    V(lambda e, p=p, b=bK: e.tensor_tensor(bek[p][:Cn, :, :], k3(b), bc3(bec), ALU.mult), [('ps', bK), 'bec'],
                      [('bek', p)])
                    V(lambda e, b=bK, h0=h0: e.tensor_tensor(kdec[:Cn, h0:h0 + 4, :], k3(b), bc3(ekd), ALU.mult),
                      [('ps', bK), 'ekd'], ['kdec'])
                    V(lambda e, p=p, b=bV: e.tensor_tensor(bv[p][:Cn, :, :], k3(b), bc3(beta), ALU.mult), [('ps', bV), 'beta'],
                      [('bv', p)])
                    bWV, bWK = nb(), nb()
                    for hh in range(4):
                        PE(lambda e, hh=hh, p=p, b=bWV, TTf=TTf: e.matmul(PS[b][:Cn, hh * 128:(hh + 1) * 128], lhsT=TTf[:Cn, hh, :Cn],
                                                                          rhs=bv[p][:Cn, hh, :], start=True, stop=True),
                           [ttag, ('bv', p)], [('ps', bWV)])
                        PE(lambda e, hh=hh, p=p, b=bWK, TTf=TTf: e.matmul(PS[b][:, hh * Cn:(hh + 1) * Cn], lhsT=bek[p][:Cn, hh, :],
                                                                          rhs=TTf[:Cn, hh, :Cn], start=True, stop=True),
                           [ttag, ('bek', p)], [('ps', bWK)])
                    A(lambda e, b=bWV, h0=h0: e.activation(wv[:Cn, h0:h0 + 4, :], k3(b), AF.Copy), [('ps', bWV)], ['wv'])
                    A(lambda e, b=bWK, h0=h0: e.activation(wkT[:, h0:h0 + 4, :Cn], PS[b][:, :W].rearrange("p (h c) -> p h c", c=Cn),
                                                           AF.Copy), [('ps', bWK)], ['wkT'])
                    bU = nb()
                    for hh in range(4):
                        h = h0 + hh
                        PE(lambda e, h=h, hh=hh, b=bU: e.matmul(PS[b][:Cn, hh * 128:(hh + 1) * 128], lhsT=wkT[:, h, :Cn], rhs=Sst[:, h, :],
                                                                start=True, stop=True), ['wkT', ('Sst', qd)], [('ps', bU)])
                    V(lambda e, p=p, b=bU, h0=h0: e.tensor_tensor(u_[p][:Cn, :, :], wv[:Cn, h0:h0 + 4, :], k3(b), ALU.subtract),
                      [('ps', bU), 'wv'], [('u', p)])
                    bO = nb()
                    for hh in range(4):
                        h = h0 + hh
                        PE(lambda e, h=h, hh=hh, b=bO: e.matmul(PS[b][:Cn, hh * 128:(hh + 1) * 128], lhsT=qdT[:, h, :Cn], rhs=Sst[:, h, :],
                                                                start=True, stop=False), ['qdT', ('Sst', qd)], [('ps', bO)])
                        PE(lambda e, h=h, hh=hh, p=p, b=bO: e.matmul(PS[b][:Cn, hh * 128:(hh + 1) * 128], lhsT=qkm[:Cn, h, :Cn],
                                                                     rhs=u_[p][:Cn, hh, :], start=False, stop=True),
                           ['qkm', ('u', p)], [('ps', bO)])
                    bS_ = nb()
                    for hh in range(4):
                        h = h0 + hh
                        PE(lambda e, h=h, hh=hh, p=p, b=bS_: e.matmul(PS[b][:, hh * 128:(hh + 1) * 128], lhsT=kdec[:Cn, h, :],
                                                                      rhs=u_[p][:Cn, hh, :], start=True, stop=True),
                           ['kdec', ('u', p)], [('ps', bS_)])
                    A(lambda e, p=p, b=bO: e.activation(osq[p][:Cn, :, :], k3(b), AF.Square), [('ps', bO)], [('osq', p)])
                    V(lambda e, p=p: e.tensor_reduce(ms[p][:Cn, :], osq[p][:Cn, :, :], AX.X, ALU.add), [('osq', p)], [('ms', p)])
                    A(lambda e, p=p: e.activation(rstd[p][:Cn, :], ms[p][:Cn, :], AF.Sqrt, bias=1e-6, scale=1.0 / 128), [('ms', p)],
                      [('rstd', p)])
                    V(lambda e, p=p: e.reciprocal(rstd[p][:Cn, :], rstd[p][:Cn, :]), [('rstd', p)], [('rstd', p)])
                    V(lambda e, p=p, b=bO: e.tensor_tensor(on_[p][:Cn, :, :], k3(b),
                                                           rstd[p][:Cn, :][:, :, None].to_broadcast([Cn, 4, 128]), ALU.mult),
                      [('ps', bO), ('rstd', p)], [('on', p)])
                    bOT = nb()
                    for hh in range(4):
                        PE(lambda e, hh=hh, p=p, b=bOT: e.transpose(PS[b][:, hh * Cn:(hh + 1) * Cn], on_[p][:Cn, hh, :], I_[:Cn, :Cn]),
                           [('on', p), 'ident'], [('ps', bOT)])
                    oc = (n % (256 // Cn)) * Cn
                    V(lambda e, b=bOT, h0=h0, oc=oc: e.scalar_tensor_tensor(
                        out=OTb[:, h0:h0 + 4, oc:oc + Cn], in0=PS[b][:, :W].rearrange("p (h c) -> p h c", c=Cn),
                        scalar=normw[:, 0:1], in1=zs[:, h0:h0 + 4, :Cn], op0=ALU.mult, op1=ALU.mult),
                      [('ps', bOT), 'zs', 'normw'], ['OTb'])
                    V(lambda e, h0=h0: e.tensor_tensor(Stmp[:, :, :], Sst[:, h0:h0 + 4, :],
                                                       gtot[:, n, h0:h0 + 4][:, :, None].to_broadcast([128, 4, 128]), ALU.mult),
                      [('Sst', qd), 'gtot'], ['Stmp'])
                    V(lambda e, h0=h0, b=bS_: e.tensor_tensor(Sst[:, h0:h0 + 4, :], Stmp[:, :, :],
                                                              PS[b][:, :512].rearrange("p (h d) -> p h d", d=128), ALU.add),
                      [('ps', bS_), 'Stmp'], [('Sst', qd)])
                for qd in range(4):
                    quad(qd)
                per = 256 // Cn
                if (n + 1) % per == 0 or n == NCH - 1:
                    nfl = ((n % per) + 1) * Cn
                    tf0 = tok0 + (n // per) * 256
                    S.dma('act', T['OT'][0:AW, tf0:tf0 + nfl].rearrange("(h p) t -> p h t", p=128), OTb[:, :, 0:nfl],
                          reads=['OTb'])
            for n in range(NCH):
                chunk(n)
            S.dma('act', s_out.rearrange("h k v -> k h v"), Sst[:], reads=[('Sst', q) for q in range(4)])

        seq(0, SEQ, 64, None, None, T['sgp'][:, :, :], T['scp'][:, :], "p")
        for s in range(NS):
            seq(SEQ + s * DS, DS, DS, T['sconv_t'][s], T['sg'][s], T['sgs'][s], T['scs'][s], "s%d" % s)
    S.barrier()


def ln_rows(S, nc, L, n, vt, vtag, gB, bB, out_ap_sb, out_tag):
    i = L.i % 2
    L.i += 1
    sm, ssq, rs = L.sm[i], L.ssq[i], L.rs[i]
    S.op('act', lambda e: e.activation(L.junk[:n, :], vt[:n, :], AF.Identity, accum_out=sm[:n, :]),
         reads=[vtag], writes=[('lnsm', i), 'lnjunk'])
    S.op('dve', lambda e: e.tensor_scalar(sm[:n, :], sm[:n, :], 1.0 / D, None, ALU.mult), reads=[('lnsm', i)],
         writes=[('lnsm', i)])
    S.op('dve', lambda e: e.tensor_scalar(vt[:n, :], vt[:n, :], sm[:n, 0:1], None, ALU.subtract),
         reads=[vtag, ('lnsm', i)], writes=[vtag])
    S.op('act', lambda e: e.activation(L.junk[:n, :], vt[:n, :], AF.Square, accum_out=ssq[:n, :]),
         reads=[vtag], writes=[('lnssq', i), 'lnjunk'])
    S.op('act', lambda e: e.activation(rs[:n, :], ssq[:n, :], AF.Sqrt, bias=L.eps[:n, 0:1], scale=1.0 / D),
         reads=[('lnssq', i), 'lneps'], writes=[('lnrs', i)])
    S.op('dve', lambda e: e.reciprocal(rs[:n, :], rs[:n, :]), reads=[('lnrs', i)], writes=[('lnrs', i)])
    S.op('dve', lambda e: e.scalar_tensor_tensor(out=vt[:n, :], in0=vt[:n, :], scalar=rs[:n, 0:1], in1=gB[:n, :],
                                                 op0=ALU.mult, op1=ALU.mult),
         reads=[vtag, ('lnrs', i), 'lng'], writes=[vtag])
    S.op('dve', lambda e: e.tensor_tensor(out_ap_sb[:n, :], vt[:n, :], bB[:n, :], ALU.add),
         reads=[vtag, 'lnb'], writes=[out_tag])


def ln_setup(S, nc, st, T, gname, bname, pref):
    L = Ctx()
    sb = lambda name, shape, dt=F32: st.enter_context(nc.sbuf_tensor(pref + name, shape, dt))
    L.i = 0
    L.sm = [sb("sm%d" % i, [128, 1]) for i in range(2)]
    L.ssq = [sb("ssq%d" % i, [128, 1]) for i in range(2)]
    L.rs = [sb("rs%d" % i, [128, 1]) for i in range(2)]
    L.eps = sb("eps", [128, 1])
    L.junk = sb("junk", [128, D], BF16)
    L.g = sb("g", [128, D])
    L.b = sb("b", [128, D])
    S.op('dve', lambda e: e.memset(L.eps[:], LN_EPS), writes=['lneps'])
    S.dma('sp', L.g[:], T[gname][:, :], writes=['lng'])
    S.dma('sp', L.b[:], T[bname][:, :], writes=['lnb'])
    return L


def row_tiles():
    return [(i * 128, 128) for i in range(16)] + [(SEQ, NS * DS)]


def x_rows(T, r0, n):
    return T['xp'][r0:r0 + n, :] if r0 < SEQ else T['xs'][r0 - SEQ:r0 - SEQ + n, :]


def y_rows(T, r0, n):
    return T['yp'][r0:r0 + n, :] if r0 < SEQ else T['ys'][r0 - SEQ:r0 - SEQ + n, :]


def phase_D(S, nc, C, T):
    import os as _os
    _kd = int(_os.environ.get("KD", "3"))
    with ExitStack() as st:
        if not (_kd & 1):
            st.close()
            return phase_D2(S, nc, C, T, _kd)
        C.XT = st.enter_context(nc.sbuf_tensor("D_XT", [128, KC, GT], BF16))
        C.wst = [st.enter_context(nc.sbuf_tensor("D_wst%d" % i, [128, KC, 128], F32)) for i in range(2)]
        C.wbf = [st.enter_context(nc.sbuf_tensor("D_wbf%d" % i, [128, KC, 512], BF16)) for i in range(2)]
        mst = [st.enter_context(nc.sbuf_tensor("D_mst%d" % i, [128, 512], F32)) for i in range(3)]
        C.wst_i = 0
        cnt = {'m': 0}
        for g in range(2):
            S.dma('sp', C.XT[:, :, 0:1024], T['OT'][:, g * 1024:(g + 1) * 1024].rearrange("(kc p) t -> p kc t", p=128),
                  writes=[('XT', i) for i in range(8)])
            S.dma('sp', C.XT[:, :, 1024:GT], T['OT'][:, SEQ + g * DS:SEQ + (g + 1) * DS].rearrange("(kc p) t -> p kc t", p=128),
                  writes=[('XT', 8)])
            load_w(S, C, T['w_out'], 0, 512, 0)
            for cg in range(8):
                if cg + 1 < 8:
                    load_w(S, C, T['w_out'], (cg + 1) * 512, 512, (cg + 1) % 2)

                def emit(b, c0, n, cg=cg, g=g):
                    mi = cnt['m'] % 3
                    cnt['m'] += 1
                    mt = mst[mi]
                    if cnt['m'] % 2:
                        S.op('act', lambda e: e.activation(mt[:n, :], C.PS[b][:n, :], AF.Copy), reads=[('ps', b)],
                             writes=[('mst', mi)])
                    else:
                        S.op('dve', lambda e: e.tensor_copy(mt[:n, :], C.PS[b][:n, :]), reads=[('ps', b)],
                             writes=[('mst', mi)])
                    r0 = tokmap(g, c0, n)
                    S.dma('act', T['MIX'][r0:r0 + n, cg * 512:(cg + 1) * 512], mt[:n, :], reads=[('mst', mi)])
                gemm_T(S, C, cg % 2, 512, emit)
    S.barrier()
    phase_D2(S, nc, C, T, _kd)


def phase_D2(S, nc, C, T, _kd):
    if not (_kd & 2):
        return
    with ExitStack() as st:
        L = ln_setup(S, nc, st, T, 'ln1g_b', 'ln1b_b', "D_ln")
        xt_ = [st.enter_context(nc.sbuf_tensor("D_x%d" % i, [128, D], F32)) for i in range(2)]
        mx_ = [st.enter_context(nc.sbuf_tensor("D_m%d" % i, [128, D], F32)) for i in range(2)]

        def tile(ti, r0, n):
            i = ti % 2
            S.dma('sp', xt_[i][:n, :], x_rows(T, r0, n), writes=[('dx', i)])
            S.dma('sp', mx_[i][:n, :], T['MIX'][r0:r0 + n, :], writes=[('dm', i)])
            S.op('dve', lambda e: e.scalar_tensor_tensor(out=xt_[i][:n, :], in0=xt_[i][:n, :], scalar=float(ALPHA),
                                                         in1=mx_[i][:n, :], op0=ALU.mult, op1=ALU.add),
                 reads=[('dx', i), ('dm', i)], writes=[('dx', i)])
            ln_rows(S, nc, L, n, xt_[i], ('dx', i), L.g, L.b, mx_[i], ('dm', i))
            S.dma('act', T['H'][r0:r0 + n, :], mx_[i][:n, :], reads=[('dm', i)])
        for ti, (r0, n) in enumerate(row_tiles()):
            tile(ti, r0, n)
    S.barrier()


def phase_E(S, nc, C, T):
    NEG = -1.0e30
    with ExitStack() as st:
        sb = lambda name, shape, dt=F32: st.enter_context(nc.sbuf_tensor("E1_" + name, shape, dt))
        C.XT = sb("XT", [128, KC, GT], BF16)
        C.xrow = [sb("xrow%d" % i, [128, D]) for i in range(2)]
        C.wst = [sb("wst%d" % i, [128, KC, 128]) for i in range(2)]
        C.wbf = [sb("wbf%d" % i, [128, KC, 128], BF16) for i in range(2)]
        C.wst_i = 0
        skn = sb("skn", [128, 16, 128])
        KT = sb("KT", [128, 16, 128])
        qst = [sb("qst%d" % i, [128, GT]) for i in range(2)]
        scs_ = [sb("scs%d" % i, [128, 128]) for i in range(3)]
        S.dma('sp', skn[:], T['subk'][:, :, :].rearrange("j n c -> n j c"), writes=['skn'])
        for j in range(16):
            b = C.ps_next()
            S.op('pe', lambda e, j=j, b=b: e.transpose(C.PS[b][:, 0:128], skn[:, j, :], C.ident[:, :]),
                 reads=['skn', 'ident'], writes=[('ps', b)])
            S.op('dve', lambda e, j=j, b=b: e.tensor_copy(KT[:, j, :], C.PS[b][:, 0:128]), reads=[('ps', b)],
                 writes=['KT'])
        cnt = {'s': 0, 'e': 0}
        for g in range(2):
            rows = [(T['H'][g * 1024 + i * 128:g * 1024 + (i + 1) * 128, :], 128, i * 128) for i in range(8)]
            rows.append((T['H'][SEQ + g * DS:SEQ + (g + 1) * DS, :], DS, 1024))
            build_xT(S, C, g, rows)
            load_w(S, C, T['w_q'], 0, 128, 0)
            for j in range(16):
                if j + 1 < 16:
                    load_w(S, C, T['w_q'], (j + 1) * 128, 128, (j + 1) % 2)
                qs = qst[j % 2]

                def emitF(b, t0, n, qs=qs, j=j):
                    cnt['e'] += 1
                    if cnt['e'] % 2:
                        S.op('act', lambda e: e.activation(qs[:, t0:t0 + n], C.PS[b][:, :n], AF.Copy),
                             reads=[('ps', b)], writes=[('qst', j % 2)])
                    else:
                        S.op('dve', lambda e: e.tensor_copy(qs[:, t0:t0 + n], C.PS[b][:, :n]),
                             reads=[('ps', b)], writes=[('qst', j % 2)])
                gemm_F(S, C, j % 2, 128, emitF)

                def score(ti, j=j, qs=qs, g=g):
                    c0 = ti * 128
                    n = 128 if ti < 8 else DS
                    b = C.ps_next()
                    S.op('pe', lambda e: e.matmul(C.PS[b][:n, 0:128], lhsT=qs[:, c0:c0 + n], rhs=KT[:, j, :],
                                                  start=True, stop=True),
                         reads=[('qst', j % 2), 'KT'], writes=[('ps', b)])
                    si = cnt['s'] % 3
                    cnt['s'] += 1
                    sc_t = scs_[si]
                    S.op('dve' if si % 2 else 'act',
                         (lambda e: e.tensor_copy(sc_t[:n, :], C.PS[b][:n, 0:128])) if si % 2 else
                         (lambda e: e.activation(sc_t[:n, :], C.PS[b][:n, 0:128], AF.Copy)),
                         reads=[('ps', b)], writes=[('scs', si)])
                    r0 = tokmap(g, c0, n)
                    S.dma('act', T['SC'][r0:r0 + n, j * 128:(j + 1) * 128], sc_t[:n, :], reads=[('scs', si)])
                for ti in range(9):
                    score(ti)
    S.barrier()
    with ExitStack() as st:
        sb = lambda name, shape, dt=F32: st.enter_context(nc.sbuf_tensor("E2_" + name, shape, dt))
        L = ln_setup(S, nc, st, T, 'ln2g_b', 'ln2b_b', "E_ln")
        sc = sb("sc", [128, 16, 128])
        sc2 = sb("sc2", [128, 16, 128])
        stop_ = sb("stop", [128, 16, 16])
        itop = sb("itop", [128, 16, 16], U32)
        itf = sb("itf", [128, 16, 16])
        cand = sb("cand", [128, 8, 256])
        cand2 = sb("cand2", [128, 8, 256])
        cidx = sb("cidx", [128, 8, 256])
        tops = sb("tops", [128, 8, 16])
        eidf = sb("eidf", [128, 128])
        eidi = sb("eidi", [128, 128], I32)
        gate = sb("gate", [128, 8, 16])
        zs_ = sb("zsum", [128, 8])
        pre = sb("pre", [128, 128])
        actv = sb("actv", [128, 128])
        ht = sb("ht", [128, D])
        acc = sb("acc", [128, D])
        NGB = 4
        ub = [sb("ub%d" % i, [128, D]) for i in range(NGB)]
        sj = [sb("sj%d" % i, [128, 256]) for i in range(2)]
        V = lambda fn, r, w: S.op('dve', fn, reads=r, writes=w)
        A = lambda fn, r, w: S.op('act', fn, reads=r, writes=w)
        G = lambda fn, r, w: S.op('pool', fn, reads=r, writes=w)

        def tile(r0, n):
            S.dma('sp', sc[:n, :, :], T['SC'][r0:r0 + n, :].rearrange("t (j k) -> t j k", k=128), writes=['sc'])
            S.dma('sp', ht[:n, :], T['H'][r0:r0 + n, :], writes=['ht'])
            for j in range(16):
                def pair(j):
                    V(lambda e: e.max(out=stop_[:n, j, 0:8], in_=sc[:n, j, :]), ['sc'], ['stop'])
                    V(lambda e: e.max_index(out=itop[:n, j, 0:8], in_max=stop_[:n, j, 0:8], in_values=sc[:n, j, :]),
                      ['sc', 'stop'], ['itop'])
                    V(lambda e: e.match_replace(out=sc2[:n, j, :], in_to_replace=stop_[:n, j, 0:8], in_values=sc[:n, j, :],
                                                imm_value=NEG), ['sc', 'stop'], ['sc2'])
                    V(lambda e: e.max(out=stop_[:n, j, 8:16], in_=sc2[:n, j, :]), ['sc2'], ['stop'])
                    V(lambda e: e.max_index(out=itop[:n, j, 8:16], in_max=stop_[:n, j, 8:16], in_values=sc2[:n, j, :]),
                      ['sc2', 'stop'], ['itop'])
                pair(j)
            V(lambda e: e.tensor_copy(itf[:n, :, :], itop[:n, :, :]), ['itop'], ['itf'])
            s4 = stop_[:n, :, :].rearrange("t (h p) k -> t h p k", p=2)
            i4 = itf[:n, :, :].rearrange("t (h p) k -> t h p k", p=2)
            c4 = lambda t: t[:n, :, :].rearrange("t h (a b) -> t h a b", b=16)
            V(lambda e: e.tensor_tensor(c4(cand), s4[:, :, 0, :][:, :, :, None].to_broadcast([n, 8, 16, 16]),
                                        s4[:, :, 1, :][:, :, None, :].to_broadcast([n, 8, 16, 16]), ALU.add),
              ['stop'], ['cand'])
            V(lambda e: e.tensor_scalar(i4[:, :, 0, :], i4[:, :, 0, :], 128.0, None, ALU.mult), ['itf'], ['itf'])
            V(lambda e: e.tensor_tensor(c4(cidx), i4[:, :, 0, :][:, :, :, None].to_broadcast([n, 8, 16, 16]),
                                        i4[:, :, 1, :][:, :, None, :].to_broadcast([n, 8, 16, 16]), ALU.add),
              ['itf'], ['cidx'])
            for hd in range(8):
                def head(hd):
                    V(lambda e: e.max(out=tops[:n, hd, 0:8], in_=cand[:n, hd, :]), ['cand'], ['tops'])
                    V(lambda e: e.match_replace(out=cand2[:n, hd, :], in_to_replace=tops[:n, hd, 0:8],
                                                in_values=cand[:n, hd, :], imm_value=NEG), ['cand', 'tops'], ['cand2'])
                    V(lambda e: e.max(out=tops[:n, hd, 8:16], in_=cand2[:n, hd, :]), ['cand2'], ['tops'])
                    for k in range(16):
                        def pick(k):
                            m = hd * 16 + k
                            V(lambda e: e.scalar_tensor_tensor(out=sj[m % 2][:n, :], in0=cand[:n, hd, :],
                                                               scalar=tops[:n, hd, k:k + 1], in1=cidx[:n, hd, :],
                                                               op0=ALU.is_equal, op1=ALU.mult,
                                                               accum_out=eidf[:n, m:m + 1]),
                              ['cand', 'tops', 'cidx'], [('sj', m % 2), 'eidf'])
                        pick(k)
                head(hd)
            V(lambda e: e.tensor_scalar(eidf[:n, :], eidf[:n, :], 16383.0, 0.0, ALU.min, ALU.max), ['eidf'], ['eidf'])
            V(lambda e: e.tensor_copy(eidi[:n, :], eidf[:n, :]), ['eidf'], ['eidi'])
            V(lambda e: e.tensor_tensor(gate[:n, :, :], tops[:n, :, :], tops[:n, :, 0:1].to_broadcast([n, 8, 16]),
                                        ALU.subtract), ['tops'], ['gate'])
            A(lambda e: e.activation(gate[:n, :, :], gate[:n, :, :], AF.Exp), ['gate'], ['gate'])
            V(lambda e: e.tensor_reduce(zs_[:n, :], gate[:n, :, :], AX.X, ALU.add), ['gate'], ['zsum'])
            V(lambda e: e.reciprocal(zs_[:n, :], zs_[:n, :]), ['zsum'], ['zsum'])
            V(lambda e: e.tensor_tensor(gate[:n, :, :], gate[:n, :, :], zs_[:n, :][:, :, None].to_broadcast([n, 8, 16]),
                                        ALU.mult), ['gate', 'zsum'], ['gate'])
            for m in range(128):
                def upick(m):
                    i = m % NGB
                    S.gather(ub[i][:n, :], T['pu'][:, :], eidi[:n, m:m + 1], reads=['eidi'], writes=[('ub', i)],
                             bounds=None)
                    V(lambda e: e.scalar_tensor_tensor(out=L.junk[:n, :], in0=ub[i][:n, :], scalar=1.0, in1=ht[:n, :],
                                                       op0=ALU.mult, op1=ALU.mult, accum_out=pre[:n, m:m + 1]),
                      [('ub', i), 'ht'], [('pre', m)])
                upick(m)
            A(lambda e: e.activation(actv[:n, :], pre[:n, :], AF.Gelu), [('pre', m) for m in range(128)], ['actv'])
            V(lambda e: e.tensor_tensor(actv[:n, :], actv[:n, :], gate[:n, :, :].rearrange("t h k -> t (h k)"), ALU.mult),
              ['actv', 'gate'], ['actv'])
            for m in range(128):
                def vpick(m):
                    i = m % NGB
                    S.gather(ub[i][:n, :], T['pv'][:, :], eidi[:n, m:m + 1], reads=['eidi'], writes=[('ub', i)],
                             bounds=None)
                    if m == 0:
                        V(lambda e: e.tensor_scalar(acc[:n, :], ub[i][:n, :], actv[:n, 0:1], None, ALU.mult),
                          [('ub', i), 'actv'], ['acc'])
                    else:
                        V(lambda e: e.scalar_tensor_tensor(out=acc[:n, :], in0=ub[i][:n, :], scalar=actv[:n, m:m + 1],
                                                           in1=acc[:n, :], op0=ALU.mult, op1=ALU.add),
                          [('ub', i), 'actv', 'acc'], ['acc'])
                vpick(m)
            V(lambda e: e.scalar_tensor_tensor(out=ht[:n, :], in0=ht[:n, :], scalar=float(ALPHA), in1=acc[:n, :],
                                               op0=ALU.mult, op1=ALU.add), ['ht', 'acc'], ['ht'])
            ln_rows(S, nc, L, n, ht, 'ht', L.g, L.b, acc, 'acc')
            S.dma('act', y_rows(T, r0, n), acc[:n, :], reads=['acc'])
        import os as _os
        for (r0, n) in row_tiles()[:int(_os.environ.get("KT", "99"))]:
            tile(r0, n)
    S.barrier()
```

```python
import numpy as np
from contextlib import ExitStack
import concourse.bass as bass
import concourse.mybir as mybir
from concourse.bass_utils import run_bass_kernel_spmd

F32 = mybir.dt.float32
BF16 = mybir.dt.bfloat16
I32 = mybir.dt.int32
U32 = mybir.dt.uint32
AF = mybir.ActivationFunctionType
ALU = mybir.AluOpType
AX = mybir.AxisListType

ENGS = ['pe', 'act', 'dve', 'pool', 'sp']


class Sched:
    def __init__(self, nc, stack, n_dma=32):
        self.nc = nc
        self.q = {e: [] for e in ENGS}
        self.sem = {e: stack.enter_context(nc.semaphore("s_" + e)) for e in ENGS}
        self.cnt = {e: 0 for e in ENGS}
        self.nd = n_dma
        self.dsem = [stack.enter_context(nc.semaphore("d%d" % i)) for i in range(n_dma)]
        self.dcnt = [0] * n_dma
        self.dnext = 0
        self.seen = {e: {} for e in ENGS}
        self.lw = {}
        self.rd = {}
        self.ninst = 0

    def _deps(self, reads, writes):
        deps = {}
        for r in reads:
            kv = self.lw.get(r)
            if kv is not None and deps.get(kv[0], 0) < kv[1]:
                deps[kv[0]] = kv[1]
        for w in writes:
            kv = self.lw.get(w)
            if kv is not None and deps.get(kv[0], 0) < kv[1]:
                deps[kv[0]] = kv[1]
            for k, v in self.rd.get(w, {}).items():
                if deps.get(k, 0) < v:
                    deps[k] = v
        return deps

    def _waits(self, eng, deps):
        for k, v in deps.items():
            if k == 'pe' and eng == 'pe':
                continue
            if self.seen[eng].get(k, 0) >= v:
                continue
            self.seen[eng][k] = v
            sem = self.sem[k] if isinstance(k, str) else self.dsem[k[1]]
            self.q[eng].append(lambda e, sem=sem, v=v: e.wait_ge(sem, v))
            self.ninst += 1

    def _mark(self, key, v, reads, writes):
        for r in reads:
            d = self.rd.setdefault(r, {})
            if d.get(key, 0) < v:
                d[key] = v
        for w in writes:
            self.lw[w] = (key, v)
            self.rd[w] = {}

    def op(self, eng, fn, reads=(), writes=()):
        self._waits(eng, self._deps(reads, writes))
        self.cnt[eng] += 1
        v = self.cnt[eng]
        sem = self.sem[eng]
        self.q[eng].append(lambda e: fn(e).then_inc(sem, 1))
        self.ninst += 1
        self._mark(eng, v, reads, writes)

    def dma(self, q, out, in_, reads=(), writes=(), **kw):
        i = self.dnext
        self.dnext = (i + 1) % self.nd
        deps = self._deps(reads, writes)
        if self.dcnt[i] > 0:
            deps[('d', i)] = max(deps.get(('d', i), 0), self.dcnt[i])
        self._waits(q, deps)
        self.dcnt[i] += 16
        v = self.dcnt[i]
        sem = self.dsem[i]
        self.q[q].append(lambda e: e.dma_start(out=out, in_=in_, **kw).then_inc(sem, 16))
        self.ninst += 1
        self._mark(('d', i), v, reads, writes)

    def gather(self, out, table, idx_ap, reads=(), writes=(), bounds=None):
        q = 'pool'
        i = self.dnext
        self.dnext = (i + 1) % self.nd
        deps = self._deps(reads, writes)
        if self.dcnt[i] > 0:
            deps[('d', i)] = max(deps.get(('d', i), 0), self.dcnt[i])
        self._waits(q, deps)
        self.dcnt[i] += 16
        v = self.dcnt[i]
        sem = self.dsem[i]

        def f(e):
            self._gdbg = getattr(self, '_gdbg', 0) + 1
            if self._gdbg > 4351:
                print("GATHER", self._gdbg, out.shape, idx_ap, flush=True)
            return e.indirect_dma_start(
                out=out, out_offset=None, in_=table,
                in_offset=bass.IndirectOffsetOnAxis(ap=idx_ap, axis=0),
                bounds_check=bounds, oob_is_err=False).then_inc(sem, 16)
        self.q[q].append(f)
        self.ninst += 1
        self._mark(('d', i), v, reads, writes)

    def barrier(self):
        for e in ENGS:
            deps = {k: self.cnt[k] for k in ENGS if self.cnt[k] > 0 and k != e}
            if e != 'pe' and self.cnt[e] > 0:
                deps[e] = self.cnt[e]
            for i in range(self.nd):
                if self.dcnt[i] > 0:
                    deps[('d', i)] = self.dcnt[i]
            self._waits(e, deps)
        self.lw = {}
        self.rd = {}

    def run(self):
        self.barrier()
        q = self.q
        with self.nc.Block() as block:
            @block.tensor
            def _(e):
                for f in q['pe']:
                    f(e)

            @block.scalar
            def _(e):
                for f in q['act']:
                    f(e)

            @block.vector
            def _(e):
                for f in q['dve']:
                    f(e)

            @block.gpsimd
            def _(e):
                for f in q['pool']:
                    f(e)

            @block.sync
            def _(e):
                for f in q['sp']:
                    f(e)


D = 4096
SEQ = 2048
DS = 32
NS = 2
PAST = 1024
NT = SEQ + NS * DS
HD = 128
NH = 16
AW = 2048
PROJ = 14384
OFF_Z = 6144
OFF_A = 8192
OFF_B = 8208
OFF_BQ = 8224
OFF_F = 14368
GT = 1056
KC = 32
LN_EPS = 1e-5
ALPHA = 2.0 ** 0.25


class Ctx:
    pass


def tokmap(g, c0, n):
    if c0 < 1024:
        return g * 1024 + c0
    return SEQ + DS * g + (c0 - 1024)


def build_xT(S, C, g, src_rows):
    XT = C.XT
    for ti, (src, n, c0) in enumerate(src_rows):
        sl = ti % 2
        xr = C.xrow[sl]
        S.dma('sp', xr[:n, :], src, writes=[('xr', sl)])
        for kq in range(8):
            b = C.ps_next()
            for j in range(4):
                kc = kq * 4 + j
                S.op('pe', lambda e, b=b, j=j, kc=kc, xr=xr, n=n: e.transpose(
                    C.PS[b][:, j * 128:j * 128 + n], xr[:n, kc * 128:(kc + 1) * 128], C.ident[:n, :n]),
                    reads=[('xr', sl), 'ident'], writes=[('ps', b)])
            src_ps = C.PS[b][:, :].rearrange("p (j t) -> p j t", j=4)[:, :, :n]
            dst = XT[:, kq * 4:(kq + 1) * 4, c0:c0 + n]
            if kq % 2 == 0:
                S.op('act', lambda e, dst=dst, src_ps=src_ps: e.activation(dst, src_ps, AF.Copy),
                     reads=[('ps', b)], writes=[('XT', c0 // 128)])
            else:
                S.op('dve', lambda e, dst=dst, src_ps=src_ps: e.tensor_copy(dst, src_ps),
                     reads=[('ps', b)], writes=[('XT', c0 // 128)])


def load_w(S, C, wdram, col0, ncols, slot):
    for p0 in range(0, ncols, 128):
        pn = min(128, ncols - p0)
        ss = C.wst_i % 2
        C.wst_i += 1
        wst = C.wst[ss]
        src = wdram[:, col0 + p0:col0 + p0 + pn].rearrange("(kc p) c -> p kc c", p=128)
        S.dma('sp', wst[:, :, :pn], src, writes=[('wst', ss)])
        dst = C.wbf[slot][:, :, p0:p0 + pn]
        S.op('pool', lambda e, dst=dst, wst=wst, pn=pn: e.tensor_copy(dst, wst[:, :, :pn]),
             reads=[('wst', ss)], writes=[('wbf', slot)])


def xt_reads(c0, n):
    return [('XT', i) for i in range(c0 // 128, (c0 + n - 1) // 128 + 1)]


def gemm_F(S, C, slot, ncols, emit):
    XT, wbf = C.XT, C.wbf[slot]
    for (t0, n) in ((0, 512), (512, 512), (1024, 32)):
        b = C.ps_next()
        for kc in range(KC):
            S.op('pe', lambda e, b=b, kc=kc, t0=t0, n=n: e.matmul(
                C.PS[b][:ncols, :n], lhsT=wbf[:, kc, :ncols], rhs=XT[:, kc, t0:t0 + n],
                start=(kc == 0), stop=(kc == KC - 1)),
                reads=[('wbf', slot)] + xt_reads(t0, n), writes=[('ps', b)])
        emit(b, t0, n)


def gemm_T(S, C, slot, ncols, emit):
    XT, wbf = C.XT, C.wbf[slot]
    for ti in range(9):
        c0 = ti * 128
        n = 128 if ti < 8 else 32
        b = C.ps_next()
        for kc in range(KC):
            S.op('pe', lambda e, b=b, kc=kc, c0=c0, n=n: e.matmul(
                C.PS[b][:n, :ncols], lhsT=XT[:, kc, c0:c0 + n], rhs=wbf[:, kc, :ncols],
                start=(kc == 0), stop=(kc == KC - 1)),
                reads=[('wbf', slot)] + xt_reads(c0, n), writes=[('ps', b)])
        emit(b, c0, n)


def phase_A(S, nc, C, T):
    with ExitStack() as st:
        C.XT = st.enter_context(nc.sbuf_tensor("XT", [128, KC, GT], BF16))
        C.xrow = [st.enter_context(nc.sbuf_tensor("xrow%d" % i, [128, D], F32)) for i in range(2)]
        C.wst = [st.enter_context(nc.sbuf_tensor("wst%d" % i, [128, KC, 128], F32)) for i in range(2)]
        C.wbf = [st.enter_context(nc.sbuf_tensor("wbf%d" % i, [128, KC, 128], BF16)) for i in range(2)]
        stg = [st.enter_context(nc.sbuf_tensor("stg%d" % i, [128, GT], F32)) for i in range(2)]
        tst = [st.enter_context(nc.sbuf_tensor("tst%d" % i, [128, 128], F32)) for i in range(4)]
        C.wst_i = 0
        cnt = {'stg': 0, 'tst': 0, 'ev': 0}

        chunks = []
        for j in range(48):
            chunks.append((j * 128, 128, 'F', (T['AqkvT'], j * 128, 1.0)))
        for j in range(16):
            chunks.append((OFF_Z + j * 128, 128, 'F', (T['AzT'], j * 128, 1.0)))
        chunks.append((OFF_A, 32, 'T', ('ab', 0)))
        for j in range(16):
            chunks.append((OFF_BQ + j * 128, 128, 'F', (T['BqT'], j * 128, HD ** -0.5)))
        for j in range(16):
            chunks.append((OFF_BQ + AW + j * 128, 128, 'FT', (T['BkT'], j * 128, 1.0, 'k', j * 128)))
        for j in range(16):
            chunks.append((OFF_BQ + 2 * AW + j * 128, 128, 'T', ('v', j * 128)))
        chunks.append((OFF_F, 16, 'F', (T['BfT'], 0, 1.0)))

        for g in range(2):
            rows = [(T['xp'][g * 1024 + i * 128:g * 1024 + (i + 1) * 128, :], 128, i * 128) for i in range(8)]
            rows.append((T['xs'][g * DS:(g + 1) * DS, :], DS, 1024))
            build_xT(S, C, g, rows)

            def emit_F(info):
                dst, row0, scale = info[0], info[1], info[2]

                def emit(b, t0, n, ncols):
                    pass
                return emit

            load_w(S, C, T['w_in'], chunks[0][0], chunks[0][1], 0)
            for ci, (col0, ncols, mode, info) in enumerate(chunks):
                slot = ci % 2
                if ci + 1 < len(chunks):
                    load_w(S, C, T['w_in'], chunks[ci + 1][0], chunks[ci + 1][1], (ci + 1) % 2)
                if 'F' in mode:
                    dst, row0, scale = info[0], info[1], info[2]
                    ss = cnt['stg'] % 2
                    cnt['stg'] += 1
                    sg_t = stg[ss]

                    def emitF(b, t0, n, ncols=ncols, sg_t=sg_t, ss=ss, scale=scale):
                        cnt['ev'] += 1
                        if cnt['ev'] % 2 == 0:
                            S.op('act', lambda e: e.activation(sg_t[:ncols, t0:t0 + n], C.PS[b][:ncols, :n],
                                                               AF.Identity, scale=float(scale)),
                                 reads=[('ps', b)], writes=[('stg', ss)])
                        else:
                            S.op('dve', lambda e: e.tensor_scalar(sg_t[:ncols, t0:t0 + n], C.PS[b][:ncols, :n],
                                                                  float(scale), None, ALU.mult),
                                 reads=[('ps', b)], writes=[('stg', ss)])
                    gemm_F(S, C, slot, ncols, emitF)
                    S.dma('act', dst[row0:row0 + ncols, g * 1024:(g + 1) * 1024], sg_t[:ncols, 0:1024],
                          reads=[('stg', ss)])
                    S.dma('act', dst[row0:row0 + ncols, SEQ + DS * g:SEQ + DS * (g + 1)], sg_t[:ncols, 1024:GT],
                          reads=[('stg', ss)])
                if 'T' in mode:
                    kind, coff = (info[3], info[4]) if mode == 'FT' else (info[0], info[1])

                    def emitT(b, c0, n, ncols=ncols, kind=kind, coff=coff):
                        ts_ = cnt['tst'] % 4
                        cnt['tst'] += 1
                        tt = tst[ts_]
                        cnt['ev'] += 1
                        if cnt['ev'] % 2 == 0:
                            S.op('act', lambda e: e.activation(tt[:n, :ncols], C.PS[b][:n, :ncols], AF.Copy),
                                 reads=[('ps', b)], writes=[('tst', ts_)])
                        else:
                            S.op('dve', lambda e: e.tensor_copy(tt[:n, :ncols], C.PS[b][:n, :ncols]),
                                 reads=[('ps', b)], writes=[('tst', ts_)])
                        if kind == 'ab':
                            r0 = tokmap(g, c0, n)
                            dd = T['Aab'][r0:r0 + n, 0:ncols]
                        else:
                            if c0 < 1024:
                                base = T['fkp'] if kind == 'k' else T['fvp']
                                r0 = g * 1024 + c0
                            else:
                                base = T['fks'] if kind == 'k' else T['fvs']
                                r0 = g * DS
                            dd = base[r0:r0 + n, coff:coff + ncols]
                        S.dma('act', dd, tt[:n, :ncols], reads=[('tst', ts_)])
                    gemm_T(S, C, slot, ncols, emitT)
    S.barrier()


IN_SPECS = [
    ("xp", [SEQ, D], F32), ("xs", [NS * DS, D], F32),
    ("ck", [NS, PAST, AW], F32), ("cv", [NS, PAST, AW], F32), ("clf", [NS, PAST, NH], F32),
    ("sg", [NS, NH, HD, HD], F32), ("sconv_t", [NS, 128, 48 * 3], F32),
    ("w_in", [D, PROJ], F32), ("w_out", [D, D], F32), ("w_q", [D, 2048], F32),
    ("subk", [16, 128, 128], F32), ("pu", [16384, D], F32), ("pv", [16384, D], F32),
    ("convw_t", [128, 48 * 4], F32), ("alog_b", [128, NH], F32), ("dtb_b", [128, NH], F32),
    ("normw_c", [128, 1], F32), ("ffb_c", [NH, 1], F32),
    ("ln1g_b", [128, D], F32), ("ln1b_b", [128, D], F32), ("ln2g_b", [128, D], F32), ("ln2b_b", [128, D], F32),
    ("ident_in", [128, 128], F32), ("cmat", [128, 8 * 128], F32), ("selc", [NH, NH * 128], F32),
]
OUT_SPECS = [
    ("yp", [SEQ, D]), ("ys", [NS * DS, D]),
    ("fkp", [SEQ, AW]), ("fvp", [SEQ, AW]), ("flp", [SEQ, NH]),
    ("sgp", [NH, HD, HD]), ("scp", [3, 3 * AW]),
    ("fks", [NS * DS, AW]), ("fvs", [NS * DS, AW]), ("fls", [NS * DS, NH]),
    ("sgs", [NS, NH, HD, HD]), ("scs", [NS, 3, 3 * AW]),
]
SCRATCH = [
    ("BIG1", [3 * AW * NT], F32), ("BIG2", [3 * AW * NT], F32), ("Aab", [NT, 32], F32),
    ("BfT", [NH, NT], F32), ("OT", [D, NT], BF16), ("PVB", [16384, D], BF16),
]


def build_program(phases="ABCDE"):
    nc = bass.Bass("TRN2", target_bir_lowering=False)
    T = {}
    for name, shape, dt in IN_SPECS:
        if name in ("pu", "pv") and 'E' not in phases:
            shape = [128, D]
        T[name] = nc.dram_tensor(name, shape, dt, kind="ExternalInput")
    for name, shape in OUT_SPECS:
        T[name] = nc.dram_tensor(name, shape, F32, kind="ExternalOutput")
    for name, shape, dt in SCRATCH:
        if name == "PVB" and 'E' not in phases:
            shape = [128, D]
        T[name] = nc.dram_tensor(name, shape, dt, kind="Internal")
    b1, b2 = T['BIG1'], T['BIG2']
    T['AqkvT'] = b1[0:3 * AW * NT].rearrange("(a b) -> a b", b=NT)
    T['MIX'] = b1[0:NT * D].rearrange("(a b) -> a b", b=D)
    T['SC'] = b1[NT * D:NT * D + NT * 2048].rearrange("(a b) -> a b", b=2048)
    T['AzT'] = b2[0:AW * NT].rearrange("(a b) -> a b", b=NT)
    T['BqT'] = b2[AW * NT:2 * AW * NT].rearrange("(a b) -> a b", b=NT)
    T['BkT'] = b2[2 * AW * NT:3 * AW * NT].rearrange("(a b) -> a b", b=NT)
    T['H'] = b2[0:NT * D].rearrange("(a b) -> a b", b=D)
    with ExitStack() as st:
        S = Sched(nc, st)
        C = Ctx()
        C.PS = [st.enter_context(nc.psum_tensor("ps%d" % i, [128, 512], F32)) for i in range(8)]
        C.ps_i = 0

        def ps_next():
            b = C.ps_i
            C.ps_i = (b + 1) % 8
            return b
        C.ps_next = ps_next
        C.ident = st.enter_context(nc.sbuf_tensor("ident_sb", [128, 128], F32))
        C.cmat = st.enter_context(nc.sbuf_tensor("cmat_sb", [128, 8 * 128], F32))
        S.dma('sp', C.ident[:], T['ident_in'][:, :], writes=['ident'])
        S.dma('sp', C.cmat[:], T['cmat'][:, :], writes=['cmat'])
        if 'A' in phases:
            phase_A(S, nc, C, T)
        if 'B' in phases:
            phase_B(S, nc, C, T)
        if 'C' in phases:
            phase_C(S, nc, C, T)
        if 'D' in phases:
            phase_D(S, nc, C, T)
        if 'E' in phases:
            phase_E(S, nc, C, T)
        S.run()
        print("instructions:", S.ninst, {e: S.cnt[e] for e in ENGS})
    return nc


def make_consts():
    c = np.zeros((128, 8, 128), np.float32)
    i = np.arange(128)
    c[:, 0, :] = (i[:, None] <= i[None, :])
    c[:, 1, :] = (i[:, None] > i[None, :])
    c[:, 2, :] = 1.0
    c[:, 3, :] = (i[:, None] < i[None, :])
    c[:, 4, :] = 128.0
    c[:, 5, :] = (i[:, None] >= i[None, :])
    return c.reshape(128, 8 * 128)


def kernel(x_prompt, x_sample, cache_fox_k, cache_fox_v, cache_fox_logf, state_gdn, state_gdn_conv,
           w_in, gdn_conv_w, gdn_a_log, gdn_dt_bias, gdn_norm_w, fox_f_bias, w_out, ln1_g, ln1_b,
           peer_w_q, peer_sub_keys, peer_u, peer_v, ln2_g, ln2_b, _phases="ABCDE", _cores=None):
    import time as _time
    _t0 = _time.time()
    f = lambda a: np.ascontiguousarray(np.asarray(a), dtype=np.float32)
    nc = build_program(_phases)
    cw = f(gdn_conv_w)[0]
    convw_t = np.ascontiguousarray(cw.reshape(4, 48, 128).transpose(2, 1, 0)).reshape(128, 48 * 4)
    bc = lambda v, n=128: np.ascontiguousarray(np.broadcast_to(f(v).reshape(1, -1), (n, f(v).size)))
    shared = {
        "w_in": f(w_in)[0], "w_out": f(w_out)[0], "w_q": f(peer_w_q)[0],
        "subk": f(peer_sub_keys)[0].reshape(16, 128, 128), "pu": f(peer_u)[0], "pv": f(peer_v)[0],
        "convw_t": convw_t, "alog_b": bc(gdn_a_log), "dtb_b": bc(gdn_dt_bias),
        "normw_c": f(gdn_norm_w).reshape(128, 1), "ffb_c": f(fox_f_bias).reshape(NH, 1),
        "ln1g_b": bc(ln1_g), "ln1b_b": bc(ln1_b), "ln2g_b": bc(ln2_g), "ln2b_b": bc(ln2_b),
        "ident_in": np.eye(128, dtype=np.float32), "cmat": make_consts(),
        "selc": np.ascontiguousarray(np.repeat(np.eye(NH, dtype=np.float32), 128, axis=1)),
    }
    xp = f(x_prompt)
    xs = f(x_sample)
    ck = f(cache_fox_k)[0]
    cv = f(cache_fox_v)[0]
    clf = f(cache_fox_logf)[0]
    sg = f(state_gdn)[0]
    sc = f(state_gdn_conv)[0]
    in_maps = []
    for c in range(8):
        m = dict(shared)
        m["xp"] = xp[c]
        m["xs"] = xs[2 * c:2 * c + 2].reshape(NS * DS, D)
        m["ck"] = ck[2 * c:2 * c + 2].reshape(NS, PAST, AW)
        m["cv"] = cv[2 * c:2 * c + 2].reshape(NS, PAST, AW)
        m["clf"] = clf[2 * c:2 * c + 2]
        m["sg"] = sg[2 * c:2 * c + 2]
        m["sconv_t"] = np.ascontiguousarray(sc[2 * c:2 * c + 2].reshape(NS, 3, 48, 128).transpose(0, 3, 2, 1)).reshape(NS, 128, 48 * 3)
        in_maps.append(m)
    if 'E' not in _phases:
        for m in in_maps:
            m["pu"] = m["pu"][:128]
            m["pv"] = m["pv"][:128]
    print("kernel: built+staged %.1fs" % (_time.time() - _t0), flush=True)
    if _cores is not None:
        import os as _os
        res = run_bass_kernel_spmd(nc, [in_maps[c] for c in _cores], core_ids=list(range(len(_cores))),
                                   trace=bool(_os.environ.get("KTRACE")))
        print("EXEC_NS", getattr(res, "exec_time_ns", None), flush=True)
        R = {c: res.results[i] for i, c in enumerate(_cores)}
        R = [R.get(c, R[_cores[0]]) for c in range(8)]
    else:
        res = run_bass_kernel_spmd(nc, in_maps, core_ids=list(range(8)))
        R = res.results
    print("kernel: ran %.1fs" % (_time.time() - _t0), flush=True)
    cat = lambda k: np.stack([np.asarray(R[c][k]) for c in range(8)], axis=0)
    yp = cat("yp")
    ys = cat("ys").reshape(16, DS, D)
    fkp = cat("fkp").reshape(1, 8, SEQ, NH, HD)
    fvp = cat("fvp").reshape(1, 8, SEQ, NH, HD)
    flp = cat("flp").reshape(1, 8, SEQ, NH)
    sgp = cat("sgp").reshape(1, 8, NH, HD, HD)
    scp = cat("scp").reshape(1, 8, 3, 3 * AW)
    fks = cat("fks").reshape(1, 16, DS, NH, HD)
    fvs = cat("fvs").reshape(1, 16, DS, NH, HD)
    fls = cat("fls").reshape(1, 16, DS, NH)
    sgs = cat("sgs").reshape(1, 16, NH, HD, HD)
    scs = cat("scs").reshape(1, 16, 3, 3 * AW)
    return (yp, ys, fkp, fvp, flp, sgp, scp, fks, fvs, fls, sgs, scs)


class KB:
    def __init__(self, c0, n, vb, qb_first, diag):
        self.c0, self.n, self.vb, self.qb_first, self.diag = c0, n, vb, qb_first, diag


def fox_core(S, C, P, kT, Vb, nck, qT, cqT, selh, nqb, QB, kblocks, out_cb, tag):
    import os as _os
    _kg = int(_os.environ.get("KG", "99"))
    for g0 in range(0, min(nqb, 4 * _kg), 4):
        qbs = list(range(g0, min(g0 + 4, nqb)))
        vis = {qb: [kb for kb in kblocks if kb.qb_first <= qb] for qb in qbs}
        for kb in [k for k in kblocks if k.qb_first <= qbs[-1]]:
            qlo = max(kb.qb_first, g0)
            q0, q1 = qlo * QB, (qbs[-1] + 1) * QB
            N = q1 - q0
            bS = 4 + (P.fx_i % 2)
            psl = P.fx_i % 2
            P.fx_i += 1
            S.op('pe', lambda e, kb=kb, bS=bS, q0=q0, q1=q1, N=N: e.matmul(
                C.PS[bS][:kb.n, :N], lhsT=kT[:, kb.c0:kb.c0 + kb.n], rhs=qT[:, q0:q1], start=True, stop=False),
                reads=tag['k'] + tag['q'], writes=[('ps', bS)])
            S.op('pe', lambda e, kb=kb, bS=bS, q0=q0, q1=q1, N=N: e.matmul(
                C.PS[bS][:kb.n, :N], lhsT=selh[:, :kb.n], rhs=cqT[:, q0:q1], start=False, stop=True),
                reads=tag['c'] + ['selc'], writes=[('ps', bS)])
            pt = P.PT[psl]
            S.op('act', lambda e, kb=kb, bS=bS, N=N, pt=pt: e.activation(
                pt[:kb.n, :N], C.PS[bS][:kb.n, :N], AF.Exp, bias=nck(kb), scale=1.0),
                reads=[('ps', bS)] + tag['nck'], writes=[('pt', psl)])
            if kb.diag is not None and qlo <= kb.diag <= qbs[-1]:
                off = (kb.diag - qlo) * QB
                S.op('dve', lambda e, kb=kb, off=off, pt=pt: e.tensor_tensor(
                    pt[:kb.n, off:off + QB], pt[:kb.n, off:off + QB], P.maskb[:kb.n, :QB], ALU.mult),
                    reads=[('pt', psl), 'maskb'], writes=[('pt', psl)])
            for qb in range(qlo, qbs[-1] + 1):
                bank = qb % 4
                first = kb is vis[qb][0]
                last = kb is vis[qb][-1]
                S.op('pe', lambda e, kb=kb, qb=qb, bank=bank, first=first, last=last, pt=pt, qlo=qlo: e.matmul(
                    C.PS[bank][:QB, :129], lhsT=pt[:kb.n, (qb - qlo) * QB:(qb - qlo + 1) * QB],
                    rhs=Vb[:kb.n, kb.vb, :], start=first, stop=last),
                    reads=[('pt', psl)] + tag['v'], writes=[('ps', bank)])
        for qb in qbs:
            out_cb(qb, qb % 4)


def fox_out(S, C, P, QB, ots, ots_tag):
    def cb(qb, bank):
        i = P.oc_i % 2
        P.oc_i += 1
        rec = P.rec[i]
        osb = P.osb[i]
        S.op('dve', lambda e: e.reciprocal(rec[:QB, :], C.PS[bank][:QB, 128:129]),
             reads=[('ps', bank)], writes=[('rec', i)])
        S.op('act', lambda e: e.activation(osb[:QB, :], C.PS[bank][:QB, 0:128], AF.Identity, scale=rec[:QB, 0:1]),
             reads=[('ps', bank), ('rec', i)], writes=[('osb', i)])
        bt = 6 + i
        S.op('pe', lambda e: e.transpose(C.PS[bt][:, :QB], osb[:QB, :], C.ident[:QB, :QB]),
             reads=[('osb', i), 'ident'], writes=[('ps', bt)])
        S.op('dve', lambda e: e.tensor_copy(ots[:, qb * QB:(qb + 1) * QB], C.PS[bt][:, :QB]),
             reads=[('ps', bt)], writes=[ots_tag])
    return cb


def phase_C(S, nc, C, T):
    with ExitStack() as st:
        P = Ctx()
        P.fx_i = 0
        P.oc_i = 0
        sb = lambda name, shape, dt=F32: st.enter_context(nc.sbuf_tensor(name, shape, dt))
        selc = sb("selc_sb", [NH, NH * 128])
        S.dma('sp', selc[:], T['selc'][:, :], writes=['selc'])
        P.maskb = sb("maskb", [128, 128], BF16)
        S.op('dve', lambda e: e.tensor_copy(P.maskb[:], C.cmat[:, 0:128]), reads=['cmat'], writes=['maskb'])
        P.PT = [sb("PT%d" % i, [128, 512], BF16) for i in range(2)]
        P.rec = [sb("rec%d" % i, [128, 1]) for i in range(2)]
        P.osb = [sb("osb%d" % i, [128, 128]) for i in range(2)]
        bfT = sb("bfT", [NH, NT])
        logfT = sb("logfT", [NH, NT])
        ffb = sb("ffb", [NH, 1])
        nffb = sb("nffb", [NH, 1])
        onesr = sb("onesr", [NH, SEQ])
        cTp = sb("cTp", [NH, SEQ])
        S.dma('sp', bfT[:], T['BfT'][:, :], writes=['bfT'])
        S.dma('sp', ffb[:], T['ffb_c'][:, :], writes=['ffb'])
        S.op('dve', lambda e: e.tensor_scalar(nffb[:], ffb[:], -1.0, None, ALU.mult), reads=['ffb'], writes=['nffb'])
        S.op('dve', lambda e: e.memset(onesr[:], 1.0), writes=['onesr'])
        S.op('act', lambda e: e.activation(logfT[:], bfT[:], AF.Exp, bias=nffb[:, 0:1], scale=-1.0),
             reads=['bfT', 'nffb'], writes=['logfT'])
        S.op('act', lambda e: e.activation(logfT[:], logfT[:], AF.Ln, bias=1.0, scale=1.0),
             reads=['logfT'], writes=['logfT'])
        S.op('dve', lambda e: e.tensor_scalar(logfT[:], logfT[:], -1.0, None, ALU.mult),
             reads=['logfT'], writes=['logfT'])
        S.op('dve', lambda e: e.tensor_tensor_scan(cTp[:], onesr[:], logfT[:, 0:SEQ], 0.0, ALU.mult, ALU.add),
             reads=['onesr', 'logfT'], writes=['cTp'])
        lf_tm = sb("lf_tm", [128, 16, NH])
        nck_p = sb("nck_p", [128, 16, NH])
        for blk in range(16):
            S.op('pe', lambda e, blk=blk: e.transpose(C.PS[6][:, blk * 16:(blk + 1) * 16],
                                                      logfT[:, blk * 128:(blk + 1) * 128], C.ident[:NH, :NH]),
                 reads=['logfT', 'ident'], writes=[('ps', 6)])
            S.op('pe', lambda e, blk=blk: e.transpose(C.PS[7][:, blk * 16:(blk + 1) * 16],
                                                      cTp[:, blk * 128:(blk + 1) * 128], C.ident[:NH, :NH]),
                 reads=['cTp', 'ident'], writes=[('ps', 7)])
        S.op('act', lambda e: e.activation(lf_tm[:].rearrange("p b h -> p (b h)"), C.PS[6][:, 0:256], AF.Copy),
             reads=[('ps', 6)], writes=['lf_tm'])
        S.op('dve', lambda e: e.tensor_scalar(nck_p[:].rearrange("p b h -> p (b h)"), C.PS[7][:, 0:256], -1.0, None,
                                              ALU.mult), reads=[('ps', 7)], writes=['nck_p'])
        S.dma('act', T['flp'][:, :].rearrange("(b p) h -> p b h", p=128), lf_tm[:], reads=['lf_tm'])
        lfs = sb("lfs", [64, NH])
        S.op('pe', lambda e: e.transpose(C.PS[6][:64, 0:NH], logfT[:, SEQ:NT], C.ident[:NH, :NH]),
             reads=['logfT', 'ident'], writes=[('ps', 6)])
        S.op('act', lambda e: e.activation(lfs[:], C.PS[6][:64, 0:NH], AF.Copy), reads=[('ps', 6)], writes=['lfs'])
        S.dma('act', T['fls'][:, :], lfs[:], reads=['lfs'])
        clf_tm = sb("clf_tm", [128, 8, NH])
        clfT = [sb("clfT%d" % s, [NH, PAST]) for s in range(NS)]
        ccT = [sb("ccT%d" % s, [NH, PAST]) for s in range(NS)]
        cnT = [sb("cnT%d" % s, [NH, DS]) for s in range(NS)]
        nck_c = [sb("nck_c%d" % s, [128, 8, NH]) for s in range(NS)]
        nck_n = [sb("nck_n%d" % s, [DS, NH]) for s in range(NS)]
        for s in range(NS):
            S.dma('sp', clf_tm[:], T['clf'][s].rearrange("(b p) h -> p b h", p=128), writes=['clf_tm'])
            for blk in range(8):
                bk = 6 + blk // 4
                S.op('pe', lambda e, blk=blk, bk=bk: e.transpose(
                    C.PS[bk][:NH, (blk % 4) * 128:(blk % 4 + 1) * 128], clf_tm[:, blk, :], C.ident[:, :]),
                    reads=['clf_tm', 'ident'], writes=[('ps', bk)])
            for hf in range(2):
                S.op('act', lambda e, hf=hf, s=s: e.activation(clfT[s][:, hf * 512:(hf + 1) * 512],
                                                              C.PS[6 + hf][:NH, :], AF.Copy),
                     reads=[('ps', 6 + hf)], writes=[('clfT', s)])
            S.op('dve', lambda e, s=s: e.tensor_tensor_scan(ccT[s][:], onesr[:, 0:PAST], clfT[s][:], 0.0,
                                                            ALU.mult, ALU.add),
                 reads=['onesr', ('clfT', s)], writes=[('ccT', s)])
            S.op('dve', lambda e, s=s: e.tensor_tensor_scan(cnT[s][:], onesr[:, 0:DS],
                                                            logfT[:, SEQ + s * DS:SEQ + (s + 1) * DS],
                                                            ccT[s][:, PAST - 1:PAST], ALU.mult, ALU.add),
                 reads=['onesr', 'logfT', ('ccT', s)], writes=[('cnT', s)])
            for blk in range(8):
                S.op('pe', lambda e, blk=blk, s=s: e.transpose(C.PS[7][:, blk * 16:(blk + 1) * 16],
                                                               ccT[s][:, blk * 128:(blk + 1) * 128],
                                                               C.ident[:NH, :NH]),
                     reads=[('ccT', s), 'ident'], writes=[('ps', 7)])
            S.op('dve', lambda e, s=s: e.tensor_scalar(nck_c[s][:].rearrange("p b h -> p (b h)"),
                                                       C.PS[7][:, 0:128], -1.0, None, ALU.mult),
                 reads=[('ps', 7)], writes=[('nck_c', s)])
            S.op('pe', lambda e, s=s: e.transpose(C.PS[6][:DS, 0:NH], cnT[s][:, :], C.ident[:NH, :NH]),
                 reads=[('cnT', s), 'ident'], writes=[('ps', 6)])
            S.op('dve', lambda e, s=s: e.tensor_scalar(nck_n[s][:], C.PS[6][:DS, 0:NH], -1.0, None, ALU.mult),
                 reads=[('ps', 6)], writes=[('nck_n', s)])

        import os as _os
        _stop = int(_os.environ.get("KSTOP", "99"))
        if _stop <= 1:
            S.barrier()
            return
        qf = [sb("qf%d" % i, [128, SEQ]) for i in range(2)]
        kf = [sb("kf%d" % i, [128, SEQ]) for i in range(2)]
        vf = [sb("vf%d" % i, [128, 16, 128]) for i in range(2)]
        qb_ = [sb("qb%d" % i, [128, SEQ], BF16) for i in range(2)]
        kb_ = [sb("kb%d" % i, [128, SEQ], BF16) for i in range(2)]
        Vb = [sb("Vb%d" % i, [128, 16, 129], BF16) for i in range(2)]
        ots = [sb("ots%d" % i, [128, SEQ], BF16) for i in range(2)]
        for i in range(2):
            S.op('pool', lambda e, i=i: e.memset(Vb[i][:, :, 128:129], 1.0), writes=[('Vb', i)])

        def load_head(h):
            i = h % 2
            S.dma('sp', qf[i][:], T['BqT'][h * 128:(h + 1) * 128, 0:SEQ], writes=[('qf', i)])
            S.dma('sp', kf[i][:], T['BkT'][h * 128:(h + 1) * 128, 0:SEQ], writes=[('kf', i)])
            S.dma('sp', vf[i][:], T['fvp'][:, h * 128:(h + 1) * 128].rearrange("(b p) d -> p b d", p=128),
                  writes=[('vf', i)])
            S.op('pool', lambda e: e.tensor_copy(qb_[i][:], qf[i][:]), reads=[('qf', i)], writes=[('qb', i)])
            S.op('pool', lambda e: e.tensor_copy(kb_[i][:], kf[i][:]), reads=[('kf', i)], writes=[('kb', i)])
            S.op('pool', lambda e: e.tensor_copy(Vb[i][:, :, 0:128], vf[i][:]), reads=[('vf', i)], writes=[('Vb', i)])

        kblocks_p = [KB(b * 128, 128, b, b, b) for b in range(16)]
        load_head(0)
        _kh = int(_os.environ.get("KH", "16"))
        for h in range(_kh):
            i = h % 2
            if h + 1 < _kh:
                load_head(h + 1)
            tag = {'k': [('kb', i)], 'q': [('qb', i)], 'c': ['cTp'], 'nck': ['nck_p'], 'v': [('Vb', i)]}
            fox_core(S, C, P, kb_[i], Vb[i], lambda kb, h=h: nck_p[:kb.n, kb.vb, h:h + 1], qb_[i], cTp,
                     selc[:, h * 128:(h + 1) * 128], 16, 128, kblocks_p,
                     fox_out(S, C, P, 128, ots[i], ('ots', i)), tag)
            S.dma('act', T['OT'][AW + h * 128:AW + (h + 1) * 128, 0:SEQ], ots[i][:], reads=[('ots', i)])

        if _stop <= 2:
            S.barrier()
            return
        ckf = [sb("ckf%d" % i, [128, 8, 128]) for i in range(2)]
        cvf = [sb("cvf%d" % i, [128, 8, 128]) for i in range(2)]
        knf = [sb("knf%d" % i, [128, DS]) for i in range(2)]
        qnf = [sb("qnf%d" % i, [128, DS]) for i in range(2)]
        vnf = [sb("vnf%d" % i, [DS, 128]) for i in range(2)]
        kTs = [sb("kTs%d" % i, [128, PAST + DS], BF16) for i in range(2)]
        qTs = [sb("qTs%d" % i, [128, DS], BF16) for i in range(2)]
        Vs = [sb("Vs%d" % i, [128, 9, 129], BF16) for i in range(2)]
        otss = [sb("otss%d" % i, [128, DS], BF16) for i in range(2)]
        for i in range(2):
            S.op('pool', lambda e, i=i: e.memset(Vs[i][:, :, 128:129], 1.0), writes=[('Vs', i)])
        kblocks_s = [KB(b * 128, 128, b, 0, None) for b in range(8)] + [KB(PAST, DS, 8, 0, 0)]
        it = 0
        for s in range(NS):
            for h in range(NH):
                i = it % 2
                it += 1
                tcol = SEQ + s * DS
                S.dma('sp', ckf[i][:], T['ck'][s][:, h * 128:(h + 1) * 128].rearrange("(b p) d -> p b d", p=128),
                      writes=[('ckf', i)])
                S.dma('sp', cvf[i][:], T['cv'][s][:, h * 128:(h + 1) * 128].rearrange("(b p) d -> p b d", p=128),
                      writes=[('cvf', i)])
                S.dma('sp', knf[i][:], T['BkT'][h * 128:(h + 1) * 128, tcol:tcol + DS], writes=[('knf', i)])
                S.dma('sp', qnf[i][:], T['BqT'][h * 128:(h + 1) * 128, tcol:tcol + DS], writes=[('qnf', i)])
                S.dma('sp', vnf[i][:], T['fvs'][s * DS:(s + 1) * DS, h * 128:(h + 1) * 128], writes=[('vnf', i)])
                for blk in range(8):
                    bt = 6 + blk % 2
                    S.op('pe', lambda e, blk=blk, bt=bt, i=i: e.transpose(C.PS[bt][:, 0:128], ckf[i][:, blk, :],
                                                                          C.ident[:, :]),
                         reads=[('ckf', i), 'ident'], writes=[('ps', bt)])
                    S.op('dve' if blk % 2 else 'act',
                         (lambda e, blk=blk, bt=bt, i=i: e.tensor_copy(kTs[i][:, blk * 128:(blk + 1) * 128],
                                                                       C.PS[bt][:, 0:128])) if blk % 2 else
                         (lambda e, blk=blk, bt=bt, i=i: e.activation(kTs[i][:, blk * 128:(blk + 1) * 128],
                                                                      C.PS[bt][:, 0:128], AF.Copy)),
                         reads=[('ps', bt)], writes=[('kTs', i)])
                S.op('pool', lambda e, i=i: e.tensor_copy(kTs[i][:, PAST:PAST + DS], knf[i][:]),
                     reads=[('knf', i)], writes=[('kTs', i)])
                S.op('pool', lambda e, i=i: e.tensor_copy(qTs[i][:], qnf[i][:]), reads=[('qnf', i)],
                     writes=[('qTs', i)])
                S.op('pool', lambda e, i=i: e.tensor_copy(Vs[i][:, 0:8, 0:128], cvf[i][:]), reads=[('cvf', i)],
                     writes=[('Vs', i)])
                S.op('pool', lambda e, i=i: e.tensor_copy(Vs[i][:DS, 8, 0:128], vnf[i][:]), reads=[('vnf', i)],
                     writes=[('Vs', i)])
                tag = {'k': [('kTs', i)], 'q': [('qTs', i)], 'c': [('cnT', s)],
                       'nck': [('nck_c', s), ('nck_n', s)], 'v': [('Vs', i)]}

                def ncks(kb, s=s, h=h):
                    if kb.vb < 8:
                        return nck_c[s][:kb.n, kb.vb, h:h + 1]
                    return nck_n[s][:kb.n, h:h + 1]
                fox_core(S, C, P, kTs[i], Vs[i], ncks, qTs[i], cnT[s], selc[:, h * 128:(h + 1) * 128], 1, DS,
                         kblocks_s, fox_out(S, C, P, DS, otss[i], ('otss', i)), tag)
                S.dma('act', T['OT'][AW + h * 128:AW + (h + 1) * 128, tcol:tcol + DS], otss[i][:],
                      reads=[('otss', i)])
    S.barrier()


def phase_B(S, nc, C, T):
    with ExitStack() as st:
        sb = lambda name, shape, dt=F32: st.enter_context(nc.sbuf_tensor("B_" + name, shape, dt))
        CM = 64
        U_ = C.cmat[:, 0:128]
        SL_ = C.cmat[:, 128:256]
        ONES = C.cmat[:, 256:384]
        SU_ = C.cmat[:, 384:512]
        C128 = C.cmat[:, 512:640]
        I_ = C.ident
        cw = sb("cw", [128, 48, 4])
        nA = sb("nA", [CM, NH])
        dtb = sb("dtb", [CM, NH])
        normw = sb("normw", [128, 1])
        S.dma('sp', cw[:].rearrange("p j r -> p (j r)"), T['convw_t'][:, :], writes=['cw'])
        S.dma('sp', nA[:], T['alog_b'][0:CM, :], writes=['nA'])
        S.dma('sp', dtb[:], T['dtb_b'][0:CM, :], writes=['dtb'])
        S.dma('sp', normw[:], T['normw_c'][:, :], writes=['normw'])
        S.op('act', lambda e: e.activation(nA[:], nA[:], AF.Exp), reads=['nA'], writes=['nA'])
        S.op('dve', lambda e: e.tensor_scalar(nA[:], nA[:], -1.0, None, ALU.mult), reads=['nA'], writes=['nA'])
        NCHM = 32
        ab = sb("ab", [CM, NCHM, 32])
        g_all = sb("g_all", [CM, NCHM, NH])
        lnb = sb("lnb_unused", [CM, 1])
        beta = sb("beta", [CM, NCHM, NH])
        ecum = sb("ecum", [CM, NCHM, NH])
        ekd = sb("ekd", [CM, NCHM, NH])
        bec = sb("bec", [CM, NCHM, NH])
        gtot = sb("gtot", [128, NCHM, NH])
        xin = [sb("xin%d" % i, [128, 48, 3 + CM]) for i in range(2)]
        zin = sb("zin", [128, NH, CM])
        zs = sb("zs", [128, NH, CM])
        ta = sb("ta", [128, 48, CM])
        tb = sb("tb", [128, 48, CM])
        ys = sb("ys", [128, 48, CM])
        qkn = sb("qkn", [128, 32, CM])
        Sst = sb("Sst", [128, NH, 128])
        Stmp = sb("Stmp", [128, 4, 128])
        OTb = sb("OTb", [128, NH, 256], BF16)
        q4 = lambda name, w: [sb(name + str(i), [CM, 4, w]) for i in range(2)]
        rhsG = q4("rhsG", CM)
        rhsGT = q4("rhsGT", CM)
        rhsB = q4("rhsB", CM)
        rhsE = q4("rhsE", CM)
        ex1 = q4("ex1", CM)
        ex2 = q4("ex2", CM)
        decU = q4("decU", CM)
        NN = [q4("NNa", CM), q4("NNb", CM)]
        NT_ = [q4("NTa", CM), q4("NTb", CM)]
        TT = [q4("TTa", CM), q4("TTb", CM)]
        bv = q4("bv", 128)
        bek = q4("bek", 128)
        wv = sb("wv", [CM, NH, 128])
        kdec = sb("kdec", [CM, NH, 128])
        qkm = sb("qkm", [CM, NH, CM])
        wkT = sb("wkT", [128, NH, CM])
        qdT = sb("qdT", [128, NH, CM])
        u_ = q4("u_", 128)
        osq = q4("osq", 128)
        on_ = q4("on_", 128)
        ms = [sb("ms%d" % i, [CM, 4]) for i in range(2)]
        rstd = [sb("rstd%d" % i, [CM, 4]) for i in range(2)]
        PS = C.PS
        nb = C.ps_next
        V = lambda fn, r, w: S.op('dve', fn, reads=r, writes=w)
        A = lambda fn, r, w: S.op('act', fn, reads=r, writes=w)
        G = lambda fn, r, w: S.op('pool', fn, reads=r, writes=w)
        PE = lambda fn, r, w: S.op('pe', fn, reads=r, writes=w)

        def seq(tok0, Tlen, Cn, conv_hist, s0, s_out, conv_out, name):
            NCH = Tlen // Cn
            S.dma('sp', ab[:Cn, :NCH, :], T['Aab'][tok0:tok0 + Tlen, :].rearrange("(n c) x -> c n x", c=Cn),
                  writes=['ab'])
            gv = g_all[:Cn, :NCH, :]
            V(lambda e: e.tensor_tensor(gv, ab[:Cn, :NCH, 0:NH], dtb[:Cn, None, :].to_broadcast([Cn, NCH, NH]),
                                        ALU.add), ['ab', 'dtb'], ['g_all'])
            A(lambda e: e.activation(gv, gv, AF.Exp), ['g_all'], ['g_all'])
            A(lambda e: e.activation(gv, gv, AF.Ln, bias=1.0, scale=1.0), ['g_all'], ['g_all'])
            V(lambda e: e.tensor_tensor(gv, gv, nA[:Cn, None, :].to_broadcast([Cn, NCH, NH]), ALU.mult),
              ['g_all', 'nA'], ['g_all'])
            bvw = beta[:Cn, :NCH, :]
            A(lambda e: e.activation(bvw, ab[:Cn, :NCH, NH:2 * NH], AF.Sigmoid), ['ab'], ['beta'])
            g2 = g_all[:Cn, :NCH, :].rearrange("c n h -> c (n h)") if NCH * NH <= 512 else None
            ncol = NCH * NH
            b1 = nb()
            PE(lambda e: e.matmul(PS[b1][:Cn, :ncol], lhsT=U_[:Cn, :Cn], rhs=g_all[:Cn, :NCH, :], start=True, stop=True),
               ['g_all', 'cmat'], [('ps', b1)])
            A(lambda e: e.activation(ecum[:Cn, :NCH, :].rearrange("c n h -> c (n h)"), PS[b1][:Cn, :ncol], AF.Exp),
              [('ps', b1)], ['ecum'])
            b2 = nb()
            PE(lambda e: e.matmul(PS[b2][:Cn, :ncol], lhsT=SL_[:Cn, :Cn], rhs=g_all[:Cn, :NCH, :], start=True, stop=True),
               ['g_all', 'cmat'], [('ps', b2)])
            A(lambda e: e.activation(ekd[:Cn, :NCH, :].rearrange("c n h -> c (n h)"), PS[b2][:Cn, :ncol], AF.Exp),
              [('ps', b2)], ['ekd'])
            b3 = nb()
            PE(lambda e: e.matmul(PS[b3][:, :ncol], lhsT=ONES[:Cn, :], rhs=g_all[:Cn, :NCH, :], start=True, stop=True),
               ['g_all', 'cmat'], [('ps', b3)])
            A(lambda e: e.activation(gtot[:, :NCH, :].rearrange("c n h -> c (n h)"), PS[b3][:, :ncol], AF.Exp),
              [('ps', b3)], ['gtot'])
            V(lambda e: e.tensor_tensor(bec[:Cn, :NCH, :], beta[:Cn, :NCH, :], ecum[:Cn, :NCH, :], ALU.mult),
              ['beta', 'ecum'], ['bec'])
            if s0 is None:
                V(lambda e: e.memset(Sst[:], 0.0), [], [('Sst', q) for q in range(4)])
            else:
                S.dma('sp', Sst[:], s0.rearrange("h k v -> k h v"), writes=[('Sst', q) for q in range(4)])

            def load_x(n):
                xi = xin[n % 2]
                t0 = tok0 + n * Cn
                if n == 0:
                    if conv_hist is None:
                        V(lambda e: e.memset(xi[:, :, 0:3], 0.0), [], [('xin', n % 2)])
                    else:
                        S.dma('sp', xi[:, :, 0:3], conv_hist.rearrange("p (j r) -> p j r", r=3), writes=[('xin', n % 2)])
                    S.dma('sp', xi[:, :, 3:3 + Cn], T['AqkvT'][:, t0:t0 + Cn].rearrange("(j p) t -> p j t", p=128),
                          writes=[('xin', n % 2)])
                else:
                    S.dma('sp', xi[:, :, 0:3 + Cn],
                          T['AqkvT'][:, t0 - 3:t0 + Cn].rearrange("(j p) t -> p j t", p=128), writes=[('xin', n % 2)])

            load_x(0)

            def chunk(n):
                if n + 1 < NCH:
                    load_x(n + 1)
                xi = xin[n % 2]
                xt = ('xin', n % 2)
                t0 = tok0 + n * Cn
                S.dma('sp', zin[:, :, :Cn], T['AzT'][:, t0:t0 + Cn].rearrange("(j p) t -> p j t", p=128), writes=['zin'])
                if n == NCH - 1:
                    for r in range(3):
                        S.dma('act', conv_out[r:r + 1, :].rearrange("o (j p) -> p j o", p=128), xi[:, :, Cn + r:Cn + r + 1],
                              reads=[xt], allow_slow_non_contiguous=True)
                cwb = lambda r: cw[:, :, r:r + 1].to_broadcast([128, 48, Cn])
                V(lambda e: e.tensor_tensor(ta[:, :, :Cn], xi[:, :, 0:Cn], cwb(0), ALU.mult), [xt, 'cw'], ['ta'])
                G(lambda e: e.tensor_tensor(tb[:, :, :Cn], xi[:, :, 1:1 + Cn], cwb(1), ALU.mult), [xt, 'cw'], ['tb'])
                V(lambda e: e.tensor_tensor(ys[:, :, :Cn], xi[:, :, 2:2 + Cn], cwb(2), ALU.mult), [xt, 'cw'], ['ys'])
                G(lambda e: e.tensor_tensor(tb[:, :, :Cn], tb[:, :, :Cn], ta[:, :, :Cn], ALU.add), ['ta', 'tb'], ['tb'])
                V(lambda e: e.tensor_tensor(ta[:, :, :Cn], xi[:, :, 3:3 + Cn], cwb(3), ALU.mult), [xt, 'cw'], ['ta'])
                V(lambda e: e.tensor_tensor(ys[:, :, :Cn], ys[:, :, :Cn], ta[:, :, :Cn], ALU.add), ['ta', 'ys'], ['ys'])
                G(lambda e: e.tensor_tensor(tb[:, :, :Cn], tb[:, :, :Cn], ys[:, :, :Cn], ALU.add), ['tb', 'ys'], ['tb'])
                A(lambda e: e.activation(ys[:, :, :Cn], tb[:, :, :Cn], AF.Silu), ['tb'], ['ys'])
                A(lambda e: e.activation(zs[:, :, :Cn], zin[:, :, :Cn], AF.Silu), ['zin'], ['zs'])
                V(lambda e: e.tensor_tensor(ta[:, 0:32, :Cn], ys[:, 0:32, :Cn], ys[:, 0:32, :Cn], ALU.mult), ['ys'], ['ta'])
                hp = 512 // Cn
                for part in range(32 // hp):
                    j0 = part * hp
                    isq = j0 < 16
                    bb = nb()
                    PE(lambda e, bb=bb, j0=j0, isq=isq: e.matmul(
                        PS[bb][:, :hp * Cn], lhsT=(C128 if isq else ONES), rhs=ta[:, j0:j0 + hp, :Cn],
                        start=True, stop=True), ['ta', 'cmat'], [('ps', bb)])
                    A(lambda e, bb=bb, j0=j0, isq=isq: e.activation(
                        tb[:, j0:j0 + hp, :Cn].rearrange("p j c -> p (j c)") if Cn == CM else tb[:, j0:j0 + hp, :Cn],
                        PS[bb][:, :hp * Cn] if Cn == CM else PS[bb][:, :hp * Cn].rearrange("p (j c) -> p j c", c=Cn),
                        AF.Sqrt, bias=(128e-6 if isq else 1e-6), scale=1.0),
                      [('ps', bb)], ['tb'])
                V(lambda e: e.reciprocal(tb[:, 0:32, :Cn], tb[:, 0:32, :Cn]), ['tb'], ['tb'])
                V(lambda e: e.tensor_tensor(qkn[:, :, :Cn], ys[:, 0:32, :Cn], tb[:, 0:32, :Cn], ALU.mult), ['ys', 'tb'],
                  ['qkn'])
                def quad(qd):
                    h0 = qd * 4
                    p = qd % 2
                    gq = g_all[:Cn, n, h0:h0 + 4]
                    W = 4 * Cn
                    flat = lambda t: t[:Cn, :, :Cn]
                    V(lambda e, p=p, gq=gq: e.tensor_tensor(flat(rhsG[p]), U_[:Cn, None, :Cn].to_broadcast([Cn, 4, Cn]),
                                                            gq[:, :, None].to_broadcast([Cn, 4, Cn]), ALU.mult),
                      ['g_all', 'cmat'], [('rhsG', p)])
                    V(lambda e, p=p, gq=gq: e.tensor_tensor(flat(rhsGT[p]), SL_[:Cn, None, :Cn].to_broadcast([Cn, 4, Cn]),
                                                            gq[:, :, None].to_broadcast([Cn, 4, Cn]), ALU.mult),
                      ['g_all', 'cmat'], [('rhsGT', p)])
                    V(lambda e, p=p, h0=h0: e.tensor_tensor(flat(rhsB[p]), I_[:Cn, None, :Cn].to_broadcast([Cn, 4, Cn]),
                                                            beta[:Cn, n, h0:h0 + 4][:, :, None].to_broadcast([Cn, 4, Cn]),
                                                            ALU.mult), ['beta', 'ident'], [('rhsB', p)])
                    V(lambda e, p=p, h0=h0: e.tensor_tensor(flat(rhsE[p]), I_[:Cn, None, :Cn].to_broadcast([Cn, 4, Cn]),
                                                            ecum[:Cn, n, h0:h0 + 4][:, :, None].to_broadcast([Cn, 4, Cn]),
                                                            ALU.mult), ['ecum', 'ident'], [('rhsE', p)])
                    bDT, bD, bB, bE, bKQ = nb(), nb(), nb(), nb(), nb()
                    v3 = lambda b, P_=Cn: PS[b][:P_, :W].rearrange("p (h c) -> p h c", c=Cn)
                    PE(lambda e, p=p, b=bDT: e.matmul(PS[b][:Cn, :W], lhsT=SL_[:Cn, :Cn], rhs=flat(rhsG[p]), start=True, stop=True),
                       [('rhsG', p), 'cmat'], [('ps', bDT)])
                    PE(lambda e, p=p, b=bD: e.matmul(PS[b][:Cn, :W], lhsT=U_[:Cn, :Cn], rhs=flat(rhsGT[p]), start=True, stop=True),
                       [('rhsGT', p), 'cmat'], [('ps', bD)])
                    PE(lambda e, p=p, b=bB: e.matmul(PS[b][:Cn, :W], lhsT=ONES[:Cn, :Cn], rhs=flat(rhsB[p]), start=True, stop=True),
                       [('rhsB', p), 'cmat'], [('ps', bB)])
                    PE(lambda e, p=p, b=bE: e.matmul(PS[b][:, :W], lhsT=ONES[:Cn, :], rhs=flat(rhsE[p]), start=True, stop=True),
                       [('rhsE', p), 'cmat'], [('ps', bE)])
                    for hh in range(4):
                        h = h0 + hh
                        PE(lambda e, h=h, hh=hh, b=bKQ: e.matmul(PS[b][:Cn, hh * Cn:(hh + 1) * Cn], lhsT=qkn[:, 16 + h, :Cn],
                                                                 rhs=qkn[:, 16 + h, :Cn], start=True, stop=True),
                           ['qkn'], [('ps', bKQ)])
                    bQK = nb()
                    for hh in range(4):
                        h = h0 + hh
                        PE(lambda e, h=h, hh=hh, b=bQK: e.matmul(PS[b][:Cn, hh * Cn:(hh + 1) * Cn], lhsT=qkn[:, 16 + h, :Cn],
                                                                 rhs=qkn[:, h, :Cn], start=True, stop=True),
                           ['qkn'], [('ps', bQK)])
                    A(lambda e, p=p, b=bDT: e.activation(flat(ex1[p]), v3(b), AF.Exp), [('ps', bDT)], [('ex1', p)])
                    A(lambda e, p=p, b=bD: e.activation(flat(ex2[p]), v3(b), AF.Exp), [('ps', bD)], [('ex2', p)])
                    V(lambda e, p=p: e.tensor_tensor(flat(decU[p]), flat(ex1[p]), U_[:Cn, None, :Cn].to_broadcast([Cn, 4, Cn]),
                                                     ALU.mult), [('ex1', p), 'cmat'], [('decU', p)])
                    V(lambda e, p=p, b=bQK, h0=h0: e.tensor_tensor(qkm[:Cn, h0:h0 + 4, :Cn], v3(b), flat(decU[p]), ALU.mult),
                      [('ps', bQK), ('decU', p)], ['qkm'])
                    V(lambda e, p=p: e.tensor_tensor(flat(ex1[p]), flat(ex1[p]), SU_[:Cn, None, :Cn].to_broadcast([Cn, 4, Cn]),
                                                     ALU.mult), [('ex1', p), 'cmat'], [('ex1', p)])
                    V(lambda e, p=p, b=bKQ: e.tensor_tensor(flat(ex1[p]), v3(b), flat(ex1[p]), ALU.mult),
                      [('ps', bKQ), ('ex1', p)], [('ex1', p)])
                    V(lambda e, p=p, b=bB: e.scalar_tensor_tensor(
                        out=NT_[0][p][:Cn, :, :Cn].rearrange("p h c -> p (h c)") if Cn == CM else NT_[0][p][:Cn, :, :Cn],
                        in0=ex1[p][:Cn, :, :Cn].rearrange("p h c -> p (h c)") if Cn == CM else ex1[p][:Cn, :, :Cn],
                        scalar=-1.0,
                        in1=PS[b][:Cn, :W] if Cn == CM else v3(b), op0=ALU.mult, op1=ALU.mult),
                      [('ps', bB), ('ex1', p)], [('NT0', p)])
                    V(lambda e, p=p: e.tensor_tensor(flat(ex2[p]), flat(ex2[p]), SL_[:Cn, None, :Cn].to_broadcast([Cn, 4, Cn]),
                                                     ALU.mult), [('ex2', p), 'cmat'], [('ex2', p)])
                    V(lambda e, p=p, b=bKQ: e.tensor_tensor(flat(ex2[p]), v3(b), flat(ex2[p]), ALU.mult),
                      [('ps', bKQ), ('ex2', p)], [('ex2', p)])
                    V(lambda e, p=p, h0=h0: e.tensor_tensor(flat(ex2[p]), flat(ex2[p]),
                                                            beta[:Cn, n, h0:h0 + 4][:, :, None].to_broadcast([Cn, 4, Cn]),
                                                            ALU.mult), [('ex2', p), 'beta'], [('ex2', p)])
                    V(lambda e, p=p: e.tensor_scalar(flat(NN[0][p]), flat(ex2[p]), -1.0, None, ALU.mult),
                      [('ex2', p)], [('NN0', p)])
                    V(lambda e, b=bE, h0=h0: e.tensor_tensor(qdT[:, h0:h0 + 4, :Cn], qkn[:, h0:h0 + 4, :Cn],
                                                             PS[b][:, :W].rearrange("p (h c) -> p h c", c=Cn), ALU.mult),
                      [('ps', bE), 'qkn'], ['qdT'])
                    V(lambda e, p=p: e.tensor_tensor(flat(TT[0][p]), flat(NT_[0][p]), I_[:Cn, None, :Cn].to_broadcast([Cn, 4, Cn]),
                                                     ALU.add), [('NT0', p), 'ident'], [('TT0', p)])
                    nlev = 5 if Cn == 64 else 4
                    for lv in range(nlev):
                        a, bq = lv % 2, (lv + 1) % 2
                        last = lv == nlev - 1
                        bN, bNT, bT = nb(), (None if last else nb()), nb()
                        for hh in range(4):
                            PE(lambda e, hh=hh, a=a, p=p, b=bN: e.matmul(PS[b][:Cn, hh * Cn:(hh + 1) * Cn], lhsT=NT_[a][p][:Cn, hh, :Cn],
                                                                         rhs=NN[a][p][:Cn, hh, :Cn], start=True, stop=True),
                               [('NT%d' % a, p), ('NN%d' % a, p)], [('ps', bN)])
                        if not last:
                            for hh in range(4):
                                PE(lambda e, hh=hh, a=a, p=p, b=bNT: e.matmul(PS[b][:Cn, hh * Cn:(hh + 1) * Cn], lhsT=NN[a][p][:Cn, hh, :Cn],
                                                                              rhs=NT_[a][p][:Cn, hh, :Cn], start=True, stop=True),
                                   [('NT%d' % a, p), ('NN%d' % a, p)], [('ps', bNT)])
                        A(lambda e, bq=bq, p=p, b=bN: e.activation(flat(NN[bq][p]), v3(b), AF.Copy), [('ps', bN)], [('NN%d' % bq, p)])
                        if not last:
                            V(lambda e, bq=bq, p=p, b=bNT: e.tensor_copy(flat(NT_[bq][p]), v3(b)), [('ps', bNT)], [('NT%d' % bq, p)])
                        for hh in range(4):
                            PE(lambda e, hh=hh, a=a, bq=bq, p=p, b=bT: e.matmul(PS[b][:Cn, hh * Cn:(hh + 1) * Cn], lhsT=NN[bq][p][:Cn, hh, :Cn],
                                                                                rhs=TT[a][p][:Cn, hh, :Cn], start=True, stop=True),
                               [('NN%d' % bq, p), ('TT%d' % a, p)], [('ps', bT)])
                        V(lambda e, a=a, bq=bq, p=p, b=bT: e.tensor_tensor(flat(TT[bq][p]), flat(TT[a][p]), v3(b), ALU.add),
                          [('ps', bT), ('TT%d' % a, p)], [('TT%d' % bq, p)])
                    tf = nlev % 2
                    TTf = TT[tf][p]
                    ttag = ('TT%d' % tf, p)
                    bK, bV = nb(), nb()
                    for hh in range(4):
                        h = h0 + hh
                        PE(lambda e, h=h, hh=hh, b=bK: e.transpose(PS[b][:Cn, hh * 128:(hh + 1) * 128], qkn[:, 16 + h, :Cn], I_[:, :]),
                           ['qkn', 'ident'], [('ps', bK)])
                        PE(lambda e, h=h, hh=hh, b=bV: e.transpose(PS[b][:Cn, hh * 128:(hh + 1) * 128], ys[:, 32 + h, :Cn], I_[:, :]),
                           ['ys', 'ident'], [('ps', bV)])
                    k3 = lambda b: PS[b][:Cn, :512].rearrange("p (h d) -> p h d", d=128)
                    bc3 = lambda t, h0=h0: t[:Cn, n, h0:h0 + 4][:, :, None].to_broadcast([Cn, 4, 128])
                    V(lambda e, p=p, b=bK: e.tensor_tensor(bek[p][:Cn, :, :], k3(b), bc3(bec), ALU.mult), [('ps', bK), 'bec'],
                      [('bek', p)])
                    V(lambda e, b=bK, h0=h0: e.tensor_tensor(kdec[:Cn, h0:h0 + 4, :], k3(b), bc3(ekd), ALU.mult),
                      [('ps', bK), 'ekd'], ['kdec'])
                    V(lambda e, p=p, b=bV: e.tensor_tensor(bv[p][:Cn, :, :], k3(b), bc3(beta), ALU.mult), [('ps', bV), 'beta'],
                      [('bv', p)])
                    bWV, bWK = nb(), nb()
                    for hh in range(4):
                        PE(lambda e, hh=hh, p=p, b=bWV, TTf=TTf: e.matmul(PS[b][:Cn, hh * 128:(hh + 1) * 128], lhsT=TTf[:Cn, hh, :Cn],
                                                                          rhs=bv[p][:Cn, hh, :], start=True, stop=True),
                           [ttag, ('bv', p)], [('ps', bWV)])
                        PE(lambda e, hh=hh, p=p, b=bWK, TTf=TTf: e.matmul(PS[b][:, hh * Cn:(hh + 1) * Cn], lhsT=bek[p][:Cn, hh, :],
                                                                          rhs=TTf[:Cn, hh, :Cn], start=True, stop=True),
                           [ttag, ('bek', p)], [('ps', bWK)])
                    A(lambda e, b=bWV, h0=h0: e.activation(wv[:Cn, h0:h0 + 4, :], k3(b), AF.Copy), [('ps', bWV)], ['wv'])
                    A(lambda e, b=bWK, h0=h0: e.activation(wkT[:, h0:h0 + 4, :Cn], PS[b][:, :W].rearrange("p (h c) -> p h c", c=Cn),
                                                           AF.Copy), [('ps', bWK)], ['wkT'])
                    bU = nb()
                    for hh in range(4):
                        h = h0 + hh
                        PE(lambda e, h=h, hh=hh, b=bU: e.matmul(PS[b][:Cn, hh * 128:(hh + 1) * 128], lhsT=wkT[:, h, :Cn], rhs=Sst[:, h, :],
                                                                start=True, stop=True), ['wkT', ('Sst', qd)], [('ps', bU)])
                    V(lambda e, p=p, b=bU, h0=h0: e.tensor_tensor(u_[p][:Cn, :, :], wv[:Cn, h0:h0 + 4, :], k3(b), ALU.subtract),
                      [('ps', bU), 'wv'], [('u', p)])
                    bO = nb()
                    for hh in range(4):
                        h = h0 + hh
                        PE(lambda e, h=h, hh=hh, b=bO: e.matmul(PS[b][:Cn, hh * 128:(hh + 1) * 128], lhsT=qdT[:, h, :Cn], rhs=Sst[:, h, :],
                                                                start=True, stop=False), ['qdT', ('Sst', qd)], [('ps', bO)])
                        PE(lambda e, h=h, hh=hh, p=p, b=bO: e.matmul(PS[b][:Cn, hh * 128:(hh + 1) * 128], lhsT=qkm[:Cn, h, :Cn],
                                                                     rhs=u_[p][:Cn, hh, :], start=False, stop=True),
                           ['qkm', ('u', p)], [('ps', bO)])
                    bS_ = nb()
                    for hh in range(4):
                        h = h0 + hh
                        PE(lambda e, h=h, hh=hh, p=p, b=bS_: e.matmul(PS[b][:, hh * 128:(hh + 1) * 128], lhsT=kdec[:Cn, h, :],
                                                                      rhs=u_[p][:Cn, hh, :], start=True, stop=True),
                           ['kdec', ('u', p)], [('ps', bS_)])
                    A(lambda e, p=p, b=bO: e.activation(osq[p][:Cn, :, :], k3(b), AF.Square), [('ps', bO)], [('osq', p)])
                    V(lambda e, p=p: e.tensor_reduce(ms[p][:Cn, :], osq[p][:Cn, :, :], AX.X, ALU.add), [('osq', p)], [('ms', p)])
                    A(lambda e, p=p: e.activation(rstd[p][:Cn, :], ms[p][:Cn, :], AF.Sqrt, bias=1e-6, scale=1.0 / 128), [('ms', p)],
                      [('rstd', p)])
                    V(lambda e, p=p: e.reciprocal(rstd[p][:Cn, :], rstd[p][:Cn, :]), [('rstd', p)], [('rstd', p)])
                    V(lambda e, p=p, b=bO: e.tensor_tensor(on_[p][:Cn, :, :], k3(b),
                                                           rstd[p][:Cn, :][:, :, None].to_broadcast([Cn, 4, 128]), ALU.mult),
                      [('ps', bO), ('rstd', p)], [('on', p)])
                    bOT = nb()
                    for hh in range(4):
                        PE(lambda e, hh=hh, p=p, b=bOT: e.transpose(PS[b][:, hh * Cn:(hh + 1) * Cn], on_[p][:Cn, hh, :], I_[:Cn, :Cn]),
                           [('on', p), 'ident'], [('ps', bOT)])
                    oc = (n % (256 // Cn)) * Cn
                    V(lambda e, b=bOT, h0=h0, oc=oc: e.scalar_tensor_tensor(
                        out=OTb[:, h0:h0 + 4, oc:oc + Cn], in0=PS[b][:, :W].rearrange("p (h c) -> p h c", c=Cn),
                        scalar=normw[:, 0:1], in1=zs[:, h0:h0 + 4, :Cn], op0=ALU.mult, op1=ALU.mult),
                      [('ps', bOT), 'zs', 'normw'], ['OTb'])
                    V(lambda e, h0=h0: e.tensor_tensor(Stmp[:, :, :], Sst[:, h0:h0 + 4, :],
                                                       gtot[:, n, h0:h0 + 4][:, :, None].to_broadcast([128, 4, 128]), ALU.mult),
                      [('Sst', qd), 'gtot'], ['Stmp'])
                    V(lambda e, h0=h0, b=bS_: e.tensor_tensor(Sst[:, h0:h0 + 4, :], Stmp[:, :, :],
                                                              PS[b][:, :512].rearrange("p (h d) -> p h d", d=128), ALU.add),
                      [('ps', bS_), 'Stmp'], [('Sst', qd)])
                for qd in range(4):
                    quad(qd)
                per = 256 // Cn
                if (n + 1) % per == 0 or n == NCH - 1:
                    nfl = ((n % per) + 1) * Cn
                    tf0 = tok0 + (n // per) * 256
                    S.dma('act', T['OT'][0:AW, tf0:tf0 + nfl].rearrange("(h p) t -> p h t", p=128), OTb[:, :, 0:nfl],
                          reads=['OTb'])
            for n in range(NCH):
                chunk(n)
            S.dma('act', s_out.rearrange("h k v -> k h v"), Sst[:], reads=[('Sst', q) for q in range(4)])

        seq(0, SEQ, 64, None, None, T['sgp'][:, :, :], T['scp'][:, :], "p")
        for s in range(NS):
            seq(SEQ + s * DS, DS, DS, T['sconv_t'][s], T['sg'][s], T['sgs'][s], T['scs'][s], "s%d" % s)
    S.barrier()


def ln_rows(S, nc, L, n, vt, vtag, gB, bB, out_ap_sb, out_tag):
    i = L.i % 2
    L.i += 1
    sm, ssq, rs = L.sm[i], L.ssq[i], L.rs[i]
    S.op('act', lambda e: e.activation(L.junk[:n, :], vt[:n, :], AF.Identity, accum_out=sm[:n, :]),
         reads=[vtag], writes=[('lnsm', i), 'lnjunk'])
    S.op('dve', lambda e: e.tensor_scalar(sm[:n, :], sm[:n, :], 1.0 / D, None, ALU.mult), reads=[('lnsm', i)],
         writes=[('lnsm', i)])
    S.op('dve', lambda e: e.tensor_scalar(vt[:n, :], vt[:n, :], sm[:n, 0:1], None, ALU.subtract),
         reads=[vtag, ('lnsm', i)], writes=[vtag])
    S.op('act', lambda e: e.activation(L.junk[:n, :], vt[:n, :], AF.Square, accum_out=ssq[:n, :]),
         reads=[vtag], writes=[('lnssq', i), 'lnjunk'])
    S.op('act', lambda e: e.activation(rs[:n, :], ssq[:n, :], AF.Sqrt, bias=L.eps[:n, 0:1], scale=1.0 / D),
         reads=[('lnssq', i), 'lneps'], writes=[('lnrs', i)])
    S.op('dve', lambda e: e.reciprocal(rs[:n, :], rs[:n, :]), reads=[('lnrs', i)], writes=[('lnrs', i)])
    S.op('dve', lambda e: e.scalar_tensor_tensor(out=vt[:n, :], in0=vt[:n, :], scalar=rs[:n, 0:1], in1=gB[:n, :],
                                                 op0=ALU.mult, op1=ALU.mult),
         reads=[vtag, ('lnrs', i), 'lng'], writes=[vtag])
    S.op('dve', lambda e: e.tensor_tensor(out_ap_sb[:n, :], vt[:n, :], bB[:n, :], ALU.add),
         reads=[vtag, 'lnb'], writes=[out_tag])


def ln_setup(S, nc, st, T, gname, bname, pref):
    L = Ctx()
    sb = lambda name, shape, dt=F32: st.enter_context(nc.sbuf_tensor(pref + name, shape, dt))
    L.i = 0
    L.sm = [sb("sm%d" % i, [128, 1]) for i in range(2)]
    L.ssq = [sb("ssq%d" % i, [128, 1]) for i in range(2)]
    L.rs = [sb("rs%d" % i, [128, 1]) for i in range(2)]
    L.eps = sb("eps", [128, 1])
    L.junk = sb("junk", [128, D], BF16)
    L.g = sb("g", [128, D])
    L.b = sb("b", [128, D])
    S.op('dve', lambda e: e.memset(L.eps[:], LN_EPS), writes=['lneps'])
    S.dma('sp', L.g[:], T[gname][:, :], writes=['lng'])
    S.dma('sp', L.b[:], T[bname][:, :], writes=['lnb'])
    return L


def row_tiles():
    return [(i * 128, 128) for i in range(16)] + [(SEQ, NS * DS)]


def x_rows(T, r0, n):
    return T['xp'][r0:r0 + n, :] if r0 < SEQ else T['xs'][r0 - SEQ:r0 - SEQ + n, :]


def y_rows(T, r0, n):
    return T['yp'][r0:r0 + n, :] if r0 < SEQ else T['ys'][r0 - SEQ:r0 - SEQ + n, :]


def phase_D(S, nc, C, T):
    import os as _os
    _kd = int(_os.environ.get("KD", "3"))
    with ExitStack() as st:
        if not (_kd & 1):
            st.close()
            return phase_D2(S, nc, C, T, _kd)
        C.XT = st.enter_context(nc.sbuf_tensor("D_XT", [128, KC, GT], BF16))
        C.wst = [st.enter_context(nc.sbuf_tensor("D_wst%d" % i, [128, KC, 128], F32)) for i in range(2)]
        C.wbf = [st.enter_context(nc.sbuf_tensor("D_wbf%d" % i, [128, KC, 512], BF16)) for i in range(2)]
        mst = [st.enter_context(nc.sbuf_tensor("D_mst%d" % i, [128, 512], F32)) for i in range(3)]
        C.wst_i = 0
        cnt = {'m': 0}
        for g in range(2):
            S.dma('sp', C.XT[:, :, 0:1024], T['OT'][:, g * 1024:(g + 1) * 1024].rearrange("(kc p) t -> p kc t", p=128),
                  writes=[('XT', i) for i in range(8)])
            S.dma('sp', C.XT[:, :, 1024:GT], T['OT'][:, SEQ + g * DS:SEQ + (g + 1) * DS].rearrange("(kc p) t -> p kc t", p=128),
                  writes=[('XT', 8)])
            load_w(S, C, T['w_out'], 0, 512, 0)
            for cg in range(8):
                if cg + 1 < 8:
                    load_w(S, C, T['w_out'], (cg + 1) * 512, 512, (cg + 1) % 2)

                def emit(b, c0, n, cg=cg, g=g):
                    mi = cnt['m'] % 3
                    cnt['m'] += 1
                    mt = mst[mi]
                    if cnt['m'] % 2:
                        S.op('act', lambda e: e.activation(mt[:n, :], C.PS[b][:n, :], AF.Copy), reads=[('ps', b)],
                             writes=[('mst', mi)])
                    else:
                        S.op('dve', lambda e: e.tensor_copy(mt[:n, :], C.PS[b][:n, :]), reads=[('ps', b)],
                             writes=[('mst', mi)])
                    r0 = tokmap(g, c0, n)
                    S.dma('act', T['MIX'][r0:r0 + n, cg * 512:(cg + 1) * 512], mt[:n, :], reads=[('mst', mi)])
                gemm_T(S, C, cg % 2, 512, emit)
    S.barrier()
    phase_D2(S, nc, C, T, _kd)


def phase_D2(S, nc, C, T, _kd):
    if not (_kd & 2):
        return
    with ExitStack() as st:
        L = ln_setup(S, nc, st, T, 'ln1g_b', 'ln1b_b', "D_ln")
        xt_ = [st.enter_context(nc.sbuf_tensor("D_x%d" % i, [128, D], F32)) for i in range(2)]
        mx_ = [st.enter_context(nc.sbuf_tensor("D_m%d" % i, [128, D], F32)) for i in range(2)]

        def tile(ti, r0, n):
            i = ti % 2
            S.dma('sp', xt_[i][:n, :], x_rows(T, r0, n), writes=[('dx', i)])
            S.dma('sp', mx_[i][:n, :], T['MIX'][r0:r0 + n, :], writes=[('dm', i)])
            S.op('dve', lambda e: e.scalar_tensor_tensor(out=xt_[i][:n, :], in0=xt_[i][:n, :], scalar=float(ALPHA),
                                                         in1=mx_[i][:n, :], op0=ALU.mult, op1=ALU.add),
                 reads=[('dx', i), ('dm', i)], writes=[('dx', i)])
            ln_rows(S, nc, L, n, xt_[i], ('dx', i), L.g, L.b, mx_[i], ('dm', i))
            S.dma('act', T['H'][r0:r0 + n, :], mx_[i][:n, :], reads=[('dm', i)])
        for ti, (r0, n) in enumerate(row_tiles()):
            tile(ti, r0, n)
    S.barrier()


def phase_E(S, nc, C, T):
    NEG = -1.0e30
    with ExitStack() as st:
        sb = lambda name, shape, dt=F32: st.enter_context(nc.sbuf_tensor("E1_" + name, shape, dt))
        C.XT = sb("XT", [128, KC, GT], BF16)
        C.xrow = [sb("xrow%d" % i, [128, D]) for i in range(2)]
        C.wst = [sb("wst%d" % i, [128, KC, 128]) for i in range(2)]
        C.wbf = [sb("wbf%d" % i, [128, KC, 128], BF16) for i in range(2)]
        C.wst_i = 0
        skn = sb("skn", [128, 16, 128])
        KT = sb("KT", [128, 16, 128])
        qst = [sb("qst%d" % i, [128, GT]) for i in range(2)]
        scs_ = [sb("scs%d" % i, [128, 128]) for i in range(3)]
        S.dma('sp', skn[:], T['subk'][:, :, :].rearrange("j n c -> n j c"), writes=['skn'])
        for j in range(16):
            b = C.ps_next()
            S.op('pe', lambda e, j=j, b=b: e.transpose(C.PS[b][:, 0:128], skn[:, j, :], C.ident[:, :]),
                 reads=['skn', 'ident'], writes=[('ps', b)])
            S.op('dve', lambda e, j=j, b=b: e.tensor_copy(KT[:, j, :], C.PS[b][:, 0:128]), reads=[('ps', b)],
                 writes=['KT'])
        cnt = {'s': 0, 'e': 0}
        for g in range(2):
            rows = [(T['H'][g * 1024 + i * 128:g * 1024 + (i + 1) * 128, :], 128, i * 128) for i in range(8)]
            rows.append((T['H'][SEQ + g * DS:SEQ + (g + 1) * DS, :], DS, 1024))
            build_xT(S, C, g, rows)
            load_w(S, C, T['w_q'], 0, 128, 0)
            for j in range(16):
                if j + 1 < 16:
                    load_w(S, C, T['w_q'], (j + 1) * 128, 128, (j + 1) % 2)
                qs = qst[j % 2]

                def emitF(b, t0, n, qs=qs, j=j):
                    cnt['e'] += 1
                    if cnt['e'] % 2:
                        S.op('act', lambda e: e.activation(qs[:, t0:t0 + n], C.PS[b][:, :n], AF.Copy),
                             reads=[('ps', b)], writes=[('qst', j % 2)])
                    else:
                        S.op('dve', lambda e: e.tensor_copy(qs[:, t0:t0 + n], C.PS[b][:, :n]),
                             reads=[('ps', b)], writes=[('qst', j % 2)])
                gemm_F(S, C, j % 2, 128, emitF)

                def score(ti, j=j, qs=qs, g=g):
                    c0 = ti * 128
                    n = 128 if ti < 8 else DS
                    b = C.ps_next()
                    S.op('pe', lambda e: e.matmul(C.PS[b][:n, 0:128], lhsT=qs[:, c0:c0 + n], rhs=KT[:, j, :],
                                                  start=True, stop=True),
                         reads=[('qst', j % 2), 'KT'], writes=[('ps', b)])
                    si = cnt['s'] % 3
                    cnt['s'] += 1
                    sc_t = scs_[si]
                    S.op('dve' if si % 2 else 'act',
                         (lambda e: e.tensor_copy(sc_t[:n, :], C.PS[b][:n, 0:128])) if si % 2 else
                         (lambda e: e.activation(sc_t[:n, :], C.PS[b][:n, 0:128], AF.Copy)),
                         reads=[('ps', b)], writes=[('scs', si)])
                    r0 = tokmap(g, c0, n)
                    S.dma('act', T['SC'][r0:r0 + n, j * 128:(j + 1) * 128], sc_t[:n, :], reads=[('scs', si)])
                for ti in range(9):
                    score(ti)
    S.barrier()
    with ExitStack() as st:
        sb = lambda name, shape, dt=F32: st.enter_context(nc.sbuf_tensor("V0_" + name, shape, dt))
        fin = [sb("fin%d" % i, [128, D]) for i in range(3)]
        fout = [sb("fout%d" % i, [128, D], BF16) for i in range(3)]
        NR = T['pv'].shape[0] // 128

        def cv_load(r):
            S.dma('sp', fin[r % 3][:], T['pv'][r * 128:(r + 1) * 128, :], writes=[('cvi', r % 3)])

        def cv_rest(r):
            i = r % 3
            if r % 2:
                S.op('act', lambda e: e.activation(fout[i][:], fin[i][:], AF.Copy), reads=[('cvi', i)], writes=[('cvo', i)])
            else:
                S.op('dve', lambda e: e.tensor_copy(fout[i][:], fin[i][:]), reads=[('cvi', i)], writes=[('cvo', i)])
            S.dma('act', T['PVB'][r * 128:(r + 1) * 128, :], fout[i][:], reads=[('cvo', i)])
        cv_load(0)
        cv_load(1)
        for r in range(NR):
            if r + 2 < NR:
                cv_load(r + 2)
            cv_rest(r)
    S.barrier()
    with ExitStack() as st:
        sb = lambda name, shape, dt=F32: st.enter_context(nc.sbuf_tensor("E2_" + name, shape, dt))
        L = ln_setup(S, nc, st, T, 'ln2g_b', 'ln2b_b', "E_ln")
        sc = sb("sc", [128, 16, 128])
        sc2 = sb("sc2", [128, 16, 128])
        stop_ = sb("stop", [128, 16, 16])
        itop = sb("itop", [128, 16, 16], U32)
        itf = sb("itf", [128, 16, 16])
        cand = sb("cand", [128, 8, 256])
        cand2 = sc2[:, :, :].rearrange("p (h c) k -> p h (c k)", c=2)
        cidx = sb("cidx", [128, 8, 256])
        tops = sb("tops", [128, 8, 16])
        eidf = sb("eidf", [128, 128])
        eidi = [sb("eidi%d" % i, [128, 128], I32) for i in range(2)]
        gate = sb("gate", [128, 8, 16])
        zs_ = sb("zsum", [128, 8])
        pre = sb("pre", [128, 128])
        actv = [sb("actv%d" % i, [128, 128]) for i in range(2)]
        ht = [sb("ht%d" % i, [128, D]) for i in range(2)]
        NGB = 3
        ub = [sb("ub%d" % i, [128, D]) for i in range(NGB)]
        vbb = [sb("vbb%d" % i, [128, D], BF16) for i in range(NGB)]
        dg = [sb("dg%d" % i, [128, 128], BF16) for i in range(NGB)]
        identb = sb("identb", [128, 128], BF16)
        sj = [sb("sj%d" % i, [128, 256]) for i in range(2)]
        V = lambda fn, r, w: S.op('dve', fn, reads=r, writes=w)
        A = lambda fn, r, w: S.op('act', fn, reads=r, writes=w)
        G = lambda fn, r, w: S.op('pool', fn, reads=r, writes=w)
        V(lambda e: e.tensor_copy(identb[:], C.ident[:]), ['ident'], ['identb'])

        def e2(k, r0, n):
            par = k % 2
            S.dma('sp', sc[:n, :, :], T['SC'][r0:r0 + n, :].rearrange("t (j k) -> t j k", k=128), writes=['sc'])
            S.dma('sp', ht[par][:n, :], T['H'][r0:r0 + n, :], writes=[('ht', par)])
            for j in range(16):
                def pair(j):
                    V(lambda e: e.max(out=stop_[:n, j, 0:8], in_=sc[:n, j, :]), ['sc'], ['stop'])
                    V(lambda e: e.max_index(out=itop[:n, j, 0:8], in_max=stop_[:n, j, 0:8], in_values=sc[:n, j, :]),
                      ['sc', 'stop'], ['itop'])
                    V(lambda e: e.match_replace(out=sc2[:n, j, :], in_to_replace=stop_[:n, j, 0:8], in_values=sc[:n, j, :],
                                                imm_value=NEG), ['sc', 'stop'], ['sc2'])
                    V(lambda e: e.max(out=stop_[:n, j, 8:16], in_=sc2[:n, j, :]), ['sc2'], ['stop'])
                    V(lambda e: e.max_index(out=itop[:n, j, 8:16], in_max=stop_[:n, j, 8:16], in_values=sc2[:n, j, :]),
                      ['sc2', 'stop'], ['itop'])
                pair(j)
            V(lambda e: e.tensor_copy(itf[:n, :, :], itop[:n, :, :]), ['itop'], ['itf'])
            s4 = stop_[:n, :, :].rearrange("t (h p) k -> t h p k", p=2)
            i4 = itf[:n, :, :].rearrange("t (h p) k -> t h p k", p=2)
            c4 = lambda t: t[:n, :, :].rearrange("t h (a b) -> t h a b", b=16)
            V(lambda e: e.tensor_tensor(c4(cand), s4[:, :, 0, :][:, :, :, None].to_broadcast([n, 8, 16, 16]),
                                        s4[:, :, 1, :][:, :, None, :].to_broadcast([n, 8, 16, 16]), ALU.add),
              ['stop'], ['cand'])
            V(lambda e: e.tensor_scalar(i4[:, :, 0, :], i4[:, :, 0, :], 128.0, None, ALU.mult), ['itf'], ['itf'])
            V(lambda e: e.tensor_tensor(c4(cidx), i4[:, :, 0, :][:, :, :, None].to_broadcast([n, 8, 16, 16]),
                                        i4[:, :, 1, :][:, :, None, :].to_broadcast([n, 8, 16, 16]), ALU.add),
              ['itf'], ['cidx'])
            for hd in range(8):
                def head(hd):
                    V(lambda e: e.max(out=tops[:n, hd, 0:8], in_=cand[:n, hd, :]), ['cand'], ['tops'])
                    V(lambda e: e.match_replace(out=cand2[:n, hd, :], in_to_replace=tops[:n, hd, 0:8],
                                                in_values=cand[:n, hd, :], imm_value=NEG), ['cand', 'tops'], ['sc2'])
                    V(lambda e: e.max(out=tops[:n, hd, 8:16], in_=cand2[:n, hd, :]), ['sc2'], ['tops'])
                    for kk in range(16):
                        def pick(kk):
                            m = hd * 16 + kk
                            V(lambda e: e.scalar_tensor_tensor(out=sj[m % 2][:n, :], in0=cand[:n, hd, :],
                                                               scalar=tops[:n, hd, kk:kk + 1], in1=cidx[:n, hd, :],
                                                               op0=ALU.is_equal, op1=ALU.mult,
                                                               accum_out=eidf[:n, m:m + 1]),
                              ['cand', 'tops', 'cidx'], [('sj', m % 2), 'eidf'])
                        pick(kk)
                head(hd)
            V(lambda e: e.tensor_scalar(eidf[:n, :], eidf[:n, :], 16383.0, 0.0, ALU.min, ALU.max), ['eidf'], ['eidf'])
            V(lambda e: e.tensor_copy(eidi[par][:n, :], eidf[:n, :]), ['eidf'], [('eidi', par)])
            V(lambda e: e.tensor_tensor(gate[:n, :, :], tops[:n, :, :], tops[:n, :, 0:1].to_broadcast([n, 8, 16]),
                                        ALU.subtract), ['tops'], ['gate'])
            A(lambda e: e.activation(gate[:n, :, :], gate[:n, :, :], AF.Exp), ['gate'], ['gate'])
            V(lambda e: e.tensor_reduce(zs_[:n, :], gate[:n, :, :], AX.X, ALU.add), ['gate'], ['zsum'])
            V(lambda e: e.reciprocal(zs_[:n, :], zs_[:n, :]), ['zsum'], ['zsum'])
            V(lambda e: e.tensor_tensor(gate[:n, :, :], gate[:n, :, :], zs_[:n, :][:, :, None].to_broadcast([n, 8, 16]),
                                        ALU.mult), ['gate', 'zsum'], ['gate'])

        def upick(k, n, m):
            par = k % 2
            i = m % NGB
            S.gather(ub[i][:n, :], T['pu'][:, :], eidi[par][:n, m:m + 1], reads=[('eidi', par)], writes=[('ub', i)],
                     bounds=None)
            V(lambda e: e.scalar_tensor_tensor(out=L.junk[:n, :], in0=ub[i][:n, :], scalar=1.0, in1=ht[par][:n, :],
                                               op0=ALU.mult, op1=ALU.mult, accum_out=pre[:n, m:m + 1]),
              [('ub', i), ('ht', par)], [('pre', m)])

        def ufinish(k, n):
            par = k % 2
            A(lambda e: e.activation(actv[par][:n, :], pre[:n, :], AF.Gelu), [('pre', m) for m in range(128)],
              [('actv', par)])
            V(lambda e: e.tensor_tensor(actv[par][:n, :], actv[par][:n, :], gate[:n, :, :].rearrange("t h k -> t (h k)"),
                                        ALU.mult), [('actv', par), 'gate'], [('actv', par)])

        def vpick(k, n, m):
            par = k % 2
            i = m % NGB
            S.gather(vbb[i][:n, :], T['PVB'][:, :], eidi[par][:n, m:m + 1], reads=[('eidi', par)], writes=[('vbb', i)],
                     bounds=None)
            G(lambda e: e.tensor_scalar(dg[i][:n, :n], identb[:n, :n], actv[par][:n, m:m + 1], None, ALU.mult),
              ['identb', ('actv', par)], [('dg', i)])
            for cb in range(8):
                S.op('pe', lambda e, cb=cb: e.matmul(C.PS[cb][:n, :], lhsT=dg[i][:n, :n], rhs=vbb[i][:n, cb * 512:(cb + 1) * 512],
                                                     start=(m == 0), stop=(m == 127)),
                     reads=[('dg', i), ('vbb', i)], writes=[('ps', cb)])

        def final(k, r0, n):
            par = k % 2
            for cb in range(8):
                V(lambda e, cb=cb: e.scalar_tensor_tensor(out=ht[par][:n, cb * 512:(cb + 1) * 512],
                                                          in0=ht[par][:n, cb * 512:(cb + 1) * 512], scalar=float(ALPHA),
                                                          in1=C.PS[cb][:n, :], op0=ALU.mult, op1=ALU.add),
                  [('ht', par), ('ps', cb)], [('ht', par)])
            ln_rows(S, nc, L, n, ht[par], ('ht', par), L.g, L.b, ub[0], ('ub', 0))
            S.dma('act', y_rows(T, r0, n), ub[0][:n, :], reads=[('ub', 0)])

        import os as _os
        tiles = row_tiles()[:int(_os.environ.get("KT", "99"))]
        NTL = len(tiles)
        e2(0, *tiles[0])
        for m in range(128):
            upick(0, tiles[0][1], m)
        ufinish(0, tiles[0][1])
        for k in range(NTL):
            r0, n = tiles[k]
            nxt = k + 1 < NTL
            if nxt:
                e2(k + 1, *tiles[k + 1])
            for m in range(128):
                vpick(k, n, m)
                if nxt:
                    upick(k + 1, tiles[k + 1][1], m)
            if nxt:
                ufinish(k + 1, tiles[k + 1][1])
            final(k, r0, n)
    S.barrier()
```

```python
import numpy as np
from contextlib import ExitStack
import concourse.bass as bass
import concourse.mybir as mybir
from concourse.bass_utils import run_bass_kernel_spmd

F32 = mybir.dt.float32
BF16 = mybir.dt.bfloat16
I32 = mybir.dt.int32
U32 = mybir.dt.uint32
AF = mybir.ActivationFunctionType
ALU = mybir.AluOpType
AX = mybir.AxisListType

ENGS = ['pe', 'act', 'dve', 'pool', 'sp']


class Sched:
    def __init__(self, nc, stack, n_dma=32):
        self.nc = nc
        self.q = {e: [] for e in ENGS}
        self.sem = {e: stack.enter_context(nc.semaphore("s_" + e)) for e in ENGS}
        self.cnt = {e: 0 for e in ENGS}
        self.nd = n_dma
        self.dsem = [stack.enter_context(nc.semaphore("d%d" % i)) for i in range(n_dma)]
        self.dcnt = [0] * n_dma
        self.dnext = 0
        self.seen = {e: {} for e in ENGS}
        self.lw = {}
        self.rd = {}
        self.ninst = 0

    def _deps(self, reads, writes):
        deps = {}
        for r in reads:
            kv = self.lw.get(r)
            if kv is not None and deps.get(kv[0], 0) < kv[1]:
                deps[kv[0]] = kv[1]
        for w in writes:
            kv = self.lw.get(w)
            if kv is not None and deps.get(kv[0], 0) < kv[1]:
                deps[kv[0]] = kv[1]
            for k, v in self.rd.get(w, {}).items():
                if deps.get(k, 0) < v:
                    deps[k] = v
        return deps

    def _waits(self, eng, deps):
        for k, v in deps.items():
            if k == 'pe' and eng == 'pe':
                continue
            if self.seen[eng].get(k, 0) >= v:
                continue
            self.seen[eng][k] = v
            sem = self.sem[k] if isinstance(k, str) else self.dsem[k[1]]
            self.q[eng].append(lambda e, sem=sem, v=v: e.wait_ge(sem, v))
            self.ninst += 1

    def _mark(self, key, v, reads, writes):
        for r in reads:
            d = self.rd.setdefault(r, {})
            if d.get(key, 0) < v:
                d[key] = v
        for w in writes:
            self.lw[w] = (key, v)
            self.rd[w] = {}

    def op(self, eng, fn, reads=(), writes=()):
        self._waits(eng, self._deps(reads, writes))
        self.cnt[eng] += 1
        v = self.cnt[eng]
        sem = self.sem[eng]
        self.q[eng].append(lambda e: fn(e).then_inc(sem, 1))
        self.ninst += 1
        self._mark(eng, v, reads, writes)

    def dma(self, q, out, in_, reads=(), writes=(), **kw):
        i = self.dnext
        self.dnext = (i + 1) % self.nd
        deps = self._deps(reads, writes)
        if self.dcnt[i] > 0:
            deps[('d', i)] = max(deps.get(('d', i), 0), self.dcnt[i])
        self._waits(q, deps)
        self.dcnt[i] += 16
        v = self.dcnt[i]
        sem = self.dsem[i]
        self.q[q].append(lambda e: e.dma_start(out=out, in_=in_, **kw).then_inc(sem, 16))
        self.ninst += 1
        self._mark(('d', i), v, reads, writes)

    def gather(self, out, table, idx_ap, reads=(), writes=(), bounds=None):
        q = 'pool'
        i = self.dnext
        self.dnext = (i + 1) % self.nd
        deps = self._deps(reads, writes)
        if self.dcnt[i] > 0:
            deps[('d', i)] = max(deps.get(('d', i), 0), self.dcnt[i])
        self._waits(q, deps)
        self.dcnt[i] += 16
        v = self.dcnt[i]
        sem = self.dsem[i]

        def f(e):
            self._gdbg = getattr(self, '_gdbg', 0) + 1
            if self._gdbg > 4351:
                print("GATHER", self._gdbg, out.shape, idx_ap, flush=True)
            return e.indirect_dma_start(
                out=out, out_offset=None, in_=table,
                in_offset=bass.IndirectOffsetOnAxis(ap=idx_ap, axis=0),
                bounds_check=bounds, oob_is_err=False).then_inc(sem, 16)
        self.q[q].append(f)
        self.ninst += 1
        self._mark(('d', i), v, reads, writes)

    def barrier(self):
        for e in ENGS:
            deps = {k: self.cnt[k] for k in ENGS if self.cnt[k] > 0 and k != e}
            if e != 'pe' and self.cnt[e] > 0:
                deps[e] = self.cnt[e]
            for i in range(self.nd):
                if self.dcnt[i] > 0:
                    deps[('d', i)] = self.dcnt[i]
            self._waits(e, deps)
        self.lw = {}
        self.rd = {}

    def run(self):
        self.barrier()
        q = self.q
        with self.nc.Block() as block:
            @block.tensor
            def _(e):
                for f in q['pe']:
                    f(e)

            @block.scalar
            def _(e):
                for f in q['act']:
                    f(e)

            @block.vector
            def _(e):
                for f in q['dve']:
                    f(e)

            @block.gpsimd
            def _(e):
                for f in q['pool']:
                    f(e)

            @block.sync
            def _(e):
                for f in q['sp']:
                    f(e)


D = 4096
SEQ = 2048
DS = 32
NS = 2
PAST = 1024
NT = SEQ + NS * DS
HD = 128
NH = 16
AW = 2048
PROJ = 14384
OFF_Z = 6144
OFF_A = 8192
OFF_B = 8208
OFF_BQ = 8224
OFF_F = 14368
GT = 1056
KC = 32
LN_EPS = 1e-5
ALPHA = 2.0 ** 0.25


class Ctx:
    pass


def tokmap(g, c0, n):
    if c0 < 1024:
        return g * 1024 + c0
    return SEQ + DS * g + (c0 - 1024)


def build_xT(S, C, g, src_rows):
    XT = C.XT
    for ti, (src, n, c0) in enumerate(src_rows):
        sl = ti % 2
        xr = C.xrow[sl]
        S.dma('sp', xr[:n, :], src, writes=[('xr', sl)])
        for kq in range(8):
            b = C.ps_next()
            for j in range(4):
                kc = kq * 4 + j
                S.op('pe', lambda e, b=b, j=j, kc=kc, xr=xr, n=n: e.transpose(
                    C.PS[b][:, j * 128:j * 128 + n], xr[:n, kc * 128:(kc + 1) * 128], C.ident[:n, :n]),
                    reads=[('xr', sl), 'ident'], writes=[('ps', b)])
            src_ps = C.PS[b][:, :].rearrange("p (j t) -> p j t", j=4)[:, :, :n]
            dst = XT[:, kq * 4:(kq + 1) * 4, c0:c0 + n]
            if kq % 2 == 0:
                S.op('act', lambda e, dst=dst, src_ps=src_ps: e.activation(dst, src_ps, AF.Copy),
                     reads=[('ps', b)], writes=[('XT', c0 // 128)])
            else:
                S.op('dve', lambda e, dst=dst, src_ps=src_ps: e.tensor_copy(dst, src_ps),
                     reads=[('ps', b)], writes=[('XT', c0 // 128)])


def load_w(S, C, wdram, col0, ncols, slot):
    for p0 in range(0, ncols, 128):
        pn = min(128, ncols - p0)
        ss = C.wst_i % 2
        C.wst_i += 1
        wst = C.wst[ss]
        src = wdram[:, col0 + p0:col0 + p0 + pn].rearrange("(kc p) c -> p kc c", p=128)
        S.dma('sp', wst[:, :, :pn], src, writes=[('wst', ss)])
        dst = C.wbf[slot][:, :, p0:p0 + pn]
        S.op('pool', lambda e, dst=dst, wst=wst, pn=pn: e.tensor_copy(dst, wst[:, :, :pn]),
             reads=[('wst', ss)], writes=[('wbf', slot)])


def xt_reads(c0, n):
    return [('XT', i) for i in range(c0 // 128, (c0 + n - 1) // 128 + 1)]


def gemm_F(S, C, slot, ncols, emit):
    XT, wbf = C.XT, C.wbf[slot]
    for (t0, n) in ((0, 512), (512, 512), (1024, 32)):
        b = C.ps_next()
        for kc in range(KC):
            S.op('pe', lambda e, b=b, kc=kc, t0=t0, n=n: e.matmul(
                C.PS[b][:ncols, :n], lhsT=wbf[:, kc, :ncols], rhs=XT[:, kc, t0:t0 + n],
                start=(kc == 0), stop=(kc == KC - 1)),
                reads=[('wbf', slot)] + xt_reads(t0, n), writes=[('ps', b)])
        emit(b, t0, n)


def gemm_T(S, C, slot, ncols, emit):
    XT, wbf = C.XT, C.wbf[slot]
    for ti in range(9):
        c0 = ti * 128
        n = 128 if ti < 8 else 32
        b = C.ps_next()
        for kc in range(KC):
            S.op('pe', lambda e, b=b, kc=kc, c0=c0, n=n: e.matmul(
                C.PS[b][:n, :ncols], lhsT=XT[:, kc, c0:c0 + n], rhs=wbf[:, kc, :ncols],
                start=(kc == 0), stop=(kc == KC - 1)),
                reads=[('wbf', slot)] + xt_reads(c0, n), writes=[('ps', b)])
        emit(b, c0, n)


def phase_A(S, nc, C, T):
    with ExitStack() as st:
        C.XT = st.enter_context(nc.sbuf_tensor("XT", [128, KC, GT], BF16))
        C.xrow = [st.enter_context(nc.sbuf_tensor("xrow%d" % i, [128, D], F32)) for i in range(2)]
        C.wst = [st.enter_context(nc.sbuf_tensor("wst%d" % i, [128, KC, 128], F32)) for i in range(2)]
        C.wbf = [st.enter_context(nc.sbuf_tensor("wbf%d" % i, [128, KC, 128], BF16)) for i in range(2)]
        stg = [st.enter_context(nc.sbuf_tensor("stg%d" % i, [128, GT], F32)) for i in range(2)]
        tst = [st.enter_context(nc.sbuf_tensor("tst%d" % i, [128, 128], F32)) for i in range(4)]
        C.wst_i = 0
        cnt = {'stg': 0, 'tst': 0, 'ev': 0}

        chunks = []
        for j in range(48):
            chunks.append((j * 128, 128, 'F', (T['AqkvT'], j * 128, 1.0)))
        for j in range(16):
            chunks.append((OFF_Z + j * 128, 128, 'F', (T['AzT'], j * 128, 1.0)))
        chunks.append((OFF_A, 32, 'T', ('ab', 0)))
        for j in range(16):
            chunks.append((OFF_BQ + j * 128, 128, 'F', (T['BqT'], j * 128, HD ** -0.5)))
        for j in range(16):
            chunks.append((OFF_BQ + AW + j * 128, 128, 'FT', (T['BkT'], j * 128, 1.0, 'k', j * 128)))
        for j in range(16):
            chunks.append((OFF_BQ + 2 * AW + j * 128, 128, 'T', ('v', j * 128)))
        chunks.append((OFF_F, 16, 'F', (T['BfT'], 0, 1.0)))

        for g in range(2):
            rows = [(T['xp'][g * 1024 + i * 128:g * 1024 + (i + 1) * 128, :], 128, i * 128) for i in range(8)]
            rows.append((T['xs'][g * DS:(g + 1) * DS, :], DS, 1024))
            build_xT(S, C, g, rows)

            def emit_F(info):
                dst, row0, scale = info[0], info[1], info[2]

                def emit(b, t0, n, ncols):
                    pass
                return emit

            load_w(S, C, T['w_in'], chunks[0][0], chunks[0][1], 0)
            for ci, (col0, ncols, mode, info) in enumerate(chunks):
                slot = ci % 2
                if ci + 1 < len(chunks):
                    load_w(S, C, T['w_in'], chunks[ci + 1][0], chunks[ci + 1][1], (ci + 1) % 2)
                if 'F' in mode:
                    dst, row0, scale = info[0], info[1], info[2]
                    ss = cnt['stg'] % 2
                    cnt['stg'] += 1
                    sg_t = stg[ss]

                    def emitF(b, t0, n, ncols=ncols, sg_t=sg_t, ss=ss, scale=scale):
                        cnt['ev'] += 1
                        if cnt['ev'] % 2 == 0:
                            S.op('act', lambda e: e.activation(sg_t[:ncols, t0:t0 + n], C.PS[b][:ncols, :n],
                                                               AF.Identity, scale=float(scale)),
                                 reads=[('ps', b)], writes=[('stg', ss)])
                        else:
                            S.op('dve', lambda e: e.tensor_scalar(sg_t[:ncols, t0:t0 + n], C.PS[b][:ncols, :n],
                                                                  float(scale), None, ALU.mult),
                                 reads=[('ps', b)], writes=[('stg', ss)])
                    gemm_F(S, C, slot, ncols, emitF)
                    S.dma('act', dst[row0:row0 + ncols, g * 1024:(g + 1) * 1024], sg_t[:ncols, 0:1024],
                          reads=[('stg', ss)])
                    S.dma('act', dst[row0:row0 + ncols, SEQ + DS * g:SEQ + DS * (g + 1)], sg_t[:ncols, 1024:GT],
                          reads=[('stg', ss)])
                if 'T' in mode:
                    kind, coff = (info[3], info[4]) if mode == 'FT' else (info[0], info[1])

                    def emitT(b, c0, n, ncols=ncols, kind=kind, coff=coff):
                        ts_ = cnt['tst'] % 4
                        cnt['tst'] += 1
                        tt = tst[ts_]
                        cnt['ev'] += 1
                        if cnt['ev'] % 2 == 0:
                            S.op('act', lambda e: e.activation(tt[:n, :ncols], C.PS[b][:n, :ncols], AF.Copy),
                                 reads=[('ps', b)], writes=[('tst', ts_)])
                        else:
                            S.op('dve', lambda e: e.tensor_copy(tt[:n, :ncols], C.PS[b][:n, :ncols]),
                                 reads=[('ps', b)], writes=[('tst', ts_)])
                        if kind == 'ab':
                            r0 = tokmap(g, c0, n)
                            dd = T['Aab'][r0:r0 + n, 0:ncols]
                        else:
                            if c0 < 1024:
                                base = T['fkp'] if kind == 'k' else T['fvp']
                                r0 = g * 1024 + c0
                            else:
                                base = T['fks'] if kind == 'k' else T['fvs']
                                r0 = g * DS
                            dd = base[r0:r0 + n, coff:coff + ncols]
                        S.dma('act', dd, tt[:n, :ncols], reads=[('tst', ts_)])
                    gemm_T(S, C, slot, ncols, emitT)
    S.barrier()


IN_SPECS = [
    ("xp", [SEQ, D], F32), ("xs", [NS * DS, D], F32),
    ("ck", [NS, PAST, AW], F32), ("cv", [NS, PAST, AW], F32), ("clf", [NS, PAST, NH], F32),
    ("sg", [NS, NH, HD, HD], F32), ("sconv_t", [NS, 128, 48 * 3], F32),
    ("w_in", [D, PROJ], F32), ("w_out", [D, D], F32), ("w_q", [D, 2048], F32),
    ("subk", [16, 128, 128], F32), ("pu", [16384, D], F32), ("pv", [16384, D], F32),
    ("convw_t", [128, 48 * 4], F32), ("alog_b", [128, NH], F32), ("dtb_b", [128, NH], F32),
    ("normw_c", [128, 1], F32), ("ffb_c", [NH, 1], F32),
    ("ln1g_b", [128, D], F32), ("ln1b_b", [128, D], F32), ("ln2g_b", [128, D], F32), ("ln2b_b", [128, D], F32),
    ("ident_in", [128, 128], F32), ("cmat", [128, 8 * 128], F32), ("selc", [NH, NH * 128], F32),
]
OUT_SPECS = [
    ("yp", [SEQ, D]), ("ys", [NS * DS, D]),
    ("fkp", [SEQ, AW]), ("fvp", [SEQ, AW]), ("flp", [SEQ, NH]),
    ("sgp", [NH, HD, HD]), ("scp", [3, 3 * AW]),
    ("fks", [NS * DS, AW]), ("fvs", [NS * DS, AW]), ("fls", [NS * DS, NH]),
    ("sgs", [NS, NH, HD, HD]), ("scs", [NS, 3, 3 * AW]),
]
SCRATCH = [
    ("BIG1", [3 * AW * NT], F32), ("BIG2", [3 * AW * NT], F32), ("Aab", [NT, 32], F32),
    ("BfT", [NH, NT], F32), ("OT", [D, NT], BF16), ("PVB", [16384, 2 * D], BF16),
]


def build_program(phases="ABCDE"):
    nc = bass.Bass("TRN2", target_bir_lowering=False)
    T = {}
    for name, shape, dt in IN_SPECS:
        if name in ("pu", "pv") and 'E' not in phases:
            shape = [128, D]
        T[name] = nc.dram_tensor(name, shape, dt, kind="ExternalInput")
    for name, shape in OUT_SPECS:
        T[name] = nc.dram_tensor(name, shape, F32, kind="ExternalOutput")
    for name, shape, dt in SCRATCH:
        if name == "PVB" and 'E' not in phases:
            shape = [128, 2 * D]
        T[name] = nc.dram_tensor(name, shape, dt, kind="Internal")
    b1, b2 = T['BIG1'], T['BIG2']
    T['AqkvT'] = b1[0:3 * AW * NT].rearrange("(a b) -> a b", b=NT)
    T['MIX'] = b1[0:NT * D].rearrange("(a b) -> a b", b=D)
    T['SC'] = b1[NT * D:NT * D + NT * 2048].rearrange("(a b) -> a b", b=2048)
    T['AzT'] = b2[0:AW * NT].rearrange("(a b) -> a b", b=NT)
    T['BqT'] = b2[AW * NT:2 * AW * NT].rearrange("(a b) -> a b", b=NT)
    T['BkT'] = b2[2 * AW * NT:3 * AW * NT].rearrange("(a b) -> a b", b=NT)
    T['H'] = b2[0:NT * D].rearrange("(a b) -> a b", b=D)
    with ExitStack() as st:
        S = Sched(nc, st)
        C = Ctx()
        C.PS = [st.enter_context(nc.psum_tensor("ps%d" % i, [128, 512], F32)) for i in range(8)]
        C.ps_i = 0

        def ps_next():
            b = C.ps_i
            C.ps_i = (b + 1) % 8
            return b
        C.ps_next = ps_next
        C.ident = st.enter_context(nc.sbuf_tensor("ident_sb", [128, 128], F32))
        C.cmat = st.enter_context(nc.sbuf_tensor("cmat_sb", [128, 8 * 128], F32))
        S.dma('sp', C.ident[:], T['ident_in'][:, :], writes=['ident'])
        S.dma('sp', C.cmat[:], T['cmat'][:, :], writes=['cmat'])
        if 'A' in phases:
            phase_A(S, nc, C, T)
        if 'B' in phases:
            phase_B(S, nc, C, T)
        if 'C' in phases:
            phase_C(S, nc, C, T)
        if 'D' in phases:
            phase_D(S, nc, C, T)
        if 'E' in phases:
            phase_E(S, nc, C, T)
        S.run()
        print("instructions:", S.ninst, {e: S.cnt[e] for e in ENGS})
    return nc


def make_consts():
    c = np.zeros((128, 8, 128), np.float32)
    i = np.arange(128)
    c[:, 0, :] = (i[:, None] <= i[None, :])
    c[:, 1, :] = (i[:, None] > i[None, :])
    c[:, 2, :] = 1.0
    c[:, 3, :] = (i[:, None] < i[None, :])
    c[:, 4, :] = 128.0
    c[:, 5, :] = (i[:, None] >= i[None, :])
    return c.reshape(128, 8 * 128)


def kernel(x_prompt, x_sample, cache_fox_k, cache_fox_v, cache_fox_logf, state_gdn, state_gdn_conv,
           w_in, gdn_conv_w, gdn_a_log, gdn_dt_bias, gdn_norm_w, fox_f_bias, w_out, ln1_g, ln1_b,
           peer_w_q, peer_sub_keys, peer_u, peer_v, ln2_g, ln2_b, _phases="ABCDE", _cores=None):
    import time as _time
    _t0 = _time.time()
    f = lambda a: np.ascontiguousarray(np.asarray(a), dtype=np.float32)
    nc = build_program(_phases)
    cw = f(gdn_conv_w)[0]
    convw_t = np.ascontiguousarray(cw.reshape(4, 48, 128).transpose(2, 1, 0)).reshape(128, 48 * 4)
    bc = lambda v, n=128: np.ascontiguousarray(np.broadcast_to(f(v).reshape(1, -1), (n, f(v).size)))
    shared = {
        "w_in": f(w_in)[0], "w_out": f(w_out)[0], "w_q": f(peer_w_q)[0],
        "subk": f(peer_sub_keys)[0].reshape(16, 128, 128), "pu": f(peer_u)[0], "pv": f(peer_v)[0],
        "convw_t": convw_t, "alog_b": bc(gdn_a_log), "dtb_b": bc(gdn_dt_bias),
        "normw_c": f(gdn_norm_w).reshape(128, 1), "ffb_c": f(fox_f_bias).reshape(NH, 1),
        "ln1g_b": bc(ln1_g), "ln1b_b": bc(ln1_b), "ln2g_b": bc(ln2_g), "ln2b_b": bc(ln2_b),
        "ident_in": np.eye(128, dtype=np.float32), "cmat": make_consts(),
        "selc": np.ascontiguousarray(np.repeat(np.eye(NH, dtype=np.float32), 128, axis=1)),
    }
    xp = f(x_prompt)
    xs = f(x_sample)
    ck = f(cache_fox_k)[0]
    cv = f(cache_fox_v)[0]
    clf = f(cache_fox_logf)[0]
    sg = f(state_gdn)[0]
    sc = f(state_gdn_conv)[0]
    in_maps = []
    for c in range(8):
        m = dict(shared)
        m["xp"] = xp[c]
        m["xs"] = xs[2 * c:2 * c + 2].reshape(NS * DS, D)
        m["ck"] = ck[2 * c:2 * c + 2].reshape(NS, PAST, AW)
        m["cv"] = cv[2 * c:2 * c + 2].reshape(NS, PAST, AW)
        m["clf"] = clf[2 * c:2 * c + 2]
        m["sg"] = sg[2 * c:2 * c + 2]
        m["sconv_t"] = np.ascontiguousarray(sc[2 * c:2 * c + 2].reshape(NS, 3, 48, 128).transpose(0, 3, 2, 1)).reshape(NS, 128, 48 * 3)
        in_maps.append(m)
    if 'E' not in _phases:
        for m in in_maps:
            m["pu"] = m["pu"][:128]
            m["pv"] = m["pv"][:128]
    print("kernel: built+staged %.1fs" % (_time.time() - _t0), flush=True)
    if _cores is not None:
        import os as _os
        res = run_bass_kernel_spmd(nc, [in_maps[c] for c in _cores], core_ids=list(range(len(_cores))),
                                   trace=bool(_os.environ.get("KTRACE")))
        print("EXEC_NS", getattr(res, "exec_time_ns", None), flush=True)
        R = {c: res.results[i] for i, c in enumerate(_cores)}
        R = [R.get(c, R[_cores[0]]) for c in range(8)]
    else:
        res = run_bass_kernel_spmd(nc, in_maps, core_ids=list(range(8)))
        R = res.results
    print("kernel: ran %.1fs" % (_time.time() - _t0), flush=True)
    cat = lambda k: np.stack([np.asarray(R[c][k]) for c in range(8)], axis=0)
    yp = cat("yp")
    ys = cat("ys").reshape(16, DS, D)
    fkp = cat("fkp").reshape(1, 8, SEQ, NH, HD)
    fvp = cat("fvp").reshape(1, 8, SEQ, NH, HD)
    flp = cat("flp").reshape(1, 8, SEQ, NH)
    sgp = cat("sgp").reshape(1, 8, NH, HD, HD)
    scp = cat("scp").reshape(1, 8, 3, 3 * AW)
    fks = cat("fks").reshape(1, 16, DS, NH, HD)
    fvs = cat("fvs").reshape(1, 16, DS, NH, HD)
    fls = cat("fls").reshape(1, 16, DS, NH)
    sgs = cat("sgs").reshape(1, 16, NH, HD, HD)
    scs = cat("scs").reshape(1, 16, 3, 3 * AW)
    return (yp, ys, fkp, fvp, flp, sgp, scp, fks, fvs, fls, sgs, scs)


class KB:
    def __init__(self, c0, n, vb, qb_first, diag):
        self.c0, self.n, self.vb, self.qb_first, self.diag = c0, n, vb, qb_first, diag


def fox_core(S, C, P, kT, Vb, nck, qT, cqT, selh, nqb, QB, kblocks, out_cb, tag):
    import os as _os
    _kg = int(_os.environ.get("KG", "99"))
    for g0 in range(0, min(nqb, 4 * _kg), 4):
        qbs = list(range(g0, min(g0 + 4, nqb)))
        vis = {qb: [kb for kb in kblocks if kb.qb_first <= qb] for qb in qbs}
        for kb in [k for k in kblocks if k.qb_first <= qbs[-1]]:
            qlo = max(kb.qb_first, g0)
            q0, q1 = qlo * QB, (qbs[-1] + 1) * QB
            N = q1 - q0
            bS = 4 + (P.fx_i % 2)
            psl = P.fx_i % 2
            P.fx_i += 1
            S.op('pe', lambda e, kb=kb, bS=bS, q0=q0, q1=q1, N=N: e.matmul(
                C.PS[bS][:kb.n, :N], lhsT=kT[:, kb.c0:kb.c0 + kb.n], rhs=qT[:, q0:q1], start=True, stop=False),
                reads=tag['k'] + tag['q'], writes=[('ps', bS)])
            S.op('pe', lambda e, kb=kb, bS=bS, q0=q0, q1=q1, N=N: e.matmul(
                C.PS[bS][:kb.n, :N], lhsT=selh[:, :kb.n], rhs=cqT[:, q0:q1], start=False, stop=True),
                reads=tag['c'] + ['selc'], writes=[('ps', bS)])
            pt = P.PT[psl]
            S.op('act', lambda e, kb=kb, bS=bS, N=N, pt=pt: e.activation(
                pt[:kb.n, :N], C.PS[bS][:kb.n, :N], AF.Exp, bias=nck(kb), scale=1.0),
                reads=[('ps', bS)] + tag['nck'], writes=[('pt', psl)])
            if kb.diag is not None and qlo <= kb.diag <= qbs[-1]:
                off = (kb.diag - qlo) * QB
                S.op('dve', lambda e, kb=kb, off=off, pt=pt: e.tensor_tensor(
                    pt[:kb.n, off:off + QB], pt[:kb.n, off:off + QB], P.maskb[:kb.n, :QB], ALU.mult),
                    reads=[('pt', psl), 'maskb'], writes=[('pt', psl)])
            for qb in range(qlo, qbs[-1] + 1):
                bank = qb % 4
                first = kb is vis[qb][0]
                last = kb is vis[qb][-1]
                S.op('pe', lambda e, kb=kb, qb=qb, bank=bank, first=first, last=last, pt=pt, qlo=qlo: e.matmul(
                    C.PS[bank][:QB, :129], lhsT=pt[:kb.n, (qb - qlo) * QB:(qb - qlo + 1) * QB],
                    rhs=Vb[:kb.n, kb.vb, :], start=first, stop=last),
                    reads=[('pt', psl)] + tag['v'], writes=[('ps', bank)])
        for qb in qbs:
            out_cb(qb, qb % 4)


def fox_out(S, C, P, QB, ots, ots_tag):
    def cb(qb, bank):
        i = P.oc_i % 2
        P.oc_i += 1
        rec = P.rec[i]
        osb = P.osb[i]
        S.op('dve', lambda e: e.reciprocal(rec[:QB, :], C.PS[bank][:QB, 128:129]),
             reads=[('ps', bank)], writes=[('rec', i)])
        S.op('act', lambda e: e.activation(osb[:QB, :], C.PS[bank][:QB, 0:128], AF.Identity, scale=rec[:QB, 0:1]),
             reads=[('ps', bank), ('rec', i)], writes=[('osb', i)])
        bt = 6 + i
        S.op('pe', lambda e: e.transpose(C.PS[bt][:, :QB], osb[:QB, :], C.ident[:QB, :QB]),
             reads=[('osb', i), 'ident'], writes=[('ps', bt)])
        S.op('dve', lambda e: e.tensor_copy(ots[:, qb * QB:(qb + 1) * QB], C.PS[bt][:, :QB]),
             reads=[('ps', bt)], writes=[ots_tag])
    return cb


def phase_C(S, nc, C, T):
    with ExitStack() as st:
        P = Ctx()
        P.fx_i = 0
        P.oc_i = 0
        sb = lambda name, shape, dt=F32: st.enter_context(nc.sbuf_tensor(name, shape, dt))
        selc = sb("selc_sb", [NH, NH * 128])
        S.dma('sp', selc[:], T['selc'][:, :], writes=['selc'])
        P.maskb = sb("maskb", [128, 128], BF16)
        S.op('dve', lambda e: e.tensor_copy(P.maskb[:], C.cmat[:, 0:128]), reads=['cmat'], writes=['maskb'])
        P.PT = [sb("PT%d" % i, [128, 512], BF16) for i in range(2)]
        P.rec = [sb("rec%d" % i, [128, 1]) for i in range(2)]
        P.osb = [sb("osb%d" % i, [128, 128]) for i in range(2)]
        bfT = sb("bfT", [NH, NT])
        logfT = sb("logfT", [NH, NT])
        ffb = sb("ffb", [NH, 1])
        nffb = sb("nffb", [NH, 1])
        onesr = sb("onesr", [NH, SEQ])
        cTp = sb("cTp", [NH, SEQ])
        S.dma('sp', bfT[:], T['BfT'][:, :], writes=['bfT'])
        S.dma('sp', ffb[:], T['ffb_c'][:, :], writes=['ffb'])
        S.op('dve', lambda e: e.tensor_scalar(nffb[:], ffb[:], -1.0, None, ALU.mult), reads=['ffb'], writes=['nffb'])
        S.op('dve', lambda e: e.memset(onesr[:], 1.0), writes=['onesr'])
        S.op('act', lambda e: e.activation(logfT[:], bfT[:], AF.Exp, bias=nffb[:, 0:1], scale=-1.0),
             reads=['bfT', 'nffb'], writes=['logfT'])
        S.op('act', lambda e: e.activation(logfT[:], logfT[:], AF.Ln, bias=1.0, scale=1.0),
             reads=['logfT'], writes=['logfT'])
        S.op('dve', lambda e: e.tensor_scalar(logfT[:], logfT[:], -1.0, None, ALU.mult),
             reads=['logfT'], writes=['logfT'])
        S.op('dve', lambda e: e.tensor_tensor_scan(cTp[:], onesr[:], logfT[:, 0:SEQ], 0.0, ALU.mult, ALU.add),
             reads=['onesr', 'logfT'], writes=['cTp'])
        lf_tm = sb("lf_tm", [128, 16, NH])
        nck_p = sb("nck_p", [128, 16, NH])
        for blk in range(16):
            S.op('pe', lambda e, blk=blk: e.transpose(C.PS[6][:, blk * 16:(blk + 1) * 16],
                                                      logfT[:, blk * 128:(blk + 1) * 128], C.ident[:NH, :NH]),
                 reads=['logfT', 'ident'], writes=[('ps', 6)])
            S.op('pe', lambda e, blk=blk: e.transpose(C.PS[7][:, blk * 16:(blk + 1) * 16],
                                                      cTp[:, blk * 128:(blk + 1) * 128], C.ident[:NH, :NH]),
                 reads=['cTp', 'ident'], writes=[('ps', 7)])
        S.op('act', lambda e: e.activation(lf_tm[:].rearrange("p b h -> p (b h)"), C.PS[6][:, 0:256], AF.Copy),
             reads=[('ps', 6)], writes=['lf_tm'])
        S.op('dve', lambda e: e.tensor_scalar(nck_p[:].rearrange("p b h -> p (b h)"), C.PS[7][:, 0:256], -1.0, None,
                                              ALU.mult), reads=[('ps', 7)], writes=['nck_p'])
        S.dma('act', T['flp'][:, :].rearrange("(b p) h -> p b h", p=128), lf_tm[:], reads=['lf_tm'])
        lfs = sb("lfs", [64, NH])
        S.op('pe', lambda e: e.transpose(C.PS[6][:64, 0:NH], logfT[:, SEQ:NT], C.ident[:NH, :NH]),
             reads=['logfT', 'ident'], writes=[('ps', 6)])
        S.op('act', lambda e: e.activation(lfs[:], C.PS[6][:64, 0:NH], AF.Copy), reads=[('ps', 6)], writes=['lfs'])
        S.dma('act', T['fls'][:, :], lfs[:], reads=['lfs'])
        clf_tm = sb("clf_tm", [128, 8, NH])
        clfT = [sb("clfT%d" % s, [NH, PAST]) for s in range(NS)]
        ccT = [sb("ccT%d" % s, [NH, PAST]) for s in range(NS)]
        cnT = [sb("cnT%d" % s, [NH, DS]) for s in range(NS)]
        nck_c = [sb("nck_c%d" % s, [128, 8, NH]) for s in range(NS)]
        nck_n = [sb("nck_n%d" % s, [DS, NH]) for s in range(NS)]
        for s in range(NS):
            S.dma('sp', clf_tm[:], T['clf'][s].rearrange("(b p) h -> p b h", p=128), writes=['clf_tm'])
            for blk in range(8):
                bk = 6 + blk // 4
                S.op('pe', lambda e, blk=blk, bk=bk: e.transpose(
                    C.PS[bk][:NH, (blk % 4) * 128:(blk % 4 + 1) * 128], clf_tm[:, blk, :], C.ident[:, :]),
                    reads=['clf_tm', 'ident'], writes=[('ps', bk)])
            for hf in range(2):
                S.op('act', lambda e, hf=hf, s=s: e.activation(clfT[s][:, hf * 512:(hf + 1) * 512],
                                                              C.PS[6 + hf][:NH, :], AF.Copy),
                     reads=[('ps', 6 + hf)], writes=[('clfT', s)])
            S.op('dve', lambda e, s=s: e.tensor_tensor_scan(ccT[s][:], onesr[:, 0:PAST], clfT[s][:], 0.0,
                                                            ALU.mult, ALU.add),
                 reads=['onesr', ('clfT', s)], writes=[('ccT', s)])
            S.op('dve', lambda e, s=s: e.tensor_tensor_scan(cnT[s][:], onesr[:, 0:DS],
                                                            logfT[:, SEQ + s * DS:SEQ + (s + 1) * DS],
                                                            ccT[s][:, PAST - 1:PAST], ALU.mult, ALU.add),
                 reads=['onesr', 'logfT', ('ccT', s)], writes=[('cnT', s)])
            for blk in range(8):
                S.op('pe', lambda e, blk=blk, s=s: e.transpose(C.PS[7][:, blk * 16:(blk + 1) * 16],
                                                               ccT[s][:, blk * 128:(blk + 1) * 128],
                                                               C.ident[:NH, :NH]),
                     reads=[('ccT', s), 'ident'], writes=[('ps', 7)])
            S.op('dve', lambda e, s=s: e.tensor_scalar(nck_c[s][:].rearrange("p b h -> p (b h)"),
                                                       C.PS[7][:, 0:128], -1.0, None, ALU.mult),
                 reads=[('ps', 7)], writes=[('nck_c', s)])
            S.op('pe', lambda e, s=s: e.transpose(C.PS[6][:DS, 0:NH], cnT[s][:, :], C.ident[:NH, :NH]),
                 reads=[('cnT', s), 'ident'], writes=[('ps', 6)])
            S.op('dve', lambda e, s=s: e.tensor_scalar(nck_n[s][:], C.PS[6][:DS, 0:NH], -1.0, None, ALU.mult),
                 reads=[('ps', 6)], writes=[('nck_n', s)])

        import os as _os
        _stop = int(_os.environ.get("KSTOP", "99"))
        if _stop <= 1:
            S.barrier()
            return
        qf = [sb("qf%d" % i, [128, SEQ]) for i in range(2)]
        kf = [sb("kf%d" % i, [128, SEQ]) for i in range(2)]
        vf = [sb("vf%d" % i, [128, 16, 128]) for i in range(2)]
        qb_ = [sb("qb%d" % i, [128, SEQ], BF16) for i in range(2)]
        kb_ = [sb("kb%d" % i, [128, SEQ], BF16) for i in range(2)]
        Vb = [sb("Vb%d" % i, [128, 16, 129], BF16) for i in range(2)]
        ots = [sb("ots%d" % i, [128, SEQ], BF16) for i in range(2)]
        for i in range(2):
            S.op('pool', lambda e, i=i: e.memset(Vb[i][:, :, 128:129], 1.0), writes=[('Vb', i)])

        def load_head(h):
            i = h % 2
            S.dma('sp', qf[i][:], T['BqT'][h * 128:(h + 1) * 128, 0:SEQ], writes=[('qf', i)])
            S.dma('sp', kf[i][:], T['BkT'][h * 128:(h + 1) * 128, 0:SEQ], writes=[('kf', i)])
            S.dma('sp', vf[i][:], T['fvp'][:, h * 128:(h + 1) * 128].rearrange("(b p) d -> p b d", p=128),
                  writes=[('vf', i)])
            S.op('pool', lambda e: e.tensor_copy(qb_[i][:], qf[i][:]), reads=[('qf', i)], writes=[('qb', i)])
            S.op('pool', lambda e: e.tensor_copy(kb_[i][:], kf[i][:]), reads=[('kf', i)], writes=[('kb', i)])
            S.op('pool', lambda e: e.tensor_copy(Vb[i][:, :, 0:128], vf[i][:]), reads=[('vf', i)], writes=[('Vb', i)])

        kblocks_p = [KB(b * 128, 128, b, b, b) for b in range(16)]
        load_head(0)
        _kh = int(_os.environ.get("KH", "16"))
        for h in range(_kh):
            i = h % 2
            if h + 1 < _kh:
                load_head(h + 1)
            tag = {'k': [('kb', i)], 'q': [('qb', i)], 'c': ['cTp'], 'nck': ['nck_p'], 'v': [('Vb', i)]}
            fox_core(S, C, P, kb_[i], Vb[i], lambda kb, h=h: nck_p[:kb.n, kb.vb, h:h + 1], qb_[i], cTp,
                     selc[:, h * 128:(h + 1) * 128], 16, 128, kblocks_p,
                     fox_out(S, C, P, 128, ots[i], ('ots', i)), tag)
            S.dma('act', T['OT'][AW + h * 128:AW + (h + 1) * 128, 0:SEQ], ots[i][:], reads=[('ots', i)])

        if _stop <= 2:
            S.barrier()
            return
        ckf = [sb("ckf%d" % i, [128, 8, 128]) for i in range(2)]
        cvf = [sb("cvf%d" % i, [128, 8, 128]) for i in range(2)]
        knf = [sb("knf%d" % i, [128, DS]) for i in range(2)]
        qnf = [sb("qnf%d" % i, [128, DS]) for i in range(2)]
        vnf = [sb("vnf%d" % i, [DS, 128]) for i in range(2)]
        kTs = [sb("kTs%d" % i, [128, PAST + DS], BF16) for i in range(2)]
        qTs = [sb("qTs%d" % i, [128, DS], BF16) for i in range(2)]
        Vs = [sb("Vs%d" % i, [128, 9, 129], BF16) for i in range(2)]
        otss = [sb("otss%d" % i, [128, DS], BF16) for i in range(2)]
        for i in range(2):
            S.op('pool', lambda e, i=i: e.memset(Vs[i][:, :, 128:129], 1.0), writes=[('Vs', i)])
        kblocks_s = [KB(b * 128, 128, b, 0, None) for b in range(8)] + [KB(PAST, DS, 8, 0, 0)]
        it = 0
        for s in range(NS):
            for h in range(NH):
                i = it % 2
                it += 1
                tcol = SEQ + s * DS
                S.dma('sp', ckf[i][:], T['ck'][s][:, h * 128:(h + 1) * 128].rearrange("(b p) d -> p b d", p=128),
                      writes=[('ckf', i)])
                S.dma('sp', cvf[i][:], T['cv'][s][:, h * 128:(h + 1) * 128].rearrange("(b p) d -> p b d", p=128),
                      writes=[('cvf', i)])
                S.dma('sp', knf[i][:], T['BkT'][h * 128:(h + 1) * 128, tcol:tcol + DS], writes=[('knf', i)])
                S.dma('sp', qnf[i][:], T['BqT'][h * 128:(h + 1) * 128, tcol:tcol + DS], writes=[('qnf', i)])
                S.dma('sp', vnf[i][:], T['fvs'][s * DS:(s + 1) * DS, h * 128:(h + 1) * 128], writes=[('vnf', i)])
                for blk in range(8):
                    bt = 6 + blk % 2
                    S.op('pe', lambda e, blk=blk, bt=bt, i=i: e.transpose(C.PS[bt][:, 0:128], ckf[i][:, blk, :],
                                                                          C.ident[:, :]),
                         reads=[('ckf', i), 'ident'], writes=[('ps', bt)])
                    S.op('dve' if blk % 2 else 'act',
                         (lambda e, blk=blk, bt=bt, i=i: e.tensor_copy(kTs[i][:, blk * 128:(blk + 1) * 128],
                                                                       C.PS[bt][:, 0:128])) if blk % 2 else
                         (lambda e, blk=blk, bt=bt, i=i: e.activation(kTs[i][:, blk * 128:(blk + 1) * 128],
                                                                      C.PS[bt][:, 0:128], AF.Copy)),
                         reads=[('ps', bt)], writes=[('kTs', i)])
                S.op('pool', lambda e, i=i: e.tensor_copy(kTs[i][:, PAST:PAST + DS], knf[i][:]),
                     reads=[('knf', i)], writes=[('kTs', i)])
                S.op('pool', lambda e, i=i: e.tensor_copy(qTs[i][:], qnf[i][:]), reads=[('qnf', i)],
                     writes=[('qTs', i)])
                S.op('pool', lambda e, i=i: e.tensor_copy(Vs[i][:, 0:8, 0:128], cvf[i][:]), reads=[('cvf', i)],
                     writes=[('Vs', i)])
                S.op('pool', lambda e, i=i: e.tensor_copy(Vs[i][:DS, 8, 0:128], vnf[i][:]), reads=[('vnf', i)],
                     writes=[('Vs', i)])
                tag = {'k': [('kTs', i)], 'q': [('qTs', i)], 'c': [('cnT', s)],
                       'nck': [('nck_c', s), ('nck_n', s)], 'v': [('Vs', i)]}

                def ncks(kb, s=s, h=h):
                    if kb.vb < 8:
                        return nck_c[s][:kb.n, kb.vb, h:h + 1]
                    return nck_n[s][:kb.n, h:h + 1]
                fox_core(S, C, P, kTs[i], Vs[i], ncks, qTs[i], cnT[s], selc[:, h * 128:(h + 1) * 128], 1, DS,
                         kblocks_s, fox_out(S, C, P, DS, otss[i], ('otss', i)), tag)
                S.dma('act', T['OT'][AW + h * 128:AW + (h + 1) * 128, tcol:tcol + DS], otss[i][:],
                      reads=[('otss', i)])
    S.barrier()


def phase_B(S, nc, C, T):
    with ExitStack() as st:
        sb = lambda name, shape, dt=F32: st.enter_context(nc.sbuf_tensor("B_" + name, shape, dt))
        CM = 64
        U_ = C.cmat[:, 0:128]
        SL_ = C.cmat[:, 128:256]
        ONES = C.cmat[:, 256:384]
        SU_ = C.cmat[:, 384:512]
        C128 = C.cmat[:, 512:640]
        I_ = C.ident
        cw = sb("cw", [128, 48, 4])
        nA = sb("nA", [CM, NH])
        dtb = sb("dtb", [CM, NH])
        normw = sb("normw", [128, 1])
        S.dma('sp', cw[:].rearrange("p j r -> p (j r)"), T['convw_t'][:, :], writes=['cw'])
        S.dma('sp', nA[:], T['alog_b'][0:CM, :], writes=['nA'])
        S.dma('sp', dtb[:], T['dtb_b'][0:CM, :], writes=['dtb'])
        S.dma('sp', normw[:], T['normw_c'][:, :], writes=['normw'])
        S.op('act', lambda e: e.activation(nA[:], nA[:], AF.Exp), reads=['nA'], writes=['nA'])
        S.op('dve', lambda e: e.tensor_scalar(nA[:], nA[:], -1.0, None, ALU.mult), reads=['nA'], writes=['nA'])
        NCHM = 32
        ab = sb("ab", [CM, NCHM, 32])
        g_all = sb("g_all", [CM, NCHM, NH])
        lnb = sb("lnb_unused", [CM, 1])
        beta = sb("beta", [CM, NCHM, NH])
        ecum = sb("ecum", [CM, NCHM, NH])
        ekd = sb("ekd", [CM, NCHM, NH])
        bec = sb("bec", [CM, NCHM, NH])
        gtot = sb("gtot", [128, NCHM, NH])
        xin = [sb("xin%d" % i, [128, 48, 3 + CM]) for i in range(2)]
        zin = sb("zin", [128, NH, CM])
        zs = sb("zs", [128, NH, CM])
        ta = sb("ta", [128, 48, CM])
        tb = sb("tb", [128, 48, CM])
        ys = sb("ys", [128, 48, CM])
        qkn = sb("qkn", [128, 32, CM])
        Sst = sb("Sst", [128, NH, 128])
        Stmp2 = [sb("Stmp%d" % i, [128, 4, 128]) for i in range(2)]
        OTb = sb("OTb", [128, NH, 256], BF16)
        q4 = lambda name, w: [sb(name + str(i), [CM, 4, w]) for i in range(2)]
        rhsG = q4("rhsG", CM)
        rhsGT = q4("rhsGT", CM)
        rhsB = q4("rhsB", CM)
        rhsE = q4("rhsE", CM)
        ex1 = q4("ex1", CM)
        ex2 = q4("ex2", CM)
        decU = q4("decU", CM)
        NN = [q4("NNa", CM), q4("NNb", CM)]
        NT_ = [q4("NTa", CM), q4("NTb", CM)]
        TT = [q4("TTa", CM), q4("TTb", CM)]
        bv = q4("bv", 128)
        bek = q4("bek", 128)
        wv = sb("wv", [CM, NH, 128])
        kdec = sb("kdec", [CM, NH, 128])
        qkm = sb("qkm", [CM, NH, CM])
        wkT = sb("wkT", [128, NH, CM])
        qdT = sb("qdT", [128, NH, CM])
        u_ = q4("u_", 128)
        osq = q4("osq", 128)
        on_ = q4("on_", 128)
        ms = [sb("ms%d" % i, [CM, 4]) for i in range(2)]
        rstd = [sb("rstd%d" % i, [CM, 4]) for i in range(2)]
        PS = C.PS
        nb = C.ps_next
        V = lambda fn, r, w: S.op('dve', fn, reads=r, writes=w)
        A = lambda fn, r, w: S.op('act', fn, reads=r, writes=w)
        G = lambda fn, r, w: S.op('pool', fn, reads=r, writes=w)
        PE = lambda fn, r, w: S.op('pe', fn, reads=r, writes=w)

        def seq(tok0, Tlen, Cn, conv_hist, s0, s_out, conv_out, name):
            NCH = Tlen // Cn
            S.dma('sp', ab[:Cn, :NCH, :], T['Aab'][tok0:tok0 + Tlen, :].rearrange("(n c) x -> c n x", c=Cn),
                  writes=['ab'])
            gv = g_all[:Cn, :NCH, :]
            V(lambda e: e.tensor_tensor(gv, ab[:Cn, :NCH, 0:NH], dtb[:Cn, None, :].to_broadcast([Cn, NCH, NH]),
                                        ALU.add), ['ab', 'dtb'], ['g_all'])
            A(lambda e: e.activation(gv, gv, AF.Exp), ['g_all'], ['g_all'])
            A(lambda e: e.activation(gv, gv, AF.Ln, bias=1.0, scale=1.0), ['g_all'], ['g_all'])
            V(lambda e: e.tensor_tensor(gv, gv, nA[:Cn, None, :].to_broadcast([Cn, NCH, NH]), ALU.mult),
              ['g_all', 'nA'], ['g_all'])
            bvw = beta[:Cn, :NCH, :]
            A(lambda e: e.activation(bvw, ab[:Cn, :NCH, NH:2 * NH], AF.Sigmoid), ['ab'], ['beta'])
            g2 = g_all[:Cn, :NCH, :].rearrange("c n h -> c (n h)") if NCH * NH <= 512 else None
            ncol = NCH * NH
            b1 = nb()
            PE(lambda e: e.matmul(PS[b1][:Cn, :ncol], lhsT=U_[:Cn, :Cn], rhs=g_all[:Cn, :NCH, :], start=True, stop=True),
               ['g_all', 'cmat'], [('ps', b1)])
            A(lambda e: e.activation(ecum[:Cn, :NCH, :].rearrange("c n h -> c (n h)"), PS[b1][:Cn, :ncol], AF.Exp),
              [('ps', b1)], ['ecum'])
            b2 = nb()
            PE(lambda e: e.matmul(PS[b2][:Cn, :ncol], lhsT=SL_[:Cn, :Cn], rhs=g_all[:Cn, :NCH, :], start=True, stop=True),
               ['g_all', 'cmat'], [('ps', b2)])
            A(lambda e: e.activation(ekd[:Cn, :NCH, :].rearrange("c n h -> c (n h)"), PS[b2][:Cn, :ncol], AF.Exp),
              [('ps', b2)], ['ekd'])
            b3 = nb()
            PE(lambda e: e.matmul(PS[b3][:, :ncol], lhsT=ONES[:Cn, :], rhs=g_all[:Cn, :NCH, :], start=True, stop=True),
               ['g_all', 'cmat'], [('ps', b3)])
            A(lambda e: e.activation(gtot[:, :NCH, :].rearrange("c n h -> c (n h)"), PS[b3][:, :ncol], AF.Exp),
              [('ps', b3)], ['gtot'])
            V(lambda e: e.tensor_tensor(bec[:Cn, :NCH, :], beta[:Cn, :NCH, :], ecum[:Cn, :NCH, :], ALU.mult),
              ['beta', 'ecum'], ['bec'])
            if s0 is None:
                V(lambda e: e.memset(Sst[:], 0.0), [], [('Sst', q) for q in range(4)])
            else:
                S.dma('sp', Sst[:], s0.rearrange("h k v -> k h v"), writes=[('Sst', q) for q in range(4)])

            def load_x(n):
                xi = xin[n % 2]
                t0 = tok0 + n * Cn
                if n == 0:
                    if conv_hist is None:
                        V(lambda e: e.memset(xi[:, :, 0:3], 0.0), [], [('xin', n % 2)])
                    else:
                        S.dma('sp', xi[:, :, 0:3], conv_hist.rearrange("p (j r) -> p j r", r=3), writes=[('xin', n % 2)])
                    S.dma('sp', xi[:, :, 3:3 + Cn], T['AqkvT'][:, t0:t0 + Cn].rearrange("(j p) t -> p j t", p=128),
                          writes=[('xin', n % 2)])
                else:
                    S.dma('sp', xi[:, :, 0:3 + Cn],
                          T['AqkvT'][:, t0 - 3:t0 + Cn].rearrange("(j p) t -> p j t", p=128), writes=[('xin', n % 2)])

            load_x(0)

            def chunk(n):
                if n + 1 < NCH:
                    load_x(n + 1)
                xi = xin[n % 2]
                xt = ('xin', n % 2)
                t0 = tok0 + n * Cn
                S.dma('sp', zin[:, :, :Cn], T['AzT'][:, t0:t0 + Cn].rearrange("(j p) t -> p j t", p=128), writes=['zin'])
                if n == NCH - 1:
                    for r in range(3):
                        S.dma('act', conv_out[r:r + 1, :].rearrange("o (j p) -> p j o", p=128), xi[:, :, Cn + r:Cn + r + 1],
                              reads=[xt], allow_slow_non_contiguous=True)
                cwb = lambda r: cw[:, :, r:r + 1].to_broadcast([128, 48, Cn])
                V(lambda e: e.tensor_tensor(ta[:, :, :Cn], xi[:, :, 0:Cn], cwb(0), ALU.mult), [xt, 'cw'], ['ta'])
                G(lambda e: e.tensor_tensor(tb[:, :, :Cn], xi[:, :, 1:1 + Cn], cwb(1), ALU.mult), [xt, 'cw'], ['tb'])
                V(lambda e: e.tensor_tensor(ys[:, :, :Cn], xi[:, :, 2:2 + Cn], cwb(2), ALU.mult), [xt, 'cw'], ['ys'])
                G(lambda e: e.tensor_tensor(tb[:, :, :Cn], tb[:, :, :Cn], ta[:, :, :Cn], ALU.add), ['ta', 'tb'], ['tb'])
                V(lambda e: e.tensor_tensor(ta[:, :, :Cn], xi[:, :, 3:3 + Cn], cwb(3), ALU.mult), [xt, 'cw'], ['ta'])
                V(lambda e: e.tensor_tensor(ys[:, :, :Cn], ys[:, :, :Cn], ta[:, :, :Cn], ALU.add), ['ta', 'ys'], ['ys'])
                G(lambda e: e.tensor_tensor(tb[:, :, :Cn], tb[:, :, :Cn], ys[:, :, :Cn], ALU.add), ['tb', 'ys'], ['tb'])
                A(lambda e: e.activation(ys[:, :, :Cn], tb[:, :, :Cn], AF.Silu), ['tb'], ['ys'])
                A(lambda e: e.activation(zs[:, :, :Cn], zin[:, :, :Cn], AF.Silu), ['zin'], ['zs'])
                V(lambda e: e.tensor_tensor(ta[:, 0:32, :Cn], ys[:, 0:32, :Cn], ys[:, 0:32, :Cn], ALU.mult), ['ys'], ['ta'])
                hp = 512 // Cn
                for part in range(32 // hp):
                    j0 = part * hp
                    isq = j0 < 16
                    bb = nb()
                    PE(lambda e, bb=bb, j0=j0, isq=isq: e.matmul(
                        PS[bb][:, :hp * Cn], lhsT=(C128 if isq else ONES), rhs=ta[:, j0:j0 + hp, :Cn],
                        start=True, stop=True), ['ta', 'cmat'], [('ps', bb)])
                    A(lambda e, bb=bb, j0=j0, isq=isq: e.activation(
                        tb[:, j0:j0 + hp, :Cn].rearrange("p j c -> p (j c)") if Cn == CM else tb[:, j0:j0 + hp, :Cn],
                        PS[bb][:, :hp * Cn] if Cn == CM else PS[bb][:, :hp * Cn].rearrange("p (j c) -> p j c", c=Cn),
                        AF.Sqrt, bias=(128e-6 if isq else 1e-6), scale=1.0),
                      [('ps', bb)], ['tb'])
                V(lambda e: e.reciprocal(tb[:, 0:32, :Cn], tb[:, 0:32, :Cn]), ['tb'], ['tb'])
                V(lambda e: e.tensor_tensor(qkn[:, :, :Cn], ys[:, 0:32, :Cn], tb[:, 0:32, :Cn], ALU.mult), ['ys', 'tb'],
                  ['qkn'])
                def quad(qd):
                    h0 = qd * 4
                    p = qd % 2
                    gq = g_all[:Cn, n, h0:h0 + 4]
                    W = 4 * Cn
                    flat = lambda t: t[:Cn, :, :Cn]
                    V(lambda e, p=p, gq=gq: e.tensor_tensor(flat(rhsG[p]), U_[:Cn, None, :Cn].to_broadcast([Cn, 4, Cn]),
                                                            gq[:, :, None].to_broadcast([Cn, 4, Cn]), ALU.mult),
                      ['g_all', 'cmat'], [('rhsG', p)])
                    V(lambda e, p=p, gq=gq: e.tensor_tensor(flat(rhsGT[p]), SL_[:Cn, None, :Cn].to_broadcast([Cn, 4, Cn]),
                                                            gq[:, :, None].to_broadcast([Cn, 4, Cn]), ALU.mult),
                      ['g_all', 'cmat'], [('rhsGT', p)])
                    V(lambda e, p=p, h0=h0: e.tensor_tensor(flat(rhsB[p]), I_[:Cn, None, :Cn].to_broadcast([Cn, 4, Cn]),
                                                            beta[:Cn, n, h0:h0 + 4][:, :, None].to_broadcast([Cn, 4, Cn]),
                                                            ALU.mult), ['beta', 'ident'], [('rhsB', p)])
                    V(lambda e, p=p, h0=h0: e.tensor_tensor(flat(rhsE[p]), I_[:Cn, None, :Cn].to_broadcast([Cn, 4, Cn]),
                                                            ecum[:Cn, n, h0:h0 + 4][:, :, None].to_broadcast([Cn, 4, Cn]),
                                                            ALU.mult), ['ecum', 'ident'], [('rhsE', p)])
                    bDT, bD, bB, bE, bKQ = nb(), nb(), nb(), nb(), nb()
                    v3 = lambda b, P_=Cn: PS[b][:P_, :W].rearrange("p (h c) -> p h c", c=Cn)
                    PE(lambda e, p=p, b=bDT: e.matmul(PS[b][:Cn, :W], lhsT=SL_[:Cn, :Cn], rhs=flat(rhsG[p]), start=True, stop=True),
                       [('rhsG', p), 'cmat'], [('ps', bDT)])
                    PE(lambda e, p=p, b=bD: e.matmul(PS[b][:Cn, :W], lhsT=U_[:Cn, :Cn], rhs=flat(rhsGT[p]), start=True, stop=True),
                       [('rhsGT', p), 'cmat'], [('ps', bD)])
                    PE(lambda e, p=p, b=bB: e.matmul(PS[b][:Cn, :W], lhsT=ONES[:Cn, :Cn], rhs=flat(rhsB[p]), start=True, stop=True),
                       [('rhsB', p), 'cmat'], [('ps', bB)])
                    PE(lambda e, p=p, b=bE: e.matmul(PS[b][:, :W], lhsT=ONES[:Cn, :], rhs=flat(rhsE[p]), start=True, stop=True),
                       [('rhsE', p), 'cmat'], [('ps', bE)])
                    for hh in range(4):
                        h = h0 + hh
                        PE(lambda e, h=h, hh=hh, b=bKQ: e.matmul(PS[b][:Cn, hh * Cn:(hh + 1) * Cn], lhsT=qkn[:, 16 + h, :Cn],
                                                                 rhs=qkn[:, 16 + h, :Cn], start=True, stop=True),
                           ['qkn'], [('ps', bKQ)])
                    bQK = nb()
                    for hh in range(4):
                        h = h0 + hh
                        PE(lambda e, h=h, hh=hh, b=bQK: e.matmul(PS[b][:Cn, hh * Cn:(hh + 1) * Cn], lhsT=qkn[:, 16 + h, :Cn],
                                                                 rhs=qkn[:, h, :Cn], start=True, stop=True),
                           ['qkn'], [('ps', bQK)])
                    A(lambda e, p=p, b=bDT: e.activation(flat(ex1[p]), v3(b), AF.Exp), [('ps', bDT)], [('ex1', p)])
                    A(lambda e, p=p, b=bD: e.activation(flat(ex2[p]), v3(b), AF.Exp), [('ps', bD)], [('ex2', p)])
                    V(lambda e, p=p: e.tensor_tensor(flat(decU[p]), flat(ex1[p]), U_[:Cn, None, :Cn].to_broadcast([Cn, 4, Cn]),
                                                     ALU.mult), [('ex1', p), 'cmat'], [('decU', p)])
                    V(lambda e, p=p, b=bQK, h0=h0: e.tensor_tensor(qkm[:Cn, h0:h0 + 4, :Cn], v3(b), flat(decU[p]), ALU.mult),
                      [('ps', bQK), ('decU', p)], [('qkm', qd)])
                    V(lambda e, p=p: e.tensor_tensor(flat(ex1[p]), flat(ex1[p]), SU_[:Cn, None, :Cn].to_broadcast([Cn, 4, Cn]),
                                                     ALU.mult), [('ex1', p), 'cmat'], [('ex1', p)])
                    V(lambda e, p=p, b=bKQ: e.tensor_tensor(flat(ex1[p]), v3(b), flat(ex1[p]), ALU.mult),
                      [('ps', bKQ), ('ex1', p)], [('ex1', p)])
                    V(lambda e, p=p, b=bB: e.scalar_tensor_tensor(
                        out=NT_[0][p][:Cn, :, :Cn].rearrange("p h c -> p (h c)") if Cn == CM else NT_[0][p][:Cn, :, :Cn],
                        in0=ex1[p][:Cn, :, :Cn].rearrange("p h c -> p (h c)") if Cn == CM else ex1[p][:Cn, :, :Cn],
                        scalar=-1.0,
                        in1=PS[b][:Cn, :W] if Cn == CM else v3(b), op0=ALU.mult, op1=ALU.mult),
                      [('ps', bB), ('ex1', p)], [('NT0', p)])
                    V(lambda e, p=p: e.tensor_tensor(flat(ex2[p]), flat(ex2[p]), SL_[:Cn, None, :Cn].to_broadcast([Cn, 4, Cn]),
                                                     ALU.mult), [('ex2', p), 'cmat'], [('ex2', p)])
                    V(lambda e, p=p, b=bKQ: e.tensor_tensor(flat(ex2[p]), v3(b), flat(ex2[p]), ALU.mult),
                      [('ps', bKQ), ('ex2', p)], [('ex2', p)])
                    V(lambda e, p=p, h0=h0: e.tensor_tensor(flat(ex2[p]), flat(ex2[p]),
                                                            beta[:Cn, n, h0:h0 + 4][:, :, None].to_broadcast([Cn, 4, Cn]),
                                                            ALU.mult), [('ex2', p), 'beta'], [('ex2', p)])
                    V(lambda e, p=p: e.tensor_scalar(flat(NN[0][p]), flat(ex2[p]), -1.0, None, ALU.mult),
                      [('ex2', p)], [('NN0', p)])
                    V(lambda e, b=bE, h0=h0: e.tensor_tensor(qdT[:, h0:h0 + 4, :Cn], qkn[:, h0:h0 + 4, :Cn],
                                                             PS[b][:, :W].rearrange("p (h c) -> p h c", c=Cn), ALU.mult),
                      [('ps', bE), 'qkn'], [('qdT', qd)])
                    V(lambda e, p=p: e.tensor_tensor(flat(TT[0][p]), flat(NT_[0][p]), I_[:Cn, None, :Cn].to_broadcast([Cn, 4, Cn]),
                                                     ALU.add), [('NT0', p), 'ident'], [('TT0', p)])
                    yield
                    nlev = 5 if Cn == 64 else 4
                    for lv in range(nlev):
                        yield
                        a, bq = lv % 2, (lv + 1) % 2
                        last = lv == nlev - 1
                        bN, bNT, bT = nb(), (None if last else nb()), nb()
                        for hh in range(4):
                            PE(lambda e, hh=hh, a=a, p=p, b=bN: e.matmul(PS[b][:Cn, hh * Cn:(hh + 1) * Cn], lhsT=NT_[a][p][:Cn, hh, :Cn],
                                                                         rhs=NN[a][p][:Cn, hh, :Cn], start=True, stop=True),
                               [('NT%d' % a, p), ('NN%d' % a, p)], [('ps', bN)])
                        if not last:
                            for hh in range(4):
                                PE(lambda e, hh=hh, a=a, p=p, b=bNT: e.matmul(PS[b][:Cn, hh * Cn:(hh + 1) * Cn], lhsT=NN[a][p][:Cn, hh, :Cn],
                                                                              rhs=NT_[a][p][:Cn, hh, :Cn], start=True, stop=True),
                                   [('NT%d' % a, p), ('NN%d' % a, p)], [('ps', bNT)])
                        yield
                        A(lambda e, bq=bq, p=p, b=bN: e.activation(flat(NN[bq][p]), v3(b), AF.Copy), [('ps', bN)], [('NN%d' % bq, p)])
                        if not last:
                            V(lambda e, bq=bq, p=p, b=bNT: e.tensor_copy(flat(NT_[bq][p]), v3(b)), [('ps', bNT)], [('NT%d' % bq, p)])
                        for hh in range(4):
                            PE(lambda e, hh=hh, a=a, bq=bq, p=p, b=bT: e.matmul(PS[b][:Cn, hh * Cn:(hh + 1) * Cn], lhsT=NN[bq][p][:Cn, hh, :Cn],
                                                                                rhs=TT[a][p][:Cn, hh, :Cn], start=True, stop=True),
                               [('NN%d' % bq, p), ('TT%d' % a, p)], [('ps', bT)])
                        yield
                        V(lambda e, a=a, bq=bq, p=p, b=bT: e.tensor_tensor(flat(TT[bq][p]), flat(TT[a][p]), v3(b), ALU.add),
                          [('ps', bT), ('TT%d' % a, p)], [('TT%d' % bq, p)])
                    yield
                    tf = nlev % 2
                    TTf = TT[tf][p]
                    ttag = ('TT%d' % tf, p)
                    bK, bV = nb(), nb()
                    for hh in range(4):
                        h = h0 + hh
                        PE(lambda e, h=h, hh=hh, b=bK: e.transpose(PS[b][:Cn, hh * 128:(hh + 1) * 128], qkn[:, 16 + h, :Cn], I_[:, :]),
                           ['qkn', 'ident'], [('ps', bK)])
                        PE(lambda e, h=h, hh=hh, b=bV: e.transpose(PS[b][:Cn, hh * 128:(hh + 1) * 128], ys[:, 32 + h, :Cn], I_[:, :]),
                           ['ys', 'ident'], [('ps', bV)])
                    k3 = lambda b: PS[b][:Cn, :512].rearrange("p (h d) -> p h d", d=128)
                    bc3 = lambda t, h0=h0: t[:Cn, n, h0:h0 + 4][:, :, None].to_broadcast([Cn, 4, 128])
                    V(lambda e, p=p, b=bK: e.tensor_tensor(bek[p][:Cn, :, :], k3(b), bc3(bec), ALU.mult), [('ps', bK), 'bec'],
                      [('bek', p)])
                    V(lambda e, b=bK, h0=h0: e.tensor_tensor(kdec[:Cn, h0:h0 + 4, :], k3(b), bc3(ekd), ALU.mult),
                      [('ps', bK), 'ekd'], [('kdec', qd)])
                    V(lambda e, p=p, b=bV: e.tensor_tensor(bv[p][:Cn, :, :], k3(b), bc3(beta), ALU.mult), [('ps', bV), 'beta'],
                      [('bv', p)])
                    bWV, bWK = nb(), nb()
                    for hh in range(4):
                        PE(lambda e, hh=hh, p=p, b=bWV, TTf=TTf: e.matmul(PS[b][:Cn, hh * 128:(hh + 1) * 128], lhsT=TTf[:Cn, hh, :Cn],
                                                                          rhs=bv[p][:Cn, hh, :], start=True, stop=True),
                           [ttag, ('bv', p)], [('ps', bWV)])
                        PE(lambda e, hh=hh, p=p, b=bWK, TTf=TTf: e.matmul(PS[b][:, hh * Cn:(hh + 1) * Cn], lhsT=bek[p][:Cn, hh, :],
                                                                          rhs=TTf[:Cn, hh, :Cn], start=True, stop=True),
                           [ttag, ('bek', p)], [('ps', bWK)])
                    A(lambda e, b=bWV, h0=h0: e.activation(wv[:Cn, h0:h0 + 4, :], k3(b), AF.Copy), [('ps', bWV)], [('wv', qd)])
                    A(lambda e, b=bWK, h0=h0: e.activation(wkT[:, h0:h0 + 4, :Cn], PS[b][:, :W].rearrange("p (h c) -> p h c", c=Cn),
                                                           AF.Copy), [('ps', bWK)], [('wkT', qd)])
                    bU = nb()
                    for hh in range(4):
                        h = h0 + hh
                        PE(lambda e, h=h, hh=hh, b=bU: e.matmul(PS[b][:Cn, hh * 128:(hh + 1) * 128], lhsT=wkT[:, h, :Cn], rhs=Sst[:, h, :],
                                                                start=True, stop=True), [('wkT', qd), ('Sst', qd)], [('ps', bU)])
                    V(lambda e, p=p, b=bU, h0=h0: e.tensor_tensor(u_[p][:Cn, :, :], wv[:Cn, h0:h0 + 4, :], k3(b), ALU.subtract),
                      [('ps', bU), ('wv', qd)], [('u', p)])
                    bO = nb()
                    for hh in range(4):
                        h = h0 + hh
                        PE(lambda e, h=h, hh=hh, b=bO: e.matmul(PS[b][:Cn, hh * 128:(hh + 1) * 128], lhsT=qdT[:, h, :Cn], rhs=Sst[:, h, :],
                                                                start=True, stop=False), [('qdT', qd), ('Sst', qd)], [('ps', bO)])
                        PE(lambda e, h=h, hh=hh, p=p, b=bO: e.matmul(PS[b][:Cn, hh * 128:(hh + 1) * 128], lhsT=qkm[:Cn, h, :Cn],
                                                                     rhs=u_[p][:Cn, hh, :], start=False, stop=True),
                           [('qkm', qd), ('u', p)], [('ps', bO)])
                    bS_ = nb()
                    for hh in range(4):
                        h = h0 + hh
                        PE(lambda e, h=h, hh=hh, p=p, b=bS_: e.matmul(PS[b][:, hh * 128:(hh + 1) * 128], lhsT=kdec[:Cn, h, :],
                                                                      rhs=u_[p][:Cn, hh, :], start=True, stop=True),
                           [('kdec', qd), ('u', p)], [('ps', bS_)])
                    A(lambda e, p=p, b=bO: e.activation(osq[p][:Cn, :, :], k3(b), AF.Square), [('ps', bO)], [('osq', p)])
                    V(lambda e, p=p: e.tensor_reduce(ms[p][:Cn, :], osq[p][:Cn, :, :], AX.X, ALU.add), [('osq', p)], [('ms', p)])
                    A(lambda e, p=p: e.activation(rstd[p][:Cn, :], ms[p][:Cn, :], AF.Sqrt, bias=1e-6, scale=1.0 / 128), [('ms', p)],
                      [('rstd', p)])
                    V(lambda e, p=p: e.reciprocal(rstd[p][:Cn, :], rstd[p][:Cn, :]), [('rstd', p)], [('rstd', p)])
                    V(lambda e, p=p, b=bO: e.tensor_tensor(on_[p][:Cn, :, :], k3(b),
                                                           rstd[p][:Cn, :][:, :, None].to_broadcast([Cn, 4, 128]), ALU.mult),
                      [('ps', bO), ('rstd', p)], [('on', p)])
                    bOT = nb()
                    for hh in range(4):
                        PE(lambda e, hh=hh, p=p, b=bOT: e.transpose(PS[b][:, hh * Cn:(hh + 1) * Cn], on_[p][:Cn, hh, :], I_[:Cn, :Cn]),
                           [('on', p), 'ident'], [('ps', bOT)])
                    oc = (n % (256 // Cn)) * Cn
                    V(lambda e, b=bOT, h0=h0, oc=oc: e.scalar_tensor_tensor(
                        out=OTb[:, h0:h0 + 4, oc:oc + Cn], in0=PS[b][:, :W].rearrange("p (h c) -> p h c", c=Cn),
                        scalar=normw[:, 0:1], in1=zs[:, h0:h0 + 4, :Cn], op0=ALU.mult, op1=ALU.mult),
                      [('ps', bOT), 'zs', 'normw'], [('OTb', qd)])
                    V(lambda e, h0=h0: e.tensor_tensor(Stmp2[p][:, :, :], Sst[:, h0:h0 + 4, :],
                                                       gtot[:, n, h0:h0 + 4][:, :, None].to_broadcast([128, 4, 128]), ALU.mult),
                      [('Sst', qd), 'gtot'], [('Stmp', qd)])
                    V(lambda e, h0=h0, b=bS_: e.tensor_tensor(Sst[:, h0:h0 + 4, :], Stmp2[p][:, :, :],
                                                              PS[b][:, :512].rearrange("p (h d) -> p h d", d=128), ALU.add),
                      [('ps', bS_), ('Stmp', qd)], [('Sst', qd)])
                for pair_ in ((0, 1), (2, 3)):
                    alive = [quad(q_) for q_ in pair_]
                    while alive:
                        for g_ in list(alive):
                            try:
                                next(g_)
                            except StopIteration:
                                alive.remove(g_)
                per = 256 // Cn
                if (n + 1) % per == 0 or n == NCH - 1:
                    nfl = ((n % per) + 1) * Cn
                    tf0 = tok0 + (n // per) * 256
                    S.dma('act', T['OT'][0:AW, tf0:tf0 + nfl].rearrange("(h p) t -> p h t", p=128), OTb[:, :, 0:nfl],
                          reads=[('OTb', q_) for q_ in range(4)])
            for n in range(NCH):
                chunk(n)
            S.dma('act', s_out.rearrange("h k v -> k h v"), Sst[:], reads=[('Sst', q) for q in range(4)])

        seq(0, SEQ, 64, None, None, T['sgp'][:, :, :], T['scp'][:, :], "p")
        for s in range(NS):
            seq(SEQ + s * DS, DS, DS, T['sconv_t'][s], T['sg'][s], T['sgs'][s], T['scs'][s], "s%d" % s)
    S.barrier()


def ln_rows(S, nc, L, n, vt, vtag, gB, bB, out_ap_sb, out_tag):
    i = L.i % 2
    L.i += 1
    sm, ssq, rs = L.sm[i], L.ssq[i], L.rs[i]
    S.op('act', lambda e: e.activation(L.junk[:n, :], vt[:n, :], AF.Identity, accum_out=sm[:n, :]),
         reads=[vtag], writes=[('lnsm', i), 'lnjunk'])
    S.op('dve', lambda e: e.tensor_scalar(sm[:n, :], sm[:n, :], 1.0 / D, None, ALU.mult), reads=[('lnsm', i)],
         writes=[('lnsm', i)])
    S.op('dve', lambda e: e.tensor_scalar(vt[:n, :], vt[:n, :], sm[:n, 0:1], None, ALU.subtract),
         reads=[vtag, ('lnsm', i)], writes=[vtag])
    S.op('act', lambda e: e.activation(L.junk[:n, :], vt[:n, :], AF.Square, accum_out=ssq[:n, :]),
         reads=[vtag], writes=[('lnssq', i), 'lnjunk'])
    S.op('act', lambda e: e.activation(rs[:n, :], ssq[:n, :], AF.Sqrt, bias=L.eps[:n, 0:1], scale=1.0 / D),
         reads=[('lnssq', i), 'lneps'], writes=[('lnrs', i)])
    S.op('dve', lambda e: e.reciprocal(rs[:n, :], rs[:n, :]), reads=[('lnrs', i)], writes=[('lnrs', i)])
    S.op('dve', lambda e: e.scalar_tensor_tensor(out=vt[:n, :], in0=vt[:n, :], scalar=rs[:n, 0:1], in1=gB[:n, :],
                                                 op0=ALU.mult, op1=ALU.mult),
         reads=[vtag, ('lnrs', i), 'lng'], writes=[vtag])
    S.op('dve', lambda e: e.tensor_tensor(out_ap_sb[:n, :], vt[:n, :], bB[:n, :], ALU.add),
         reads=[vtag, 'lnb'], writes=[out_tag])


def ln_setup(S, nc, st, T, gname, bname, pref):
    L = Ctx()
    sb = lambda name, shape, dt=F32: st.enter_context(nc.sbuf_tensor(pref + name, shape, dt))
    L.i = 0
    L.sm = [sb("sm%d" % i, [128, 1]) for i in range(2)]
    L.ssq = [sb("ssq%d" % i, [128, 1]) for i in range(2)]
    L.rs = [sb("rs%d" % i, [128, 1]) for i in range(2)]
    L.eps = sb("eps", [128, 1])
    L.junk = sb("junk", [128, D], BF16)
    L.g = sb("g", [128, D])
    L.b = sb("b", [128, D])
    S.op('dve', lambda e: e.memset(L.eps[:], LN_EPS), writes=['lneps'])
    S.dma('sp', L.g[:], T[gname][:, :], writes=['lng'])
    S.dma('sp', L.b[:], T[bname][:, :], writes=['lnb'])
    return L


def row_tiles():
    return [(i * 128, 128) for i in range(16)] + [(SEQ, NS * DS)]


def x_rows(T, r0, n):
    return T['xp'][r0:r0 + n, :] if r0 < SEQ else T['xs'][r0 - SEQ:r0 - SEQ + n, :]


def y_rows(T, r0, n):
    return T['yp'][r0:r0 + n, :] if r0 < SEQ else T['ys'][r0 - SEQ:r0 - SEQ + n, :]


def phase_D(S, nc, C, T):
    import os as _os
    _kd = int(_os.environ.get("KD", "3"))
    with ExitStack() as st:
        if not (_kd & 1):
            st.close()
            return phase_D2(S, nc, C, T, _kd)
        C.XT = st.enter_context(nc.sbuf_tensor("D_XT", [128, KC, GT], BF16))
        C.wst = [st.enter_context(nc.sbuf_tensor("D_wst%d" % i, [128, KC, 128], F32)) for i in range(2)]
        C.wbf = [st.enter_context(nc.sbuf_tensor("D_wbf%d" % i, [128, KC, 512], BF16)) for i in range(2)]
        mst = [st.enter_context(nc.sbuf_tensor("D_mst%d" % i, [128, 512], F32)) for i in range(3)]
        C.wst_i = 0
        cnt = {'m': 0}
        for g in range(2):
            S.dma('sp', C.XT[:, :, 0:1024], T['OT'][:, g * 1024:(g + 1) * 1024].rearrange("(kc p) t -> p kc t", p=128),
                  writes=[('XT', i) for i in range(8)])
            S.dma('sp', C.XT[:, :, 1024:GT], T['OT'][:, SEQ + g * DS:SEQ + (g + 1) * DS].rearrange("(kc p) t -> p kc t", p=128),
                  writes=[('XT', 8)])
            load_w(S, C, T['w_out'], 0, 512, 0)
            for cg in range(8):
                if cg + 1 < 8:
                    load_w(S, C, T['w_out'], (cg + 1) * 512, 512, (cg + 1) % 2)

                def emit(b, c0, n, cg=cg, g=g):
                    mi = cnt['m'] % 3
                    cnt['m'] += 1
                    mt = mst[mi]
                    if cnt['m'] % 2:
                        S.op('act', lambda e: e.activation(mt[:n, :], C.PS[b][:n, :], AF.Copy), reads=[('ps', b)],
                             writes=[('mst', mi)])
                    else:
                        S.op('dve', lambda e: e.tensor_copy(mt[:n, :], C.PS[b][:n, :]), reads=[('ps', b)],
                             writes=[('mst', mi)])
                    r0 = tokmap(g, c0, n)
                    S.dma('act', T['MIX'][r0:r0 + n, cg * 512:(cg + 1) * 512], mt[:n, :], reads=[('mst', mi)])
                gemm_T(S, C, cg % 2, 512, emit)
    S.barrier()
    phase_D2(S, nc, C, T, _kd)


def phase_D2(S, nc, C, T, _kd):
    if not (_kd & 2):
        return
    with ExitStack() as st:
        L = ln_setup(S, nc, st, T, 'ln1g_b', 'ln1b_b', "D_ln")
        xt_ = [st.enter_context(nc.sbuf_tensor("D_x%d" % i, [128, D], F32)) for i in range(2)]
        mx_ = [st.enter_context(nc.sbuf_tensor("D_m%d" % i, [128, D], F32)) for i in range(2)]

        def tile(ti, r0, n):
            i = ti % 2
            S.dma('sp', xt_[i][:n, :], x_rows(T, r0, n), writes=[('dx', i)])
            S.dma('sp', mx_[i][:n, :], T['MIX'][r0:r0 + n, :], writes=[('dm', i)])
            S.op('dve', lambda e: e.scalar_tensor_tensor(out=xt_[i][:n, :], in0=xt_[i][:n, :], scalar=float(ALPHA),
                                                         in1=mx_[i][:n, :], op0=ALU.mult, op1=ALU.add),
                 reads=[('dx', i), ('dm', i)], writes=[('dx', i)])
            ln_rows(S, nc, L, n, xt_[i], ('dx', i), L.g, L.b, mx_[i], ('dm', i))
            S.dma('act', T['H'][r0:r0 + n, :], mx_[i][:n, :], reads=[('dm', i)])
        for ti, (r0, n) in enumerate(row_tiles()):
            tile(ti, r0, n)
    S.barrier()


def phase_E(S, nc, C, T):
    NEG = -1.0e30
    with ExitStack() as st:
        sb = lambda name, shape, dt=F32: st.enter_context(nc.sbuf_tensor("E1_" + name, shape, dt))
        C.XT = sb("XT", [128, KC, GT], BF16)
        C.xrow = [sb("xrow%d" % i, [128, D]) for i in range(2)]
        C.wst = [sb("wst%d" % i, [128, KC, 128]) for i in range(2)]
        C.wbf = [sb("wbf%d" % i, [128, KC, 128], BF16) for i in range(2)]
        C.wst_i = 0
        skn = sb("skn", [128, 16, 128])
        KT = sb("KT", [128, 16, 128])
        qst = [sb("qst%d" % i, [128, GT]) for i in range(2)]
        scs_ = [sb("scs%d" % i, [128, 128]) for i in range(3)]
        S.dma('sp', skn[:], T['subk'][:, :, :].rearrange("j n c -> n j c"), writes=['skn'])
        for j in range(16):
            b = C.ps_next()
            S.op('pe', lambda e, j=j, b=b: e.transpose(C.PS[b][:, 0:128], skn[:, j, :], C.ident[:, :]),
                 reads=['skn', 'ident'], writes=[('ps', b)])
            S.op('dve', lambda e, j=j, b=b: e.tensor_copy(KT[:, j, :], C.PS[b][:, 0:128]), reads=[('ps', b)],
                 writes=['KT'])
        cnt = {'s': 0, 'e': 0}
        for g in range(2):
            rows = [(T['H'][g * 1024 + i * 128:g * 1024 + (i + 1) * 128, :], 128, i * 128) for i in range(8)]
            rows.append((T['H'][SEQ + g * DS:SEQ + (g + 1) * DS, :], DS, 1024))
            build_xT(S, C, g, rows)
            load_w(S, C, T['w_q'], 0, 128, 0)
            for j in range(16):
                if j + 1 < 16:
                    load_w(S, C, T['w_q'], (j + 1) * 128, 128, (j + 1) % 2)
                qs = qst[j % 2]

                def emitF(b, t0, n, qs=qs, j=j):
                    cnt['e'] += 1
                    if cnt['e'] % 2:
                        S.op('act', lambda e: e.activation(qs[:, t0:t0 + n], C.PS[b][:, :n], AF.Copy),
                             reads=[('ps', b)], writes=[('qst', j % 2)])
                    else:
                        S.op('dve', lambda e: e.tensor_copy(qs[:, t0:t0 + n], C.PS[b][:, :n]),
                             reads=[('ps', b)], writes=[('qst', j % 2)])
                gemm_F(S, C, j % 2, 128, emitF)

                def score(ti, j=j, qs=qs, g=g):
                    c0 = ti * 128
                    n = 128 if ti < 8 else DS
                    b = C.ps_next()
                    S.op('pe', lambda e: e.matmul(C.PS[b][:n, 0:128], lhsT=qs[:, c0:c0 + n], rhs=KT[:, j, :],
                                                  start=True, stop=True),
                         reads=[('qst', j % 2), 'KT'], writes=[('ps', b)])
                    si = cnt['s'] % 3
                    cnt['s'] += 1
                    sc_t = scs_[si]
                    S.op('dve' if si % 2 else 'act',
                         (lambda e: e.tensor_copy(sc_t[:n, :], C.PS[b][:n, 0:128])) if si % 2 else
                         (lambda e: e.activation(sc_t[:n, :], C.PS[b][:n, 0:128], AF.Copy)),
                         reads=[('ps', b)], writes=[('scs', si)])
                    r0 = tokmap(g, c0, n)
                    S.dma('act', T['SC'][r0:r0 + n, j * 128:(j + 1) * 128], sc_t[:n, :], reads=[('scs', si)])
                for ti in range(9):
                    score(ti)
    S.barrier()
    with ExitStack() as st:
        sb = lambda name, shape, dt=F32: st.enter_context(nc.sbuf_tensor("V0_" + name, shape, dt))
        finu = [sb("finu%d" % i, [128, D]) for i in range(3)]
        finv = [sb("finv%d" % i, [128, D]) for i in range(3)]
        fout = [sb("fout%d" % i, [128, 2 * D], BF16) for i in range(3)]
        NR = T['pv'].shape[0] // 128

        def cv_load(r):
            S.dma('sp', finu[r % 3][:], T['pu'][r * 128:(r + 1) * 128, :], writes=[('cvu', r % 3)])
            S.dma('sp', finv[r % 3][:], T['pv'][r * 128:(r + 1) * 128, :], writes=[('cvv', r % 3)])

        def cv_rest(r):
            i = r % 3
            S.op('act', lambda e: e.activation(fout[i][:, 0:D], finu[i][:], AF.Copy), reads=[('cvu', i)], writes=[('cvo', i)])
            S.op('dve', lambda e: e.tensor_copy(fout[i][:, D:2 * D], finv[i][:]), reads=[('cvv', i)], writes=[('cvo2', i)])
            S.dma('act', T['PVB'][r * 128:(r + 1) * 128, :], fout[i][:], reads=[('cvo', i), ('cvo2', i)])
        cv_load(0)
        cv_load(1)
        for r in range(NR):
            if r + 2 < NR:
                cv_load(r + 2)
            cv_rest(r)
    S.barrier()
    with ExitStack() as st:
        sb = lambda name, shape, dt=F32: st.enter_context(nc.sbuf_tensor("E2_" + name, shape, dt))
        L = ln_setup(S, nc, st, T, 'ln2g_b', 'ln2b_b', "E_ln")
        sc = sb("sc", [128, 16, 128])
        sc2 = sb("sc2", [128, 16, 128])
        stop_ = sb("stop", [128, 16, 16])
        itop = sb("itop", [128, 16, 16], U32)
        itf = sb("itf", [128, 16, 16])
        cand = sb("cand", [128, 8, 256])
        cand2 = sc2[:, :, :].rearrange("p (h c) k -> p h (c k)", c=2)
        cidx = sb("cidx", [128, 8, 256])
        tops = sb("tops", [128, 8, 16])
        eidf = sb("eidf", [128, 128])
        eidi = [sb("eidi%d" % i, [128, 128], I32) for i in range(2)]
        gate = sb("gate", [128, 8, 16])
        zs_ = sb("zsum", [128, 8])
        pre = sb("pre", [128, 128])
        actv = [sb("actv%d" % i, [128, 128]) for i in range(2)]
        ht = [sb("ht%d" % i, [128, D]) for i in range(2)]
        NGB = 4
        cbuf = [sb("cb%d" % i, [128, 2 * D], BF16) for i in range(NGB)]
        lnout = sb("lnout", [128, D])
        dg = [sb("dg%d" % i, [128, 128], BF16) for i in range(NGB)]
        identb = sb("identb", [128, 128], BF16)
        sj = [sb("sj%d" % i, [128, 256]) for i in range(2)]
        V = lambda fn, r, w: S.op('dve', fn, reads=r, writes=w)
        A = lambda fn, r, w: S.op('act', fn, reads=r, writes=w)
        G = lambda fn, r, w: S.op('pool', fn, reads=r, writes=w)
        V(lambda e: e.tensor_copy(identb[:], C.ident[:]), ['ident'], ['identb'])

        def e2(k, r0, n):
            par = k % 2
            S.dma('sp', sc[:n, :, :], T['SC'][r0:r0 + n, :].rearrange("t (j k) -> t j k", k=128), writes=['sc'])
            S.dma('sp', ht[par][:n, :], T['H'][r0:r0 + n, :], writes=[('ht', par)])
            for j in range(16):
                def pair(j):
                    V(lambda e: e.max(out=stop_[:n, j, 0:8], in_=sc[:n, j, :]), ['sc'], [('stopA', j)])
                    V(lambda e: e.max_index(out=itop[:n, j, 0:8], in_max=stop_[:n, j, 0:8], in_values=sc[:n, j, :]),
                      ['sc', ('stopA', j)], [('itopA', j)])
                    V(lambda e: e.match_replace(out=sc2[:n, j, :], in_to_replace=stop_[:n, j, 0:8], in_values=sc[:n, j, :],
                                                imm_value=NEG), ['sc', ('stopA', j)], [('sc2', j)])
                    V(lambda e: e.max(out=stop_[:n, j, 8:16], in_=sc2[:n, j, :]), [('sc2', j)], [('stopB', j)])
                    V(lambda e: e.max_index(out=itop[:n, j, 8:16], in_max=stop_[:n, j, 8:16], in_values=sc2[:n, j, :]),
                      [('sc2', j), ('stopB', j)], [('itopB', j)])
                pair(j)
            V(lambda e: e.tensor_copy(itf[:n, :, :], itop[:n, :, :]), [('itopA', j_) for j_ in range(16)] + [('itopB', j_) for j_ in range(16)], ['itf'])
            s4 = stop_[:n, :, :].rearrange("t (h p) k -> t h p k", p=2)
            i4 = itf[:n, :, :].rearrange("t (h p) k -> t h p k", p=2)
            c4 = lambda t: t[:n, :, :].rearrange("t h (a b) -> t h a b", b=16)
            V(lambda e: e.tensor_tensor(c4(cand), s4[:, :, 0, :][:, :, :, None].to_broadcast([n, 8, 16, 16]),
                                        s4[:, :, 1, :][:, :, None, :].to_broadcast([n, 8, 16, 16]), ALU.add),
              [('stopA', j_) for j_ in range(16)] + [('stopB', j_) for j_ in range(16)], ['cand'])
            V(lambda e: e.tensor_scalar(i4[:, :, 0, :], i4[:, :, 0, :], 128.0, None, ALU.mult), ['itf'], ['itf'])
            V(lambda e: e.tensor_tensor(c4(cidx), i4[:, :, 0, :][:, :, :, None].to_broadcast([n, 8, 16, 16]),
                                        i4[:, :, 1, :][:, :, None, :].to_broadcast([n, 8, 16, 16]), ALU.add),
              ['itf'], ['cidx'])
            for hd in range(8):
                def head(hd):
                    V(lambda e: e.max(out=tops[:n, hd, 0:8], in_=cand[:n, hd, :]), ['cand'], [('tops', hd)])
                    V(lambda e: e.match_replace(out=cand2[:n, hd, :], in_to_replace=tops[:n, hd, 0:8],
                                                in_values=cand[:n, hd, :], imm_value=NEG), ['cand', ('tops', hd)], [('sc2', 2 * hd), ('sc2', 2 * hd + 1)])
                    V(lambda e: e.max(out=tops[:n, hd, 8:16], in_=cand2[:n, hd, :]), [('sc2', 2 * hd), ('sc2', 2 * hd + 1)], [('tops', hd)])
                    for kk in range(16):
                        def pick(kk):
                            m = hd * 16 + kk
                            V(lambda e: e.scalar_tensor_tensor(out=sj[m % 2][:n, :], in0=cand[:n, hd, :],
                                                               scalar=tops[:n, hd, kk:kk + 1], in1=cidx[:n, hd, :],
                                                               op0=ALU.is_equal, op1=ALU.mult,
                                                               accum_out=eidf[:n, m:m + 1]),
                              ['cand', ('tops', hd), 'cidx'], [('sj', m % 2), ('eidf', m)])
                        pick(kk)
                head(hd)
            V(lambda e: e.tensor_scalar(eidf[:n, :], eidf[:n, :], 16383.0, 0.0, ALU.min, ALU.max), [('eidf', m_) for m_ in range(128)], ['eidf'] + [('eidf', m_) for m_ in range(128)])
            V(lambda e: e.tensor_copy(eidi[par][:n, :], eidf[:n, :]), ['eidf'], [('eidi', par)])
            V(lambda e: e.tensor_tensor(gate[:n, :, :], tops[:n, :, :], tops[:n, :, 0:1].to_broadcast([n, 8, 16]),
                                        ALU.subtract), [('tops', h_) for h_ in range(8)], ['gate'])
            A(lambda e: e.activation(gate[:n, :, :], gate[:n, :, :], AF.Exp), ['gate'], ['gate'])
            V(lambda e: e.tensor_reduce(zs_[:n, :], gate[:n, :, :], AX.X, ALU.add), ['gate'], ['zsum'])
            V(lambda e: e.reciprocal(zs_[:n, :], zs_[:n, :]), ['zsum'], ['zsum'])
            V(lambda e: e.tensor_tensor(gate[:n, :, :], gate[:n, :, :], zs_[:n, :][:, :, None].to_broadcast([n, 8, 16]),
                                        ALU.mult), ['gate', 'zsum'], ['gate'])

        def pick(k, n, m):
            par = k % 2
            i = m % NGB
            S.gather(cbuf[i][:n, :], T['PVB'][:, :], eidi[par][:n, m:m + 1], reads=[('eidi', par)], writes=[('cb', i)],
                     bounds=None)
            V(lambda e: e.scalar_tensor_tensor(out=L.junk[:n, :], in0=cbuf[i][:n, 0:D], scalar=1.0, in1=ht[par][:n, :],
                                               op0=ALU.mult, op1=ALU.mult, accum_out=pre[:n, m:m + 1]),
              [('cb', i), ('ht', par)], [('pre', m)])
            A(lambda e: e.activation(actv[par][:n, m:m + 1], pre[:n, m:m + 1], AF.Gelu), [('pre', m)], [('actv', m)])

        def pickB(k, n, m):
            par = k % 2
            i = m % NGB
            gflat = gate[:n, :, :].rearrange("t h k -> t (h k)")
            V(lambda e: e.tensor_scalar(dg[i][:n, :n], identb[:n, :n], actv[par][:n, m:m + 1], gflat[:, m:m + 1],
                                        ALU.mult, ALU.mult), ['identb', ('actv', m), 'gate'], [('dg', i)])
            for cb in range(8):
                S.op('pe', lambda e, cb=cb: e.matmul(C.PS[cb][:n, :], lhsT=dg[i][:n, :n],
                                                     rhs=cbuf[i][:n, D + cb * 512:D + (cb + 1) * 512],
                                                     start=(m == 0), stop=(m == 127)),
                     reads=[('dg', i), ('cb', i)], writes=[('ps', cb)])

        def final(k, r0, n):
            par = k % 2
            for cb in range(8):
                V(lambda e, cb=cb: e.scalar_tensor_tensor(out=ht[par][:n, cb * 512:(cb + 1) * 512],
                                                          in0=ht[par][:n, cb * 512:(cb + 1) * 512], scalar=float(ALPHA),
                                                          in1=C.PS[cb][:n, :], op0=ALU.mult, op1=ALU.add),
                  [('ht', par), ('ps', cb)], [('ht', par)])
            ln_rows(S, nc, L, n, ht[par], ('ht', par), L.g, L.b, lnout, 'lnout')
            S.dma('act', y_rows(T, r0, n), lnout[:n, :], reads=['lnout'])

        import os as _os
        tiles = row_tiles()[:int(_os.environ.get("KT", "99"))]
        for k, (r0, n) in enumerate(tiles):
            e2(k, r0, n)
            for m in range(128):
                pick(k, n, m)
                if m >= 1:
                    pickB(k, n, m - 1)
            pickB(k, n, 127)
            final(k, r0, n)
    S.barrier()
```

```python
import numpy as np
from contextlib import ExitStack
import concourse.bass as bass
import concourse.mybir as mybir
from concourse.bass_utils import run_bass_kernel_spmd

F32 = mybir.dt.float32
BF16 = mybir.dt.bfloat16
I32 = mybir.dt.int32
U32 = mybir.dt.uint32
AF = mybir.ActivationFunctionType
ALU = mybir.AluOpType
AX = mybir.AxisListType

ENGS = ['pe', 'act', 'dve', 'pool', 'sp']


class Sched:
    def __init__(self, nc, stack, n_dma=32):
        self.nc = nc
        self.q = {e: [] for e in ENGS}
        self.sem = {e: stack.enter_context(nc.semaphore("s_" + e)) for e in ENGS}
        self.cnt = {e: 0 for e in ENGS}
        self.nd = n_dma
        self.dsem = [stack.enter_context(nc.semaphore("d%d" % i)) for i in range(n_dma)]
        self.dcnt = [0] * n_dma
        self.dnext = 0
        self.seen = {e: {} for e in ENGS}
        self.lw = {}
        self.rd = {}
        self.ninst = 0

    def _deps(self, reads, writes):
        deps = {}
        for r in reads:
            kv = self.lw.get(r)
            if kv is not None and deps.get(kv[0], 0) < kv[1]:
                deps[kv[0]] = kv[1]
        for w in writes:
            kv = self.lw.get(w)
            if kv is not None and deps.get(kv[0], 0) < kv[1]:
                deps[kv[0]] = kv[1]
            for k, v in self.rd.get(w, {}).items():
                if deps.get(k, 0) < v:
                    deps[k] = v
        return deps

    def _waits(self, eng, deps):
        for k, v in deps.items():
            if k == 'pe' and eng == 'pe':
                continue
            if self.seen[eng].get(k, 0) >= v:
                continue
            self.seen[eng][k] = v
            sem = self.sem[k] if isinstance(k, str) else self.dsem[k[1]]
            self.q[eng].append(lambda e, sem=sem, v=v: e.wait_ge(sem, v))
            self.ninst += 1

    def _mark(self, key, v, reads, writes):
        for r in reads:
            d = self.rd.setdefault(r, {})
            if d.get(key, 0) < v:
                d[key] = v
        for w in writes:
            self.lw[w] = (key, v)
            self.rd[w] = {}

    def op(self, eng, fn, reads=(), writes=()):
        self._waits(eng, self._deps(reads, writes))
        self.cnt[eng] += 1
        v = self.cnt[eng]
        sem = self.sem[eng]
        self.q[eng].append(lambda e: fn(e).then_inc(sem, 1))
        self.ninst += 1
        self._mark(eng, v, reads, writes)

    def dma(self, q, out, in_, reads=(), writes=(), **kw):
        i = self.dnext
        self.dnext = (i + 1) % self.nd
        deps = self._deps(reads, writes)
        if self.dcnt[i] > 0:
            deps[('d', i)] = max(deps.get(('d', i), 0), self.dcnt[i])
        self._waits(q, deps)
        self.dcnt[i] += 16
        v = self.dcnt[i]
        sem = self.dsem[i]
        self.q[q].append(lambda e: e.dma_start(out=out, in_=in_, **kw).then_inc(sem, 16))
        self.ninst += 1
        self._mark(('d', i), v, reads, writes)

    def gather(self, out, table, idx_ap, reads=(), writes=(), bounds=None):
        q = 'pool'
        i = self.dnext
        self.dnext = (i + 1) % self.nd
        deps = self._deps(reads, writes)
        if self.dcnt[i] > 0:
            deps[('d', i)] = max(deps.get(('d', i), 0), self.dcnt[i])
        self._waits(q, deps)
        self.dcnt[i] += 16
        v = self.dcnt[i]
        sem = self.dsem[i]

        def f(e):
            self._gdbg = getattr(self, '_gdbg', 0) + 1
            if self._gdbg > 4351:
                print("GATHER", self._gdbg, out.shape, idx_ap, flush=True)
            return e.indirect_dma_start(
                out=out, out_offset=None, in_=table,
                in_offset=bass.IndirectOffsetOnAxis(ap=idx_ap, axis=0),
                bounds_check=bounds, oob_is_err=False).then_inc(sem, 16)
        self.q[q].append(f)
        self.ninst += 1
        self._mark(('d', i), v, reads, writes)

    def barrier(self):
        for e in ENGS:
            deps = {k: self.cnt[k] for k in ENGS if self.cnt[k] > 0 and k != e}
            if e != 'pe' and self.cnt[e] > 0:
                deps[e] = self.cnt[e]
            for i in range(self.nd):
                if self.dcnt[i] > 0:
                    deps[('d', i)] = self.dcnt[i]
            self._waits(e, deps)
        self.lw = {}
        self.rd = {}

    def run(self):
        self.barrier()
        q = self.q
        with self.nc.Block() as block:
            @block.tensor
            def _(e):
                for f in q['pe']:
                    f(e)

            @block.scalar
            def _(e):
                for f in q['act']:
                    f(e)

            @block.vector
            def _(e):
                for f in q['dve']:
                    f(e)

            @block.gpsimd
            def _(e):
                for f in q['pool']:
                    f(e)

            @block.sync
            def _(e):
                for f in q['sp']:
                    f(e)


D = 4096
SEQ = 2048
DS = 32
NS = 2
PAST = 1024
NT = SEQ + NS * DS
HD = 128
NH = 16
AW = 2048
PROJ = 14384
OFF_Z = 6144
OFF_A = 8192
OFF_B = 8208
OFF_BQ = 8224
OFF_F = 14368
GT = 1056
KC = 32
LN_EPS = 1e-5
ALPHA = 2.0 ** 0.25


class Ctx:
    pass


def tokmap(g, c0, n):
    if c0 < 1024:
        return g * 1024 + c0
    return SEQ + DS * g + (c0 - 1024)


def build_xT(S, C, g, src_rows):
    XT = C.XT
    for ti, (src, n, c0) in enumerate(src_rows):
        sl = ti % 2
        xr = C.xrow[sl]
        S.dma('sp', xr[:n, :], src, writes=[('xr', sl)])
        for kq in range(8):
            b = C.ps_next()
            for j in range(4):
                kc = kq * 4 + j
                S.op('pe', lambda e, b=b, j=j, kc=kc, xr=xr, n=n: e.transpose(
                    C.PS[b][:, j * 128:j * 128 + n], xr[:n, kc * 128:(kc + 1) * 128], C.ident[:n, :n]),
                    reads=[('xr', sl), 'ident'], writes=[('ps', b)])
            src_ps = C.PS[b][:, :].rearrange("p (j t) -> p j t", j=4)[:, :, :n]
            dst = XT[:, kq * 4:(kq + 1) * 4, c0:c0 + n]
            if kq % 2 == 0:
                S.op('act', lambda e, dst=dst, src_ps=src_ps: e.activation(dst, src_ps, AF.Copy),
                     reads=[('ps', b)], writes=[('XT', c0 // 128)])
            else:
                S.op('dve', lambda e, dst=dst, src_ps=src_ps: e.tensor_copy(dst, src_ps),
                     reads=[('ps', b)], writes=[('XT', c0 // 128)])


def load_w(S, C, wdram, col0, ncols, slot):
    for p0 in range(0, ncols, 128):
        pn = min(128, ncols - p0)
        ss = C.wst_i % 2
        C.wst_i += 1
        wst = C.wst[ss]
        src = wdram[:, col0 + p0:col0 + p0 + pn].rearrange("(kc p) c -> p kc c", p=128)
        S.dma('sp', wst[:, :, :pn], src, writes=[('wst', ss)])
        dst = C.wbf[slot][:, :, p0:p0 + pn]
        S.op('pool', lambda e, dst=dst, wst=wst, pn=pn: e.tensor_copy(dst, wst[:, :, :pn]),
             reads=[('wst', ss)], writes=[('wbf', slot)])


def xt_reads(c0, n):
    return [('XT', i) for i in range(c0 // 128, (c0 + n - 1) // 128 + 1)]


def gemm_F(S, C, slot, ncols, emit):
    XT, wbf = C.XT, C.wbf[slot]
    for (t0, n) in ((0, 512), (512, 512), (1024, 32)):
        b = C.ps_next()
        for kc in range(KC):
            S.op('pe', lambda e, b=b, kc=kc, t0=t0, n=n: e.matmul(
                C.PS[b][:ncols, :n], lhsT=wbf[:, kc, :ncols], rhs=XT[:, kc, t0:t0 + n],
                start=(kc == 0), stop=(kc == KC - 1)),
                reads=[('wbf', slot)] + xt_reads(t0, n), writes=[('ps', b)])
        emit(b, t0, n)


def gemm_T(S, C, slot, ncols, emit):
    XT, wbf = C.XT, C.wbf[slot]
    for ti in range(9):
        c0 = ti * 128
        n = 128 if ti < 8 else 32
        b = C.ps_next()
        for kc in range(KC):
            S.op('pe', lambda e, b=b, kc=kc, c0=c0, n=n: e.matmul(
                C.PS[b][:n, :ncols], lhsT=XT[:, kc, c0:c0 + n], rhs=wbf[:, kc, :ncols],
                start=(kc == 0), stop=(kc == KC - 1)),
                reads=[('wbf', slot)] + xt_reads(c0, n), writes=[('ps', b)])
        emit(b, c0, n)


def phase_A(S, nc, C, T):
    with ExitStack() as st:
        C.XT = st.enter_context(nc.sbuf_tensor("XT", [128, KC, GT], BF16))
        C.xrow = [st.enter_context(nc.sbuf_tensor("xrow%d" % i, [128, D], F32)) for i in range(2)]
        C.wst = [st.enter_context(nc.sbuf_tensor("wst%d" % i, [128, KC, 128], F32)) for i in range(2)]
        C.wbf = [st.enter_context(nc.sbuf_tensor("wbf%d" % i, [128, KC, 128], BF16)) for i in range(2)]
        stg = [st.enter_context(nc.sbuf_tensor("stg%d" % i, [128, GT], F32)) for i in range(2)]
        tst = [st.enter_context(nc.sbuf_tensor("tst%d" % i, [128, 128], F32)) for i in range(4)]
        C.wst_i = 0
        cnt = {'stg': 0, 'tst': 0, 'ev': 0}

        chunks = []
        for j in range(48):
            chunks.append((j * 128, 128, 'F', (T['AqkvT'], j * 128, 1.0)))
        for j in range(16):
            chunks.append((OFF_Z + j * 128, 128, 'F', (T['AzT'], j * 128, 1.0)))
        chunks.append((OFF_A, 32, 'T', ('ab', 0)))
        for j in range(16):
            chunks.append((OFF_BQ + j * 128, 128, 'F', (T['BqT'], j * 128, HD ** -0.5)))
        for j in range(16):
            chunks.append((OFF_BQ + AW + j * 128, 128, 'FT', (T['BkT'], j * 128, 1.0, 'k', j * 128)))
        for j in range(16):
            chunks.append((OFF_BQ + 2 * AW + j * 128, 128, 'T', ('v', j * 128)))
        chunks.append((OFF_F, 16, 'F', (T['BfT'], 0, 1.0)))

        cvs = {'r': 0}
        NRT = T['pv'].shape[0] // 128

        def conv_step():
            r = cvs['r']
            if r >= NRT:
                return
            cvs['r'] += 1
            for tname, c0 in (('pu', 0), ('pv', D)):
                for hf in range(2):
                    S.dma('pool', T['PVB'][r * 128:(r + 1) * 128, c0 + hf * 2048:c0 + (hf + 1) * 2048],
                          T[tname][r * 128:(r + 1) * 128, hf * 2048:(hf + 1) * 2048])
        for g in range(2):
            rows = [(T['xp'][g * 1024 + i * 128:g * 1024 + (i + 1) * 128, :], 128, i * 128) for i in range(8)]
            rows.append((T['xs'][g * DS:(g + 1) * DS, :], DS, 1024))
            build_xT(S, C, g, rows)

            def emit_F(info):
                dst, row0, scale = info[0], info[1], info[2]

                def emit(b, t0, n, ncols):
                    pass
                return emit

            load_w(S, C, T['w_in'], chunks[0][0], chunks[0][1], 0)
            for ci, (col0, ncols, mode, info) in enumerate(chunks):
                slot = ci % 2
                conv_step()
                if ci + 1 < len(chunks):
                    load_w(S, C, T['w_in'], chunks[ci + 1][0], chunks[ci + 1][1], (ci + 1) % 2)
                if 'F' in mode:
                    dst, row0, scale = info[0], info[1], info[2]
                    ss = cnt['stg'] % 2
                    cnt['stg'] += 1
                    sg_t = stg[ss]

                    def emitF(b, t0, n, ncols=ncols, sg_t=sg_t, ss=ss, scale=scale):
                        cnt['ev'] += 1
                        if cnt['ev'] % 2 == 0:
                            S.op('act', lambda e: e.activation(sg_t[:ncols, t0:t0 + n], C.PS[b][:ncols, :n],
                                                               AF.Identity, scale=float(scale)),
                                 reads=[('ps', b)], writes=[('stg', ss)])
                        else:
                            S.op('dve', lambda e: e.tensor_scalar(sg_t[:ncols, t0:t0 + n], C.PS[b][:ncols, :n],
                                                                  float(scale), None, ALU.mult),
                                 reads=[('ps', b)], writes=[('stg', ss)])
                    gemm_F(S, C, slot, ncols, emitF)
                    S.dma('act', dst[row0:row0 + ncols, g * 1024:(g + 1) * 1024], sg_t[:ncols, 0:1024],
                          reads=[('stg', ss)])
                    S.dma('act', dst[row0:row0 + ncols, SEQ + DS * g:SEQ + DS * (g + 1)], sg_t[:ncols, 1024:GT],
                          reads=[('stg', ss)])
                if 'T' in mode:
                    kind, coff = (info[3], info[4]) if mode == 'FT' else (info[0], info[1])

                    def emitT(b, c0, n, ncols=ncols, kind=kind, coff=coff):
                        ts_ = cnt['tst'] % 4
                        cnt['tst'] += 1
                        tt = tst[ts_]
                        cnt['ev'] += 1
                        if cnt['ev'] % 2 == 0:
                            S.op('act', lambda e: e.activation(tt[:n, :ncols], C.PS[b][:n, :ncols], AF.Copy),
                                 reads=[('ps', b)], writes=[('tst', ts_)])
                        else:
                            S.op('dve', lambda e: e.tensor_copy(tt[:n, :ncols], C.PS[b][:n, :ncols]),
                                 reads=[('ps', b)], writes=[('tst', ts_)])
                        if kind == 'ab':
                            r0 = tokmap(g, c0, n)
                            dd = T['Aab'][r0:r0 + n, 0:ncols]
                        else:
                            if c0 < 1024:
                                base = T['fkp'] if kind == 'k' else T['fvp']
                                r0 = g * 1024 + c0
                            else:
                                base = T['fks'] if kind == 'k' else T['fvs']
                                r0 = g * DS
                            dd = base[r0:r0 + n, coff:coff + ncols]
                        S.dma('act', dd, tt[:n, :ncols], reads=[('tst', ts_)])
                    gemm_T(S, C, slot, ncols, emitT)
    S.barrier()


IN_SPECS = [
    ("xp", [SEQ, D], F32), ("xs", [NS * DS, D], F32),
    ("ck", [NS, PAST, AW], F32), ("cv", [NS, PAST, AW], F32), ("clf", [NS, PAST, NH], F32),
    ("sg", [NS, NH, HD, HD], F32), ("sconv_t", [NS, 128, 48 * 3], F32),
    ("w_in", [D, PROJ], F32), ("w_out", [D, D], F32), ("w_q", [D, 2048], F32),
    ("subk", [16, 128, 128], F32), ("pu", [16384, D], F32), ("pv", [16384, D], F32),
    ("convw_t", [128, 48 * 4], F32), ("alog_b", [128, NH], F32), ("dtb_b", [128, NH], F32),
    ("normw_c", [128, 1], F32), ("ffb_c", [NH, 1], F32),
    ("ln1g_b", [128, D], F32), ("ln1b_b", [128, D], F32), ("ln2g_b", [128, D], F32), ("ln2b_b", [128, D], F32),
    ("ident_in", [128, 128], F32), ("cmat", [128, 8 * 128], F32), ("selc", [NH, NH * 128], F32),
]
OUT_SPECS = [
    ("yp", [SEQ, D]), ("ys", [NS * DS, D]),
    ("fkp", [SEQ, AW]), ("fvp", [SEQ, AW]), ("flp", [SEQ, NH]),
    ("sgp", [NH, HD, HD]), ("scp", [3, 3 * AW]),
    ("fks", [NS * DS, AW]), ("fvs", [NS * DS, AW]), ("fls", [NS * DS, NH]),
    ("sgs", [NS, NH, HD, HD]), ("scs", [NS, 3, 3 * AW]),
]
SCRATCH = [
    ("BIG1", [3 * AW * NT], F32), ("BIG2", [3 * AW * NT], F32), ("Aab", [NT, 32], F32),
    ("BfT", [NH, NT], F32), ("OT", [D, NT], BF16), ("PVB", [16384, 2 * D], BF16),
]


def build_program(phases="ABCDE"):
    nc = bass.Bass("TRN2", target_bir_lowering=False)
    T = {}
    for name, shape, dt in IN_SPECS:
        if name in ("pu", "pv") and 'E' not in phases:
            shape = [128, D]
        T[name] = nc.dram_tensor(name, shape, dt, kind="ExternalInput")
    for name, shape in OUT_SPECS:
        T[name] = nc.dram_tensor(name, shape, F32, kind="ExternalOutput")
    for name, shape, dt in SCRATCH:
        if name == "PVB" and 'E' not in phases:
            shape = [128, 2 * D]
        T[name] = nc.dram_tensor(name, shape, dt, kind="Internal")
    b1, b2 = T['BIG1'], T['BIG2']
    T['AqkvT'] = b1[0:3 * AW * NT].rearrange("(a b) -> a b", b=NT)
    T['MIX'] = b1[0:NT * D].rearrange("(a b) -> a b", b=D)
    T['SC'] = b1[NT * D:NT * D + NT * 2048].rearrange("(a b) -> a b", b=2048)
    T['AzT'] = b2[0:AW * NT].rearrange("(a b) -> a b", b=NT)
    T['BqT'] = b2[AW * NT:2 * AW * NT].rearrange("(a b) -> a b", b=NT)
    T['BkT'] = b2[2 * AW * NT:3 * AW * NT].rearrange("(a b) -> a b", b=NT)
    T['H'] = b2[0:NT * D].rearrange("(a b) -> a b", b=D)
    with ExitStack() as st:
        S = Sched(nc, st)
        C = Ctx()
        C.PS = [st.enter_context(nc.psum_tensor("ps%d" % i, [128, 512], F32)) for i in range(8)]
        C.ps_i = 0

        def ps_next():
            b = C.ps_i
            C.ps_i = (b + 1) % 8
            return b
        C.ps_next = ps_next
        C.ident = st.enter_context(nc.sbuf_tensor("ident_sb", [128, 128], F32))
        C.cmat = st.enter_context(nc.sbuf_tensor("cmat_sb", [128, 8 * 128], F32))
        S.dma('sp', C.ident[:], T['ident_in'][:, :], writes=['ident'])
        S.dma('sp', C.cmat[:], T['cmat'][:, :], writes=['cmat'])
        if 'A' in phases:
            phase_A(S, nc, C, T)
        if 'B' in phases:
            phase_B(S, nc, C, T)
        if 'C' in phases:
            phase_C(S, nc, C, T)
        if 'D' in phases:
            phase_D(S, nc, C, T)
        if 'E' in phases:
            phase_E(S, nc, C, T)
        S.run()
        print("instructions:", S.ninst, {e: S.cnt[e] for e in ENGS})
    return nc


def make_consts():
    c = np.zeros((128, 8, 128), np.float32)
    i = np.arange(128)
    c[:, 0, :] = (i[:, None] <= i[None, :])
    c[:, 1, :] = (i[:, None] > i[None, :])
    c[:, 2, :] = 1.0
    c[:, 3, :] = (i[:, None] < i[None, :])
    c[:, 4, :] = 128.0
    c[:, 5, :] = (i[:, None] >= i[None, :])
    return c.reshape(128, 8 * 128)


def kernel(x_prompt, x_sample, cache_fox_k, cache_fox_v, cache_fox_logf, state_gdn, state_gdn_conv,
           w_in, gdn_conv_w, gdn_a_log, gdn_dt_bias, gdn_norm_w, fox_f_bias, w_out, ln1_g, ln1_b,
           peer_w_q, peer_sub_keys, peer_u, peer_v, ln2_g, ln2_b, _phases="ABCDE", _cores=None):
    import time as _time
    _t0 = _time.time()
    f = lambda a: np.ascontiguousarray(np.asarray(a), dtype=np.float32)
    nc = build_program(_phases)
    cw = f(gdn_conv_w)[0]
    convw_t = np.ascontiguousarray(cw.reshape(4, 48, 128).transpose(2, 1, 0)).reshape(128, 48 * 4)
    bc = lambda v, n=128: np.ascontiguousarray(np.broadcast_to(f(v).reshape(1, -1), (n, f(v).size)))
    shared = {
        "w_in": f(w_in)[0], "w_out": f(w_out)[0], "w_q": f(peer_w_q)[0],
        "subk": f(peer_sub_keys)[0].reshape(16, 128, 128), "pu": f(peer_u)[0], "pv": f(peer_v)[0],
        "convw_t": convw_t, "alog_b": bc(gdn_a_log), "dtb_b": bc(gdn_dt_bias),
        "normw_c": f(gdn_norm_w).reshape(128, 1), "ffb_c": f(fox_f_bias).reshape(NH, 1),
        "ln1g_b": bc(ln1_g), "ln1b_b": bc(ln1_b), "ln2g_b": bc(ln2_g), "ln2b_b": bc(ln2_b),
        "ident_in": np.eye(128, dtype=np.float32), "cmat": make_consts(),
        "selc": np.ascontiguousarray(np.repeat(np.eye(NH, dtype=np.float32), 128, axis=1)),
    }
    xp = f(x_prompt)
    xs = f(x_sample)
    ck = f(cache_fox_k)[0]
    cv = f(cache_fox_v)[0]
    clf = f(cache_fox_logf)[0]
    sg = f(state_gdn)[0]
    sc = f(state_gdn_conv)[0]
    in_maps = []
    for c in range(8):
        m = dict(shared)
        m["xp"] = xp[c]
        m["xs"] = xs[2 * c:2 * c + 2].reshape(NS * DS, D)
        m["ck"] = ck[2 * c:2 * c + 2].reshape(NS, PAST, AW)
        m["cv"] = cv[2 * c:2 * c + 2].reshape(NS, PAST, AW)
        m["clf"] = clf[2 * c:2 * c + 2]
        m["sg"] = sg[2 * c:2 * c + 2]
        m["sconv_t"] = np.ascontiguousarray(sc[2 * c:2 * c + 2].reshape(NS, 3, 48, 128).transpose(0, 3, 2, 1)).reshape(NS, 128, 48 * 3)
        in_maps.append(m)
    if 'E' not in _phases:
        for m in in_maps:
            m["pu"] = m["pu"][:128]
            m["pv"] = m["pv"][:128]
    print("kernel: built+staged %.1fs" % (_time.time() - _t0), flush=True)
    if _cores is not None:
        import os as _os
        res = run_bass_kernel_spmd(nc, [in_maps[c] for c in _cores], core_ids=list(range(len(_cores))),
                                   trace=bool(_os.environ.get("KTRACE")))
        print("EXEC_NS", getattr(res, "exec_time_ns", None), flush=True)
        R = {c: res.results[i] for i, c in enumerate(_cores)}
        R = [R.get(c, R[_cores[0]]) for c in range(8)]
    else:
        res = run_bass_kernel_spmd(nc, in_maps, core_ids=list(range(8)))
        R = res.results
    print("kernel: ran %.1fs" % (_time.time() - _t0), flush=True)
    cat = lambda k: np.stack([np.asarray(R[c][k]) for c in range(8)], axis=0)
    yp = cat("yp")
    ys = cat("ys").reshape(16, DS, D)
    fkp = cat("fkp").reshape(1, 8, SEQ, NH, HD)
    fvp = cat("fvp").reshape(1, 8, SEQ, NH, HD)
    flp = cat("flp").reshape(1, 8, SEQ, NH)
    sgp = cat("sgp").reshape(1, 8, NH, HD, HD)
    scp = cat("scp").reshape(1, 8, 3, 3 * AW)
    fks = cat("fks").reshape(1, 16, DS, NH, HD)
    fvs = cat("fvs").reshape(1, 16, DS, NH, HD)
    fls = cat("fls").reshape(1, 16, DS, NH)
    sgs = cat("sgs").reshape(1, 16, NH, HD, HD)
    scs = cat("scs").reshape(1, 16, 3, 3 * AW)
    return (yp, ys, fkp, fvp, flp, sgp, scp, fks, fvs, fls, sgs, scs)


class KB:
    def __init__(self, c0, n, vb, qb_first, diag):
        self.c0, self.n, self.vb, self.qb_first, self.diag = c0, n, vb, qb_first, diag


def fox_core(S, C, P, kT, Vb, nck, qT, cqT, selh, nqb, QB, kblocks, out_cb, tag):
    import os as _os
    _kg = int(_os.environ.get("KG", "99"))
    for g0 in range(0, min(nqb, 4 * _kg), 4):
        qbs = list(range(g0, min(g0 + 4, nqb)))
        vis = {qb: [kb for kb in kblocks if kb.qb_first <= qb] for qb in qbs}
        for kb in [k for k in kblocks if k.qb_first <= qbs[-1]]:
            qlo = max(kb.qb_first, g0)
            q0, q1 = qlo * QB, (qbs[-1] + 1) * QB
            N = q1 - q0
            bS = 4 + (P.fx_i % 2)
            psl = P.fx_i % 2
            P.fx_i += 1
            S.op('pe', lambda e, kb=kb, bS=bS, q0=q0, q1=q1, N=N: e.matmul(
                C.PS[bS][:kb.n, :N], lhsT=kT[:, kb.c0:kb.c0 + kb.n], rhs=qT[:, q0:q1], start=True, stop=False),
                reads=tag['k'] + tag['q'], writes=[('ps', bS)])
            S.op('pe', lambda e, kb=kb, bS=bS, q0=q0, q1=q1, N=N: e.matmul(
                C.PS[bS][:kb.n, :N], lhsT=selh[:, :kb.n], rhs=cqT[:, q0:q1], start=False, stop=True),
                reads=tag['c'] + ['selc'], writes=[('ps', bS)])
            pt = P.PT[psl]
            S.op('act', lambda e, kb=kb, bS=bS, N=N, pt=pt: e.activation(
                pt[:kb.n, :N], C.PS[bS][:kb.n, :N], AF.Exp, bias=nck(kb), scale=1.0),
                reads=[('ps', bS)] + tag['nck'], writes=[('pt', psl)])
            if kb.diag is not None and qlo <= kb.diag <= qbs[-1]:
                off = (kb.diag - qlo) * QB
                S.op('dve', lambda e, kb=kb, off=off, pt=pt: e.tensor_tensor(
                    pt[:kb.n, off:off + QB], pt[:kb.n, off:off + QB], P.maskb[:kb.n, :QB], ALU.mult),
                    reads=[('pt', psl), 'maskb'], writes=[('pt', psl)])
            for qb in range(qlo, qbs[-1] + 1):
                bank = qb % 4
                first = kb is vis[qb][0]
                last = kb is vis[qb][-1]
                S.op('pe', lambda e, kb=kb, qb=qb, bank=bank, first=first, last=last, pt=pt, qlo=qlo: e.matmul(
                    C.PS[bank][:QB, :129], lhsT=pt[:kb.n, (qb - qlo) * QB:(qb - qlo + 1) * QB],
                    rhs=Vb[:kb.n, kb.vb, :], start=first, stop=last),
                    reads=[('pt', psl)] + tag['v'], writes=[('ps', bank)])
        for qb in qbs:
            out_cb(qb, qb % 4)


def fox_out(S, C, P, QB, ots, ots_tag):
    def cb(qb, bank):
        i = P.oc_i % 2
        P.oc_i += 1
        rec = P.rec[i]
        osb = P.osb[i]
        S.op('dve', lambda e: e.reciprocal(rec[:QB, :], C.PS[bank][:QB, 128:129]),
             reads=[('ps', bank)], writes=[('rec', i)])
        S.op('act', lambda e: e.activation(osb[:QB, :], C.PS[bank][:QB, 0:128], AF.Identity, scale=rec[:QB, 0:1]),
             reads=[('ps', bank), ('rec', i)], writes=[('osb', i)])
        bt = 6 + i
        S.op('pe', lambda e: e.transpose(C.PS[bt][:, :QB], osb[:QB, :], C.ident[:QB, :QB]),
             reads=[('osb', i), 'ident'], writes=[('ps', bt)])
        S.op('dve', lambda e: e.tensor_copy(ots[:, qb * QB:(qb + 1) * QB], C.PS[bt][:, :QB]),
             reads=[('ps', bt)], writes=[ots_tag])
    return cb


def phase_C(S, nc, C, T):
    with ExitStack() as st:
        P = Ctx()
        P.fx_i = 0
        P.oc_i = 0
        sb = lambda name, shape, dt=F32: st.enter_context(nc.sbuf_tensor(name, shape, dt))
        selc = sb("selc_sb", [NH, NH * 128])
        S.dma('sp', selc[:], T['selc'][:, :], writes=['selc'])
        P.maskb = sb("maskb", [128, 128], BF16)
        S.op('dve', lambda e: e.tensor_copy(P.maskb[:], C.cmat[:, 0:128]), reads=['cmat'], writes=['maskb'])
        P.PT = [sb("PT%d" % i, [128, 512], BF16) for i in range(2)]
        P.rec = [sb("rec%d" % i, [128, 1]) for i in range(2)]
        P.osb = [sb("osb%d" % i, [128, 128]) for i in range(2)]
        bfT = sb("bfT", [NH, NT])
        logfT = sb("logfT", [NH, NT])
        ffb = sb("ffb", [NH, 1])
        nffb = sb("nffb", [NH, 1])
        onesr = sb("onesr", [NH, SEQ])
        cTp = sb("cTp", [NH, SEQ])
        S.dma('sp', bfT[:], T['BfT'][:, :], writes=['bfT'])
        S.dma('sp', ffb[:], T['ffb_c'][:, :], writes=['ffb'])
        S.op('dve', lambda e: e.tensor_scalar(nffb[:], ffb[:], -1.0, None, ALU.mult), reads=['ffb'], writes=['nffb'])
        S.op('dve', lambda e: e.memset(onesr[:], 1.0), writes=['onesr'])
        S.op('act', lambda e: e.activation(logfT[:], bfT[:], AF.Exp, bias=nffb[:, 0:1], scale=-1.0),
             reads=['bfT', 'nffb'], writes=['logfT'])
        S.op('act', lambda e: e.activation(logfT[:], logfT[:], AF.Ln, bias=1.0, scale=1.0),
             reads=['logfT'], writes=['logfT'])
        S.op('dve', lambda e: e.tensor_scalar(logfT[:], logfT[:], -1.0, None, ALU.mult),
             reads=['logfT'], writes=['logfT'])
        S.op('dve', lambda e: e.tensor_tensor_scan(cTp[:], onesr[:], logfT[:, 0:SEQ], 0.0, ALU.mult, ALU.add),
             reads=['onesr', 'logfT'], writes=['cTp'])
        lf_tm = sb("lf_tm", [128, 16, NH])
        nck_p = sb("nck_p", [128, 16, NH])
        for blk in range(16):
            S.op('pe', lambda e, blk=blk: e.transpose(C.PS[6][:, blk * 16:(blk + 1) * 16],
                                                      logfT[:, blk * 128:(blk + 1) * 128], C.ident[:NH, :NH]),
                 reads=['logfT', 'ident'], writes=[('ps', 6)])
            S.op('pe', lambda e, blk=blk: e.transpose(C.PS[7][:, blk * 16:(blk + 1) * 16],
                                                      cTp[:, blk * 128:(blk + 1) * 128], C.ident[:NH, :NH]),
                 reads=['cTp', 'ident'], writes=[('ps', 7)])
        S.op('act', lambda e: e.activation(lf_tm[:].rearrange("p b h -> p (b h)"), C.PS[6][:, 0:256], AF.Copy),
             reads=[('ps', 6)], writes=['lf_tm'])
        S.op('dve', lambda e: e.tensor_scalar(nck_p[:].rearrange("p b h -> p (b h)"), C.PS[7][:, 0:256], -1.0, None,
                                              ALU.mult), reads=[('ps', 7)], writes=['nck_p'])
        S.dma('act', T['flp'][:, :].rearrange("(b p) h -> p b h", p=128), lf_tm[:], reads=['lf_tm'])
        lfs = sb("lfs", [64, NH])
        S.op('pe', lambda e: e.transpose(C.PS[6][:64, 0:NH], logfT[:, SEQ:NT], C.ident[:NH, :NH]),
             reads=['logfT', 'ident'], writes=[('ps', 6)])
        S.op('act', lambda e: e.activation(lfs[:], C.PS[6][:64, 0:NH], AF.Copy), reads=[('ps', 6)], writes=['lfs'])
        S.dma('act', T['fls'][:, :], lfs[:], reads=['lfs'])
        clf_tm = sb("clf_tm", [128, 8, NH])
        clfT = [sb("clfT%d" % s, [NH, PAST]) for s in range(NS)]
        ccT = [sb("ccT%d" % s, [NH, PAST]) for s in range(NS)]
        cnT = [sb("cnT%d" % s, [NH, DS]) for s in range(NS)]
        nck_c = [sb("nck_c%d" % s, [128, 8, NH]) for s in range(NS)]
        nck_n = [sb("nck_n%d" % s, [DS, NH]) for s in range(NS)]
        for s in range(NS):
            S.dma('sp', clf_tm[:], T['clf'][s].rearrange("(b p) h -> p b h", p=128), writes=['clf_tm'])
            for blk in range(8):
                bk = 6 + blk // 4
                S.op('pe', lambda e, blk=blk, bk=bk: e.transpose(
                    C.PS[bk][:NH, (blk % 4) * 128:(blk % 4 + 1) * 128], clf_tm[:, blk, :], C.ident[:, :]),
                    reads=['clf_tm', 'ident'], writes=[('ps', bk)])
            for hf in range(2):
                S.op('act', lambda e, hf=hf, s=s: e.activation(clfT[s][:, hf * 512:(hf + 1) * 512],
                                                              C.PS[6 + hf][:NH, :], AF.Copy),
                     reads=[('ps', 6 + hf)], writes=[('clfT', s)])
            S.op('dve', lambda e, s=s: e.tensor_tensor_scan(ccT[s][:], onesr[:, 0:PAST], clfT[s][:], 0.0,
                                                            ALU.mult, ALU.add),
                 reads=['onesr', ('clfT', s)], writes=[('ccT', s)])
            S.op('dve', lambda e, s=s: e.tensor_tensor_scan(cnT[s][:], onesr[:, 0:DS],
                                                            logfT[:, SEQ + s * DS:SEQ + (s + 1) * DS],
                                                            ccT[s][:, PAST - 1:PAST], ALU.mult, ALU.add),
                 reads=['onesr', 'logfT', ('ccT', s)], writes=[('cnT', s)])
            for blk in range(8):
                S.op('pe', lambda e, blk=blk, s=s: e.transpose(C.PS[7][:, blk * 16:(blk + 1) * 16],
                                                               ccT[s][:, blk * 128:(blk + 1) * 128],
                                                               C.ident[:NH, :NH]),
                     reads=[('ccT', s), 'ident'], writes=[('ps', 7)])
            S.op('dve', lambda e, s=s: e.tensor_scalar(nck_c[s][:].rearrange("p b h -> p (b h)"),
                                                       C.PS[7][:, 0:128], -1.0, None, ALU.mult),
                 reads=[('ps', 7)], writes=[('nck_c', s)])
            S.op('pe', lambda e, s=s: e.transpose(C.PS[6][:DS, 0:NH], cnT[s][:, :], C.ident[:NH, :NH]),
                 reads=[('cnT', s), 'ident'], writes=[('ps', 6)])
            S.op('dve', lambda e, s=s: e.tensor_scalar(nck_n[s][:], C.PS[6][:DS, 0:NH], -1.0, None, ALU.mult),
                 reads=[('ps', 6)], writes=[('nck_n', s)])

        import os as _os
        _stop = int(_os.environ.get("KSTOP", "99"))
        if _stop <= 1:
            S.barrier()
            return
        qf = [sb("qf%d" % i, [128, SEQ]) for i in range(2)]
        kf = [sb("kf%d" % i, [128, SEQ]) for i in range(2)]
        vf = [sb("vf%d" % i, [128, 16, 128]) for i in range(2)]
        qb_ = [sb("qb%d" % i, [128, SEQ], BF16) for i in range(2)]
        kb_ = [sb("kb%d" % i, [128, SEQ], BF16) for i in range(2)]
        Vb = [sb("Vb%d" % i, [128, 16, 129], BF16) for i in range(2)]
        ots = [sb("ots%d" % i, [128, SEQ], BF16) for i in range(2)]
        for i in range(2):
            S.op('pool', lambda e, i=i: e.memset(Vb[i][:, :, 128:129], 1.0), writes=[('Vb', i)])

        def load_head(h):
            i = h % 2
            S.dma('sp', qf[i][:], T['BqT'][h * 128:(h + 1) * 128, 0:SEQ], writes=[('qf', i)])
            S.dma('sp', kf[i][:], T['BkT'][h * 128:(h + 1) * 128, 0:SEQ], writes=[('kf', i)])
            S.dma('sp', vf[i][:], T['fvp'][:, h * 128:(h + 1) * 128].rearrange("(b p) d -> p b d", p=128),
                  writes=[('vf', i)])
            S.op('pool', lambda e: e.tensor_copy(qb_[i][:], qf[i][:]), reads=[('qf', i)], writes=[('qb', i)])
            S.op('pool', lambda e: e.tensor_copy(kb_[i][:], kf[i][:]), reads=[('kf', i)], writes=[('kb', i)])
            S.op('pool', lambda e: e.tensor_copy(Vb[i][:, :, 0:128], vf[i][:]), reads=[('vf', i)], writes=[('Vb', i)])

        kblocks_p = [KB(b * 128, 128, b, b, b) for b in range(16)]
        load_head(0)
        _kh = int(_os.environ.get("KH", "16"))
        for h in range(_kh):
            i = h % 2
            if h + 1 < _kh:
                load_head(h + 1)
            tag = {'k': [('kb', i)], 'q': [('qb', i)], 'c': ['cTp'], 'nck': ['nck_p'], 'v': [('Vb', i)]}
            fox_core(S, C, P, kb_[i], Vb[i], lambda kb, h=h: nck_p[:kb.n, kb.vb, h:h + 1], qb_[i], cTp,
                     selc[:, h * 128:(h + 1) * 128], 16, 128, kblocks_p,
                     fox_out(S, C, P, 128, ots[i], ('ots', i)), tag)
            S.dma('act', T['OT'][AW + h * 128:AW + (h + 1) * 128, 0:SEQ], ots[i][:], reads=[('ots', i)])

        if _stop <= 2:
            S.barrier()
            return
        ckf = [sb("ckf%d" % i, [128, 8, 128]) for i in range(2)]
        cvf = [sb("cvf%d" % i, [128, 8, 128]) for i in range(2)]
        knf = [sb("knf%d" % i, [128, DS]) for i in range(2)]
        qnf = [sb("qnf%d" % i, [128, DS]) for i in range(2)]
        vnf = [sb("vnf%d" % i, [DS, 128]) for i in range(2)]
        kTs = [sb("kTs%d" % i, [128, PAST + DS], BF16) for i in range(2)]
        qTs = [sb("qTs%d" % i, [128, DS], BF16) for i in range(2)]
        Vs = [sb("Vs%d" % i, [128, 9, 129], BF16) for i in range(2)]
        otss = [sb("otss%d" % i, [128, DS], BF16) for i in range(2)]
        for i in range(2):
            S.op('pool', lambda e, i=i: e.memset(Vs[i][:, :, 128:129], 1.0), writes=[('Vs', i)])
        kblocks_s = [KB(b * 128, 128, b, 0, None) for b in range(8)] + [KB(PAST, DS, 8, 0, 0)]
        it = 0
        for s in range(NS):
            for h in range(NH):
                i = it % 2
                it += 1
                tcol = SEQ + s * DS
                S.dma('sp', ckf[i][:], T['ck'][s][:, h * 128:(h + 1) * 128].rearrange("(b p) d -> p b d", p=128),
                      writes=[('ckf', i)])
                S.dma('sp', cvf[i][:], T['cv'][s][:, h * 128:(h + 1) * 128].rearrange("(b p) d -> p b d", p=128),
                      writes=[('cvf', i)])
                S.dma('sp', knf[i][:], T['BkT'][h * 128:(h + 1) * 128, tcol:tcol + DS], writes=[('knf', i)])
                S.dma('sp', qnf[i][:], T['BqT'][h * 128:(h + 1) * 128, tcol:tcol + DS], writes=[('qnf', i)])
                S.dma('sp', vnf[i][:], T['fvs'][s * DS:(s + 1) * DS, h * 128:(h + 1) * 128], writes=[('vnf', i)])
                for blk in range(8):
                    bt = 6 + blk % 2
                    S.op('pe', lambda e, blk=blk, bt=bt, i=i: e.transpose(C.PS[bt][:, 0:128], ckf[i][:, blk, :],
                                                                          C.ident[:, :]),
                         reads=[('ckf', i), 'ident'], writes=[('ps', bt)])
                    S.op('dve' if blk % 2 else 'act',
                         (lambda e, blk=blk, bt=bt, i=i: e.tensor_copy(kTs[i][:, blk * 128:(blk + 1) * 128],
                                                                       C.PS[bt][:, 0:128])) if blk % 2 else
                         (lambda e, blk=blk, bt=bt, i=i: e.activation(kTs[i][:, blk * 128:(blk + 1) * 128],
                                                                      C.PS[bt][:, 0:128], AF.Copy)),
                         reads=[('ps', bt)], writes=[('kTs', i)])
                S.op('pool', lambda e, i=i: e.tensor_copy(kTs[i][:, PAST:PAST + DS], knf[i][:]),
                     reads=[('knf', i)], writes=[('kTs', i)])
                S.op('pool', lambda e, i=i: e.tensor_copy(qTs[i][:], qnf[i][:]), reads=[('qnf', i)],
                     writes=[('qTs', i)])
                S.op('pool', lambda e, i=i: e.tensor_copy(Vs[i][:, 0:8, 0:128], cvf[i][:]), reads=[('cvf', i)],
                     writes=[('Vs', i)])
                S.op('pool', lambda e, i=i: e.tensor_copy(Vs[i][:DS, 8, 0:128], vnf[i][:]), reads=[('vnf', i)],
                     writes=[('Vs', i)])
                tag = {'k': [('kTs', i)], 'q': [('qTs', i)], 'c': [('cnT', s)],
                       'nck': [('nck_c', s), ('nck_n', s)], 'v': [('Vs', i)]}

                def ncks(kb, s=s, h=h):
                    if kb.vb < 8:
                        return nck_c[s][:kb.n, kb.vb, h:h + 1]
                    return nck_n[s][:kb.n, h:h + 1]
                fox_core(S, C, P, kTs[i], Vs[i], ncks, qTs[i], cnT[s], selc[:, h * 128:(h + 1) * 128], 1, DS,
                         kblocks_s, fox_out(S, C, P, DS, otss[i], ('otss', i)), tag)
                S.dma('act', T['OT'][AW + h * 128:AW + (h + 1) * 128, tcol:tcol + DS], otss[i][:],
                      reads=[('otss', i)])
    S.barrier()


def phase_B(S, nc, C, T):
    with ExitStack() as st:
        sb = lambda name, shape, dt=F32: st.enter_context(nc.sbuf_tensor("B_" + name, shape, dt))
        CM = 64
        U_ = C.cmat[:, 0:128]
        SL_ = C.cmat[:, 128:256]
        ONES = C.cmat[:, 256:384]
        SU_ = C.cmat[:, 384:512]
        C128 = C.cmat[:, 512:640]
        I_ = C.ident
        cw = sb("cw", [128, 48, 4])
        nA = sb("nA", [CM, NH])
        dtb = sb("dtb", [CM, NH])
        normw = sb("normw", [128, 1])
        S.dma('sp', cw[:].rearrange("p j r -> p (j r)"), T['convw_t'][:, :], writes=['cw'])
        S.dma('sp', nA[:], T['alog_b'][0:CM, :], writes=['nA'])
        S.dma('sp', dtb[:], T['dtb_b'][0:CM, :], writes=['dtb'])
        S.dma('sp', normw[:], T['normw_c'][:, :], writes=['normw'])
        S.op('act', lambda e: e.activation(nA[:], nA[:], AF.Exp), reads=['nA'], writes=['nA'])
        S.op('dve', lambda e: e.tensor_scalar(nA[:], nA[:], -1.0, None, ALU.mult), reads=['nA'], writes=['nA'])
        NCHM = 32
        ab = sb("ab", [CM, NCHM, 32])
        g_all = sb("g_all", [CM, NCHM, NH])
        lnb = sb("lnb_unused", [CM, 1])
        beta = sb("beta", [CM, NCHM, NH])
        ecum = sb("ecum", [CM, NCHM, NH])
        ekd = sb("ekd", [CM, NCHM, NH])
        bec = sb("bec", [CM, NCHM, NH])
        gtot = sb("gtot", [128, NCHM, NH])
        xin = [sb("xin%d" % i, [128, 48, 3 + CM]) for i in range(2)]
        zin = sb("zin", [128, NH, CM])
        zs = sb("zs", [128, NH, CM])
        ta = sb("ta", [128, 48, CM])
        tb = sb("tb", [128, 48, CM])
        ys = sb("ys", [128, 48, CM])
        qkn = sb("qkn", [128, 32, CM])
        Sst = sb("Sst", [128, NH, 128])
        Stmp2 = [sb("Stmp%d" % i, [128, 4, 128]) for i in range(2)]
        OTb = sb("OTb", [128, NH, 256], BF16)
        q4 = lambda name, w: [sb(name + str(i), [CM, 4, w]) for i in range(2)]
        rhsG = q4("rhsG", CM)
        rhsGT = q4("rhsGT", CM)
        rhsB = q4("rhsB", CM)
        rhsE = q4("rhsE", CM)
        ex1 = q4("ex1", CM)
        ex2 = q4("ex2", CM)
        decU = q4("decU", CM)
        NN = [q4("NNa", CM), q4("NNb", CM)]
        NT_ = [q4("NTa", CM), q4("NTb", CM)]
        TT = [q4("TTa", CM), q4("TTb", CM)]
        bv = q4("bv", 128)
        bek = q4("bek", 128)
        wv = sb("wv", [CM, NH, 128])
        kdec = sb("kdec", [CM, NH, 128])
        qkm = sb("qkm", [CM, NH, CM])
        wkT = sb("wkT", [128, NH, CM])
        qdT = sb("qdT", [128, NH, CM])
        u_ = q4("u_", 128)
        osq = q4("osq", 128)
        on_ = q4("on_", 128)
        ms = [sb("ms%d" % i, [CM, 4]) for i in range(2)]
        rstd = [sb("rstd%d" % i, [CM, 4]) for i in range(2)]
        PS = C.PS
        nb = C.ps_next
        V = lambda fn, r, w: S.op('dve', fn, reads=r, writes=w)
        A = lambda fn, r, w: S.op('act', fn, reads=r, writes=w)
        G = lambda fn, r, w: S.op('pool', fn, reads=r, writes=w)
        PE = lambda fn, r, w: S.op('pe', fn, reads=r, writes=w)

        def seq(tok0, Tlen, Cn, conv_hist, s0, s_out, conv_out, name):
            NCH = Tlen // Cn
            S.dma('sp', ab[:Cn, :NCH, :], T['Aab'][tok0:tok0 + Tlen, :].rearrange("(n c) x -> c n x", c=Cn),
                  writes=['ab'])
            gv = g_all[:Cn, :NCH, :]
            V(lambda e: e.tensor_tensor(gv, ab[:Cn, :NCH, 0:NH], dtb[:Cn, None, :].to_broadcast([Cn, NCH, NH]),
                                        ALU.add), ['ab', 'dtb'], ['g_all'])
            A(lambda e: e.activation(gv, gv, AF.Exp), ['g_all'], ['g_all'])
            A(lambda e: e.activation(gv, gv, AF.Ln, bias=1.0, scale=1.0), ['g_all'], ['g_all'])
            V(lambda e: e.tensor_tensor(gv, gv, nA[:Cn, None, :].to_broadcast([Cn, NCH, NH]), ALU.mult),
              ['g_all', 'nA'], ['g_all'])
            bvw = beta[:Cn, :NCH, :]
            A(lambda e: e.activation(bvw, ab[:Cn, :NCH, NH:2 * NH], AF.Sigmoid), ['ab'], ['beta'])
            g2 = g_all[:Cn, :NCH, :].rearrange("c n h -> c (n h)") if NCH * NH <= 512 else None
            ncol = NCH * NH
            b1 = nb()
            PE(lambda e: e.matmul(PS[b1][:Cn, :ncol], lhsT=U_[:Cn, :Cn], rhs=g_all[:Cn, :NCH, :], start=True, stop=True),
               ['g_all', 'cmat'], [('ps', b1)])
            A(lambda e: e.activation(ecum[:Cn, :NCH, :].rearrange("c n h -> c (n h)"), PS[b1][:Cn, :ncol], AF.Exp),
              [('ps', b1)], ['ecum'])
            b2 = nb()
            PE(lambda e: e.matmul(PS[b2][:Cn, :ncol], lhsT=SL_[:Cn, :Cn], rhs=g_all[:Cn, :NCH, :], start=True, stop=True),
               ['g_all', 'cmat'], [('ps', b2)])
            A(lambda e: e.activation(ekd[:Cn, :NCH, :].rearrange("c n h -> c (n h)"), PS[b2][:Cn, :ncol], AF.Exp),
              [('ps', b2)], ['ekd'])
            b3 = nb()
            PE(lambda e: e.matmul(PS[b3][:, :ncol], lhsT=ONES[:Cn, :], rhs=g_all[:Cn, :NCH, :], start=True, stop=True),
               ['g_all', 'cmat'], [('ps', b3)])
            A(lambda e: e.activation(gtot[:, :NCH, :].rearrange("c n h -> c (n h)"), PS[b3][:, :ncol], AF.Exp),
              [('ps', b3)], ['gtot'])
            V(lambda e: e.tensor_tensor(bec[:Cn, :NCH, :], beta[:Cn, :NCH, :], ecum[:Cn, :NCH, :], ALU.mult),
              ['beta', 'ecum'], ['bec'])
            if s0 is None:
                V(lambda e: e.memset(Sst[:], 0.0), [], [('Sst', q) for q in range(4)])
            else:
                S.dma('sp', Sst[:], s0.rearrange("h k v -> k h v"), writes=[('Sst', q) for q in range(4)])

            def load_x(n):
                xi = xin[n % 2]
                t0 = tok0 + n * Cn
                if n == 0:
                    if conv_hist is None:
                        V(lambda e: e.memset(xi[:, :, 0:3], 0.0), [], [('xin', n % 2)])
                    else:
                        S.dma('sp', xi[:, :, 0:3], conv_hist.rearrange("p (j r) -> p j r", r=3), writes=[('xin', n % 2)])
                    S.dma('sp', xi[:, :, 3:3 + Cn], T['AqkvT'][:, t0:t0 + Cn].rearrange("(j p) t -> p j t", p=128),
                          writes=[('xin', n % 2)])
                else:
                    S.dma('sp', xi[:, :, 0:3 + Cn],
                          T['AqkvT'][:, t0 - 3:t0 + Cn].rearrange("(j p) t -> p j t", p=128), writes=[('xin', n % 2)])

            load_x(0)

            def chunk(n):
                if n + 1 < NCH:
                    load_x(n + 1)
                xi = xin[n % 2]
                xt = ('xin', n % 2)
                t0 = tok0 + n * Cn
                S.dma('sp', zin[:, :, :Cn], T['AzT'][:, t0:t0 + Cn].rearrange("(j p) t -> p j t", p=128), writes=['zin'])
                if n == NCH - 1:
                    for r in range(3):
                        S.dma('act', conv_out[r:r + 1, :].rearrange("o (j p) -> p j o", p=128), xi[:, :, Cn + r:Cn + r + 1],
                              reads=[xt], allow_slow_non_contiguous=True)
                cwb = lambda r: cw[:, :, r:r + 1].to_broadcast([128, 48, Cn])
                V(lambda e: e.tensor_tensor(ta[:, :, :Cn], xi[:, :, 0:Cn], cwb(0), ALU.mult), [xt, 'cw'], ['ta'])
                G(lambda e: e.tensor_tensor(tb[:, :, :Cn], xi[:, :, 1:1 + Cn], cwb(1), ALU.mult), [xt, 'cw'], ['tb'])
                V(lambda e: e.tensor_tensor(ys[:, :, :Cn], xi[:, :, 2:2 + Cn], cwb(2), ALU.mult), [xt, 'cw'], ['ys'])
                G(lambda e: e.tensor_tensor(tb[:, :, :Cn], tb[:, :, :Cn], ta[:, :, :Cn], ALU.add), ['ta', 'tb'], ['tb'])
                V(lambda e: e.tensor_tensor(ta[:, :, :Cn], xi[:, :, 3:3 + Cn], cwb(3), ALU.mult), [xt, 'cw'], ['ta'])
                V(lambda e: e.tensor_tensor(ys[:, :, :Cn], ys[:, :, :Cn], ta[:, :, :Cn], ALU.add), ['ta', 'ys'], ['ys'])
                G(lambda e: e.tensor_tensor(tb[:, :, :Cn], tb[:, :, :Cn], ys[:, :, :Cn], ALU.add), ['tb', 'ys'], ['tb'])
                A(lambda e: e.activation(ys[:, :, :Cn], tb[:, :, :Cn], AF.Silu), ['tb'], ['ys'])
                A(lambda e: e.activation(zs[:, :, :Cn], zin[:, :, :Cn], AF.Silu), ['zin'], ['zs'])
                V(lambda e: e.tensor_tensor(ta[:, 0:32, :Cn], ys[:, 0:32, :Cn], ys[:, 0:32, :Cn], ALU.mult), ['ys'], ['ta'])
                hp = 512 // Cn
                for part in range(32 // hp):
                    j0 = part * hp
                    isq = j0 < 16
                    bb = nb()
                    PE(lambda e, bb=bb, j0=j0, isq=isq: e.matmul(
                        PS[bb][:, :hp * Cn], lhsT=(C128 if isq else ONES), rhs=ta[:, j0:j0 + hp, :Cn],
                        start=True, stop=True), ['ta', 'cmat'], [('ps', bb)])
                    A(lambda e, bb=bb, j0=j0, isq=isq: e.activation(
                        tb[:, j0:j0 + hp, :Cn].rearrange("p j c -> p (j c)") if Cn == CM else tb[:, j0:j0 + hp, :Cn],
                        PS[bb][:, :hp * Cn] if Cn == CM else PS[bb][:, :hp * Cn].rearrange("p (j c) -> p j c", c=Cn),
                        AF.Sqrt, bias=(128e-6 if isq else 1e-6), scale=1.0),
                      [('ps', bb)], ['tb'])
                V(lambda e: e.reciprocal(tb[:, 0:32, :Cn], tb[:, 0:32, :Cn]), ['tb'], ['tb'])
                V(lambda e: e.tensor_tensor(qkn[:, :, :Cn], ys[:, 0:32, :Cn], tb[:, 0:32, :Cn], ALU.mult), ['ys', 'tb'],
                  ['qkn'])
                def quad(qd):
                    h0 = qd * 4
                    p = qd % 2
                    gq = g_all[:Cn, n, h0:h0 + 4]
                    W = 4 * Cn
                    flat = lambda t: t[:Cn, :, :Cn]
                    V(lambda e, p=p, gq=gq: e.tensor_tensor(flat(rhsG[p]), U_[:Cn, None, :Cn].to_broadcast([Cn, 4, Cn]),
                                                            gq[:, :, None].to_broadcast([Cn, 4, Cn]), ALU.mult),
                      ['g_all', 'cmat'], [('rhsG', p)])
                    V(lambda e, p=p, gq=gq: e.tensor_tensor(flat(rhsGT[p]), SL_[:Cn, None, :Cn].to_broadcast([Cn, 4, Cn]),
                                                            gq[:, :, None].to_broadcast([Cn, 4, Cn]), ALU.mult),
                      ['g_all', 'cmat'], [('rhsGT', p)])
                    V(lambda e, p=p, h0=h0: e.tensor_tensor(flat(rhsB[p]), I_[:Cn, None, :Cn].to_broadcast([Cn, 4, Cn]),
                                                            beta[:Cn, n, h0:h0 + 4][:, :, None].to_broadcast([Cn, 4, Cn]),
                                                            ALU.mult), ['beta', 'ident'], [('rhsB', p)])
                    V(lambda e, p=p, h0=h0: e.tensor_tensor(flat(rhsE[p]), I_[:Cn, None, :Cn].to_broadcast([Cn, 4, Cn]),
                                                            ecum[:Cn, n, h0:h0 + 4][:, :, None].to_broadcast([Cn, 4, Cn]),
                                                            ALU.mult), ['ecum', 'ident'], [('rhsE', p)])
                    bDT, bD, bB, bE, bKQ = nb(), nb(), nb(), nb(), nb()
                    v3 = lambda b, P_=Cn: PS[b][:P_, :W].rearrange("p (h c) -> p h c", c=Cn)
                    PE(lambda e, p=p, b=bDT: e.matmul(PS[b][:Cn, :W], lhsT=SL_[:Cn, :Cn], rhs=flat(rhsG[p]), start=True, stop=True),
                       [('rhsG', p), 'cmat'], [('ps', bDT)])
                    PE(lambda e, p=p, b=bD: e.matmul(PS[b][:Cn, :W], lhsT=U_[:Cn, :Cn], rhs=flat(rhsGT[p]), start=True, stop=True),
                       [('rhsGT', p), 'cmat'], [('ps', bD)])
                    PE(lambda e, p=p, b=bB: e.matmul(PS[b][:Cn, :W], lhsT=ONES[:Cn, :Cn], rhs=flat(rhsB[p]), start=True, stop=True),
                       [('rhsB', p), 'cmat'], [('ps', bB)])
                    PE(lambda e, p=p, b=bE: e.matmul(PS[b][:, :W], lhsT=ONES[:Cn, :], rhs=flat(rhsE[p]), start=True, stop=True),
                       [('rhsE', p), 'cmat'], [('ps', bE)])
                    for hh in range(4):
                        h = h0 + hh
                        PE(lambda e, h=h, hh=hh, b=bKQ: e.matmul(PS[b][:Cn, hh * Cn:(hh + 1) * Cn], lhsT=qkn[:, 16 + h, :Cn],
                                                                 rhs=qkn[:, 16 + h, :Cn], start=True, stop=True),
                           ['qkn'], [('ps', bKQ)])
                    bQK = nb()
                    for hh in range(4):
                        h = h0 + hh
                        PE(lambda e, h=h, hh=hh, b=bQK: e.matmul(PS[b][:Cn, hh * Cn:(hh + 1) * Cn], lhsT=qkn[:, 16 + h, :Cn],
                                                                 rhs=qkn[:, h, :Cn], start=True, stop=True),
                           ['qkn'], [('ps', bQK)])
                    A(lambda e, p=p, b=bDT: e.activation(flat(ex1[p]), v3(b), AF.Exp), [('ps', bDT)], [('ex1', p)])
                    A(lambda e, p=p, b=bD: e.activation(flat(ex2[p]), v3(b), AF.Exp), [('ps', bD)], [('ex2', p)])
                    V(lambda e, p=p: e.tensor_tensor(flat(decU[p]), flat(ex1[p]), U_[:Cn, None, :Cn].to_broadcast([Cn, 4, Cn]),
                                                     ALU.mult), [('ex1', p), 'cmat'], [('decU', p)])
                    V(lambda e, p=p, b=bQK, h0=h0: e.tensor_tensor(qkm[:Cn, h0:h0 + 4, :Cn], v3(b), flat(decU[p]), ALU.mult),
                      [('ps', bQK), ('decU', p)], [('qkm', qd)])
                    V(lambda e, p=p: e.tensor_tensor(flat(ex1[p]), flat(ex1[p]), SU_[:Cn, None, :Cn].to_broadcast([Cn, 4, Cn]),
                                                     ALU.mult), [('ex1', p), 'cmat'], [('ex1', p)])
                    V(lambda e, p=p, b=bKQ: e.tensor_tensor(flat(ex1[p]), v3(b), flat(ex1[p]), ALU.mult),
                      [('ps', bKQ), ('ex1', p)], [('ex1', p)])
                    V(lambda e, p=p, b=bB: e.scalar_tensor_tensor(
                        out=NT_[0][p][:Cn, :, :Cn].rearrange("p h c -> p (h c)") if Cn == CM else NT_[0][p][:Cn, :, :Cn],
                        in0=ex1[p][:Cn, :, :Cn].rearrange("p h c -> p (h c)") if Cn == CM else ex1[p][:Cn, :, :Cn],
                        scalar=-1.0,
                        in1=PS[b][:Cn, :W] if Cn == CM else v3(b), op0=ALU.mult, op1=ALU.mult),
                      [('ps', bB), ('ex1', p)], [('NT0', p)])
                    V(lambda e, p=p: e.tensor_tensor(flat(ex2[p]), flat(ex2[p]), SL_[:Cn, None, :Cn].to_broadcast([Cn, 4, Cn]),
                                                     ALU.mult), [('ex2', p), 'cmat'], [('ex2', p)])
                    V(lambda e, p=p, b=bKQ: e.tensor_tensor(flat(ex2[p]), v3(b), flat(ex2[p]), ALU.mult),
                      [('ps', bKQ), ('ex2', p)], [('ex2', p)])
                    V(lambda e, p=p, h0=h0: e.tensor_tensor(flat(ex2[p]), flat(ex2[p]),
                                                            beta[:Cn, n, h0:h0 + 4][:, :, None].to_broadcast([Cn, 4, Cn]),
                                                            ALU.mult), [('ex2', p), 'beta'], [('ex2', p)])
                    V(lambda e, p=p: e.tensor_scalar(flat(NN[0][p]), flat(ex2[p]), -1.0, None, ALU.mult),
                      [('ex2', p)], [('NN0', p)])
                    V(lambda e, b=bE, h0=h0: e.tensor_tensor(qdT[:, h0:h0 + 4, :Cn], qkn[:, h0:h0 + 4, :Cn],
                                                             PS[b][:, :W].rearrange("p (h c) -> p h c", c=Cn), ALU.mult),
                      [('ps', bE), 'qkn'], [('qdT', qd)])
                    V(lambda e, p=p: e.tensor_tensor(flat(TT[0][p]), flat(NT_[0][p]), I_[:Cn, None, :Cn].to_broadcast([Cn, 4, Cn]),
                                                     ALU.add), [('NT0', p), 'ident'], [('TT0', p)])
                    yield
                    nlev = 5 if Cn == 64 else 4
                    for lv in range(nlev):
                        yield
                        a, bq = lv % 2, (lv + 1) % 2
                        last = lv == nlev - 1
                        bN, bNT, bT = nb(), (None if last else nb()), nb()
                        for hh in range(4):
                            PE(lambda e, hh=hh, a=a, p=p, b=bN: e.matmul(PS[b][:Cn, hh * Cn:(hh + 1) * Cn], lhsT=NT_[a][p][:Cn, hh, :Cn],
                                                                         rhs=NN[a][p][:Cn, hh, :Cn], start=True, stop=True),
                               [('NT%d' % a, p), ('NN%d' % a, p)], [('ps', bN)])
                        if not last:
                            for hh in range(4):
                                PE(lambda e, hh=hh, a=a, p=p, b=bNT: e.matmul(PS[b][:Cn, hh * Cn:(hh + 1) * Cn], lhsT=NN[a][p][:Cn, hh, :Cn],
                                                                              rhs=NT_[a][p][:Cn, hh, :Cn], start=True, stop=True),
                                   [('NT%d' % a, p), ('NN%d' % a, p)], [('ps', bNT)])
                        yield
                        A(lambda e, bq=bq, p=p, b=bN: e.activation(flat(NN[bq][p]), v3(b), AF.Copy), [('ps', bN)], [('NN%d' % bq, p)])
                        if not last:
                            V(lambda e, bq=bq, p=p, b=bNT: e.tensor_copy(flat(NT_[bq][p]), v3(b)), [('ps', bNT)], [('NT%d' % bq, p)])
                        for hh in range(4):
                            PE(lambda e, hh=hh, a=a, bq=bq, p=p, b=bT: e.matmul(PS[b][:Cn, hh * Cn:(hh + 1) * Cn], lhsT=NN[bq][p][:Cn, hh, :Cn],
                                                                                rhs=TT[a][p][:Cn, hh, :Cn], start=True, stop=True),
                               [('NN%d' % bq, p), ('TT%d' % a, p)], [('ps', bT)])
                        yield
                        V(lambda e, a=a, bq=bq, p=p, b=bT: e.tensor_tensor(flat(TT[bq][p]), flat(TT[a][p]), v3(b), ALU.add),
                          [('ps', bT), ('TT%d' % a, p)], [('TT%d' % bq, p)])
                    yield
                    tf = nlev % 2
                    TTf = TT[tf][p]
                    ttag = ('TT%d' % tf, p)
                    bK, bV = nb(), nb()
                    for hh in range(4):
                        h = h0 + hh
                        PE(lambda e, h=h, hh=hh, b=bK: e.transpose(PS[b][:Cn, hh * 128:(hh + 1) * 128], qkn[:, 16 + h, :Cn], I_[:, :]),
                           ['qkn', 'ident'], [('ps', bK)])
                        PE(lambda e, h=h, hh=hh, b=bV: e.transpose(PS[b][:Cn, hh * 128:(hh + 1) * 128], ys[:, 32 + h, :Cn], I_[:, :]),
                           ['ys', 'ident'], [('ps', bV)])
                    k3 = lambda b: PS[b][:Cn, :512].rearrange("p (h d) -> p h d", d=128)
                    bc3 = lambda t, h0=h0: t[:Cn, n, h0:h0 + 4][:, :, None].to_broadcast([Cn, 4, 128])
                    V(lambda e, p=p, b=bK: e.tensor_tensor(bek[p][:Cn, :, :], k3(b), bc3(bec), ALU.mult), [('ps', bK), 'bec'],
                      [('bek', p)])
                    V(lambda e, b=bK, h0=h0: e.tensor_tensor(kdec[:Cn, h0:h0 + 4, :], k3(b), bc3(ekd), ALU.mult),
                      [('ps', bK), 'ekd'], [('kdec', qd)])
                    V(lambda e, p=p, b=bV: e.tensor_tensor(bv[p][:Cn, :, :], k3(b), bc3(beta), ALU.mult), [('ps', bV), 'beta'],
                      [('bv', p)])
                    bWV, bWK = nb(), nb()
                    for hh in range(4):
                        PE(lambda e, hh=hh, p=p, b=bWV, TTf=TTf: e.matmul(PS[b][:Cn, hh * 128:(hh + 1) * 128], lhsT=TTf[:Cn, hh, :Cn],
                                                                          rhs=bv[p][:Cn, hh, :], start=True, stop=True),
                           [ttag, ('bv', p)], [('ps', bWV)])
                        PE(lambda e, hh=hh, p=p, b=bWK, TTf=TTf: e.matmul(PS[b][:, hh * Cn:(hh + 1) * Cn], lhsT=bek[p][:Cn, hh, :],
                                                                          rhs=TTf[:Cn, hh, :Cn], start=True, stop=True),
                           [ttag, ('bek', p)], [('ps', bWK)])
                    A(lambda e, b=bWV, h0=h0: e.activation(wv[:Cn, h0:h0 + 4, :], k3(b), AF.Copy), [('ps', bWV)], [('wv', qd)])
                    A(lambda e, b=bWK, h0=h0: e.activation(wkT[:, h0:h0 + 4, :Cn], PS[b][:, :W].rearrange("p (h c) -> p h c", c=Cn),
                                                           AF.Copy), [('ps', bWK)], [('wkT', qd)])
                    bU = nb()
                    for hh in range(4):
                        h = h0 + hh
                        PE(lambda e, h=h, hh=hh, b=bU: e.matmul(PS[b][:Cn, hh * 128:(hh + 1) * 128], lhsT=wkT[:, h, :Cn], rhs=Sst[:, h, :],
                                                                start=True, stop=True), [('wkT', qd), ('Sst', qd)], [('ps', bU)])
                    V(lambda e, p=p, b=bU, h0=h0: e.tensor_tensor(u_[p][:Cn, :, :], wv[:Cn, h0:h0 + 4, :], k3(b), ALU.subtract),
                      [('ps', bU), ('wv', qd)], [('u', p)])
                    bO = nb()
                    for hh in range(4):
                        h = h0 + hh
                        PE(lambda e, h=h, hh=hh, b=bO: e.matmul(PS[b][:Cn, hh * 128:(hh + 1) * 128], lhsT=qdT[:, h, :Cn], rhs=Sst[:, h, :],
                                                                start=True, stop=False), [('qdT', qd), ('Sst', qd)], [('ps', bO)])
                        PE(lambda e, h=h, hh=hh, p=p, b=bO: e.matmul(PS[b][:Cn, hh * 128:(hh + 1) * 128], lhsT=qkm[:Cn, h, :Cn],
                                                                     rhs=u_[p][:Cn, hh, :], start=False, stop=True),
                           [('qkm', qd), ('u', p)], [('ps', bO)])
                    bS_ = nb()
                    for hh in range(4):
                        h = h0 + hh
                        PE(lambda e, h=h, hh=hh, p=p, b=bS_: e.matmul(PS[b][:, hh * 128:(hh + 1) * 128], lhsT=kdec[:Cn, h, :],
                                                                      rhs=u_[p][:Cn, hh, :], start=True, stop=True),
                           [('kdec', qd), ('u', p)], [('ps', bS_)])
                    A(lambda e, p=p, b=bO: e.activation(osq[p][:Cn, :, :], k3(b), AF.Square), [('ps', bO)], [('osq', p)])
                    V(lambda e, p=p: e.tensor_reduce(ms[p][:Cn, :], osq[p][:Cn, :, :], AX.X, ALU.add), [('osq', p)], [('ms', p)])
                    A(lambda e, p=p: e.activation(rstd[p][:Cn, :], ms[p][:Cn, :], AF.Sqrt, bias=1e-6, scale=1.0 / 128), [('ms', p)],
                      [('rstd', p)])
                    V(lambda e, p=p: e.reciprocal(rstd[p][:Cn, :], rstd[p][:Cn, :]), [('rstd', p)], [('rstd', p)])
                    V(lambda e, p=p, b=bO: e.tensor_tensor(on_[p][:Cn, :, :], k3(b),
                                                           rstd[p][:Cn, :][:, :, None].to_broadcast([Cn, 4, 128]), ALU.mult),
                      [('ps', bO), ('rstd', p)], [('on', p)])
                    bOT = nb()
                    for hh in range(4):
                        PE(lambda e, hh=hh, p=p, b=bOT: e.transpose(PS[b][:, hh * Cn:(hh + 1) * Cn], on_[p][:Cn, hh, :], I_[:Cn, :Cn]),
                           [('on', p), 'ident'], [('ps', bOT)])
                    oc = (n % (256 // Cn)) * Cn
                    V(lambda e, b=bOT, h0=h0, oc=oc: e.scalar_tensor_tensor(
                        out=OTb[:, h0:h0 + 4, oc:oc + Cn], in0=PS[b][:, :W].rearrange("p (h c) -> p h c", c=Cn),
                        scalar=normw[:, 0:1], in1=zs[:, h0:h0 + 4, :Cn], op0=ALU.mult, op1=ALU.mult),
                      [('ps', bOT), 'zs', 'normw'], [('OTb', qd)])
                    V(lambda e, h0=h0: e.tensor_tensor(Stmp2[p][:, :, :], Sst[:, h0:h0 + 4, :],
                                                       gtot[:, n, h0:h0 + 4][:, :, None].to_broadcast([128, 4, 128]), ALU.mult),
                      [('Sst', qd), 'gtot'], [('Stmp', qd)])
                    V(lambda e, h0=h0, b=bS_: e.tensor_tensor(Sst[:, h0:h0 + 4, :], Stmp2[p][:, :, :],
                                                              PS[b][:, :512].rearrange("p (h d) -> p h d", d=128), ALU.add),
                      [('ps', bS_), ('Stmp', qd)], [('Sst', qd)])
                for pair_ in ((0, 1), (2, 3)):
                    alive = [quad(q_) for q_ in pair_]
                    while alive:
                        for g_ in list(alive):
                            try:
                                next(g_)
                            except StopIteration:
                                alive.remove(g_)
                per = 256 // Cn
                if (n + 1) % per == 0 or n == NCH - 1:
                    nfl = ((n % per) + 1) * Cn
                    tf0 = tok0 + (n // per) * 256
                    S.dma('act', T['OT'][0:AW, tf0:tf0 + nfl].rearrange("(h p) t -> p h t", p=128), OTb[:, :, 0:nfl],
                          reads=[('OTb', q_) for q_ in range(4)])
            for n in range(NCH):
                chunk(n)
            S.dma('act', s_out.rearrange("h k v -> k h v"), Sst[:], reads=[('Sst', q) for q in range(4)])

        seq(0, SEQ, 64, None, None, T['sgp'][:, :, :], T['scp'][:, :], "p")
        for s in range(NS):
            seq(SEQ + s * DS, DS, DS, T['sconv_t'][s], T['sg'][s], T['sgs'][s], T['scs'][s], "s%d" % s)
    S.barrier()


def ln_rows(S, nc, L, n, vt, vtag, gB, bB, out_ap_sb, out_tag):
    i = L.i % 2
    L.i += 1
    sm, ssq, rs = L.sm[i], L.ssq[i], L.rs[i]
    S.op('act', lambda e: e.activation(L.junk[:n, :], vt[:n, :], AF.Identity, accum_out=sm[:n, :]),
         reads=[vtag], writes=[('lnsm', i), 'lnjunk'])
    S.op('dve', lambda e: e.tensor_scalar(sm[:n, :], sm[:n, :], 1.0 / D, None, ALU.mult), reads=[('lnsm', i)],
         writes=[('lnsm', i)])
    S.op('dve', lambda e: e.tensor_scalar(vt[:n, :], vt[:n, :], sm[:n, 0:1], None, ALU.subtract),
         reads=[vtag, ('lnsm', i)], writes=[vtag])
    S.op('act', lambda e: e.activation(L.junk[:n, :], vt[:n, :], AF.Square, accum_out=ssq[:n, :]),
         reads=[vtag], writes=[('lnssq', i), 'lnjunk'])
    S.op('act', lambda e: e.activation(rs[:n, :], ssq[:n, :], AF.Sqrt, bias=L.eps[:n, 0:1], scale=1.0 / D),
         reads=[('lnssq', i), 'lneps'], writes=[('lnrs', i)])
    S.op('dve', lambda e: e.reciprocal(rs[:n, :], rs[:n, :]), reads=[('lnrs', i)], writes=[('lnrs', i)])
    S.op('dve', lambda e: e.scalar_tensor_tensor(out=vt[:n, :], in0=vt[:n, :], scalar=rs[:n, 0:1], in1=gB[:n, :],
                                                 op0=ALU.mult, op1=ALU.mult),
         reads=[vtag, ('lnrs', i), 'lng'], writes=[vtag])
    S.op('dve', lambda e: e.tensor_tensor(out_ap_sb[:n, :], vt[:n, :], bB[:n, :], ALU.add),
         reads=[vtag, 'lnb'], writes=[out_tag])


def ln_setup(S, nc, st, T, gname, bname, pref):
    L = Ctx()
    sb = lambda name, shape, dt=F32: st.enter_context(nc.sbuf_tensor(pref + name, shape, dt))
    L.i = 0
    L.sm = [sb("sm%d" % i, [128, 1]) for i in range(2)]
    L.ssq = [sb("ssq%d" % i, [128, 1]) for i in range(2)]
    L.rs = [sb("rs%d" % i, [128, 1]) for i in range(2)]
    L.eps = sb("eps", [128, 1])
    L.junk = sb("junk", [128, D], BF16)
    L.g = sb("g", [128, D])
    L.b = sb("b", [128, D])
    S.op('dve', lambda e: e.memset(L.eps[:], LN_EPS), writes=['lneps'])
    S.dma('sp', L.g[:], T[gname][:, :], writes=['lng'])
    S.dma('sp', L.b[:], T[bname][:, :], writes=['lnb'])
    return L


def row_tiles():
    return [(i * 128, 128) for i in range(16)] + [(SEQ, NS * DS)]


def x_rows(T, r0, n):
    return T['xp'][r0:r0 + n, :] if r0 < SEQ else T['xs'][r0 - SEQ:r0 - SEQ + n, :]


def y_rows(T, r0, n):
    return T['yp'][r0:r0 + n, :] if r0 < SEQ else T['ys'][r0 - SEQ:r0 - SEQ + n, :]


def phase_D(S, nc, C, T):
    import os as _os
    _kd = int(_os.environ.get("KD", "3"))
    with ExitStack() as st:
        if not (_kd & 1):
            st.close()
            return phase_D2(S, nc, C, T, _kd)
        C.XT = st.enter_context(nc.sbuf_tensor("D_XT", [128, KC, GT], BF16))
        C.wst = [st.enter_context(nc.sbuf_tensor("D_wst%d" % i, [128, KC, 128], F32)) for i in range(2)]
        C.wbf = [st.enter_context(nc.sbuf_tensor("D_wbf%d" % i, [128, KC, 512], BF16)) for i in range(2)]
        mst = [st.enter_context(nc.sbuf_tensor("D_mst%d" % i, [128, 512], F32)) for i in range(3)]
        C.wst_i = 0
        cnt = {'m': 0}
        for g in range(2):
            S.dma('sp', C.XT[:, :, 0:1024], T['OT'][:, g * 1024:(g + 1) * 1024].rearrange("(kc p) t -> p kc t", p=128),
                  writes=[('XT', i) for i in range(8)])
            S.dma('sp', C.XT[:, :, 1024:GT], T['OT'][:, SEQ + g * DS:SEQ + (g + 1) * DS].rearrange("(kc p) t -> p kc t", p=128),
                  writes=[('XT', 8)])
            load_w(S, C, T['w_out'], 0, 512, 0)
            for cg in range(8):
                if cg + 1 < 8:
                    load_w(S, C, T['w_out'], (cg + 1) * 512, 512, (cg + 1) % 2)

                def emit(b, c0, n, cg=cg, g=g):
                    mi = cnt['m'] % 3
                    cnt['m'] += 1
                    mt = mst[mi]
                    if cnt['m'] % 2:
                        S.op('act', lambda e: e.activation(mt[:n, :], C.PS[b][:n, :], AF.Copy), reads=[('ps', b)],
                             writes=[('mst', mi)])
                    else:
                        S.op('dve', lambda e: e.tensor_copy(mt[:n, :], C.PS[b][:n, :]), reads=[('ps', b)],
                             writes=[('mst', mi)])
                    r0 = tokmap(g, c0, n)
                    S.dma('act', T['MIX'][r0:r0 + n, cg * 512:(cg + 1) * 512], mt[:n, :], reads=[('mst', mi)])
                gemm_T(S, C, cg % 2, 512, emit)
    S.barrier()
    phase_D2(S, nc, C, T, _kd)


def phase_D2(S, nc, C, T, _kd):
    if not (_kd & 2):
        return
    with ExitStack() as st:
        L = ln_setup(S, nc, st, T, 'ln1g_b', 'ln1b_b', "D_ln")
        xt_ = [st.enter_context(nc.sbuf_tensor("D_x%d" % i, [128, D], F32)) for i in range(2)]
        mx_ = [st.enter_context(nc.sbuf_tensor("D_m%d" % i, [128, D], F32)) for i in range(2)]

        def tile(ti, r0, n):
            i = ti % 2
            S.dma('sp', xt_[i][:n, :], x_rows(T, r0, n), writes=[('dx', i)])
            S.dma('sp', mx_[i][:n, :], T['MIX'][r0:r0 + n, :], writes=[('dm', i)])
            S.op('dve', lambda e: e.scalar_tensor_tensor(out=xt_[i][:n, :], in0=xt_[i][:n, :], scalar=float(ALPHA),
                                                         in1=mx_[i][:n, :], op0=ALU.mult, op1=ALU.add),
                 reads=[('dx', i), ('dm', i)], writes=[('dx', i)])
            ln_rows(S, nc, L, n, xt_[i], ('dx', i), L.g, L.b, mx_[i], ('dm', i))
            S.dma('act', T['H'][r0:r0 + n, :], mx_[i][:n, :], reads=[('dm', i)])
        for ti, (r0, n) in enumerate(row_tiles()):
            tile(ti, r0, n)
    S.barrier()


def phase_E(S, nc, C, T):
    NEG = -1.0e30
    with ExitStack() as st:
        sb = lambda name, shape, dt=F32: st.enter_context(nc.sbuf_tensor("E1_" + name, shape, dt))
        C.XT = sb("XT", [128, KC, GT], BF16)
        C.xrow = [sb("xrow%d" % i, [128, D]) for i in range(2)]
        C.wst = [sb("wst%d" % i, [128, KC, 128]) for i in range(2)]
        C.wbf = [sb("wbf%d" % i, [128, KC, 128], BF16) for i in range(2)]
        C.wst_i = 0
        skn = sb("skn", [128, 16, 128])
        KT = sb("KT", [128, 16, 128])
        qst = [sb("qst%d" % i, [128, GT]) for i in range(2)]
        scs_ = [sb("scs%d" % i, [128, 128]) for i in range(3)]
        S.dma('sp', skn[:], T['subk'][:, :, :].rearrange("j n c -> n j c"), writes=['skn'])
        for j in range(16):
            b = C.ps_next()
            S.op('pe', lambda e, j=j, b=b: e.transpose(C.PS[b][:, 0:128], skn[:, j, :], C.ident[:, :]),
                 reads=['skn', 'ident'], writes=[('ps', b)])
            S.op('dve', lambda e, j=j, b=b: e.tensor_copy(KT[:, j, :], C.PS[b][:, 0:128]), reads=[('ps', b)],
                 writes=['KT'])
        cnt = {'s': 0, 'e': 0}
        for g in range(2):
            rows = [(T['H'][g * 1024 + i * 128:g * 1024 + (i + 1) * 128, :], 128, i * 128) for i in range(8)]
            rows.append((T['H'][SEQ + g * DS:SEQ + (g + 1) * DS, :], DS, 1024))
            build_xT(S, C, g, rows)
            load_w(S, C, T['w_q'], 0, 128, 0)
            for j in range(16):
                if j + 1 < 16:
                    load_w(S, C, T['w_q'], (j + 1) * 128, 128, (j + 1) % 2)
                qs = qst[j % 2]

                def emitF(b, t0, n, qs=qs, j=j):
                    cnt['e'] += 1
                    if cnt['e'] % 2:
                        S.op('act', lambda e: e.activation(qs[:, t0:t0 + n], C.PS[b][:, :n], AF.Copy),
                             reads=[('ps', b)], writes=[('qst', j % 2)])
                    else:
                        S.op('dve', lambda e: e.tensor_copy(qs[:, t0:t0 + n], C.PS[b][:, :n]),
                             reads=[('ps', b)], writes=[('qst', j % 2)])
                gemm_F(S, C, j % 2, 128, emitF)

                def score(ti, j=j, qs=qs, g=g):
                    c0 = ti * 128
                    n = 128 if ti < 8 else DS
                    b = C.ps_next()
                    S.op('pe', lambda e: e.matmul(C.PS[b][:n, 0:128], lhsT=qs[:, c0:c0 + n], rhs=KT[:, j, :],
                                                  start=True, stop=True),
                         reads=[('qst', j % 2), 'KT'], writes=[('ps', b)])
                    si = cnt['s'] % 3
                    cnt['s'] += 1
                    sc_t = scs_[si]
                    S.op('dve' if si % 2 else 'act',
                         (lambda e: e.tensor_copy(sc_t[:n, :], C.PS[b][:n, 0:128])) if si % 2 else
                         (lambda e: e.activation(sc_t[:n, :], C.PS[b][:n, 0:128], AF.Copy)),
                         reads=[('ps', b)], writes=[('scs', si)])
                    r0 = tokmap(g, c0, n)
                    S.dma('act', T['SC'][r0:r0 + n, j * 128:(j + 1) * 128], sc_t[:n, :], reads=[('scs', si)])
                for ti in range(9):
                    score(ti)
    S.barrier()
    with ExitStack() as st:
        sb = lambda name, shape, dt=F32: st.enter_context(nc.sbuf_tensor("E2_" + name, shape, dt))
        L = ln_setup(S, nc, st, T, 'ln2g_b', 'ln2b_b', "E_ln")
        sc = sb("sc", [128, 16, 128])
        sc2 = sb("sc2", [128, 16, 128])
        stop_ = sb("stop", [128, 16, 16])
        itop = sb("itop", [128, 16, 16], U32)
        itf = sb("itf", [128, 16, 16])
        cand = sb("cand", [128, 8, 256])
        cand2 = sc2[:, :, :].rearrange("p (h c) k -> p h (c k)", c=2)
        cidx = sb("cidx", [128, 8, 256])
        tops = sb("tops", [128, 8, 16])
        eidf = sb("eidf", [128, 128])
        eidi = [sb("eidi%d" % i, [128, 128], I32) for i in range(2)]
        gate = sb("gate", [128, 8, 16])
        zs_ = sb("zsum", [128, 8])
        pre = sb("pre", [128, 128])
        actv = [sb("actv%d" % i, [128, 128]) for i in range(2)]
        ht = [sb("ht%d" % i, [128, D]) for i in range(2)]
        NGB = 4
        cbuf = [sb("cb%d" % i, [128, 2 * D], BF16) for i in range(NGB)]
        lnout = sb("lnout", [128, D])
        dg = [sb("dg%d" % i, [128, 128], BF16) for i in range(NGB)]
        identb = sb("identb", [128, 128], BF16)
        sj = [sb("sj%d" % i, [128, 256]) for i in range(2)]
        V = lambda fn, r, w: S.op('dve', fn, reads=r, writes=w)
        A = lambda fn, r, w: S.op('act', fn, reads=r, writes=w)
        G = lambda fn, r, w: S.op('pool', fn, reads=r, writes=w)
        V(lambda e: e.tensor_copy(identb[:], C.ident[:]), ['ident'], ['identb'])

        def e2(k, r0, n):
            par = k % 2
            S.dma('sp', sc[:n, :, :], T['SC'][r0:r0 + n, :].rearrange("t (j k) -> t j k", k=128), writes=['sc'])
            S.dma('sp', ht[par][:n, :], T['H'][r0:r0 + n, :], writes=[('ht', par)])
            for j in range(16):
                def pair(j):
                    V(lambda e: e.max(out=stop_[:n, j, 0:8], in_=sc[:n, j, :]), ['sc'], [('stopA', j)])
                    V(lambda e: e.max_index(out=itop[:n, j, 0:8], in_max=stop_[:n, j, 0:8], in_values=sc[:n, j, :]),
                      ['sc', ('stopA', j)], [('itopA', j)])
                    V(lambda e: e.match_replace(out=sc2[:n, j, :], in_to_replace=stop_[:n, j, 0:8], in_values=sc[:n, j, :],
                                                imm_value=NEG), ['sc', ('stopA', j)], [('sc2', j)])
                    V(lambda e: e.max(out=stop_[:n, j, 8:16], in_=sc2[:n, j, :]), [('sc2', j)], [('stopB', j)])
                    V(lambda e: e.max_index(out=itop[:n, j, 8:16], in_max=stop_[:n, j, 8:16], in_values=sc2[:n, j, :]),
                      [('sc2', j), ('stopB', j)], [('itopB', j)])
                pair(j)
            V(lambda e: e.tensor_copy(itf[:n, :, :], itop[:n, :, :]), [('itopA', j_) for j_ in range(16)] + [('itopB', j_) for j_ in range(16)], ['itf'])
            s4 = stop_[:n, :, :].rearrange("t (h p) k -> t h p k", p=2)
            i4 = itf[:n, :, :].rearrange("t (h p) k -> t h p k", p=2)
            c4 = lambda t: t[:n, :, :].rearrange("t h (a b) -> t h a b", b=16)
            V(lambda e: e.tensor_tensor(c4(cand), s4[:, :, 0, :][:, :, :, None].to_broadcast([n, 8, 16, 16]),
                                        s4[:, :, 1, :][:, :, None, :].to_broadcast([n, 8, 16, 16]), ALU.add),
              [('stopA', j_) for j_ in range(16)] + [('stopB', j_) for j_ in range(16)], ['cand'])
            V(lambda e: e.tensor_scalar(i4[:, :, 0, :], i4[:, :, 0, :], 128.0, None, ALU.mult), ['itf'], ['itf'])
            V(lambda e: e.tensor_tensor(c4(cidx), i4[:, :, 0, :][:, :, :, None].to_broadcast([n, 8, 16, 16]),
                                        i4[:, :, 1, :][:, :, None, :].to_broadcast([n, 8, 16, 16]), ALU.add),
              ['itf'], ['cidx'])
            for hd in range(8):
                def head(hd):
                    V(lambda e: e.max(out=tops[:n, hd, 0:8], in_=cand[:n, hd, :]), ['cand'], [('tops', hd)])
                    V(lambda e: e.match_replace(out=cand2[:n, hd, :], in_to_replace=tops[:n, hd, 0:8],
                                                in_values=cand[:n, hd, :], imm_value=NEG), ['cand', ('tops', hd)], [('sc2', 2 * hd), ('sc2', 2 * hd + 1)])
                    V(lambda e: e.max(out=tops[:n, hd, 8:16], in_=cand2[:n, hd, :]), [('sc2', 2 * hd), ('sc2', 2 * hd + 1)], [('tops', hd)])
                    for kk in range(16):
                        def pick(kk):
                            m = hd * 16 + kk
                            V(lambda e: e.scalar_tensor_tensor(out=sj[m % 2][:n, :], in0=cand[:n, hd, :],
                                                               scalar=tops[:n, hd, kk:kk + 1], in1=cidx[:n, hd, :],
                                                               op0=ALU.is_equal, op1=ALU.mult,
                                                               accum_out=eidf[:n, m:m + 1]),
                              ['cand', ('tops', hd), 'cidx'], [('sj', m % 2), ('eidf', m)])
                        pick(kk)
                head(hd)
            V(lambda e: e.tensor_scalar(eidf[:n, :], eidf[:n, :], 16383.0, 0.0, ALU.min, ALU.max), [('eidf', m_) for m_ in range(128)], ['eidf'] + [('eidf', m_) for m_ in range(128)])
            V(lambda e: e.tensor_copy(eidi[par][:n, :], eidf[:n, :]), ['eidf'], [('eidi', par)])
            V(lambda e: e.tensor_tensor(gate[:n, :, :], tops[:n, :, :], tops[:n, :, 0:1].to_broadcast([n, 8, 16]),
                                        ALU.subtract), [('tops', h_) for h_ in range(8)], ['gate'])
            A(lambda e: e.activation(gate[:n, :, :], gate[:n, :, :], AF.Exp), ['gate'], ['gate'])
            V(lambda e: e.tensor_reduce(zs_[:n, :], gate[:n, :, :], AX.X, ALU.add), ['gate'], ['zsum'])
            V(lambda e: e.reciprocal(zs_[:n, :], zs_[:n, :]), ['zsum'], ['zsum'])
            V(lambda e: e.tensor_tensor(gate[:n, :, :], gate[:n, :, :], zs_[:n, :][:, :, None].to_broadcast([n, 8, 16]),
                                        ALU.mult), ['gate', 'zsum'], ['gate'])

        def pick(k, n, m):
            par = k % 2
            i = m % NGB
            S.gather(cbuf[i][:n, :], T['PVB'][:, :], eidi[par][:n, m:m + 1], reads=[('eidi', par)], writes=[('cb', i)],
                     bounds=None)
            V(lambda e: e.scalar_tensor_tensor(out=L.junk[:n, :], in0=cbuf[i][:n, 0:D], scalar=1.0, in1=ht[par][:n, :],
                                               op0=ALU.mult, op1=ALU.mult, accum_out=pre[:n, m:m + 1]),
              [('cb', i), ('ht', par)], [('pre', m)])
            A(lambda e: e.activation(actv[par][:n, m:m + 1], pre[:n, m:m + 1], AF.Gelu), [('pre', m)], [('actv', m)])

        def pickB(k, n, m):
            par = k % 2
            i = m % NGB
            gflat = gate[:n, :, :].rearrange("t h k -> t (h k)")
            V(lambda e: e.tensor_scalar(dg[i][:n, :n], identb[:n, :n], actv[par][:n, m:m + 1], gflat[:, m:m + 1],
                                        ALU.mult, ALU.mult), ['identb', ('actv', m), 'gate'], [('dg', i)])
            for cb in range(8):
                S.op('pe', lambda e, cb=cb: e.matmul(C.PS[cb][:n, :], lhsT=dg[i][:n, :n],
                                                     rhs=cbuf[i][:n, D + cb * 512:D + (cb + 1) * 512],
                                                     start=(m == 0), stop=(m == 127)),
                     reads=[('dg', i), ('cb', i)], writes=[('ps', cb)])

        def final(k, r0, n):
            par = k % 2
            for cb in range(8):
                V(lambda e, cb=cb: e.scalar_tensor_tensor(out=ht[par][:n, cb * 512:(cb + 1) * 512],
                                                          in0=ht[par][:n, cb * 512:(cb + 1) * 512], scalar=float(ALPHA),
                                                          in1=C.PS[cb][:n, :], op0=ALU.mult, op1=ALU.add),
                  [('ht', par), ('ps', cb)], [('ht', par)])
            ln_rows(S, nc, L, n, ht[par], ('ht', par), L.g, L.b, lnout, 'lnout')
            S.dma('act', y_rows(T, r0, n), lnout[:n, :], reads=['lnout'])

        import os as _os
        tiles = row_tiles()[:int(_os.environ.get("KT", "99"))]
        for k, (r0, n) in enumerate(tiles):
            e2(k, r0, n)
            for m in range(128):
                pick(k, n, m)
                if m >= 1:
                    pickB(k, n, m - 1)
            pickB(k, n, 127)
            final(k, r0, n)
    S.barrier()
```

```python
import numpy as np
from contextlib import ExitStack
import concourse.bass as bass
import concourse.mybir as mybir
from concourse.bass_utils import run_bass_kernel_spmd

F32 = mybir.dt.float32
BF16 = mybir.dt.bfloat16
I32 = mybir.dt.int32
U32 = mybir.dt.uint32
AF = mybir.ActivationFunctionType
ALU = mybir.AluOpType
AX = mybir.AxisListType

ENGS = ['pe', 'act', 'dve', 'pool', 'sp']


class Sched:
    def __init__(self, nc, stack, n_dma=32):
        self.nc = nc
        self.q = {e: [] for e in ENGS}
        self.sem = {e: stack.enter_context(nc.semaphore("s_" + e)) for e in ENGS}
        self.cnt = {e: 0 for e in ENGS}
        self.nd = n_dma
        self.dsem = [stack.enter_context(nc.semaphore("d%d" % i)) for i in range(n_dma)]
        self.dcnt = [0] * n_dma
        self.dnext = 0
        self.seen = {e: {} for e in ENGS}
        self.lw = {}
        self.rd = {}
        self.ninst = 0

    def _deps(self, reads, writes):
        deps = {}
        for r in reads:
            kv = self.lw.get(r)
            if kv is not None and deps.get(kv[0], 0) < kv[1]:
                deps[kv[0]] = kv[1]
        for w in writes:
            kv = self.lw.get(w)
            if kv is not None and deps.get(kv[0], 0) < kv[1]:
                deps[kv[0]] = kv[1]
            for k, v in self.rd.get(w, {}).items():
                if deps.get(k, 0) < v:
                    deps[k] = v
        return deps

    def _waits(self, eng, deps):
        for k, v in deps.items():
            if k == 'pe' and eng == 'pe':
                continue
            if self.seen[eng].get(k, 0) >= v:
                continue
            self.seen[eng][k] = v
            sem = self.sem[k] if isinstance(k, str) else self.dsem[k[1]]
            self.q[eng].append(lambda e, sem=sem, v=v: e.wait_ge(sem, v))
            self.ninst += 1

    def _mark(self, key, v, reads, writes):
        for r in reads:
            d = self.rd.setdefault(r, {})
            if d.get(key, 0) < v:
                d[key] = v
        for w in writes:
            self.lw[w] = (key, v)
            self.rd[w] = {}

    def op(self, eng, fn, reads=(), writes=()):
        self._waits(eng, self._deps(reads, writes))
        self.cnt[eng] += 1
        v = self.cnt[eng]
        sem = self.sem[eng]
        self.q[eng].append(lambda e: fn(e).then_inc(sem, 1))
        self.ninst += 1
        self._mark(eng, v, reads, writes)

    def dma(self, q, out, in_, reads=(), writes=(), **kw):
        i = self.dnext
        self.dnext = (i + 1) % self.nd
        deps = self._deps(reads, writes)
        if self.dcnt[i] > 0:
            deps[('d', i)] = max(deps.get(('d', i), 0), self.dcnt[i])
        self._waits(q, deps)
        self.dcnt[i] += 16
        v = self.dcnt[i]
        sem = self.dsem[i]
        self.q[q].append(lambda e: e.dma_start(out=out, in_=in_, **kw).then_inc(sem, 16))
        self.ninst += 1
        self._mark(('d', i), v, reads, writes)

    def gather(self, out, table, idx_ap, reads=(), writes=(), bounds=None):
        q = 'pool'
        i = self.dnext
        self.dnext = (i + 1) % self.nd
        deps = self._deps(reads, writes)
        if self.dcnt[i] > 0:
            deps[('d', i)] = max(deps.get(('d', i), 0), self.dcnt[i])
        self._waits(q, deps)
        self.dcnt[i] += 16
        v = self.dcnt[i]
        sem = self.dsem[i]

        def f(e):
            self._gdbg = getattr(self, '_gdbg', 0) + 1
            if self._gdbg > 4351:
                print("GATHER", self._gdbg, out.shape, idx_ap, flush=True)
            return e.indirect_dma_start(
                out=out, out_offset=None, in_=table,
                in_offset=bass.IndirectOffsetOnAxis(ap=idx_ap, axis=0),
                bounds_check=bounds, oob_is_err=False).then_inc(sem, 16)
        self.q[q].append(f)
        self.ninst += 1
        self._mark(('d', i), v, reads, writes)

    def barrier(self):
        for e in ENGS:
            deps = {k: self.cnt[k] for k in ENGS if self.cnt[k] > 0 and k != e}
            if e != 'pe' and self.cnt[e] > 0:
                deps[e] = self.cnt[e]
            for i in range(self.nd):
                if self.dcnt[i] > 0:
                    deps[('d', i)] = self.dcnt[i]
            self._waits(e, deps)
        self.lw = {}
        self.rd = {}

    def run(self):
        self.barrier()
        q = self.q
        with self.nc.Block() as block:
            @block.tensor
            def _(e):
                for f in q['pe']:
                    f(e)

            @block.scalar
            def _(e):
                for f in q['act']:
                    f(e)

            @block.vector
            def _(e):
                for f in q['dve']:
                    f(e)

            @block.gpsimd
            def _(e):
                for f in q['pool']:
                    f(e)

            @block.sync
            def _(e):
                for f in q['sp']:
                    f(e)


D = 4096
SEQ = 2048
DS = 32
NS = 2
PAST = 1024
NT = SEQ + NS * DS
HD = 128
NH = 16
AW = 2048
PROJ = 14384
OFF_Z = 6144
OFF_A = 8192
OFF_B = 8208
OFF_BQ = 8224
OFF_F = 14368
GT = 1056
KC = 32
LN_EPS = 1e-5
ALPHA = 2.0 ** 0.25


class Ctx:
    pass


def tokmap(g, c0, n):
    if c0 < 1024:
        return g * 1024 + c0
    return SEQ + DS * g + (c0 - 1024)


def build_xT(S, C, g, src_rows):
    XT = C.XT
    for ti, (src, n, c0) in enumerate(src_rows):
        sl = ti % 2
        xr = C.xrow[sl]
        S.dma('sp', xr[:n, :], src, writes=[('xr', sl)])
        for kq in range(8):
            b = C.ps_next()
            for j in range(4):
                kc = kq * 4 + j
                S.op('pe', lambda e, b=b, j=j, kc=kc, xr=xr, n=n: e.transpose(
                    C.PS[b][:, j * 128:j * 128 + n], xr[:n, kc * 128:(kc + 1) * 128], C.ident[:n, :n]),
                    reads=[('xr', sl), 'ident'], writes=[('ps', b)])
            src_ps = C.PS[b][:, :].rearrange("p (j t) -> p j t", j=4)[:, :, :n]
            dst = XT[:, kq * 4:(kq + 1) * 4, c0:c0 + n]
            if kq % 2 == 0:
                S.op('act', lambda e, dst=dst, src_ps=src_ps: e.activation(dst, src_ps, AF.Copy),
                     reads=[('ps', b)], writes=[('XT', c0 // 128)])
            else:
                S.op('dve', lambda e, dst=dst, src_ps=src_ps: e.tensor_copy(dst, src_ps),
                     reads=[('ps', b)], writes=[('XT', c0 // 128)])


def load_w(S, C, wdram, col0, ncols, slot):
    for p0 in range(0, ncols, 128):
        pn = min(128, ncols - p0)
        ss = C.wst_i % 2
        C.wst_i += 1
        wst = C.wst[ss]
        src = wdram[:, col0 + p0:col0 + p0 + pn].rearrange("(kc p) c -> p kc c", p=128)
        S.dma('sp', wst[:, :, :pn], src, writes=[('wst', ss)])
        dst = C.wbf[slot][:, :, p0:p0 + pn]
        S.op('pool', lambda e, dst=dst, wst=wst, pn=pn: e.tensor_copy(dst, wst[:, :, :pn]),
             reads=[('wst', ss)], writes=[('wbf', slot)])


def xt_reads(c0, n):
    return [('XT', i) for i in range(c0 // 128, (c0 + n - 1) // 128 + 1)]


def gemm_F(S, C, slot, ncols, emit):
    XT, wbf = C.XT, C.wbf[slot]
    for (t0, n) in ((0, 512), (512, 512), (1024, 32)):
        b = C.ps_next()
        for kc in range(KC):
            S.op('pe', lambda e, b=b, kc=kc, t0=t0, n=n: e.matmul(
                C.PS[b][:ncols, :n], lhsT=wbf[:, kc, :ncols], rhs=XT[:, kc, t0:t0 + n],
                start=(kc == 0), stop=(kc == KC - 1)),
                reads=[('wbf', slot)] + xt_reads(t0, n), writes=[('ps', b)])
        emit(b, t0, n)


def gemm_T(S, C, slot, ncols, emit):
    XT, wbf = C.XT, C.wbf[slot]
    for ti in range(9):
        c0 = ti * 128
        n = 128 if ti < 8 else 32
        b = C.ps_next()
        for kc in range(KC):
            S.op('pe', lambda e, b=b, kc=kc, c0=c0, n=n: e.matmul(
                C.PS[b][:n, :ncols], lhsT=XT[:, kc, c0:c0 + n], rhs=wbf[:, kc, :ncols],
                start=(kc == 0), stop=(kc == KC - 1)),
                reads=[('wbf', slot)] + xt_reads(c0, n), writes=[('ps', b)])
        emit(b, c0, n)


def phase_A(S, nc, C, T):
    with ExitStack() as st:
        C.XT = st.enter_context(nc.sbuf_tensor("XT", [128, KC, GT], BF16))
        C.xrow = [st.enter_context(nc.sbuf_tensor("xrow%d" % i, [128, D], F32)) for i in range(2)]
        C.wst = [st.enter_context(nc.sbuf_tensor("wst%d" % i, [128, KC, 128], F32)) for i in range(2)]
        C.wbf = [st.enter_context(nc.sbuf_tensor("wbf%d" % i, [128, KC, 128], BF16)) for i in range(2)]
        stg = [st.enter_context(nc.sbuf_tensor("stg%d" % i, [128, GT], F32)) for i in range(2)]
        tst = [st.enter_context(nc.sbuf_tensor("tst%d" % i, [128, 128], F32)) for i in range(4)]
        C.wst_i = 0
        cnt = {'stg': 0, 'tst': 0, 'ev': 0}

        chunks = []
        for j in range(48):
            chunks.append((j * 128, 128, 'F', (T['AqkvT'], j * 128, 1.0)))
        for j in range(16):
            chunks.append((OFF_Z + j * 128, 128, 'F', (T['AzT'], j * 128, 1.0)))
        chunks.append((OFF_A, 32, 'T', ('ab', 0)))
        for j in range(16):
            chunks.append((OFF_BQ + j * 128, 128, 'F', (T['BqT'], j * 128, HD ** -0.5)))
        for j in range(16):
            chunks.append((OFF_BQ + AW + j * 128, 128, 'FT', (T['BkT'], j * 128, 1.0, 'k', j * 128)))
        for j in range(16):
            chunks.append((OFF_BQ + 2 * AW + j * 128, 128, 'T', ('v', j * 128)))
        chunks.append((OFF_F, 16, 'F', (T['BfT'], 0, 1.0)))

        cvs = {'r': 0}
        NRT = T['pv'].shape[0] // 128

        def conv_step():
            r = cvs['r']
            if r >= NRT:
                return
            cvs['r'] += 1
            for tname, c0 in (('pu', 0), ('pv', D)):
                for hf in range(2):
                    S.dma('pool', T['PVB'][r * 128:(r + 1) * 128, c0 + hf * 2048:c0 + (hf + 1) * 2048],
                          T[tname][r * 128:(r + 1) * 128, hf * 2048:(hf + 1) * 2048])
        for g in range(2):
            rows = [(T['xp'][g * 1024 + i * 128:g * 1024 + (i + 1) * 128, :], 128, i * 128) for i in range(8)]
            rows.append((T['xs'][g * DS:(g + 1) * DS, :], DS, 1024))
            build_xT(S, C, g, rows)

            def emit_F(info):
                dst, row0, scale = info[0], info[1], info[2]

                def emit(b, t0, n, ncols):
                    pass
                return emit

            load_w(S, C, T['w_in'], chunks[0][0], chunks[0][1], 0)
            for ci, (col0, ncols, mode, info) in enumerate(chunks):
                slot = ci % 2
                conv_step()
                if ci + 1 < len(chunks):
                    load_w(S, C, T['w_in'], chunks[ci + 1][0], chunks[ci + 1][1], (ci + 1) % 2)
                if 'F' in mode:
                    dst, row0, scale = info[0], info[1], info[2]
                    ss = cnt['stg'] % 2
                    cnt['stg'] += 1
                    sg_t = stg[ss]

                    def emitF(b, t0, n, ncols=ncols, sg_t=sg_t, ss=ss, scale=scale):
                        cnt['ev'] += 1
                        if cnt['ev'] % 2 == 0:
                            S.op('act', lambda e: e.activation(sg_t[:ncols, t0:t0 + n], C.PS[b][:ncols, :n],
                                                               AF.Identity, scale=float(scale)),
                                 reads=[('ps', b)], writes=[('stg', ss)])
                        else:
                            S.op('dve', lambda e: e.tensor_scalar(sg_t[:ncols, t0:t0 + n], C.PS[b][:ncols, :n],
                                                                  float(scale), None, ALU.mult),
                                 reads=[('ps', b)], writes=[('stg', ss)])
                    gemm_F(S, C, slot, ncols, emitF)
                    S.dma('act', dst[row0:row0 + ncols, g * 1024:(g + 1) * 1024], sg_t[:ncols, 0:1024],
                          reads=[('stg', ss)])
                    S.dma('act', dst[row0:row0 + ncols, SEQ + DS * g:SEQ + DS * (g + 1)], sg_t[:ncols, 1024:GT],
                          reads=[('stg', ss)])
                if 'T' in mode:
                    kind, coff = (info[3], info[4]) if mode == 'FT' else (info[0], info[1])

                    def emitT(b, c0, n, ncols=ncols, kind=kind, coff=coff):
                        ts_ = cnt['tst'] % 4
                        cnt['tst'] += 1
                        tt = tst[ts_]
                        cnt['ev'] += 1
                        if cnt['ev'] % 2 == 0:
                            S.op('act', lambda e: e.activation(tt[:n, :ncols], C.PS[b][:n, :ncols], AF.Copy),
                                 reads=[('ps', b)], writes=[('tst', ts_)])
                        else:
                            S.op('dve', lambda e: e.tensor_copy(tt[:n, :ncols], C.PS[b][:n, :ncols]),
                                 reads=[('ps', b)], writes=[('tst', ts_)])
                        if kind == 'ab':
                            r0 = tokmap(g, c0, n)
                            dd = T['Aab'][r0:r0 + n, 0:ncols]
                        else:
                            if c0 < 1024:
                                base = T['fkp'] if kind == 'k' else T['fvp']
                                r0 = g * 1024 + c0
                            else:
                                base = T['fks'] if kind == 'k' else T['fvs']
                                r0 = g * DS
                            dd = base[r0:r0 + n, coff:coff + ncols]
                        S.dma('act', dd, tt[:n, :ncols], reads=[('tst', ts_)])
                    gemm_T(S, C, slot, ncols, emitT)
    S.barrier()


IN_SPECS = [
    ("xp", [SEQ, D], F32), ("xs", [NS * DS, D], F32),
    ("ck", [NS, PAST, AW], F32), ("cv", [NS, PAST, AW], F32), ("clf", [NS, PAST, NH], F32),
    ("sg", [NS, NH, HD, HD], F32), ("sconv_t", [NS, 128, 48 * 3], F32),
    ("w_in", [D, PROJ], F32), ("w_out", [D, D], F32), ("w_q", [D, 2048], F32),
    ("subk", [16, 128, 128], F32), ("pu", [16384, D], F32), ("pv", [16384, D], F32),
    ("convw_t", [128, 48 * 4], F32), ("alog_b", [128, NH], F32), ("dtb_b", [128, NH], F32),
    ("normw_c", [128, 1], F32), ("ffb_c", [NH, 1], F32),
    ("ln1g_b", [128, D], F32), ("ln1b_b", [128, D], F32), ("ln2g_b", [128, D], F32), ("ln2b_b", [128, D], F32),
    ("ident_in", [128, 128], F32), ("cmat", [128, 8 * 128], F32), ("selc", [NH, NH * 128], F32),
]
OUT_SPECS = [
    ("yp", [SEQ, D]), ("ys", [NS * DS, D]),
    ("fkp", [SEQ, AW]), ("fvp", [SEQ, AW]), ("flp", [SEQ, NH]),
    ("sgp", [NH, HD, HD]), ("scp", [3, 3 * AW]),
    ("fks", [NS * DS, AW]), ("fvs", [NS * DS, AW]), ("fls", [NS * DS, NH]),
    ("sgs", [NS, NH, HD, HD]), ("scs", [NS, 3, 3 * AW]),
]
SCRATCH = [
    ("BIG1", [3 * AW * NT], F32), ("BIG2", [3 * AW * NT], F32), ("Aab", [NT, 32], F32),
    ("BfT", [NH, NT], F32), ("OT", [D, NT], BF16), ("PVB", [16384, 2 * D], BF16),
]


def build_program(phases="ABCDE"):
    nc = bass.Bass("TRN2", target_bir_lowering=False)
    T = {}
    for name, shape, dt in IN_SPECS:
        if name in ("pu", "pv") and 'E' not in phases:
            shape = [128, D]
        T[name] = nc.dram_tensor(name, shape, dt, kind="ExternalInput")
    for name, shape in OUT_SPECS:
        T[name] = nc.dram_tensor(name, shape, F32, kind="ExternalOutput")
    for name, shape, dt in SCRATCH:
        if name == "PVB" and 'E' not in phases:
            shape = [128, 2 * D]
        T[name] = nc.dram_tensor(name, shape, dt, kind="Internal")
    b1, b2 = T['BIG1'], T['BIG2']
    T['AqkvT'] = b1[0:3 * AW * NT].rearrange("(a b) -> a b", b=NT)
    T['MIX'] = b1[0:NT * D].rearrange("(a b) -> a b", b=D)
    T['SC'] = b1[NT * D:NT * D + NT * 2048].rearrange("(a b) -> a b", b=2048)
    T['AzT'] = b2[0:AW * NT].rearrange("(a b) -> a b", b=NT)
    T['BqT'] = b2[AW * NT:2 * AW * NT].rearrange("(a b) -> a b", b=NT)
    T['BkT'] = b2[2 * AW * NT:3 * AW * NT].rearrange("(a b) -> a b", b=NT)
    T['H'] = b2[0:NT * D].rearrange("(a b) -> a b", b=D)
    with ExitStack() as st:
        S = Sched(nc, st)
        C = Ctx()
        C.PS = [st.enter_context(nc.psum_tensor("ps%d" % i, [128, 512], F32)) for i in range(8)]
        C.ps_i = 0

        def ps_next():
            b = C.ps_i
            C.ps_i = (b + 1) % 8
            return b
        C.ps_next = ps_next
        C.ident = st.enter_context(nc.sbuf_tensor("ident_sb", [128, 128], F32))
        C.cmat = st.enter_context(nc.sbuf_tensor("cmat_sb", [128, 8 * 128], F32))
        S.dma('sp', C.ident[:], T['ident_in'][:, :], writes=['ident'])
        S.dma('sp', C.cmat[:], T['cmat'][:, :], writes=['cmat'])
        if 'A' in phases:
            phase_A(S, nc, C, T)
        if 'B' in phases:
            phase_B(S, nc, C, T)
        if 'C' in phases:
            phase_C(S, nc, C, T)
        if 'D' in phases:
            phase_D(S, nc, C, T)
        if 'E' in phases:
            phase_E(S, nc, C, T)
        S.run()
        print("instructions:", S.ninst, {e: S.cnt[e] for e in ENGS})
    return nc


def make_consts():
    c = np.zeros((128, 8, 128), np.float32)
    i = np.arange(128)
    c[:, 0, :] = (i[:, None] <= i[None, :])
    c[:, 1, :] = (i[:, None] > i[None, :])
    c[:, 2, :] = 1.0
    c[:, 3, :] = (i[:, None] < i[None, :])
    c[:, 4, :] = 128.0
    c[:, 5, :] = (i[:, None] >= i[None, :])
    return c.reshape(128, 8 * 128)


def kernel(x_prompt, x_sample, cache_fox_k, cache_fox_v, cache_fox_logf, state_gdn, state_gdn_conv,
           w_in, gdn_conv_w, gdn_a_log, gdn_dt_bias, gdn_norm_w, fox_f_bias, w_out, ln1_g, ln1_b,
           peer_w_q, peer_sub_keys, peer_u, peer_v, ln2_g, ln2_b, _phases="ABCDE", _cores=None):
    import time as _time
    _t0 = _time.time()
    f = lambda a: np.ascontiguousarray(np.asarray(a), dtype=np.float32)
    nc = build_program(_phases)
    cw = f(gdn_conv_w)[0]
    convw_t = np.ascontiguousarray(cw.reshape(4, 48, 128).transpose(2, 1, 0)).reshape(128, 48 * 4)
    bc = lambda v, n=128: np.ascontiguousarray(np.broadcast_to(f(v).reshape(1, -1), (n, f(v).size)))
    shared = {
        "w_in": f(w_in)[0], "w_out": f(w_out)[0], "w_q": f(peer_w_q)[0],
        "subk": f(peer_sub_keys)[0].reshape(16, 128, 128), "pu": f(peer_u)[0], "pv": f(peer_v)[0],
        "convw_t": convw_t, "alog_b": bc(gdn_a_log), "dtb_b": bc(gdn_dt_bias),
        "normw_c": f(gdn_norm_w).reshape(128, 1), "ffb_c": f(fox_f_bias).reshape(NH, 1),
        "ln1g_b": bc(ln1_g), "ln1b_b": bc(ln1_b), "ln2g_b": bc(ln2_g), "ln2b_b": bc(ln2_b),
        "ident_in": np.eye(128, dtype=np.float32), "cmat": make_consts(),
        "selc": np.ascontiguousarray(np.repeat(np.eye(NH, dtype=np.float32), 128, axis=1)),
    }
    xp = f(x_prompt)
    xs = f(x_sample)
    ck = f(cache_fox_k)[0]
    cv = f(cache_fox_v)[0]
    clf = f(cache_fox_logf)[0]
    sg = f(state_gdn)[0]
    sc = f(state_gdn_conv)[0]
    in_maps = []
    for c in range(8):
        m = dict(shared)
        m["xp"] = xp[c]
        m["xs"] = xs[2 * c:2 * c + 2].reshape(NS * DS, D)
        m["ck"] = ck[2 * c:2 * c + 2].reshape(NS, PAST, AW)
        m["cv"] = cv[2 * c:2 * c + 2].reshape(NS, PAST, AW)
        m["clf"] = clf[2 * c:2 * c + 2]
        m["sg"] = sg[2 * c:2 * c + 2]
        m["sconv_t"] = np.ascontiguousarray(sc[2 * c:2 * c + 2].reshape(NS, 3, 48, 128).transpose(0, 3, 2, 1)).reshape(NS, 128, 48 * 3)
        in_maps.append(m)
    if 'E' not in _phases:
        for m in in_maps:
            m["pu"] = m["pu"][:128]
            m["pv"] = m["pv"][:128]
    print("kernel: built+staged %.1fs" % (_time.time() - _t0), flush=True)
    if _cores is not None:
        import os as _os
        res = run_bass_kernel_spmd(nc, [in_maps[c] for c in _cores], core_ids=list(range(len(_cores))),
                                   trace=bool(_os.environ.get("KTRACE")))
        print("EXEC_NS", getattr(res, "exec_time_ns", None), flush=True)
        R = {c: res.results[i] for i, c in enumerate(_cores)}
        R = [R.get(c, R[_cores[0]]) for c in range(8)]
    else:
        res = run_bass_kernel_spmd(nc, in_maps, core_ids=list(range(8)))
        R = res.results
    print("kernel: ran %.1fs" % (_time.time() - _t0), flush=True)
    cat = lambda k: np.stack([np.asarray(R[c][k]) for c in range(8)], axis=0)
    yp = cat("yp")
    ys = cat("ys").reshape(16, DS, D)
    fkp = cat("fkp").reshape(1, 8, SEQ, NH, HD)
    fvp = cat("fvp").reshape(1, 8, SEQ, NH, HD)
    flp = cat("flp").reshape(1, 8, SEQ, NH)
    sgp = cat("sgp").reshape(1, 8, NH, HD, HD)
    scp = cat("scp").reshape(1, 8, 3, 3 * AW)
    fks = cat("fks").reshape(1, 16, DS, NH, HD)
    fvs = cat("fvs").reshape(1, 16, DS, NH, HD)
    fls = cat("fls").reshape(1, 16, DS, NH)
    sgs = cat("sgs").reshape(1, 16, NH, HD, HD)
    scs = cat("scs").reshape(1, 16, 3, 3 * AW)
    return (yp, ys, fkp, fvp, flp, sgp, scp, fks, fvs, fls, sgs, scs)


class KB:
    def __init__(self, c0, n, vb, qb_first, diag):
        self.c0, self.n, self.vb, self.qb_first, self.diag = c0, n, vb, qb_first, diag


def fox_core(S, C, P, kT, Vb, nck, qT, cqT, selh, nqb, QB, kblocks, out_cb, tag):
    import os as _os
    _kg = int(_os.environ.get("KG", "99"))
    for g0 in range(0, min(nqb, 4 * _kg), 4):
        qbs = list(range(g0, min(g0 + 4, nqb)))
        vis = {qb: [kb for kb in kblocks if kb.qb_first <= qb] for qb in qbs}
        for kb in [k for k in kblocks if k.qb_first <= qbs[-1]]:
            qlo = max(kb.qb_first, g0)
            q0, q1 = qlo * QB, (qbs[-1] + 1) * QB
            N = q1 - q0
            bS = 4 + (P.fx_i % 2)
            psl = P.fx_i % 2
            P.fx_i += 1
            S.op('pe', lambda e, kb=kb, bS=bS, q0=q0, q1=q1, N=N: e.matmul(
                C.PS[bS][:kb.n, :N], lhsT=kT[:, kb.c0:kb.c0 + kb.n], rhs=qT[:, q0:q1], start=True, stop=False),
                reads=tag['k'] + tag['q'], writes=[('ps', bS)])
            S.op('pe', lambda e, kb=kb, bS=bS, q0=q0, q1=q1, N=N: e.matmul(
                C.PS[bS][:kb.n, :N], lhsT=selh[:, :kb.n], rhs=cqT[:, q0:q1], start=False, stop=True),
                reads=tag['c'] + ['selc'], writes=[('ps', bS)])
            pt = P.PT[psl]
            S.op('act', lambda e, kb=kb, bS=bS, N=N, pt=pt: e.activation(
                pt[:kb.n, :N], C.PS[bS][:kb.n, :N], AF.Exp, bias=nck(kb), scale=1.0),
                reads=[('ps', bS)] + tag['nck'], writes=[('pt', psl)])
            if kb.diag is not None and qlo <= kb.diag <= qbs[-1]:
                off = (kb.diag - qlo) * QB
                S.op('dve', lambda e, kb=kb, off=off, pt=pt: e.tensor_tensor(
                    pt[:kb.n, off:off + QB], pt[:kb.n, off:off + QB], P.maskb[:kb.n, :QB], ALU.mult),
                    reads=[('pt', psl), 'maskb'], writes=[('pt', psl)])
            for qb in range(qlo, qbs[-1] + 1):
                bank = qb % 4
                first = kb is vis[qb][0]
                last = kb is vis[qb][-1]
                S.op('pe', lambda e, kb=kb, qb=qb, bank=bank, first=first, last=last, pt=pt, qlo=qlo: e.matmul(
                    C.PS[bank][:QB, :129], lhsT=pt[:kb.n, (qb - qlo) * QB:(qb - qlo + 1) * QB],
                    rhs=Vb[:kb.n, kb.vb, :], start=first, stop=last),
                    reads=[('pt', psl)] + tag['v'], writes=[('ps', bank)])
        for qb in qbs:
            out_cb(qb, qb % 4)


def fox_out(S, C, P, QB, ots, ots_tag):
    def cb(qb, bank):
        i = P.oc_i % 2
        P.oc_i += 1
        rec = P.rec[i]
        osb = P.osb[i]
        S.op('dve', lambda e: e.reciprocal(rec[:QB, :], C.PS[bank][:QB, 128:129]),
             reads=[('ps', bank)], writes=[('rec', i)])
        S.op('act', lambda e: e.activation(osb[:QB, :], C.PS[bank][:QB, 0:128], AF.Identity, scale=rec[:QB, 0:1]),
             reads=[('ps', bank), ('rec', i)], writes=[('osb', i)])
        bt = 6 + i
        S.op('pe', lambda e: e.transpose(C.PS[bt][:, :QB], osb[:QB, :], C.ident[:QB, :QB]),
             reads=[('osb', i), 'ident'], writes=[('ps', bt)])
        S.op('dve', lambda e: e.tensor_copy(ots[:, qb * QB:(qb + 1) * QB], C.PS[bt][:, :QB]),
             reads=[('ps', bt)], writes=[ots_tag])
    return cb


def phase_C(S, nc, C, T):
    with ExitStack() as st:
        P = Ctx()
        P.fx_i = 0
        P.oc_i = 0
        sb = lambda name, shape, dt=F32: st.enter_context(nc.sbuf_tensor(name, shape, dt))
        selc = sb("selc_sb", [NH, NH * 128])
        S.dma('sp', selc[:], T['selc'][:, :], writes=['selc'])
        P.maskb = sb("maskb", [128, 128], BF16)
        S.op('dve', lambda e: e.tensor_copy(P.maskb[:], C.cmat[:, 0:128]), reads=['cmat'], writes=['maskb'])
        P.PT = [sb("PT%d" % i, [128, 512], BF16) for i in range(2)]
        P.rec = [sb("rec%d" % i, [128, 1]) for i in range(2)]
        P.osb = [sb("osb%d" % i, [128, 128]) for i in range(2)]
        bfT = sb("bfT", [NH, NT])
        logfT = sb("logfT", [NH, NT])
        ffb = sb("ffb", [NH, 1])
        nffb = sb("nffb", [NH, 1])
        onesr = sb("onesr", [NH, SEQ])
        cTp = sb("cTp", [NH, SEQ])
        S.dma('sp', bfT[:], T['BfT'][:, :], writes=['bfT'])
        S.dma('sp', ffb[:], T['ffb_c'][:, :], writes=['ffb'])
        S.op('dve', lambda e: e.tensor_scalar(nffb[:], ffb[:], -1.0, None, ALU.mult), reads=['ffb'], writes=['nffb'])
        S.op('dve', lambda e: e.memset(onesr[:], 1.0), writes=['onesr'])
        S.op('act', lambda e: e.activation(logfT[:], bfT[:], AF.Exp, bias=nffb[:, 0:1], scale=-1.0),
             reads=['bfT', 'nffb'], writes=['logfT'])
        S.op('act', lambda e: e.activation(logfT[:], logfT[:], AF.Ln, bias=1.0, scale=1.0),
             reads=['logfT'], writes=['logfT'])
        S.op('dve', lambda e: e.tensor_scalar(logfT[:], logfT[:], -1.0, None, ALU.mult),
             reads=['logfT'], writes=['logfT'])
        S.op('dve', lambda e: e.tensor_tensor_scan(cTp[:], onesr[:], logfT[:, 0:SEQ], 0.0, ALU.mult, ALU.add),
             reads=['onesr', 'logfT'], writes=['cTp'])
        lf_tm = sb("lf_tm", [128, 16, NH])
        nck_p = sb("nck_p", [128, 16, NH])
        for blk in range(16):
            S.op('pe', lambda e, blk=blk: e.transpose(C.PS[6][:, blk * 16:(blk + 1) * 16],
                                                      logfT[:, blk * 128:(blk + 1) * 128], C.ident[:NH, :NH]),
                 reads=['logfT', 'ident'], writes=[('ps', 6)])
            S.op('pe', lambda e, blk=blk: e.transpose(C.PS[7][:, blk * 16:(blk + 1) * 16],
                                                      cTp[:, blk * 128:(blk + 1) * 128], C.ident[:NH, :NH]),
                 reads=['cTp', 'ident'], writes=[('ps', 7)])
        S.op('act', lambda e: e.activation(lf_tm[:].rearrange("p b h -> p (b h)"), C.PS[6][:, 0:256], AF.Copy),
             reads=[('ps', 6)], writes=['lf_tm'])
        S.op('dve', lambda e: e.tensor_scalar(nck_p[:].rearrange("p b h -> p (b h)"), C.PS[7][:, 0:256], -1.0, None,
                                              ALU.mult), reads=[('ps', 7)], writes=['nck_p'])
        S.dma('act', T['flp'][:, :].rearrange("(b p) h -> p b h", p=128), lf_tm[:], reads=['lf_tm'])
        lfs = sb("lfs", [64, NH])
        S.op('pe', lambda e: e.transpose(C.PS[6][:64, 0:NH], logfT[:, SEQ:NT], C.ident[:NH, :NH]),
             reads=['logfT', 'ident'], writes=[('ps', 6)])
        S.op('act', lambda e: e.activation(lfs[:], C.PS[6][:64, 0:NH], AF.Copy), reads=[('ps', 6)], writes=['lfs'])
        S.dma('act', T['fls'][:, :], lfs[:], reads=['lfs'])
        clf_tm = sb("clf_tm", [128, 8, NH])
        clfT = [sb("clfT%d" % s, [NH, PAST]) for s in range(NS)]
        ccT = [sb("ccT%d" % s, [NH, PAST]) for s in range(NS)]
        cnT = [sb("cnT%d" % s, [NH, DS]) for s in range(NS)]
        nck_c = [sb("nck_c%d" % s, [128, 8, NH]) for s in range(NS)]
        nck_n = [sb("nck_n%d" % s, [DS, NH]) for s in range(NS)]
        for s in range(NS):
            S.dma('sp', clf_tm[:], T['clf'][s].rearrange("(b p) h -> p b h", p=128), writes=['clf_tm'])
            for blk in range(8):
                bk = 6 + blk // 4
                S.op('pe', lambda e, blk=blk, bk=bk: e.transpose(
                    C.PS[bk][:NH, (blk % 4) * 128:(blk % 4 + 1) * 128], clf_tm[:, blk, :], C.ident[:, :]),
                    reads=['clf_tm', 'ident'], writes=[('ps', bk)])
            for hf in range(2):
                S.op('act', lambda e, hf=hf, s=s: e.activation(clfT[s][:, hf * 512:(hf + 1) * 512],
                                                              C.PS[6 + hf][:NH, :], AF.Copy),
                     reads=[('ps', 6 + hf)], writes=[('clfT', s)])
            S.op('dve', lambda e, s=s: e.tensor_tensor_scan(ccT[s][:], onesr[:, 0:PAST], clfT[s][:], 0.0,
                                                            ALU.mult, ALU.add),
                 reads=['onesr', ('clfT', s)], writes=[('ccT', s)])
            S.op('dve', lambda e, s=s: e.tensor_tensor_scan(cnT[s][:], onesr[:, 0:DS],
                                                            logfT[:, SEQ + s * DS:SEQ + (s + 1) * DS],
                                                            ccT[s][:, PAST - 1:PAST], ALU.mult, ALU.add),
                 reads=['onesr', 'logfT', ('ccT', s)], writes=[('cnT', s)])
            for blk in range(8):
                S.op('pe', lambda e, blk=blk, s=s: e.transpose(C.PS[7][:, blk * 16:(blk + 1) * 16],
                                                               ccT[s][:, blk * 128:(blk + 1) * 128],
                                                               C.ident[:NH, :NH]),
                     reads=[('ccT', s), 'ident'], writes=[('ps', 7)])
            S.op('dve', lambda e, s=s: e.tensor_scalar(nck_c[s][:].rearrange("p b h -> p (b h)"),
                                                       C.PS[7][:, 0:128], -1.0, None, ALU.mult),
                 reads=[('ps', 7)], writes=[('nck_c', s)])
            S.op('pe', lambda e, s=s: e.transpose(C.PS[6][:DS, 0:NH], cnT[s][:, :], C.ident[:NH, :NH]),
                 reads=[('cnT', s), 'ident'], writes=[('ps', 6)])
            S.op('dve', lambda e, s=s: e.tensor_scalar(nck_n[s][:], C.PS[6][:DS, 0:NH], -1.0, None, ALU.mult),
                 reads=[('ps', 6)], writes=[('nck_n', s)])

        import os as _os
        _stop = int(_os.environ.get("KSTOP", "99"))
        if _stop <= 1:
            S.barrier()
            return
        qf = [sb("qf%d" % i, [128, SEQ]) for i in range(2)]
        kf = [sb("kf%d" % i, [128, SEQ]) for i in range(2)]
        vf = [sb("vf%d" % i, [128, 16, 128]) for i in range(2)]
        qb_ = [sb("qb%d" % i, [128, SEQ], BF16) for i in range(2)]
        kb_ = [sb("kb%d" % i, [128, SEQ], BF16) for i in range(2)]
        Vb = [sb("Vb%d" % i, [128, 16, 129], BF16) for i in range(2)]
        ots = [sb("ots%d" % i, [128, SEQ], BF16) for i in range(2)]
        for i in range(2):
            S.op('pool', lambda e, i=i: e.memset(Vb[i][:, :, 128:129], 1.0), writes=[('Vb', i)])

        def load_head(h):
            i = h % 2
            S.dma('sp', qf[i][:], T['BqT'][h * 128:(h + 1) * 128, 0:SEQ], writes=[('qf', i)])
            S.dma('sp', kf[i][:], T['BkT'][h * 128:(h + 1) * 128, 0:SEQ], writes=[('kf', i)])
            S.dma('sp', vf[i][:], T['fvp'][:, h * 128:(h + 1) * 128].rearrange("(b p) d -> p b d", p=128),
                  writes=[('vf', i)])
            S.op('pool', lambda e: e.tensor_copy(qb_[i][:], qf[i][:]), reads=[('qf', i)], writes=[('qb', i)])
            S.op('pool', lambda e: e.tensor_copy(kb_[i][:], kf[i][:]), reads=[('kf', i)], writes=[('kb', i)])
            S.op('pool', lambda e: e.tensor_copy(Vb[i][:, :, 0:128], vf[i][:]), reads=[('vf', i)], writes=[('Vb', i)])

        kblocks_p = [KB(b * 128, 128, b, b, b) for b in range(16)]
        load_head(0)
        _kh = int(_os.environ.get("KH", "16"))
        for h in range(_kh):
            i = h % 2
            if h + 1 < _kh:
                load_head(h + 1)
            tag = {'k': [('kb', i)], 'q': [('qb', i)], 'c': ['cTp'], 'nck': ['nck_p'], 'v': [('Vb', i)]}
            fox_core(S, C, P, kb_[i], Vb[i], lambda kb, h=h: nck_p[:kb.n, kb.vb, h:h + 1], qb_[i], cTp,
                     selc[:, h * 128:(h + 1) * 128], 16, 128, kblocks_p,
                     fox_out(S, C, P, 128, ots[i], ('ots', i)), tag)
            S.dma('act', T['OT'][AW + h * 128:AW + (h + 1) * 128, 0:SEQ], ots[i][:], reads=[('ots', i)])

        if _stop <= 2:
            S.barrier()
            return
        ckf = [sb("ckf%d" % i, [128, 8, 128]) for i in range(2)]
        cvf = [sb("cvf%d" % i, [128, 8, 128]) for i in range(2)]
        knf = [sb("knf%d" % i, [128, DS]) for i in range(2)]
        qnf = [sb("qnf%d" % i, [128, DS]) for i in range(2)]
        vnf = [sb("vnf%d" % i, [DS, 128]) for i in range(2)]
        kTs = [sb("kTs%d" % i, [128, PAST + DS], BF16) for i in range(2)]
        qTs = [sb("qTs%d" % i, [128, DS], BF16) for i in range(2)]
        Vs = [sb("Vs%d" % i, [128, 9, 129], BF16) for i in range(2)]
        otss = [sb("otss%d" % i, [128, DS], BF16) for i in range(2)]
        for i in range(2):
            S.op('pool', lambda e, i=i: e.memset(Vs[i][:, :, 128:129], 1.0), writes=[('Vs', i)])
        kblocks_s = [KB(b * 128, 128, b, 0, None) for b in range(8)] + [KB(PAST, DS, 8, 0, 0)]
        it = 0
        for s in range(NS):
            for h in range(NH):
                i = it % 2
                it += 1
                tcol = SEQ + s * DS
                S.dma('sp', ckf[i][:], T['ck'][s][:, h * 128:(h + 1) * 128].rearrange("(b p) d -> p b d", p=128),
                      writes=[('ckf', i)])
                S.dma('sp', cvf[i][:], T['cv'][s][:, h * 128:(h + 1) * 128].rearrange("(b p) d -> p b d", p=128),
                      writes=[('cvf', i)])
                S.dma('sp', knf[i][:], T['BkT'][h * 128:(h + 1) * 128, tcol:tcol + DS], writes=[('knf', i)])
                S.dma('sp', qnf[i][:], T['BqT'][h * 128:(h + 1) * 128, tcol:tcol + DS], writes=[('qnf', i)])
                S.dma('sp', vnf[i][:], T['fvs'][s * DS:(s + 1) * DS, h * 128:(h + 1) * 128], writes=[('vnf', i)])
                for blk in range(8):
                    bt = 6 + blk % 2
                    S.op('pe', lambda e, blk=blk, bt=bt, i=i: e.transpose(C.PS[bt][:, 0:128], ckf[i][:, blk, :],
                                                                          C.ident[:, :]),
                         reads=[('ckf', i), 'ident'], writes=[('ps', bt)])
                    S.op('dve' if blk % 2 else 'act',
                         (lambda e, blk=blk, bt=bt, i=i: e.tensor_copy(kTs[i][:, blk * 128:(blk + 1) * 128],
                                                                       C.PS[bt][:, 0:128])) if blk % 2 else
                         (lambda e, blk=blk, bt=bt, i=i: e.activation(kTs[i][:, blk * 128:(blk + 1) * 128],
                                                                      C.PS[bt][:, 0:128], AF.Copy)),
                         reads=[('ps', bt)], writes=[('kTs', i)])
                S.op('pool', lambda e, i=i: e.tensor_copy(kTs[i][:, PAST:PAST + DS], knf[i][:]),
                     reads=[('knf', i)], writes=[('kTs', i)])
                S.op('pool', lambda e, i=i: e.tensor_copy(qTs[i][:], qnf[i][:]), reads=[('qnf', i)],
                     writes=[('qTs', i)])
                S.op('pool', lambda e, i=i: e.tensor_copy(Vs[i][:, 0:8, 0:128], cvf[i][:]), reads=[('cvf', i)],
                     writes=[('Vs', i)])
                S.op('pool', lambda e, i=i: e.tensor_copy(Vs[i][:DS, 8, 0:128], vnf[i][:]), reads=[('vnf', i)],
                     writes=[('Vs', i)])
                tag = {'k': [('kTs', i)], 'q': [('qTs', i)], 'c': [('cnT', s)],
                       'nck': [('nck_c', s), ('nck_n', s)], 'v': [('Vs', i)]}

                def ncks(kb, s=s, h=h):
                    if kb.vb < 8:
                        return nck_c[s][:kb.n, kb.vb, h:h + 1]
                    return nck_n[s][:kb.n, h:h + 1]
                fox_core(S, C, P, kTs[i], Vs[i], ncks, qTs[i], cnT[s], selc[:, h * 128:(h + 1) * 128], 1, DS,
                         kblocks_s, fox_out(S, C, P, DS, otss[i], ('otss', i)), tag)
                S.dma('act', T['OT'][AW + h * 128:AW + (h + 1) * 128, tcol:tcol + DS], otss[i][:],
                      reads=[('otss', i)])
    S.barrier()


def phase_B(S, nc, C, T):
    with ExitStack() as st:
        sb = lambda name, shape, dt=F32: st.enter_context(nc.sbuf_tensor("B_" + name, shape, dt))
        CM = 64
        U_ = C.cmat[:, 0:128]
        SL_ = C.cmat[:, 128:256]
        ONES = C.cmat[:, 256:384]
        SU_ = C.cmat[:, 384:512]
        C128 = C.cmat[:, 512:640]
        I_ = C.ident
        cw = sb("cw", [128, 48, 4])
        nA = sb("nA", [CM, NH])
        dtb = sb("dtb", [CM, NH])
        normw = sb("normw", [128, 1])
        S.dma('sp', cw[:].rearrange("p j r -> p (j r)"), T['convw_t'][:, :], writes=['cw'])
        S.dma('sp', nA[:], T['alog_b'][0:CM, :], writes=['nA'])
        S.dma('sp', dtb[:], T['dtb_b'][0:CM, :], writes=['dtb'])
        S.dma('sp', normw[:], T['normw_c'][:, :], writes=['normw'])
        S.op('act', lambda e: e.activation(nA[:], nA[:], AF.Exp), reads=['nA'], writes=['nA'])
        S.op('dve', lambda e: e.tensor_scalar(nA[:], nA[:], -1.0, None, ALU.mult), reads=['nA'], writes=['nA'])
        NCHM = 32
        ab = sb("ab", [CM, NCHM, 32])
        g_all = sb("g_all", [CM, NCHM, NH])
        lnb = sb("lnb_unused", [CM, 1])
        beta = sb("beta", [CM, NCHM, NH])
        ecum = sb("ecum", [CM, NCHM, NH])
        ekd = sb("ekd", [CM, NCHM, NH])
        bec = sb("bec", [CM, NCHM, NH])
        gtot = sb("gtot", [128, NCHM, NH])
        xin = [sb("xin%d" % i, [128, 48, 3 + CM]) for i in range(2)]
        zin = sb("zin", [128, NH, CM])
        zs = sb("zs", [128, NH, CM])
        ta = sb("ta", [128, 48, CM])
        tb = sb("tb", [128, 48, CM])
        ys = sb("ys", [128, 48, CM])
        qkn = sb("qkn", [128, 32, CM])
        Sst = sb("Sst", [128, NH, 128])
        Stmp2 = [sb("Stmp%d" % i, [128, 4, 128]) for i in range(2)]
        OTb = sb("OTb", [128, NH, 256], BF16)
        q4 = lambda name, w: [sb(name + str(i), [CM, 4, w]) for i in range(2)]
        rhsG = q4("rhsG", CM)
        rhsGT = q4("rhsGT", CM)
        rhsB = q4("rhsB", CM)
        rhsE = q4("rhsE", CM)
        ex1 = q4("ex1", CM)
        ex2 = q4("ex2", CM)
        decU = q4("decU", CM)
        NN = [q4("NNa", CM), q4("NNb", CM)]
        NT_ = [q4("NTa", CM), q4("NTb", CM)]
        TT = [q4("TTa", CM), q4("TTb", CM)]
        bv = q4("bv", 128)
        bek = q4("bek", 128)
        wv = sb("wv", [CM, NH, 128])
        kdec = sb("kdec", [CM, NH, 128])
        qkm = sb("qkm", [CM, NH, CM])
        wkT = sb("wkT", [128, NH, CM])
        qdT = sb("qdT", [128, NH, CM])
        u_ = q4("u_", 128)
        osq = q4("osq", 128)
        on_ = q4("on_", 128)
        ms = [sb("ms%d" % i, [CM, 4]) for i in range(2)]
        rstd = [sb("rstd%d" % i, [CM, 4]) for i in range(2)]
        PS = C.PS
        nb = C.ps_next
        V = lambda fn, r, w: S.op('dve', fn, reads=r, writes=w)
        A = lambda fn, r, w: S.op('act', fn, reads=r, writes=w)
        G = lambda fn, r, w: S.op('pool', fn, reads=r, writes=w)
        PE = lambda fn, r, w: S.op('pe', fn, reads=r, writes=w)

        def seq(tok0, Tlen, Cn, conv_hist, s0, s_out, conv_out, name):
            NCH = Tlen // Cn
            S.dma('sp', ab[:Cn, :NCH, :], T['Aab'][tok0:tok0 + Tlen, :].rearrange("(n c) x -> c n x", c=Cn),
                  writes=['ab'])
            gv = g_all[:Cn, :NCH, :]
            V(lambda e: e.tensor_tensor(gv, ab[:Cn, :NCH, 0:NH], dtb[:Cn, None, :].to_broadcast([Cn, NCH, NH]),
                                        ALU.add), ['ab', 'dtb'], ['g_all'])
            A(lambda e: e.activation(gv, gv, AF.Exp), ['g_all'], ['g_all'])
            A(lambda e: e.activation(gv, gv, AF.Ln, bias=1.0, scale=1.0), ['g_all'], ['g_all'])
            V(lambda e: e.tensor_tensor(gv, gv, nA[:Cn, None, :].to_broadcast([Cn, NCH, NH]), ALU.mult),
              ['g_all', 'nA'], ['g_all'])
            bvw = beta[:Cn, :NCH, :]
            A(lambda e: e.activation(bvw, ab[:Cn, :NCH, NH:2 * NH], AF.Sigmoid), ['ab'], ['beta'])
            g2 = g_all[:Cn, :NCH, :].rearrange("c n h -> c (n h)") if NCH * NH <= 512 else None
            ncol = NCH * NH
            b1 = nb()
            PE(lambda e: e.matmul(PS[b1][:Cn, :ncol], lhsT=U_[:Cn, :Cn], rhs=g_all[:Cn, :NCH, :], start=True, stop=True),
               ['g_all', 'cmat'], [('ps', b1)])
            A(lambda e: e.activation(ecum[:Cn, :NCH, :].rearrange("c n h -> c (n h)"), PS[b1][:Cn, :ncol], AF.Exp),
              [('ps', b1)], ['ecum'])
            b2 = nb()
            PE(lambda e: e.matmul(PS[b2][:Cn, :ncol], lhsT=SL_[:Cn, :Cn], rhs=g_all[:Cn, :NCH, :], start=True, stop=True),
               ['g_all', 'cmat'], [('ps', b2)])
            A(lambda e: e.activation(ekd[:Cn, :NCH, :].rearrange("c n h -> c (n h)"), PS[b2][:Cn, :ncol], AF.Exp),
              [('ps', b2)], ['ekd'])
            b3 = nb()
            PE(lambda e: e.matmul(PS[b3][:, :ncol], lhsT=ONES[:Cn, :], rhs=g_all[:Cn, :NCH, :], start=True, stop=True),
               ['g_all', 'cmat'], [('ps', b3)])
            A(lambda e: e.activation(gtot[:, :NCH, :].rearrange("c n h -> c (n h)"), PS[b3][:, :ncol], AF.Exp),
              [('ps', b3)], ['gtot'])
            V(lambda e: e.tensor_tensor(bec[:Cn, :NCH, :], beta[:Cn, :NCH, :], ecum[:Cn, :NCH, :], ALU.mult),
              ['beta', 'ecum'], ['bec'])
            if s0 is None:
                V(lambda e: e.memset(Sst[:], 0.0), [], [('Sst', q) for q in range(4)])
            else:
                S.dma('sp', Sst[:], s0.rearrange("h k v -> k h v"), writes=[('Sst', q) for q in range(4)])

            def load_x(n):
                xi = xin[n % 2]
                t0 = tok0 + n * Cn
                if n == 0:
                    if conv_hist is None:
                        V(lambda e: e.memset(xi[:, :, 0:3], 0.0), [], [('xin', n % 2)])
                    else:
                        S.dma('sp', xi[:, :, 0:3], conv_hist.rearrange("p (j r) -> p j r", r=3), writes=[('xin', n % 2)])
                    S.dma('sp', xi[:, :, 3:3 + Cn], T['AqkvT'][:, t0:t0 + Cn].rearrange("(j p) t -> p j t", p=128),
                          writes=[('xin', n % 2)])
                else:
                    S.dma('sp', xi[:, :, 0:3 + Cn],
                          T['AqkvT'][:, t0 - 3:t0 + Cn].rearrange("(j p) t -> p j t", p=128), writes=[('xin', n % 2)])

            load_x(0)

            def chunk(n):
                if n + 1 < NCH:
                    load_x(n + 1)
                xi = xin[n % 2]
                xt = ('xin', n % 2)
                t0 = tok0 + n * Cn
                S.dma('sp', zin[:, :, :Cn], T['AzT'][:, t0:t0 + Cn].rearrange("(j p) t -> p j t", p=128), writes=['zin'])
                if n == NCH - 1:
                    for r in range(3):
                        S.dma('act', conv_out[r:r + 1, :].rearrange("o (j p) -> p j o", p=128), xi[:, :, Cn + r:Cn + r + 1],
                              reads=[xt], allow_slow_non_contiguous=True)
                cwb = lambda r: cw[:, :, r:r + 1].to_broadcast([128, 48, Cn])
                V(lambda e: e.tensor_tensor(ta[:, :, :Cn], xi[:, :, 0:Cn], cwb(0), ALU.mult), [xt, 'cw'], ['ta'])
                V(lambda e: e.tensor_tensor(tb[:, :, :Cn], xi[:, :, 1:1 + Cn], cwb(1), ALU.mult), [xt, 'cw'], ['tb'])
                V(lambda e: e.tensor_tensor(ys[:, :, :Cn], xi[:, :, 2:2 + Cn], cwb(2), ALU.mult), [xt, 'cw'], ['ys'])
                V(lambda e: e.tensor_tensor(tb[:, :, :Cn], tb[:, :, :Cn], ta[:, :, :Cn], ALU.add), ['ta', 'tb'], ['tb'])
                V(lambda e: e.tensor_tensor(ta[:, :, :Cn], xi[:, :, 3:3 + Cn], cwb(3), ALU.mult), [xt, 'cw'], ['ta'])
                V(lambda e: e.tensor_tensor(ys[:, :, :Cn], ys[:, :, :Cn], ta[:, :, :Cn], ALU.add), ['ta', 'ys'], ['ys'])
                V(lambda e: e.tensor_tensor(tb[:, :, :Cn], tb[:, :, :Cn], ys[:, :, :Cn], ALU.add), ['tb', 'ys'], ['tb'])
                A(lambda e: e.activation(ys[:, :, :Cn], tb[:, :, :Cn], AF.Silu), ['tb'], ['ys'])
                A(lambda e: e.activation(zs[:, :, :Cn], zin[:, :, :Cn], AF.Silu), ['zin'], ['zs'])
                V(lambda e: e.tensor_tensor(ta[:, 0:32, :Cn], ys[:, 0:32, :Cn], ys[:, 0:32, :Cn], ALU.mult), ['ys'], ['ta'])
                hp = 512 // Cn
                for part in range(32 // hp):
                    j0 = part * hp
                    isq = j0 < 16
                    bb = nb()
                    PE(lambda e, bb=bb, j0=j0, isq=isq: e.matmul(
                        PS[bb][:, :hp * Cn], lhsT=(C128 if isq else ONES), rhs=ta[:, j0:j0 + hp, :Cn],
                        start=True, stop=True), ['ta', 'cmat'], [('ps', bb)])
                    A(lambda e, bb=bb, j0=j0, isq=isq: e.activation(
                        tb[:, j0:j0 + hp, :Cn].rearrange("p j c -> p (j c)") if Cn == CM else tb[:, j0:j0 + hp, :Cn],
                        PS[bb][:, :hp * Cn] if Cn == CM else PS[bb][:, :hp * Cn].rearrange("p (j c) -> p j c", c=Cn),
                        AF.Sqrt, bias=(128e-6 if isq else 1e-6), scale=1.0),
                      [('ps', bb)], ['tb'])
                V(lambda e: e.reciprocal(tb[:, 0:32, :Cn], tb[:, 0:32, :Cn]), ['tb'], ['tb'])
                V(lambda e: e.tensor_tensor(qkn[:, :, :Cn], ys[:, 0:32, :Cn], tb[:, 0:32, :Cn], ALU.mult), ['ys', 'tb'],
                  ['qkn'])
                def quad(qd):
                    h0 = qd * 4
                    p = qd % 2
                    gq = g_all[:Cn, n, h0:h0 + 4]
                    W = 4 * Cn
                    flat = lambda t: t[:Cn, :, :Cn]
                    V(lambda e, p=p, gq=gq: e.tensor_tensor(flat(rhsG[p]), U_[:Cn, None, :Cn].to_broadcast([Cn, 4, Cn]),
                                                            gq[:, :, None].to_broadcast([Cn, 4, Cn]), ALU.mult),
                      ['g_all', 'cmat'], [('rhsG', p)])
                    V(lambda e, p=p, gq=gq: e.tensor_tensor(flat(rhsGT[p]), SL_[:Cn, None, :Cn].to_broadcast([Cn, 4, Cn]),
                                                            gq[:, :, None].to_broadcast([Cn, 4, Cn]), ALU.mult),
                      ['g_all', 'cmat'], [('rhsGT', p)])
                    V(lambda e, p=p, h0=h0: e.tensor_tensor(flat(rhsB[p]), I_[:Cn, None, :Cn].to_broadcast([Cn, 4, Cn]),
                                                            beta[:Cn, n, h0:h0 + 4][:, :, None].to_broadcast([Cn, 4, Cn]),
                                                            ALU.mult), ['beta', 'ident'], [('rhsB', p)])
                    V(lambda e, p=p, h0=h0: e.tensor_tensor(flat(rhsE[p]), I_[:Cn, None, :Cn].to_broadcast([Cn, 4, Cn]),
                                                            ecum[:Cn, n, h0:h0 + 4][:, :, None].to_broadcast([Cn, 4, Cn]),
                                                            ALU.mult), ['ecum', 'ident'], [('rhsE', p)])
                    bDT, bD, bB, bE, bKQ = nb(), nb(), nb(), nb(), nb()
                    v3 = lambda b, P_=Cn: PS[b][:P_, :W].rearrange("p (h c) -> p h c", c=Cn)
                    PE(lambda e, p=p, b=bDT: e.matmul(PS[b][:Cn, :W], lhsT=SL_[:Cn, :Cn], rhs=flat(rhsG[p]), start=True, stop=True),
                       [('rhsG', p), 'cmat'], [('ps', bDT)])
                    PE(lambda e, p=p, b=bD: e.matmul(PS[b][:Cn, :W], lhsT=U_[:Cn, :Cn], rhs=flat(rhsGT[p]), start=True, stop=True),
                       [('rhsGT', p), 'cmat'], [('ps', bD)])
                    PE(lambda e, p=p, b=bB: e.matmul(PS[b][:Cn, :W], lhsT=ONES[:Cn, :Cn], rhs=flat(rhsB[p]), start=True, stop=True),
                       [('rhsB', p), 'cmat'], [('ps', bB)])
                    PE(lambda e, p=p, b=bE: e.matmul(PS[b][:, :W], lhsT=ONES[:Cn, :], rhs=flat(rhsE[p]), start=True, stop=True),
                       [('rhsE', p), 'cmat'], [('ps', bE)])
                    for hh in range(4):
                        h = h0 + hh
                        PE(lambda e, h=h, hh=hh, b=bKQ: e.matmul(PS[b][:Cn, hh * Cn:(hh + 1) * Cn], lhsT=qkn[:, 16 + h, :Cn],
                                                                 rhs=qkn[:, 16 + h, :Cn], start=True, stop=True),
                           ['qkn'], [('ps', bKQ)])
                    bQK = nb()
                    for hh in range(4):
                        h = h0 + hh
                        PE(lambda e, h=h, hh=hh, b=bQK: e.matmul(PS[b][:Cn, hh * Cn:(hh + 1) * Cn], lhsT=qkn[:, 16 + h, :Cn],
                                                                 rhs=qkn[:, h, :Cn], start=True, stop=True),
                           ['qkn'], [('ps', bQK)])
                    A(lambda e, p=p, b=bDT: e.activation(flat(ex1[p]), v3(b), AF.Exp), [('ps', bDT)], [('ex1', p)])
                    A(lambda e, p=p, b=bD: e.activation(flat(ex2[p]), v3(b), AF.Exp), [('ps', bD)], [('ex2', p)])
                    V(lambda e, p=p: e.tensor_tensor(flat(decU[p]), flat(ex1[p]), U_[:Cn, None, :Cn].to_broadcast([Cn, 4, Cn]),
                                                     ALU.mult), [('ex1', p), 'cmat'], [('decU', p)])
                    V(lambda e, p=p, b=bQK, h0=h0: e.tensor_tensor(qkm[:Cn, h0:h0 + 4, :Cn], v3(b), flat(decU[p]), ALU.mult),
                      [('ps', bQK), ('decU', p)], [('qkm', qd)])
                    V(lambda e, p=p: e.tensor_tensor(flat(ex1[p]), flat(ex1[p]), SU_[:Cn, None, :Cn].to_broadcast([Cn, 4, Cn]),
                                                     ALU.mult), [('ex1', p), 'cmat'], [('ex1', p)])
                    V(lambda e, p=p, b=bKQ: e.tensor_tensor(flat(ex1[p]), v3(b), flat(ex1[p]), ALU.mult),
                      [('ps', bKQ), ('ex1', p)], [('ex1', p)])
                    V(lambda e, p=p, b=bB: e.scalar_tensor_tensor(
                        out=NT_[0][p][:Cn, :, :Cn].rearrange("p h c -> p (h c)") if Cn == CM else NT_[0][p][:Cn, :, :Cn],
                        in0=ex1[p][:Cn, :, :Cn].rearrange("p h c -> p (h c)") if Cn == CM else ex1[p][:Cn, :, :Cn],
                        scalar=-1.0,
                        in1=PS[b][:Cn, :W] if Cn == CM else v3(b), op0=ALU.mult, op1=ALU.mult),
                      [('ps', bB), ('ex1', p)], [('NT0', p)])
                    V(lambda e, p=p: e.tensor_tensor(flat(ex2[p]), flat(ex2[p]), SL_[:Cn, None, :Cn].to_broadcast([Cn, 4, Cn]),
                                                     ALU.mult), [('ex2', p), 'cmat'], [('ex2', p)])
                    V(lambda e, p=p, b=bKQ: e.tensor_tensor(flat(ex2[p]), v3(b), flat(ex2[p]), ALU.mult),
                      [('ps', bKQ), ('ex2', p)], [('ex2', p)])
                    V(lambda e, p=p, h0=h0: e.tensor_tensor(flat(ex2[p]), flat(ex2[p]),
                                                            beta[:Cn, n, h0:h0 + 4][:, :, None].to_broadcast([Cn, 4, Cn]),
                                                            ALU.mult), [('ex2', p), 'beta'], [('ex2', p)])
                    V(lambda e, p=p: e.tensor_scalar(flat(NN[0][p]), flat(ex2[p]), -1.0, None, ALU.mult),
                      [('ex2', p)], [('NN0', p)])
                    V(lambda e, b=bE, h0=h0: e.tensor_tensor(qdT[:, h0:h0 + 4, :Cn], qkn[:, h0:h0 + 4, :Cn],
                                                             PS[b][:, :W].rearrange("p (h c) -> p h c", c=Cn), ALU.mult),
                      [('ps', bE), 'qkn'], [('qdT', qd)])
                    V(lambda e, p=p: e.tensor_tensor(flat(TT[0][p]), flat(NT_[0][p]), I_[:Cn, None, :Cn].to_broadcast([Cn, 4, Cn]),
                                                     ALU.add), [('NT0', p), 'ident'], [('TT0', p)])
                    yield
                    nlev = 5 if Cn == 64 else 4
                    for lv in range(nlev):
                        yield
                        a, bq = lv % 2, (lv + 1) % 2
                        last = lv == nlev - 1
                        bN, bNT, bT = nb(), (None if last else nb()), nb()
                        for hh in range(4):
                            PE(lambda e, hh=hh, a=a, p=p, b=bN: e.matmul(PS[b][:Cn, hh * Cn:(hh + 1) * Cn], lhsT=NT_[a][p][:Cn, hh, :Cn],
                                                                         rhs=NN[a][p][:Cn, hh, :Cn], start=True, stop=True),
                               [('NT%d' % a, p), ('NN%d' % a, p)], [('ps', bN)])
                        if not last:
                            for hh in range(4):
                                PE(lambda e, hh=hh, a=a, p=p, b=bNT: e.matmul(PS[b][:Cn, hh * Cn:(hh + 1) * Cn], lhsT=NN[a][p][:Cn, hh, :Cn],
                                                                              rhs=NT_[a][p][:Cn, hh, :Cn], start=True, stop=True),
                                   [('NT%d' % a, p), ('NN%d' % a, p)], [('ps', bNT)])
                        yield
                        A(lambda e, bq=bq, p=p, b=bN: e.activation(flat(NN[bq][p]), v3(b), AF.Copy), [('ps', bN)], [('NN%d' % bq, p)])
                        if not last:
                            V(lambda e, bq=bq, p=p, b=bNT: e.tensor_copy(flat(NT_[bq][p]), v3(b)), [('ps', bNT)], [('NT%d' % bq, p)])
                        for hh in range(4):
                            PE(lambda e, hh=hh, a=a, bq=bq, p=p, b=bT: e.matmul(PS[b][:Cn, hh * Cn:(hh + 1) * Cn], lhsT=NN[bq][p][:Cn, hh, :Cn],
                                                                                rhs=TT[a][p][:Cn, hh, :Cn], start=True, stop=True),
                               [('NN%d' % bq, p), ('TT%d' % a, p)], [('ps', bT)])
                        yield
                        V(lambda e, a=a, bq=bq, p=p, b=bT: e.tensor_tensor(flat(TT[bq][p]), flat(TT[a][p]), v3(b), ALU.add),
                          [('ps', bT), ('TT%d' % a, p)], [('TT%d' % bq, p)])
                    yield
                    tf = nlev % 2
                    TTf = TT[tf][p]
                    ttag = ('TT%d' % tf, p)
                    bK, bV = nb(), nb()
                    for hh in range(4):
                        h = h0 + hh
                        PE(lambda e, h=h, hh=hh, b=bK: e.transpose(PS[b][:Cn, hh * 128:(hh + 1) * 128], qkn[:, 16 + h, :Cn], I_[:, :]),
                           ['qkn', 'ident'], [('ps', bK)])
                        PE(lambda e, h=h, hh=hh, b=bV: e.transpose(PS[b][:Cn, hh * 128:(hh + 1) * 128], ys[:, 32 + h, :Cn], I_[:, :]),
                           ['ys', 'ident'], [('ps', bV)])
                    k3 = lambda b: PS[b][:Cn, :512].rearrange("p (h d) -> p h d", d=128)
                    bc3 = lambda t, h0=h0: t[:Cn, n, h0:h0 + 4][:, :, None].to_broadcast([Cn, 4, 128])
                    V(lambda e, p=p, b=bK: e.tensor_tensor(bek[p][:Cn, :, :], k3(b), bc3(bec), ALU.mult), [('ps', bK), 'bec'],
                      [('bek', p)])
                    V(lambda e, b=bK, h0=h0: e.tensor_tensor(kdec[:Cn, h0:h0 + 4, :], k3(b), bc3(ekd), ALU.mult),
                      [('ps', bK), 'ekd'], [('kdec', qd)])
                    V(lambda e, p=p, b=bV: e.tensor_tensor(bv[p][:Cn, :, :], k3(b), bc3(beta), ALU.mult), [('ps', bV), 'beta'],
                      [('bv', p)])
                    bWV, bWK = nb(), nb()
                    for hh in range(4):
                        PE(lambda e, hh=hh, p=p, b=bWV, TTf=TTf: e.matmul(PS[b][:Cn, hh * 128:(hh + 1) * 128], lhsT=TTf[:Cn, hh, :Cn],
                                                                          rhs=bv[p][:Cn, hh, :], start=True, stop=True),
                           [ttag, ('bv', p)], [('ps', bWV)])
                        PE(lambda e, hh=hh, p=p, b=bWK, TTf=TTf: e.matmul(PS[b][:, hh * Cn:(hh + 1) * Cn], lhsT=bek[p][:Cn, hh, :],
                                                                          rhs=TTf[:Cn, hh, :Cn], start=True, stop=True),
                           [ttag, ('bek', p)], [('ps', bWK)])
                    A(lambda e, b=bWV, h0=h0: e.activation(wv[:Cn, h0:h0 + 4, :], k3(b), AF.Copy), [('ps', bWV)], [('wv', qd)])
                    A(lambda e, b=bWK, h0=h0: e.activation(wkT[:, h0:h0 + 4, :Cn], PS[b][:, :W].rearrange("p (h c) -> p h c", c=Cn),
                                                           AF.Copy), [('ps', bWK)], [('wkT', qd)])
                    bU = nb()
                    for hh in range(4):
                        h = h0 + hh
                        PE(lambda e, h=h, hh=hh, b=bU: e.matmul(PS[b][:Cn, hh * 128:(hh + 1) * 128], lhsT=wkT[:, h, :Cn], rhs=Sst[:, h, :],
                                                                start=True, stop=True), [('wkT', qd), ('Sst', qd)], [('ps', bU)])
                    V(lambda e, p=p, b=bU, h0=h0: e.tensor_tensor(u_[p][:Cn, :, :], wv[:Cn, h0:h0 + 4, :], k3(b), ALU.subtract),
                      [('ps', bU), ('wv', qd)], [('u', p)])
                    bO = nb()
                    for hh in range(4):
                        h = h0 + hh
                        PE(lambda e, h=h, hh=hh, b=bO: e.matmul(PS[b][:Cn, hh * 128:(hh + 1) * 128], lhsT=qdT[:, h, :Cn], rhs=Sst[:, h, :],
                                                                start=True, stop=False), [('qdT', qd), ('Sst', qd)], [('ps', bO)])
                        PE(lambda e, h=h, hh=hh, p=p, b=bO: e.matmul(PS[b][:Cn, hh * 128:(hh + 1) * 128], lhsT=qkm[:Cn, h, :Cn],
                                                                     rhs=u_[p][:Cn, hh, :], start=False, stop=True),
                           [('qkm', qd), ('u', p)], [('ps', bO)])
                    bS_ = nb()
                    for hh in range(4):
                        h = h0 + hh
                        PE(lambda e, h=h, hh=hh, p=p, b=bS_: e.matmul(PS[b][:, hh * 128:(hh + 1) * 128], lhsT=kdec[:Cn, h, :],
                                                                      rhs=u_[p][:Cn, hh, :], start=True, stop=True),
                           [('kdec', qd), ('u', p)], [('ps', bS_)])
                    A(lambda e, p=p, b=bO: e.activation(osq[p][:Cn, :, :], k3(b), AF.Square), [('ps', bO)], [('osq', p)])
                    V(lambda e, p=p: e.tensor_reduce(ms[p][:Cn, :], osq[p][:Cn, :, :], AX.X, ALU.add), [('osq', p)], [('ms', p)])
                    A(lambda e, p=p: e.activation(rstd[p][:Cn, :], ms[p][:Cn, :], AF.Sqrt, bias=1e-6, scale=1.0 / 128), [('ms', p)],
                      [('rstd', p)])
                    V(lambda e, p=p: e.reciprocal(rstd[p][:Cn, :], rstd[p][:Cn, :]), [('rstd', p)], [('rstd', p)])
                    V(lambda e, p=p, b=bO: e.tensor_tensor(on_[p][:Cn, :, :], k3(b),
                                                           rstd[p][:Cn, :][:, :, None].to_broadcast([Cn, 4, 128]), ALU.mult),
                      [('ps', bO), ('rstd', p)], [('on', p)])
                    bOT = nb()
                    for hh in range(4):
                        PE(lambda e, hh=hh, p=p, b=bOT: e.transpose(PS[b][:, hh * Cn:(hh + 1) * Cn], on_[p][:Cn, hh, :], I_[:Cn, :Cn]),
                           [('on', p), 'ident'], [('ps', bOT)])
                    oc = (n % (256 // Cn)) * Cn
                    V(lambda e, b=bOT, h0=h0, oc=oc: e.scalar_tensor_tensor(
                        out=OTb[:, h0:h0 + 4, oc:oc + Cn], in0=PS[b][:, :W].rearrange("p (h c) -> p h c", c=Cn),
                        scalar=normw[:, 0:1], in1=zs[:, h0:h0 + 4, :Cn], op0=ALU.mult, op1=ALU.mult),
                      [('ps', bOT), 'zs', 'normw'], [('OTb', qd)])
                    V(lambda e, h0=h0: e.tensor_tensor(Stmp2[p][:, :, :], Sst[:, h0:h0 + 4, :],
                                                       gtot[:, n, h0:h0 + 4][:, :, None].to_broadcast([128, 4, 128]), ALU.mult),
                      [('Sst', qd), 'gtot'], [('Stmp', qd)])
                    V(lambda e, h0=h0, b=bS_: e.tensor_tensor(Sst[:, h0:h0 + 4, :], Stmp2[p][:, :, :],
                                                              PS[b][:, :512].rearrange("p (h d) -> p h d", d=128), ALU.add),
                      [('ps', bS_), ('Stmp', qd)], [('Sst', qd)])
                for pair_ in ((0, 1), (2, 3)):
                    alive = [quad(q_) for q_ in pair_]
                    while alive:
                        for g_ in list(alive):
                            try:
                                next(g_)
                            except StopIteration:
                                alive.remove(g_)
                per = 256 // Cn
                if (n + 1) % per == 0 or n == NCH - 1:
                    nfl = ((n % per) + 1) * Cn
                    tf0 = tok0 + (n // per) * 256
                    S.dma('act', T['OT'][0:AW, tf0:tf0 + nfl].rearrange("(h p) t -> p h t", p=128), OTb[:, :, 0:nfl],
                          reads=[('OTb', q_) for q_ in range(4)])
            for n in range(NCH):
                chunk(n)
            S.dma('act', s_out.rearrange("h k v -> k h v"), Sst[:], reads=[('Sst', q) for q in range(4)])

        seq(0, SEQ, 64, None, None, T['sgp'][:, :, :], T['scp'][:, :], "p")
        for s in range(NS):
            seq(SEQ + s * DS, DS, DS, T['sconv_t'][s], T['sg'][s], T['sgs'][s], T['scs'][s], "s%d" % s)
    S.barrier()


def ln_rows(S, nc, L, n, vt, vtag, gB, bB, out_ap_sb, out_tag):
    i = L.i % 2
    L.i += 1
    sm, ssq, rs = L.sm[i], L.ssq[i], L.rs[i]
    S.op('act', lambda e: e.activation(L.junk[:n, :], vt[:n, :], AF.Identity, accum_out=sm[:n, :]),
         reads=[vtag], writes=[('lnsm', i), 'lnjunk'])
    S.op('dve', lambda e: e.tensor_scalar(sm[:n, :], sm[:n, :], 1.0 / D, None, ALU.mult), reads=[('lnsm', i)],
         writes=[('lnsm', i)])
    S.op('dve', lambda e: e.tensor_scalar(vt[:n, :], vt[:n, :], sm[:n, 0:1], None, ALU.subtract),
         reads=[vtag, ('lnsm', i)], writes=[vtag])
    S.op('act', lambda e: e.activation(L.junk[:n, :], vt[:n, :], AF.Square, accum_out=ssq[:n, :]),
         reads=[vtag], writes=[('lnssq', i), 'lnjunk'])
    S.op('act', lambda e: e.activation(rs[:n, :], ssq[:n, :], AF.Sqrt, bias=L.eps[:n, 0:1], scale=1.0 / D),
         reads=[('lnssq', i), 'lneps'], writes=[('lnrs', i)])
    S.op('dve', lambda e: e.reciprocal(rs[:n, :], rs[:n, :]), reads=[('lnrs', i)], writes=[('lnrs', i)])
    S.op('dve', lambda e: e.scalar_tensor_tensor(out=vt[:n, :], in0=vt[:n, :], scalar=rs[:n, 0:1], in1=gB[:n, :],
                                                 op0=ALU.mult, op1=ALU.mult),
         reads=[vtag, ('lnrs', i), 'lng'], writes=[vtag])
    S.op('dve', lambda e: e.tensor_tensor(out_ap_sb[:n, :], vt[:n, :], bB[:n, :], ALU.add),
         reads=[vtag, 'lnb'], writes=[out_tag])


def ln_setup(S, nc, st, T, gname, bname, pref):
    L = Ctx()
    sb = lambda name, shape, dt=F32: st.enter_context(nc.sbuf_tensor(pref + name, shape, dt))
    L.i = 0
    L.sm = [sb("sm%d" % i, [128, 1]) for i in range(2)]
    L.ssq = [sb("ssq%d" % i, [128, 1]) for i in range(2)]
    L.rs = [sb("rs%d" % i, [128, 1]) for i in range(2)]
    L.eps = sb("eps", [128, 1])
    L.junk = sb("junk", [128, D], BF16)
    L.g = sb("g", [128, D])
    L.b = sb("b", [128, D])
    S.op('dve', lambda e: e.memset(L.eps[:], LN_EPS), writes=['lneps'])
    S.dma('sp', L.g[:], T[gname][:, :], writes=['lng'])
    S.dma('sp', L.b[:], T[bname][:, :], writes=['lnb'])
    return L


def row_tiles():
    return [(i * 128, 128) for i in range(16)] + [(SEQ, NS * DS)]


def x_rows(T, r0, n):
    return T['xp'][r0:r0 + n, :] if r0 < SEQ else T['xs'][r0 - SEQ:r0 - SEQ + n, :]


def y_rows(T, r0, n):
    return T['yp'][r0:r0 + n, :] if r0 < SEQ else T['ys'][r0 - SEQ:r0 - SEQ + n, :]


def phase_D(S, nc, C, T):
    import os as _os
    _kd = int(_os.environ.get("KD", "3"))
    with ExitStack() as st:
        if not (_kd & 1):
            st.close()
            return phase_D2(S, nc, C, T, _kd)
        C.XT = st.enter_context(nc.sbuf_tensor("D_XT", [128, KC, GT], BF16))
        C.wst = [st.enter_context(nc.sbuf_tensor("D_wst%d" % i, [128, KC, 128], F32)) for i in range(2)]
        C.wbf = [st.enter_context(nc.sbuf_tensor("D_wbf%d" % i, [128, KC, 512], BF16)) for i in range(2)]
        mst = [st.enter_context(nc.sbuf_tensor("D_mst%d" % i, [128, 512], F32)) for i in range(3)]
        C.wst_i = 0
        cnt = {'m': 0}
        for g in range(2):
            S.dma('sp', C.XT[:, :, 0:1024], T['OT'][:, g * 1024:(g + 1) * 1024].rearrange("(kc p) t -> p kc t", p=128),
                  writes=[('XT', i) for i in range(8)])
            S.dma('sp', C.XT[:, :, 1024:GT], T['OT'][:, SEQ + g * DS:SEQ + (g + 1) * DS].rearrange("(kc p) t -> p kc t", p=128),
                  writes=[('XT', 8)])
            load_w(S, C, T['w_out'], 0, 512, 0)
            for cg in range(8):
                if cg + 1 < 8:
                    load_w(S, C, T['w_out'], (cg + 1) * 512, 512, (cg + 1) % 2)

                def emit(b, c0, n, cg=cg, g=g):
                    mi = cnt['m'] % 3
                    cnt['m'] += 1
                    mt = mst[mi]
                    if cnt['m'] % 2:
                        S.op('act', lambda e: e.activation(mt[:n, :], C.PS[b][:n, :], AF.Copy), reads=[('ps', b)],
                             writes=[('mst', mi)])
                    else:
                        S.op('dve', lambda e: e.tensor_copy(mt[:n, :], C.PS[b][:n, :]), reads=[('ps', b)],
                             writes=[('mst', mi)])
                    r0 = tokmap(g, c0, n)
                    S.dma('act', T['MIX'][r0:r0 + n, cg * 512:(cg + 1) * 512], mt[:n, :], reads=[('mst', mi)])
                gemm_T(S, C, cg % 2, 512, emit)
    S.barrier()
    phase_D2(S, nc, C, T, _kd)


def phase_D2(S, nc, C, T, _kd):
    if not (_kd & 2):
        return
    with ExitStack() as st:
        L = ln_setup(S, nc, st, T, 'ln1g_b', 'ln1b_b', "D_ln")
        xt_ = [st.enter_context(nc.sbuf_tensor("D_x%d" % i, [128, D], F32)) for i in range(2)]
        mx_ = [st.enter_context(nc.sbuf_tensor("D_m%d" % i, [128, D], F32)) for i in range(2)]

        def tile(ti, r0, n):
            i = ti % 2
            S.dma('sp', xt_[i][:n, :], x_rows(T, r0, n), writes=[('dx', i)])
            S.dma('sp', mx_[i][:n, :], T['MIX'][r0:r0 + n, :], writes=[('dm', i)])
            S.op('dve', lambda e: e.scalar_tensor_tensor(out=xt_[i][:n, :], in0=xt_[i][:n, :], scalar=float(ALPHA),
                                                         in1=mx_[i][:n, :], op0=ALU.mult, op1=ALU.add),
                 reads=[('dx', i), ('dm', i)], writes=[('dx', i)])
            ln_rows(S, nc, L, n, xt_[i], ('dx', i), L.g, L.b, mx_[i], ('dm', i))
            S.dma('act', T['H'][r0:r0 + n, :], mx_[i][:n, :], reads=[('dm', i)])
        for ti, (r0, n) in enumerate(row_tiles()):
            tile(ti, r0, n)
    S.barrier()


def phase_E(S, nc, C, T):
    NEG = -1.0e30
    with ExitStack() as st:
        sb = lambda name, shape, dt=F32: st.enter_context(nc.sbuf_tensor("E1_" + name, shape, dt))
        C.XT = sb("XT", [128, KC, GT], BF16)
        C.xrow = [sb("xrow%d" % i, [128, D]) for i in range(2)]
        C.wst = [sb("wst%d" % i, [128, KC, 128]) for i in range(2)]
        C.wbf = [sb("wbf%d" % i, [128, KC, 128], BF16) for i in range(2)]
        C.wst_i = 0
        skn = sb("skn", [128, 16, 128])
        KT = sb("KT", [128, 16, 128])
        qst = [sb("qst%d" % i, [128, GT]) for i in range(2)]
        scs_ = [sb("scs%d" % i, [128, 128]) for i in range(3)]
        S.dma('sp', skn[:], T['subk'][:, :, :].rearrange("j n c -> n j c"), writes=['skn'])
        for j in range(16):
            b = C.ps_next()
            S.op('pe', lambda e, j=j, b=b: e.transpose(C.PS[b][:, 0:128], skn[:, j, :], C.ident[:, :]),
                 reads=['skn', 'ident'], writes=[('ps', b)])
            S.op('dve', lambda e, j=j, b=b: e.tensor_copy(KT[:, j, :], C.PS[b][:, 0:128]), reads=[('ps', b)],
                 writes=['KT'])
        cnt = {'s': 0, 'e': 0}
        for g in range(2):
            rows = [(T['H'][g * 1024 + i * 128:g * 1024 + (i + 1) * 128, :], 128, i * 128) for i in range(8)]
            rows.append((T['H'][SEQ + g * DS:SEQ + (g + 1) * DS, :], DS, 1024))
            build_xT(S, C, g, rows)
            load_w(S, C, T['w_q'], 0, 128, 0)
            for j in range(16):
                if j + 1 < 16:
                    load_w(S, C, T['w_q'], (j + 1) * 128, 128, (j + 1) % 2)
                qs = qst[j % 2]

                def emitF(b, t0, n, qs=qs, j=j):
                    cnt['e'] += 1
                    if cnt['e'] % 2:
                        S.op('act', lambda e: e.activation(qs[:, t0:t0 + n], C.PS[b][:, :n], AF.Copy),
                             reads=[('ps', b)], writes=[('qst', j % 2)])
                    else:
                        S.op('dve', lambda e: e.tensor_copy(qs[:, t0:t0 + n], C.PS[b][:, :n]),
                             reads=[('ps', b)], writes=[('qst', j % 2)])
                gemm_F(S, C, j % 2, 128, emitF)

                def score(ti, j=j, qs=qs, g=g):
                    c0 = ti * 128
                    n = 128 if ti < 8 else DS
                    b = C.ps_next()
                    S.op('pe', lambda e: e.matmul(C.PS[b][:n, 0:128], lhsT=qs[:, c0:c0 + n], rhs=KT[:, j, :],
                                                  start=True, stop=True),
                         reads=[('qst', j % 2), 'KT'], writes=[('ps', b)])
                    si = cnt['s'] % 3
                    cnt['s'] += 1
                    sc_t = scs_[si]
                    S.op('dve' if si % 2 else 'act',
                         (lambda e: e.tensor_copy(sc_t[:n, :], C.PS[b][:n, 0:128])) if si % 2 else
                         (lambda e: e.activation(sc_t[:n, :], C.PS[b][:n, 0:128], AF.Copy)),
                         reads=[('ps', b)], writes=[('scs', si)])
                    r0 = tokmap(g, c0, n)
                    S.dma('act', T['SC'][r0:r0 + n, j * 128:(j + 1) * 128], sc_t[:n, :], reads=[('scs', si)])
                for ti in range(9):
                    score(ti)
    S.barrier()
    with ExitStack() as st:
        sb = lambda name, shape, dt=F32: st.enter_context(nc.sbuf_tensor("E2_" + name, shape, dt))
        L = ln_setup(S, nc, st, T, 'ln2g_b', 'ln2b_b', "E_ln")
        sc = sb("sc", [128, 16, 128])
        sc2 = sb("sc2", [128, 16, 128])
        stop_ = sb("stop", [128, 16, 16])
        itop = sb("itop", [128, 16, 16], U32)
        itf = sb("itf", [128, 16, 16])
        cand = sb("cand", [128, 8, 256])
        cand2 = sc2[:, :, :].rearrange("p (h c) k -> p h (c k)", c=2)
        cidx = sb("cidx", [128, 8, 256])
        tops = sb("tops", [128, 8, 16])
        eidf = sb("eidf", [128, 128])
        eidi = [sb("eidi%d" % i, [128, 128], I32) for i in range(2)]
        gate = sb("gate", [128, 8, 16])
        zs_ = sb("zsum", [128, 8])
        pre = sb("pre", [128, 128])
        actv = [sb("actv%d" % i, [128, 128]) for i in range(2)]
        ht = [sb("ht%d" % i, [128, D]) for i in range(2)]
        NGB = 4
        cbuf = [sb("cb%d" % i, [128, 2 * D], BF16) for i in range(NGB)]
        lnout = sb("lnout", [128, D])
        dg = [sb("dg%d" % i, [128, 128], BF16) for i in range(NGB)]
        identb = sb("identb", [128, 128], BF16)
        sj = [sb("sj%d" % i, [128, 256]) for i in range(2)]
        V = lambda fn, r, w: S.op('dve', fn, reads=r, writes=w)
        A = lambda fn, r, w: S.op('act', fn, reads=r, writes=w)
        G = lambda fn, r, w: S.op('pool', fn, reads=r, writes=w)
        V(lambda e: e.tensor_copy(identb[:], C.ident[:]), ['ident'], ['identb'])

        def e2(k, r0, n):
            par = k % 2
            S.dma('sp', sc[:n, :, :], T['SC'][r0:r0 + n, :].rearrange("t (j k) -> t j k", k=128), writes=['sc'])
            S.dma('sp', ht[par][:n, :], T['H'][r0:r0 + n, :], writes=[('ht', par)])
            for j in range(16):
                def pair(j):
                    V(lambda e: e.max(out=stop_[:n, j, 0:8], in_=sc[:n, j, :]), ['sc'], [('stopA', j)])
                    V(lambda e: e.max_index(out=itop[:n, j, 0:8], in_max=stop_[:n, j, 0:8], in_values=sc[:n, j, :]),
                      ['sc', ('stopA', j)], [('itopA', j)])
                    V(lambda e: e.match_replace(out=sc2[:n, j, :], in_to_replace=stop_[:n, j, 0:8], in_values=sc[:n, j, :],
                                                imm_value=NEG), ['sc', ('stopA', j)], [('sc2', j)])
                    V(lambda e: e.max(out=stop_[:n, j, 8:16], in_=sc2[:n, j, :]), [('sc2', j)], [('stopB', j)])
                    V(lambda e: e.max_index(out=itop[:n, j, 8:16], in_max=stop_[:n, j, 8:16], in_values=sc2[:n, j, :]),
                      [('sc2', j), ('stopB', j)], [('itopB', j)])
                pair(j)
            V(lambda e: e.tensor_copy(itf[:n, :, :], itop[:n, :, :]), [('itopA', j_) for j_ in range(16)] + [('itopB', j_) for j_ in range(16)], ['itf'])
            s4 = stop_[:n, :, :].rearrange("t (h p) k -> t h p k", p=2)
            i4 = itf[:n, :, :].rearrange("t (h p) k -> t h p k", p=2)
            c4 = lambda t: t[:n, :, :].rearrange("t h (a b) -> t h a b", b=16)
            V(lambda e: e.tensor_tensor(c4(cand), s4[:, :, 0, :][:, :, :, None].to_broadcast([n, 8, 16, 16]),
                                        s4[:, :, 1, :][:, :, None, :].to_broadcast([n, 8, 16, 16]), ALU.add),
              [('stopA', j_) for j_ in range(16)] + [('stopB', j_) for j_ in range(16)], ['cand'])
            V(lambda e: e.tensor_scalar(i4[:, :, 0, :], i4[:, :, 0, :], 128.0, None, ALU.mult), ['itf'], ['itf'])
            V(lambda e: e.tensor_tensor(c4(cidx), i4[:, :, 0, :][:, :, :, None].to_broadcast([n, 8, 16, 16]),
                                        i4[:, :, 1, :][:, :, None, :].to_broadcast([n, 8, 16, 16]), ALU.add),
              ['itf'], ['cidx'])
            for hd in range(8):
                def head(hd):
                    V(lambda e: e.max(out=tops[:n, hd, 0:8], in_=cand[:n, hd, :]), ['cand'], [('tops', hd)])
                    V(lambda e: e.match_replace(out=cand2[:n, hd, :], in_to_replace=tops[:n, hd, 0:8],
                                                in_values=cand[:n, hd, :], imm_value=NEG), ['cand', ('tops', hd)], [('sc2', 2 * hd), ('sc2', 2 * hd + 1)])
                    V(lambda e: e.max(out=tops[:n, hd, 8:16], in_=cand2[:n, hd, :]), [('sc2', 2 * hd), ('sc2', 2 * hd + 1)], [('tops', hd)])
                    for kk in range(16):
                        def pick(kk):
                            m = hd * 16 + kk
                            V(lambda e: e.scalar_tensor_tensor(out=sj[m % 2][:n, :], in0=cand[:n, hd, :],
                                                               scalar=tops[:n, hd, kk:kk + 1], in1=cidx[:n, hd, :],
                                                               op0=ALU.is_equal, op1=ALU.mult,
                                                               accum_out=eidf[:n, m:m + 1]),
                              ['cand', ('tops', hd), 'cidx'], [('sj', m % 2), ('eidf', m)])
                        pick(kk)
                head(hd)
            V(lambda e: e.tensor_scalar(eidf[:n, :], eidf[:n, :], 16383.0, 0.0, ALU.min, ALU.max), [('eidf', m_) for m_ in range(128)], ['eidf'] + [('eidf', m_) for m_ in range(128)])
            V(lambda e: e.tensor_copy(eidi[par][:n, :], eidf[:n, :]), ['eidf'], [('eidi', par)])
            V(lambda e: e.tensor_tensor(gate[:n, :, :], tops[:n, :, :], tops[:n, :, 0:1].to_broadcast([n, 8, 16]),
                                        ALU.subtract), [('tops', h_) for h_ in range(8)], ['gate'])
            A(lambda e: e.activation(gate[:n, :, :], gate[:n, :, :], AF.Exp), ['gate'], ['gate'])
            V(lambda e: e.tensor_reduce(zs_[:n, :], gate[:n, :, :], AX.X, ALU.add), ['gate'], ['zsum'])
            V(lambda e: e.reciprocal(zs_[:n, :], zs_[:n, :]), ['zsum'], ['zsum'])
            V(lambda e: e.tensor_tensor(gate[:n, :, :], gate[:n, :, :], zs_[:n, :][:, :, None].to_broadcast([n, 8, 16]),
                                        ALU.mult), ['gate', 'zsum'], ['gate'])

        def pick(k, n, m):
            par = k % 2
            i = m % NGB
            S.gather(cbuf[i][:n, :], T['PVB'][:, :], eidi[par][:n, m:m + 1], reads=[('eidi', par)], writes=[('cb', i)],
                     bounds=None)
            V(lambda e: e.scalar_tensor_tensor(out=L.junk[:n, :], in0=cbuf[i][:n, 0:D], scalar=1.0, in1=ht[par][:n, :],
                                               op0=ALU.mult, op1=ALU.mult, accum_out=pre[:n, m:m + 1]),
              [('cb', i), ('ht', par)], [('pre', m)])
            A(lambda e: e.activation(actv[par][:n, m:m + 1], pre[:n, m:m + 1], AF.Gelu), [('pre', m)], [('actv', m)])

        def pickB(k, n, m):
            par = k % 2
            i = m % NGB
            gflat = gate[:n, :, :].rearrange("t h k -> t (h k)")
            V(lambda e: e.tensor_scalar(dg[i][:n, :n], identb[:n, :n], actv[par][:n, m:m + 1], gflat[:, m:m + 1],
                                        ALU.mult, ALU.mult), ['identb', ('actv', m), 'gate'], [('dg', i)])
            for cb in range(8):
                S.op('pe', lambda e, cb=cb: e.matmul(C.PS[cb][:n, :], lhsT=dg[i][:n, :n],
                                                     rhs=cbuf[i][:n, D + cb * 512:D + (cb + 1) * 512],
                                                     start=(m == 0), stop=(m == 127)),
                     reads=[('dg', i), ('cb', i)], writes=[('ps', cb)])

        def final(k, r0, n):
            par = k % 2
            for cb in range(8):
                V(lambda e, cb=cb: e.scalar_tensor_tensor(out=ht[par][:n, cb * 512:(cb + 1) * 512],
                                                          in0=ht[par][:n, cb * 512:(cb + 1) * 512], scalar=float(ALPHA),
                                                          in1=C.PS[cb][:n, :], op0=ALU.mult, op1=ALU.add),
                  [('ht', par), ('ps', cb)], [('ht', par)])
            ln_rows(S, nc, L, n, ht[par], ('ht', par), L.g, L.b, lnout, 'lnout')
            S.dma('act', y_rows(T, r0, n), lnout[:n, :], reads=['lnout'])

        import os as _os
        tiles = row_tiles()[:int(_os.environ.get("KT", "99"))]
        for k, (r0, n) in enumerate(tiles):
            e2(k, r0, n)
            for m in range(128):
                pick(k, n, m)
                if m >= 1:
                    pickB(k, n, m - 1)
            pickB(k, n, 127)
            final(k, r0, n)
    S.barrier()
```
